# Optimizing a Trainium2 kernel written in Bass

```python
import jax, jax.numpy as jnp
from jax import lax
import numpy as np

D_MODEL = 1024
BATCH = 4
SEQ = 8192
DEPTH = 1
DEC_BATCH = 128
DEC_SEQ = 1
PAST_LEN = 16384
PAGE_SIZE = 128

N_HEADS = 8
Q_LORA = 256
KV_LORA = 128
QK_NOPE = 64
QK_ROPE = 32
V_HEAD = 64
MLA_WIDTH = N_HEADS * V_HEAD
ROPE_THETA = 10000.0
MLA_SCALE = (QK_NOPE + QK_ROPE) ** -0.5
CONV_WIDTH = 256
CONV_K = 3
N_MEM = 256
MEM_HEADS = 4
MEM_HEAD_DIM = 64
MEM_WIDTH = MEM_HEADS * MEM_HEAD_DIM
MEM_SCALE = MEM_HEAD_DIM ** -0.5
D_MIX = MLA_WIDTH + CONV_WIDTH + MEM_WIDTH
D_FF = 2816
EPS = 1e-6
Q_BLOCK = 128
IN_SPLIT_SIZES = (Q_LORA, KV_LORA, QK_ROPE, CONV_WIDTH, CONV_WIDTH, CONV_WIDTH, MEM_WIDTH)
IN_WIDTH = Q_LORA + KV_LORA + QK_ROPE + 3 * CONV_WIDTH + MEM_WIDTH

kernel_name = "hymba_mla_shortconv_memory_macaron_step"


def rmsnorm(x, g):
    xf = x.astype(jnp.float32)
    inv = lax.rsqrt(jnp.mean(xf * xf, axis=-1, keepdims=True) + EPS)
    return (xf * inv * g.astype(jnp.float32)).astype(x.dtype)


def swiglu(x, w_gate, w_up, w_down):
    return (jax.nn.silu(x @ w_gate) * (x @ w_up)) @ w_down


def rope_cos_sin(pos):
    inv_freq = ROPE_THETA ** (-jnp.arange(0, QK_ROPE, 2, dtype=jnp.float32) / QK_ROPE)
    ang = pos.astype(jnp.float32)[:, None] * inv_freq[None, :]
    return jnp.cos(ang), jnp.sin(ang)


def apply_rope(x, cos, sin):
    xf = x.astype(jnp.float32)
    x1, x2 = jnp.split(xf, 2, axis=-1)
    return jnp.concatenate([x1 * cos - x2 * sin, x1 * sin + x2 * cos], axis=-1).astype(x.dtype)


def pre_mix(x, cos, sin, g_ffn1, w1_gate, w1_up, w1_down, g_mix, w_in, g_q_lora, w_uq, g_qn, g_qr,
            g_kv_lora, g_kr, g_mq):
    x = x + 0.5 * swiglu(rmsnorm(x, g_ffn1), w1_gate, w1_up, w1_down)
    h = rmsnorm(x, g_mix)
    z = h @ w_in
    split_at = np.cumsum(IN_SPLIT_SIZES)[:-1].tolist()
    c_q, c_kv, k_r, u_in, gate_b, gate_c, m_q = jnp.split(z, split_at, axis=-1)
    B, T = x.shape[0], x.shape[1]
    q = (rmsnorm(c_q, g_q_lora) @ w_uq).reshape(B, T, N_HEADS, QK_NOPE + QK_ROPE)
    q_nope = rmsnorm(q[..., :QK_NOPE], g_qn)
    q_rope = apply_rope(rmsnorm(q[..., QK_NOPE:], g_qr), cos[:, None, :], sin[:, None, :])
    c_kv = rmsnorm(c_kv, g_kv_lora)
    k_rope = apply_rope(rmsnorm(k_r, g_kr), cos, sin)
    u = gate_c * u_in
    m_q = rmsnorm(m_q.reshape(B, T, MEM_HEADS, MEM_HEAD_DIM), g_mq)
    return x, q_nope, q_rope, c_kv, k_rope, u, gate_b, m_q


def latent_to_kv(c_kv, w_uk, w_uv, g_kn):
    k_nope = rmsnorm(jnp.einsum("bkl,lhd->bkhd", c_kv, w_uk), g_kn)
    v = jnp.einsum("bkl,lhd->bkhd", c_kv, w_uv)
    return k_nope, v


def mla_scores(q_nope, q_rope, k_nope, k_rope):
    s = jnp.einsum("bthd,bkhd->bhtk", q_nope, k_nope, preferred_element_type=jnp.float32)
    s = s + jnp.einsum("bthr,bkr->bhtk", q_rope, k_rope, preferred_element_type=jnp.float32)
    return s * MLA_SCALE


def mla_prompt(q_nope, q_rope, c_kv, k_rope, w_uk, w_uv, g_kn):
    B, S = q_nope.shape[0], q_nope.shape[1]
    k_nope, v = latent_to_kv(c_kv, w_uk, w_uv, g_kn)
    n_blk = S // Q_BLOCK
    qn_b = q_nope.reshape(B, n_blk, Q_BLOCK, N_HEADS, QK_NOPE).transpose(1, 0, 2, 3, 4)
    qr_b = q_rope.reshape(B, n_blk, Q_BLOCK, N_HEADS, QK_ROPE).transpose(1, 0, 2, 3, 4)
    key_pos = jnp.arange(S)

    def block(args):
        i, qn, qr = args
        s = mla_scores(qn, qr, k_nope, k_rope)
        q_pos = i * Q_BLOCK + jnp.arange(Q_BLOCK)
        mask = key_pos[None, :] <= q_pos[:, None]
        p = jax.nn.softmax(jnp.where(mask, s, -jnp.inf), axis=-1).astype(v.dtype)
        return jnp.einsum("bhqk,bkhd->bqhd", p, v)

    o = lax.map(block, (jnp.arange(n_blk), qn_b, qr_b))
    return o.transpose(1, 0, 2, 3, 4).reshape(B, S, MLA_WIDTH)


def mla_sample(q_nope, q_rope, c_kv_new, k_rope_new, cache_ckv, cache_krope, page_table, w_uk, w_uv, g_kn):
    B, T = q_nope.shape[0], q_nope.shape[1]

    def merge(carry, s, v):
        m, l, acc = carry
        m_new = jnp.maximum(m, jnp.max(s, axis=-1))
        corr = jnp.exp(m - m_new)
        p = jnp.exp(s - m_new[..., None])
        l_new = l * corr + jnp.sum(p, axis=-1)
        acc_new = acc * corr[..., None] + jnp.einsum("bhtk,bkhd->bhtd", p, v.astype(jnp.float32))
        return (m_new, l_new, acc_new)

    def page_step(carry, page_ids):
        ckv = cache_ckv[page_ids]
        kr = cache_krope[page_ids]
        k_nope, v = latent_to_kv(ckv, w_uk, w_uv, g_kn)
        s = mla_scores(q_nope, q_rope, k_nope, kr)
        return merge(carry, s, v), None

    init = (jnp.full((B, N_HEADS, T), -jnp.inf, jnp.float32),
            jnp.zeros((B, N_HEADS, T), jnp.float32),
            jnp.zeros((B, N_HEADS, T, V_HEAD), jnp.float32))
    carry, _ = lax.scan(page_step, init, page_table.T)
    k_nope, v = latent_to_kv(c_kv_new, w_uk, w_uv, g_kn)
    s = mla_scores(q_nope, q_rope, k_nope, k_rope_new)
    causal = jnp.arange(T)[None, :] <= jnp.arange(T)[:, None]
    m, l, acc = merge(carry, jnp.where(causal, s, -jnp.inf), v)
    o = (acc / l[..., None]).astype(q_nope.dtype)
    return o.transpose(0, 2, 1, 3).reshape(B, T, MLA_WIDTH)


def short_conv(u_ext, conv_w):
    T = u_ext.shape[1] - (CONV_K - 1)
    y = conv_w[0] * u_ext[:, 0:T]
    for k in range(1, CONV_K):
        y = y + conv_w[k] * u_ext[:, k:k + T]
    return y


def mem_kv(mem, g_mem, w_mem_k, w_mem_v, g_mk):
    B, N = mem.shape[0], mem.shape[1]
    hm = rmsnorm(mem, g_mem)
    k = rmsnorm((hm @ w_mem_k).reshape(B, N, MEM_HEADS, MEM_HEAD_DIM), g_mk)
    v = (hm @ w_mem_v).reshape(B, N, MEM_HEADS, MEM_HEAD_DIM)
    return k, v


def mem_attend(m_q, mem_k, mem_v):
    B, T = m_q.shape[0], m_q.shape[1]
    s = jnp.einsum("bthd,bnhd->bhtn", m_q, mem_k, preferred_element_type=jnp.float32) * MEM_SCALE
    p = jax.nn.softmax(s, axis=-1).astype(mem_v.dtype)
    return jnp.einsum("bhtn,bnhd->bthd", p, mem_v).reshape(B, T, MEM_WIDTH)


def post_mix(x, o_mla, y_conv, o_mem, g_out_mla, g_out_conv, g_out_mem, w_o, g_ffn2, w2_gate, w2_up, w2_down):
    o = jnp.concatenate([rmsnorm(o_mla, g_out_mla), rmsnorm(y_conv, g_out_conv), rmsnorm(o_mem, g_out_mem)], axis=-1)
    x = x + o @ w_o
    return x + 0.5 * swiglu(rmsnorm(x, g_ffn2), w2_gate, w2_up, w2_down)


def setup_inputs(seed: int = 0) -> dict:
    key = jax.random.key(seed)
    ks = iter(jax.random.split(key, 48))

    def nrm(shape, scale=1.0):
        return jax.random.normal(next(ks), shape, jnp.float32) * scale

    def gain(width):
        return 1.0 + nrm((DEPTH, width), 0.05)

    n_pages = PAST_LEN // PAGE_SIZE
    n_used = DEC_BATCH * n_pages
    n_phys = n_used + n_used // 4
    page_table = jax.random.permutation(next(ks), n_phys)[:n_used].reshape(DEC_BATCH, n_pages).astype(jnp.int32)
    return {
        "x_prompt": nrm((BATCH, SEQ, D_MODEL)),
        "mem_prompt": nrm((BATCH, N_MEM, D_MODEL)),
        "x_sample": nrm((DEC_BATCH, DEC_SEQ, D_MODEL)),
        "cache_ckv": nrm((DEPTH, n_phys, PAGE_SIZE, KV_LORA)),
        "cache_krope": nrm((DEPTH, n_phys, PAGE_SIZE, QK_ROPE)),
        "page_table": page_table,
        "state_conv": nrm((DEPTH, DEC_BATCH, CONV_K - 1, CONV_WIDTH)),
        "cache_mem_k": nrm((DEPTH, DEC_BATCH, N_MEM, MEM_HEADS, MEM_HEAD_DIM)),
        "cache_mem_v": nrm((DEPTH, DEC_BATCH, N_MEM, MEM_HEADS, MEM_HEAD_DIM)),
        "g_ffn1": gain(D_MODEL),
        "w1_gate": nrm((DEPTH, D_MODEL, D_FF), D_MODEL ** -0.5),
        "w1_up": nrm((DEPTH, D_MODEL, D_FF), D_MODEL ** -0.5),
        "w1_down": nrm((DEPTH, D_FF, D_MODEL), D_FF ** -0.5),
        "g_mix": gain(D_MODEL),
        "w_in": nrm((DEPTH, D_MODEL, IN_WIDTH), D_MODEL ** -0.5),
        "g_q_lora": gain(Q_LORA),
        "w_uq": nrm((DEPTH, Q_LORA, N_HEADS * (QK_NOPE + QK_ROPE)), Q_LORA ** -0.5),
        "g_qn": gain(QK_NOPE),
        "g_qr": gain(QK_ROPE),
        "g_kv_lora": gain(KV_LORA),
        "w_uk": nrm((DEPTH, KV_LORA, N_HEADS, QK_NOPE), KV_LORA ** -0.5),
        "w_uv": nrm((DEPTH, KV_LORA, N_HEADS, V_HEAD), KV_LORA ** -0.5),
        "g_kn": gain(QK_NOPE),
        "g_kr": gain(QK_ROPE),
        "conv_w": nrm((DEPTH, CONV_K, CONV_WIDTH), CONV_K ** -0.5),
        "g_mem": gain(D_MODEL),
        "w_mem_k": nrm((DEPTH, D_MODEL, MEM_WIDTH), D_MODEL ** -0.5),
        "w_mem_v": nrm((DEPTH, D_MODEL, MEM_WIDTH), D_MODEL ** -0.5),
        "g_mq": gain(MEM_HEAD_DIM),
        "g_mk": gain(MEM_HEAD_DIM),
        "g_out_mla": gain(MLA_WIDTH),
        "g_out_conv": gain(CONV_WIDTH),
        "g_out_mem": gain(MEM_WIDTH),
        "w_o": nrm((DEPTH, D_MIX, D_MODEL), D_MIX ** -0.5),
        "g_ffn2": gain(D_MODEL),
        "w2_gate": nrm((DEPTH, D_MODEL, D_FF), D_MODEL ** -0.5),
        "w2_up": nrm((DEPTH, D_MODEL, D_FF), D_MODEL ** -0.5),
        "w2_down": nrm((DEPTH, D_FF, D_MODEL), D_FF ** -0.5),
    }


def reference(x_prompt, mem_prompt, x_sample, cache_ckv, cache_krope, page_table, state_conv, cache_mem_k,
              cache_mem_v, g_ffn1, w1_gate, w1_up, w1_down, g_mix, w_in, g_q_lora, w_uq, g_qn, g_qr, g_kv_lora,
              w_uk, w_uv, g_kn, g_kr, conv_w, g_mem, w_mem_k, w_mem_v, g_mq, g_mk, g_out_mla, g_out_conv,
              g_out_mem, w_o, g_ffn2, w2_gate, w2_up, w2_down):
    cos_p, sin_p = rope_cos_sin(jnp.arange(SEQ))
    cos_s, sin_s = rope_cos_sin(PAST_LEN + jnp.arange(DEC_SEQ))
    xp, xs = x_prompt, x_sample
    ckv_p_l, kr_p_l, conv_p_l, mk_p_l, mv_p_l = [], [], [], [], []
    ckv_s_l, kr_s_l, conv_s_l = [], [], []
    for l in range(DEPTH):
        pre_w = (g_ffn1[l], w1_gate[l], w1_up[l], w1_down[l], g_mix[l], w_in[l], g_q_lora[l], w_uq[l],
                 g_qn[l], g_qr[l], g_kv_lora[l], g_kr[l], g_mq[l])
        post_w = (g_out_mla[l], g_out_conv[l], g_out_mem[l], w_o[l], g_ffn2[l], w2_gate[l], w2_up[l], w2_down[l])

        xp, qn, qr, ckv, kr, u, gb, mq = pre_mix(xp, cos_p, sin_p, *pre_w)
        o_mla = mla_prompt(qn, qr, ckv, kr, w_uk[l], w_uv[l], g_kn[l])
        u_ext = jnp.pad(u, ((0, 0), (CONV_K - 1, 0), (0, 0)))
        y_conv = gb * short_conv(u_ext, conv_w[l])
        mk, mv = mem_kv(mem_prompt, g_mem[l], w_mem_k[l], w_mem_v[l], g_mk[l])
        o_mem = mem_attend(mq, mk, mv)
        xp = post_mix(xp, o_mla, y_conv, o_mem, *post_w)
        ckv_p_l.append(ckv)
        kr_p_l.append(kr)
        conv_p_l.append(u[:, -(CONV_K - 1):])
        mk_p_l.append(mk)
        mv_p_l.append(mv)

        xs, qn, qr, ckv, kr, u, gb, mq = pre_mix(xs, cos_s, sin_s, *pre_w)
        o_mla = mla_sample(qn, qr, ckv, kr, cache_ckv[l], cache_krope[l], page_table, w_uk[l], w_uv[l], g_kn[l])
        u_ext = jnp.concatenate([state_conv[l].astype(u.dtype), u], axis=1)
        y_conv = gb * short_conv(u_ext, conv_w[l])
        o_mem = mem_attend(mq, cache_mem_k[l], cache_mem_v[l])
        xs = post_mix(xs, o_mla, y_conv, o_mem, *post_w)
        ckv_s_l.append(ckv)
        kr_s_l.append(kr)
        conv_s_l.append(u_ext[:, -(CONV_K - 1):])

    return (xp, xs, jnp.stack(ckv_p_l), jnp.stack(kr_p_l), jnp.stack(conv_p_l), jnp.stack(mk_p_l),
            jnp.stack(mv_p_l), jnp.stack(ckv_s_l), jnp.stack(kr_s_l), jnp.stack(conv_s_l))
```

```python
import contextlib
import numpy as np
import concourse.bass as bass
import concourse.mybir as mybir
from concourse.bass_utils import run_bass_kernel_spmd

F32 = mybir.dt.float32
BF16 = mybir.dt.bfloat16
I32 = mybir.dt.int32
ALU = mybir.AluOpType
AF = mybir.ActivationFunctionType
AX = mybir.AxisListType

D = 1024
DFF = 2816
NM = DFF // 128
SEQ = 8192
NT = 512
NSLOT = 16
NOWN = 8
DEC = 128
EPS = 1e-6
INW = 1440
PAST = 16384
N_MEM = 256
MLA_SCALE = float(96 ** -0.5)
MEM_SCALE = float(64 ** -0.5)
NCONST = 1024
NS = 16
N_PHYS = 20480
NGAIN = 64

G_FFN1, G_MIX, G_MEM, G_KV, G_KR, G_MK, G_QL, G_Q96, G_KN = 0, 8, 16, 24, 25, 26, 27, 29, 30
G_CW0, G_CW1, G_CW2, G_OCONV, G_MQ, G_OMEM, G_OMLA, G_FFN2, G_FLA, G_FLB, G_EXPB, G_KN2 = 31, 33, 35, 37, 39, 40, 42, 46, 54, 55, 56, 57
C_ONES, C_BLK64, C_ROT32, C_ONES32, C_BLK96, C_ROT96, C_TRI, C_ONES64, C_ID = 0, 128, 256, 288, 320, 416, 512, 640, 704


class Buf:
    def __init__(self, name, t=None):
        self.name = name
        self.t = t
        self.last_w = None
        self.readers = []
        self.dma_sem = None
        self.dma_issued = 0

    def __getitem__(self, k):
        return self.t[k]


class Op:
    __slots__ = ("eng", "fn", "waits", "signaled", "sigval", "is_dma", "key", "dma_waits")

    def __init__(self, eng, fn, is_dma=False, key=None):
        self.eng = eng
        self.fn = fn
        self.waits = []
        self.dma_waits = []
        self.signaled = False
        self.sigval = None
        self.is_dma = is_dma
        self.key = key


class Prog:
    ENGS = ("tensor", "scalar", "vector", "gpsimd", "sync")

    def __init__(self, nc, stack):
        self.nc = nc
        self.stack = stack
        self.ops = {e: [] for e in self.ENGS}
        self.esem = {}
        self.dma_keys = []
        self.free_sems = []

    def _dep(self, op, prev):
        if prev is None or prev is op:
            return
        if prev.is_dma:
            op.dma_waits.append((prev.key, prev.key.dma_issued * 16))
            return
        if prev.eng == "tensor" and op.eng == "tensor":
            return
        op.waits.append(prev)
        prev.signaled = True

    def _track(self, op, reads, writes):
        for b in reads:
            self._dep(op, b.last_w)
        for b in writes:
            self._dep(op, b.last_w)
            for r in b.readers:
                self._dep(op, r)
        for b in reads:
            b.readers.append(op)
        for b in writes:
            b.last_w = op
            b.readers = []

    def op(self, eng, fn, reads=(), writes=()):
        o = Op(eng, fn)
        self._track(o, reads, writes)
        self.ops[eng].append(o)
        return o

    def dma(self, eng, fn, key, reads=(), writes=()):
        o = Op(eng, fn, is_dma=True, key=key)
        self._track(o, reads, writes)
        key.dma_issued += 1
        if key.dma_sem is None:
            key.dma_sem = self.stack.enter_context(self.nc.semaphore("d%d_%s" % (len(self.dma_keys), key.name)))
            self.dma_keys.append(key)
        self.ops[eng].append(o)
        return o

    def barrier(self):
        targets = []
        for e in self.ENGS:
            for o in reversed(self.ops[e]):
                if o.fn is not None and not o.is_dma:
                    targets.append(o)
                    o.signaled = True
                    break
        dma_targets = [(k, k.dma_issued * 16) for k in self.dma_keys]
        for e in self.ENGS:
            o = Op(e, None)
            o.waits = [t for t in targets if not (t.eng == e and e in ("tensor", "sync"))]
            o.dma_waits = list(dma_targets)
            self.ops[e].append(o)

    def emit(self):
        nc = self.nc
        for e in self.ENGS:
            self.esem[e] = self.stack.enter_context(nc.semaphore("e_" + e))
            n = 0
            for o in self.ops[e]:
                if o.signaled and not o.is_dma and o.fn is not None:
                    n += 1
                    o.sigval = n
        prog = self

        def run(e, eng):
            waited = {}
            for o in prog.ops[e]:
                need = {}
                for p in o.waits:
                    s = prog.esem[p.eng]
                    need[s] = max(need.get(s, 0), p.sigval)
                for k, v in o.dma_waits:
                    need[k.dma_sem] = max(need.get(k.dma_sem, 0), v)
                for s, v in need.items():
                    if waited.get(s, 0) < v:
                        eng.wait_ge(s, v)
                        waited[s] = v
                if o.fn is None:
                    continue
                ins = o.fn(eng)
                if o.is_dma:
                    ins.then_inc(o.key.dma_sem, 16)
                elif o.signaled:
                    ins.then_inc(prog.esem[e], 1)
            if e == "sync":
                for k in prog.dma_keys:
                    eng.wait_ge(k.dma_sem, k.dma_issued * 16)

        with nc.Block() as block:
            @block.tensor
            def _(eng):
                run("tensor", eng)

            @block.scalar
            def _(eng):
                run("scalar", eng)

            @block.vector
            def _(eng):
                run("vector", eng)

            @block.gpsimd
            def _(eng):
                run("gpsimd", eng)

            @block.sync
            def _(eng):
                run("sync", eng)


class Arena:
    BASE = 16512
    TOP = 229344

    def __init__(self, nc):
        self.nc = nc
        self.ptr = self.BASE
        self.n = 0
        self.peak = 0

    def alloc(self, name, shape, dt):
        esz = 2 if dt == BF16 else 4
        size = int(np.prod(shape[1:])) * esz
        size = (size + 63) // 64 * 64
        assert self.ptr + size <= self.TOP, "SBUF overflow at %s: %d + %d" % (name, self.ptr, size)
        self.n += 1
        h = self.nc.alloc_sbuf_tensor_at("%s_%d" % (name, self.n), list(shape), dt, offset=self.ptr)
        self.ptr += size
        self.peak = max(self.peak, self.ptr)
        return Buf(name, h)

    def mark(self):
        return self.ptr

    def reset(self, m):
        self.ptr = m


def build_program(do_att=True, do_post=True, do_sample=True):
    nc = bass.Bass("TRN2", target_bir_lowering=False)
    stack = contextlib.ExitStack()
    P = Prog(nc, stack)
    A = Arena(nc)

    def dram_in(name, shape, dt=F32):
        return Buf(name, nc.dram_tensor(name, list(shape), dt, kind="ExternalInput").ap())

    def dram_out(name, shape, dt=F32):
        return Buf(name, nc.dram_tensor(name, list(shape), dt, kind="ExternalOutput").ap())

    def dram_tmp(name, shape, dt):
        return Buf(name, nc.dram_tensor(name, list(shape), dt).ap())

    sb = A.alloc

    xT = dram_in("xT", (D, NSLOT * NT))
    memT = dram_in("memT", (D, N_MEM))
    ropeC = dram_in("ropeC", (32, NSLOT * NT))
    ropeS = dram_in("ropeS", (32, NSLOT * NT))
    w1g = dram_in("w1_gate", (D, DFF))
    w1u = dram_in("w1_up", (D, DFF))
    w1d = dram_in("w1_down", (DFF, D))
    w2g = dram_in("w2_gate", (D, DFF))
    w2u = dram_in("w2_up", (D, DFF))
    w2d = dram_in("w2_down", (DFF, D))
    w_in = dram_in("w_in", (D, INW))
    w_uq = dram_in("w_uq", (256, 768))
    w_uk = dram_in("w_uk", (128, 512))
    w_uv = dram_in("w_uv", (128, 512))
    w_o = dram_in("w_o", (D, D))
    wmk = dram_in("w_mem_k", (D, 256))
    wmv = dram_in("w_mem_v", (D, 256))
    gains = dram_in("gains", (128, NGAIN))
    xsT = dram_in("xsT", (D, NS))
    ropeCs = dram_in("ropeCs", (32, NS))
    ropeSs = dram_in("ropeSs", (32, NS))
    stateT = dram_in("stateT", (256, 2, NS))
    ptT = dram_in("ptT", (128, NS), I32)
    cckv = dram_in("cache_ckv", (N_PHYS * 8, 2048))
    ckr = dram_in("cache_krope", (N_PHYS * 2, 2048))
    cmk = dram_in("cache_mem_k", (NS, N_MEM, 256))
    cmv = dram_in("cache_mem_v", (NS, N_MEM, 256))
    w_ukT = dram_in("w_ukT", (64, 8 * 128))
    consts = dram_in("consts", (128, NCONST))

    o_y = dram_out("o_y", (D, NOWN * NT))
    o_ckv = dram_out("o_ckv", (128, NSLOT * NT))
    o_kr = dram_out("o_kr", (32, NSLOT * NT))
    o_conv = dram_out("o_conv", (256, NSLOT * 2))
    o_mk = dram_out("o_mk", (256, N_MEM))
    o_mv = dram_out("o_mv", (256, N_MEM))
    o_ys = dram_out("o_ys", (D, NS))
    o_ckv_s = dram_out("o_ckv_s", (128, NS))
    o_kr_s = dram_out("o_kr_s", (32, NS))
    o_conv_s = dram_out("o_conv_s", (256, NS))
    o_state_s = dram_out("o_state_s", (256, NS))

    sc1 = [dram_tmp("sc1g", (D, DFF), BF16), dram_tmp("sc1u", (D, DFF), BF16), dram_tmp("sc1d", (DFF, D), BF16)]
    sc2 = [dram_tmp("sc2g", (D, DFF), BF16), dram_tmp("sc2u", (D, DFF), BF16), dram_tmp("sc2d", (DFF, D), BF16)]
    X1 = dram_tmp("X1", (D, NOWN * NT), F32)
    KN = dram_tmp("KN", (8, 64, NSLOT * NT), BF16)
    KR = dram_tmp("KR", (32, NSLOT * NT), BF16)
    VS = dram_tmp("VS", (NSLOT * NT, 512), BF16)
    QS = dram_tmp("QS", (8, 96, NOWN * NT), BF16)

    pb = [Buf("pb%d" % i, stack.enter_context(nc.psum_tensor("pb%d" % i, [128, 512], F32))) for i in range(8)]

    gn = sb("gn", (128, NGAIN), F32)
    cst = sb("cst", (128, NCONST), F32)
    cstb = sb("cstb", (128, NCONST), BF16)
    YC = sb("YC", (128, 2, NOWN * NT), BF16)
    OM = sb("OM", (128, 2, NOWN * NT), BF16)
    memKT = sb("memKT", (128, 2, N_MEM), BF16)
    memVA = sb("memVA", (128, 2, 4, 128), BF16)
    UB = sb("UB", (128, 2, NSLOT, 2), F32)
    xs = sb("xs", (128, 8, NS), F32)
    qabs = sb("qabs", (128, NS, 8), BF16)
    qrs = sb("qrs", (32, NS, 8), BF16)
    pnew = sb("pnew", (128, 8, NS), F32)
    ckvs = sb("ckvs", (128, NS), F32)
    YCs = sb("YCs", (128, 2, NS), BF16)
    OMs = sb("OMs", (128, 2, NS), BF16)
    mqs = sb("mqs", (128, 2, NS), F32)
    OAs = sb("OAs", (128, 4, NS), F32)
    onorm_s = sb("onorm_s", (128, 4, NS), BF16)
    persist_mark = A.mark()

    def load(eng, dst, dst_ap, src, src_ap):
        P.dma(eng, lambda e: e.dma_start(out=dst_ap, in_=src_ap), key=dst, reads=[src], writes=[dst])

    def load_w(dst, src, nk, m):
        v = src.t.rearrange("(c p) m -> p c m", p=128)
        step = 1024
        for c in range(nk):
            for m0 in range(0, m, step):
                m1 = min(m, m0 + step)
                load("gpsimd", dst, dst[:, c, m0:m1], src, v[:, c, m0:m1])

    def mm(out_buf, out_ap, lhsT_buf, lhsT_ap, rhs_buf, rhs_ap, start, stop):
        P.op("tensor", lambda e: e.matmul(out_ap, lhsT_ap, rhs_ap, start=start, stop=stop),
             reads=[lhsT_buf, rhs_buf], writes=[out_buf])

    def ew(eng, fn, reads, writes):
        P.op(eng, fn, reads=reads, writes=writes)

    load("sync", gn, gn[:, :], gains, gains[:, :])
    load("sync", cst, cst[:, :], consts, consts[:, :])
    ew("vector", lambda e: e.tensor_copy(out=cstb[:, :], in_=cst[:, :]), [cst], [cstb])
    ew("gpsimd", lambda e: e.memset(memVA[:, :, :, 64:128], 1.0), [], [memVA])
    ew("gpsimd", lambda e: e.memset(UB[:, :, :, :], 0.0), [], [UB])

    def cast_w(dst, src, rows, cols):
        step = 1408 if cols == DFF else 1024
        for r0 in range(0, rows, 128):
            for m0 in range(0, cols, step):
                P.dma("gpsimd", lambda e, r0=r0, m0=m0: e.dma_start(out=dst[r0:r0 + 128, m0:m0 + step],
                                                                    in_=src[r0:r0 + 128, m0:m0 + step]),
                      key=dst, reads=[], writes=[])

    for dst, src, r, c in ((sc1[0], w1g, D, DFF), (sc1[1], w1u, D, DFF), (sc1[2], w1d, DFF, D),
                           (sc2[0], w2g, D, DFF), (sc2[1], w2u, D, DFF), (sc2[2], w2d, DFF, D)):
        cast_w(dst, src, r, c)

    def alloc_common():
        T = {}
        T["hb"] = sb("hb", (128, 8, NT), BF16)
        T["ab"] = sb("ab", (128, NM, NT), BF16)
        T["std"] = sb("std", (128, NT), F32)
        T["rstd"] = sb("rstd", (128, NT), F32)
        T["t0"] = sb("t0", (128, NT), F32)
        T["t1"] = sb("t1", (128, NT), F32)
        T["t2"] = sb("t2", (128, NT), F32)
        T["sgb"] = [sb("sgb%d" % i, (128, NT), BF16) for i in range(2)]
        T["wgc"] = [sb("wgc%d" % i, (128, 8, 128), BF16) for i in range(2)]
        T["wuc"] = [sb("wuc%d" % i, (128, 8, 128), BF16) for i in range(2)]
        T["wdc"] = [sb("wdc%d" % i, (128, NM, 128), BF16) for i in range(2)]
        T["sq"] = T["ab"]
        T["pn"] = pb[6]
        return T

    pn = pb[6]
    pz = pb[7]

    def rms_rstd(T, srcs, src_bufs, ones_ap, rows, n, count):
        sqb, pn_, std, rstd = T["sq"], T["pn"], T["std"], T["rstd"]
        for i, (a, b) in enumerate(zip(srcs, src_bufs)):
            if b in pb:
                ew("scalar", lambda e, a=a, i=i: e.activation(out=sqb[0:rows, i, 0:n], in_=a, func=AF.Square), [b], [sqb])
            else:
                ew("gpsimd", lambda e, a=a, i=i: e.tensor_tensor(out=sqb[0:rows, i, 0:n], in0=a, in1=a, op=ALU.mult),
                   [b], [sqb])
        for i in range(len(srcs)):
            mm(pn_, pn_[0:rows, 0:n], cstb, ones_ap, sqb, sqb[0:rows, i, 0:n], i == 0, i == len(srcs) - 1)
        ew("scalar", lambda e: e.activation(out=std[0:rows, 0:n], in_=pn_[0:rows, 0:n], func=AF.Sqrt,
                                            bias=EPS, scale=1.0 / count), [pn_], [std])
        ew("vector", lambda e: e.reciprocal(out=rstd[0:rows, 0:n], in_=std[0:rows, 0:n]), [std], [rstd])

    def norm_apply(T, out_buf, out_ap, src_buf, src_ap, gcol, rows, n):
        rstd = T["rstd"]
        ew("vector", lambda e: e.scalar_tensor_tensor(out=out_ap, in0=src_ap, scalar=gn[0:rows, gcol:gcol + 1],
                                                      in1=rstd[0:rows, 0:n], op0=ALU.mult, op1=ALU.mult),
           [src_buf, gn, rstd], [out_buf])

    def rmsnorm_d(T, x, n, gbase):
        hb = T["hb"]
        rms_rstd(T, [x[:, c, 0:n] for c in range(8)], [x] * 8, cstb[:, C_ONES:C_ONES + 128], 128, n, float(D))
        for c in range(8):
            norm_apply(T, hb, hb[:, c, 0:n], x, x[:, c, 0:n], gbase + c, 128, n)

    def ffn_half(T, x, n, sc, gbase):
        hb, ab = T["hb"], T["ab"]
        sgv = sc[0].t.rearrange("(c p) m -> p c m", p=128)
        suv = sc[1].t.rearrange("(c p) m -> p c m", p=128)
        sdv = sc[2].t.rearrange("(m p) c -> p m c", p=128)
        rmsnorm_d(T, x, n, gbase)
        for m in range(NM):
            g, u, sg = pb[m % 2], pb[2 + m % 2], T["sgb"][m % 2]
            wg_, wu_ = T["wgc"][m % 2], T["wuc"][m % 2]
            P.dma("sync", lambda e, wg_=wg_, m=m: e.dma_start(out=wg_[:, :, :], in_=sgv[:, :, m * 128:(m + 1) * 128]),
                  key=wg_, reads=[], writes=[wg_])
            P.dma("sync", lambda e, wu_=wu_, m=m: e.dma_start(out=wu_[:, :, :], in_=suv[:, :, m * 128:(m + 1) * 128]),
                  key=wu_, reads=[], writes=[wu_])
            for c in range(8):
                mm(g, g[:, 0:n], wg_, wg_[:, c, :], hb, hb[:, c, 0:n], c == 0, c == 7)
            for c in range(8):
                mm(u, u[:, 0:n], wu_, wu_[:, c, :], hb, hb[:, c, 0:n], c == 0, c == 7)
            ew("scalar", lambda e, g=g, sg=sg: e.activation(out=sg[:, 0:n], in_=g[:, 0:n], func=AF.Silu), [g], [sg])
            ew("vector", lambda e, u=u, sg=sg, m=m: e.tensor_tensor(out=ab[:, m, 0:n], in0=sg[:, 0:n], in1=u[:, 0:n],
                                                                    op=ALU.mult), [sg, u], [ab])
        for c in range(8):
            y = pb[4 + c % 2]
            wd_ = T["wdc"][c % 2]
            P.dma("sync", lambda e, wd_=wd_, c=c: e.dma_start(out=wd_[:, :, :], in_=sdv[:, :, c * 128:(c + 1) * 128]),
                  key=wd_, reads=[], writes=[wd_])
            for m in range(NM):
                mm(y, y[:, 0:n], wd_, wd_[:, m, :], ab, ab[:, m, 0:n], m == 0, m == NM - 1)
            ew("vector", lambda e, y=y, c=c: e.scalar_tensor_tensor(out=x[:, c, 0:n], in0=y[:, 0:n], scalar=0.5,
                                                                    in1=x[:, c, 0:n], op0=ALU.mult, op1=ALU.add),
               [y, x], [x])

    def proj(T, out_ps, rows, n, wbuf, col0):
        hb = T["hb"]
        for c in range(8):
            mm(out_ps, out_ps[0:rows, 0:n], wbuf, wbuf[:, c, col0:col0 + rows], hb, hb[:, c, 0:n], c == 0, c == 7)

    m0 = A.mark()
    T0 = alloc_common()
    wmk_sb = sb("wmk_sb", (128, 8, 256), BF16)
    wmv_sb = sb("wmv_sb", (128, 8, 256), BF16)
    xm = sb("xm", (128, 8, N_MEM), F32)
    mko = [sb("mko%d" % i, (128, N_MEM), F32) for i in range(2)]
    mvo = sb("mvo", (128, 2, N_MEM), F32)
    load_w(wmk_sb, wmk, 8, 256)
    load_w(wmv_sb, wmv, 8, 256)
    load("sync", xm, xm[:, :, :], memT, memT.t.rearrange("(c p) t -> p c t", p=128))
    rmsnorm_d(T0, xm, N_MEM, G_MEM)
    for j in range(2):
        proj(T0, pz, 128, N_MEM, wmk_sb, 128 * j)
        ew("scalar", lambda e: e.copy(out=T0["t0"][:, 0:N_MEM], in_=pz[:, 0:N_MEM]), [pz], [T0["t0"]])
        rms_rstd(T0, [T0["t0"][:, 0:N_MEM]], [T0["t0"]], cstb[:, C_BLK64:C_BLK64 + 128], 128, N_MEM, 64.0)
        norm_apply(T0, mko[j], mko[j][:, 0:N_MEM], T0["t0"], T0["t0"][:, 0:N_MEM], G_MK, 128, N_MEM)
        P.dma("sync", lambda e, j=j: e.dma_start(out=o_mk[j * 128:(j + 1) * 128, :], in_=mko[j][:, 0:N_MEM]),
              key=mko[j], reads=[mko[j]], writes=[])
        ew("gpsimd", lambda e, j=j: e.tensor_copy(out=memKT[:, j, :], in_=mko[j][:, 0:N_MEM]), [mko[j]], [memKT])
        proj(T0, pz, 128, N_MEM, wmv_sb, 128 * j)
        ew("scalar", lambda e, j=j: e.copy(out=mvo[:, j, 0:N_MEM], in_=pz[:, 0:N_MEM]), [pz], [mvo])
        P.dma("sync", lambda e, j=j: e.dma_start(out=o_mv[j * 128:(j + 1) * 128, :], in_=mvo[:, j, 0:N_MEM]),
              key=mvo, reads=[mvo], writes=[])
    for kb in range(2):
        for c in range(8):
            mm(pz, pz[:, 0:256], T0["hb"], T0["hb"][:, c, kb * 128:(kb + 1) * 128], wmv_sb, wmv_sb[:, c, :], c == 0, c == 7)
        ew("scalar", lambda e, kb=kb: e.copy(out=memVA[:, kb, :, 0:64],
                                             in_=pz[:, 0:256].rearrange("p (h d) -> p h d", h=4)), [pz], [memVA])
    P.barrier()
    A.reset(m0)

    T = alloc_common()
    win_sb = sb("win_sb", (128, 8, INW), BF16)
    wuq_sb = sb("wuq_sb", (128, 2, 768), BF16)
    wuk_sb = sb("wuk_sb", (128, 1, 512), BF16)
    wuv_sb = sb("wuv_sb", (128, 1, 512), BF16)
    load_w(win_sb, w_in, 8, INW)
    load_w(wuq_sb, w_uq, 2, 768)
    load_w(wuk_sb, w_uk, 1, 512)
    load_w(wuv_sb, w_uv, 1, 512)
    xb = sb("xb", (128, 8, NT), F32)
    ckv_o = sb("ckv_o", (128, NT), F32)
    ckvb = sb("ckvb", (128, NT), BF16)
    kr_o = sb("kr_o", (32, NT), F32)
    krb = sb("krb", (32, NT), BF16)
    uext = sb("uext", (128, 2, NT + 2), F32)
    rC = sb("rC", (32, NT), F32)
    rS = sb("rS", (32, NT), F32)
    rC96 = sb("rC96", (96, NT), F32)
    rS96 = sb("rS96", (96, NT), F32)
    knb = [sb("knb%d" % i, (128, NT), BF16) for i in range(2)]
    PZ = [pb[7], pb[5]]
    CS = [{"sq": sb("csq0", (128, 2, NT), BF16), "pn": pb[6], "std": T["std"], "rstd": T["rstd"]},
          {"sq": sb("csq1", (128, 2, NT), BF16), "pn": pb[4], "std": sb("cstd1", (128, NT), F32), "rstd": sb("crstd1", (128, NT), F32)}]
    vb = [sb("vb%d" % i, (128, 512), BF16) for i in range(2)]
    cq = sb("cq", (128, 2, NT), F32)
    cqn = sb("cqn", (128, 2, NT), BF16)
    qn = sb("qn", (96, NT), F32)
    qb16 = [sb("qb16%d" % i, (96, NT), BF16) for i in range(2)]
    gbt = sb("gbt", (128, 2, NT), F32)
    mqn = sb("mqn", (128, 2, NT), BF16)
    ptm = [sb("ptm%d" % i, (128, NT), BF16) for i in range(2)]
    rec = sb("rec", (128, NT), F32)
    t0, t1, t2 = T["t0"], T["t1"], T["t2"]
    ew("gpsimd", lambda e: e.memset(rC96[0:64, :], 1.0), [], [rC96])
    ew("gpsimd", lambda e: e.memset(rS96[0:64, :], 0.0), [], [rS96])

    xv = xT.t.rearrange("(c p) t -> p c t", p=128)
    x1v = X1.t.rearrange("(c p) t -> p c t", p=128)

    def phase1(s):
        own = s < NOWN
        n = NT
        x = xb
        c0 = s * NT
        load("sync", x, x[:, :, :], xT, xv[:, :, c0:c0 + NT])
        load("sync", rC, rC[:, :], ropeC, ropeC[:, c0:c0 + NT])
        load("sync", rS, rS[:, :], ropeS, ropeS[:, c0:c0 + NT])
        if own:
            load("sync", rC96, rC96[64:96, :], ropeC, ropeC[:, c0:c0 + NT])
            load("sync", rS96, rS96[64:96, :], ropeS, ropeS[:, c0:c0 + NT])
        ffn_half(T, x, n, sc1, G_FFN1)
        if own:
            P.dma("sync", lambda e: e.dma_start(out=x1v[:, :, c0:c0 + NT], in_=x[:, :, :]), key=x, reads=[x], writes=[])
        rmsnorm_d(T, x, n, G_MIX)
        pzA, pzB = PZ
        proj(T, pzA, 128, n, win_sb, 256)
        rms_rstd(CS[0], [pzA[:, 0:n]], [pzA], cstb[:, C_ONES:C_ONES + 128], 128, n, 128.0)
        norm_apply(CS[0], ckv_o, ckv_o[:, 0:n], pzA, pzA[:, 0:n], G_KV, 128, n)
        P.dma("sync", lambda e: e.dma_start(out=o_ckv[:, c0:c0 + NT], in_=ckv_o[:, 0:n]), key=ckv_o, reads=[ckv_o], writes=[])
        ew("gpsimd", lambda e: e.tensor_copy(out=ckvb[:, 0:n], in_=ckv_o[:, 0:n]), [ckv_o], [ckvb])
        proj(T, pzB, 32, n, win_sb, 384)
        rms_rstd(CS[1], [pzB[0:32, 0:n]], [pzB], cstb[0:32, C_ONES32:C_ONES32 + 32], 32, n, 32.0)
        norm_apply(CS[1], t2, t2[0:32, 0:n], pzB, pzB[0:32, 0:n], G_KR, 32, n)
        mm(pzB, pzB[0:32, 0:n], cst, cst[0:32, C_ROT32:C_ROT32 + 32], t2, t2[0:32, 0:n], True, True)
        ew("vector", lambda e: e.tensor_tensor(out=t1[0:32, 0:n], in0=pzB[0:32, 0:n], in1=rS[0:32, 0:n], op=ALU.mult),
           [pzB, rS], [t1])
        ew("gpsimd", lambda e: e.tensor_tensor(out=t2[0:32, 0:n], in0=t2[0:32, 0:n], in1=rC[0:32, 0:n], op=ALU.mult),
           [t2, rC], [t2])
        ew("vector", lambda e: e.tensor_tensor(out=kr_o[0:32, 0:n], in0=t1[0:32, 0:n], in1=t2[0:32, 0:n], op=ALU.add),
           [t1, t2], [kr_o])
        P.dma("sync", lambda e: e.dma_start(out=o_kr[:, c0:c0 + NT], in_=kr_o[0:32, 0:n]), key=kr_o, reads=[kr_o], writes=[])
        ew("gpsimd", lambda e: e.tensor_copy(out=krb[0:32, 0:n], in_=kr_o[0:32, 0:n]), [kr_o], [krb])
        P.dma("sync", lambda e: e.dma_start(out=KR[:, c0:c0 + NT], in_=krb[0:32, 0:n]), key=krb, reads=[krb], writes=[])
        for hp in range(4):
            k2 = hp % 2
            pz_, cs_, kb_ = PZ[k2], CS[k2], knb[k2]
            mm(pz_, pz_[:, 0:n], wuk_sb, wuk_sb[:, 0, hp * 128:(hp + 1) * 128], ckvb, ckvb[:, 0:n], True, True)
            rms_rstd(cs_, [pz_[:, 0:n]], [pz_], cstb[:, C_BLK64:C_BLK64 + 128], 128, n, 64.0)
            norm_apply(cs_, kb_, kb_[:, 0:n], pz_, pz_[:, 0:n], G_KN2, 128, n)
            for hl in range(2):
                P.dma("sync", lambda e, hp=hp, hl=hl, kb_=kb_: e.dma_start(out=KN[2 * hp + hl, :, c0:c0 + NT],
                                                                           in_=kb_[hl * 64:(hl + 1) * 64, 0:n]),
                      key=kb_, reads=[kb_], writes=[])
        for tb in range(n // 128):
            v_ = vb[tb % 2]
            pz_ = PZ[tb % 2]
            mm(pz_, pz_[:, 0:512], ckvb, ckvb[:, tb * 128:(tb + 1) * 128], wuv_sb, wuv_sb[:, 0, :], True, True)
            ew("scalar", lambda e, v_=v_, pz_=pz_: e.copy(out=v_[:, :], in_=pz_[:, 0:512]), [pz_], [v_])
            P.dma("sync", lambda e, v_=v_, tb=tb: e.dma_start(out=VS[c0 + tb * 128:c0 + (tb + 1) * 128, :], in_=v_[:, :]),
                  key=v_, reads=[v_], writes=[])
        for j in range(2):
            proj(T, pzA, 128, n, win_sb, 416 + 128 * j)
            ew("scalar", lambda e: e.copy(out=t0[:, 0:n], in_=pzA[:, 0:n]), [pzA], [t0])
            proj(T, pzB, 128, n, win_sb, 928 + 128 * j)
            ew("vector", lambda e, j=j: e.tensor_tensor(out=uext[:, j, 2:n + 2], in0=t0[:, 0:n], in1=pzB[:, 0:n], op=ALU.mult),
               [t0, pzB], [uext])
        ew("gpsimd", lambda e: e.tensor_copy(out=UB[:, :, s, :], in_=uext[:, :, n:n + 2]), [uext], [UB])
        for j in range(2):
            P.dma("sync", lambda e, j=j: e.dma_start(out=o_conv[j * 128:(j + 1) * 128, 2 * s:2 * s + 2],
                                                     in_=uext[:, j, n:n + 2]), key=uext, reads=[uext], writes=[])
        if not own:
            return
        j_own = s
        if j_own == 0:
            ew("vector", lambda e: e.tensor_scalar(out=uext[:, :, 0:2], in0=UB[:, :, 8, :], scalar1=gn[:, G_FLB:G_FLB + 1],
                                                   scalar2=None, op0=ALU.mult), [UB, gn], [uext])
        else:
            ew("vector", lambda e: e.tensor_scalar(out=uext[:, :, 0:2], in0=UB[:, :, 8 + j_own, :],
                                                   scalar1=gn[:, G_FLB:G_FLB + 1], scalar2=None, op0=ALU.mult),
               [UB, gn], [uext])
            ew("vector", lambda e: e.scalar_tensor_tensor(out=uext[:, :, 0:2], in0=UB[:, :, 8 + j_own - 1, :],
                                                          scalar=gn[:, G_FLA:G_FLA + 1], in1=uext[:, :, 0:2],
                                                          op0=ALU.mult, op1=ALU.add), [UB, gn, uext], [uext])
        for j in range(2):
            pz_ = PZ[j]
            tt = (t0, t1)[j]
            proj(T, pz_, 128, n, win_sb, 672 + 128 * j)
            ew("gpsimd", lambda e, j=j, tt=tt: e.tensor_scalar(out=tt[:, 0:n], in0=uext[:, j, 0:n], scalar1=gn[:, G_CW0 + j:G_CW0 + j + 1],
                                                               scalar2=None, op0=ALU.mult), [uext, gn], [tt])
            ew("vector", lambda e, j=j, tt=tt: e.scalar_tensor_tensor(out=tt[:, 0:n], in0=uext[:, j, 1:n + 1],
                                                                      scalar=gn[:, G_CW1 + j:G_CW1 + j + 1], in1=tt[:, 0:n],
                                                                      op0=ALU.mult, op1=ALU.add), [uext, gn, tt], [tt])
            ew("vector", lambda e, j=j, tt=tt: e.scalar_tensor_tensor(out=tt[:, 0:n], in0=uext[:, j, 2:n + 2],
                                                                      scalar=gn[:, G_CW2 + j:G_CW2 + j + 1], in1=tt[:, 0:n],
                                                                      op0=ALU.mult, op1=ALU.add), [uext, gn, tt], [tt])
            ew("vector", lambda e, j=j, tt=tt, pz_=pz_: e.tensor_tensor(out=gbt[:, j, 0:n], in0=tt[:, 0:n], in1=pz_[:, 0:n], op=ALU.mult),
               [tt, pz_], [gbt])
        rms_rstd(CS[0], [gbt[:, 0, 0:n], gbt[:, 1, 0:n]], [gbt, gbt], cstb[:, C_ONES:C_ONES + 128], 128, n, 256.0)
        for j in range(2):
            norm_apply(CS[0], YC, YC[:, j, c0:c0 + n], gbt, gbt[:, j, 0:n], G_OCONV + j, 128, n)
        proj(T, pzA, 128, n, win_sb, 0)
        proj(T, pzB, 128, n, win_sb, 128)
        rms_rstd(CS[1], [pzA[:, 0:n], pzB[:, 0:n]], [pzA, pzB], cstb[:, C_ONES:C_ONES + 128], 128, n, 256.0)
        norm_apply(CS[1], cqn, cqn[:, 0, 0:n], pzA, pzA[:, 0:n], G_QL, 128, n)
        norm_apply(CS[1], cqn, cqn[:, 1, 0:n], pzB, pzB[:, 0:n], G_QL + 1, 128, n)
        for h in range(8):
            k2 = h % 2
            pz_, cs_, qb_ = PZ[k2], CS[k2], qb16[k2]
            qn_, tq_ = (qn, t2)[k2], (t1, t0)[k2]
            for j in range(2):
                mm(pz_, pz_[0:96, 0:n], wuq_sb, wuq_sb[:, j, h * 96:(h + 1) * 96], cqn, cqn[:, j, 0:n], j == 0, j == 1)
            rms_rstd(cs_, [pz_[0:96, 0:n]], [pz_], cstb[0:96, C_BLK96:C_BLK96 + 96], 96, n, 1.0)
            norm_apply(cs_, qn_, qn_[0:96, 0:n], pz_, pz_[0:96, 0:n], G_Q96, 96, n)
            mm(pz_, pz_[0:96, 0:n], cst, cst[0:96, C_ROT96:C_ROT96 + 96], qn_, qn_[0:96, 0:n], True, True)
            ew("vector", lambda e, pz_=pz_, tq_=tq_: e.tensor_tensor(out=tq_[0:96, 0:n], in0=pz_[0:96, 0:n], in1=rS96[0:96, 0:n], op=ALU.mult),
               [pz_, rS96], [tq_])
            ew("gpsimd", lambda e, qn_=qn_: e.tensor_tensor(out=qn_[0:96, 0:n], in0=qn_[0:96, 0:n], in1=rC96[0:96, 0:n], op=ALU.mult),
               [qn_, rC96], [qn_])
            ew("vector", lambda e, qb_=qb_, tq_=tq_, qn_=qn_: e.tensor_tensor(out=qb_[0:96, 0:n], in0=tq_[0:96, 0:n], in1=qn_[0:96, 0:n], op=ALU.add),
               [tq_, qn_], [qb_])
            P.dma("sync", lambda e, h=h, qb_=qb_: e.dma_start(out=QS[h, :, c0:c0 + NT], in_=qb_[0:96, 0:n]),
                  key=qb_, reads=[qb_], writes=[])
        for j in range(2):
            pz_, cs_ = PZ[j], CS[j]
            proj(T, pz_, 128, n, win_sb, 1184 + 128 * j)
            rms_rstd(cs_, [pz_[:, 0:n]], [pz_], cstb[:, C_BLK64:C_BLK64 + 128], 128, n, 64.0)
            norm_apply(cs_, mqn, mqn[:, j, 0:n], pz_, pz_[:, 0:n], G_MQ, 128, n)
        for h in range(4):
            j, r0 = h // 2, (h % 2) * 64
            po = pb[4 + h % 2]
            for kb in range(2):
                ps_ = pb[2 * (h % 2) + kb]
                pt_ = ptm[kb]
                mm(ps_, ps_[:, 0:n], memKT, memKT[r0:r0 + 64, j, kb * 128:(kb + 1) * 128], mqn, mqn[r0:r0 + 64, j, 0:n], True, True)
                ew("scalar", lambda e, ps_=ps_, pt_=pt_: e.activation(out=pt_[:, 0:n], in_=ps_[:, 0:n], func=AF.Exp, scale=MEM_SCALE),
                   [ps_], [pt_])
                mm(po, po[:, 0:n], memVA, memVA[:, kb, h, :], pt_, pt_[:, 0:n], kb == 0, kb == 1)
            ew("vector", lambda e, po=po: e.reciprocal(out=rec[64:128, 0:n], in_=po[64:128, 0:n]), [po], [rec])
            ew("vector", lambda e, po=po, r0=r0, j=j: e.tensor_tensor(out=gbt[r0:r0 + 64, j, 0:n], in0=po[0:64, 0:n],
                                                                      in1=rec[64:128, 0:n], op=ALU.mult), [po, rec], [gbt])
        rms_rstd(CS[0], [gbt[:, 0, 0:n], gbt[:, 1, 0:n]], [gbt, gbt], cstb[:, C_ONES:C_ONES + 128], 128, n, 256.0)
        for j in range(2):
            norm_apply(CS[0], OM, OM[:, j, c0:c0 + n], gbt, gbt[:, j, 0:n], G_OMEM + j, 128, n)

    for s in list(range(NOWN, NSLOT)) + list(range(NOWN)):
        phase1(s)

    wukT_sb = sb("wukT_sb", (64, 1, 8 * 128), BF16)
    kfull = sb("kfull", (96, NS), F32)
    qfs = sb("qfs", (96, NS), F32)
    stt = sb("stt", (128, 2, 2, NS), F32)
    qg = sb("qg", (64, NS), BF16)

    def sampleA():
        n = NS
        x = xs
        P.dma("gpsimd", lambda e: e.dma_start(out=wukT_sb[:, 0, :], in_=w_ukT[:, :]), key=wukT_sb, reads=[], writes=[wukT_sb])
        load("sync", x, x[:, :, :], xsT, xsT.t.rearrange("(c p) t -> p c t", p=128))
        load("sync", rC, rC[:, 0:n], ropeCs, ropeCs[:, :])
        load("sync", rS, rS[:, 0:n], ropeSs, ropeSs[:, :])
        load("sync", rC96, rC96[64:96, 0:n], ropeCs, ropeCs[:, :])
        load("sync", rS96, rS96[64:96, 0:n], ropeSs, ropeSs[:, :])
        load("sync", stt, stt[:, :, :, :], stateT, stateT.t.rearrange("(c p) k t -> p c k t", p=128))
        for j in range(2):
            P.dma("sync", lambda e, j=j: e.dma_start(out=o_state_s[j * 128:(j + 1) * 128, :], in_=stt[:, j, 1, :]),
                  key=stt, reads=[stt], writes=[])
        ffn_half(T, x, n, sc1, G_FFN1)
        rmsnorm_d(T, x, n, G_MIX)
        proj(T, pz, 128, n, win_sb, 256)
        ew("scalar", lambda e: e.copy(out=t0[:, 0:n], in_=pz[:, 0:n]), [pz], [t0])
        rms_rstd(T, [t0[:, 0:n]], [t0], cstb[:, C_ONES:C_ONES + 128], 128, n, 128.0)
        norm_apply(T, ckvs, ckvs[:, 0:n], t0, t0[:, 0:n], G_KV, 128, n)
        P.dma("sync", lambda e: e.dma_start(out=o_ckv_s[:, :], in_=ckvs[:, 0:n]), key=ckvs, reads=[ckvs], writes=[])
        ew("gpsimd", lambda e: e.tensor_copy(out=ckvb[:, 0:n], in_=ckvs[:, 0:n]), [ckvs], [ckvb])
        proj(T, pz, 32, n, win_sb, 384)
        ew("scalar", lambda e: e.copy(out=t1[0:32, 0:n], in_=pz[0:32, 0:n]), [pz], [t1])
        rms_rstd(T, [t1[0:32, 0:n]], [t1], cstb[0:32, C_ONES32:C_ONES32 + 32], 32, n, 32.0)
        norm_apply(T, t2, t2[0:32, 0:n], t1, t1[0:32, 0:n], G_KR, 32, n)
        mm(pz, pz[0:32, 0:n], cst, cst[0:32, C_ROT32:C_ROT32 + 32], t2, t2[0:32, 0:n], True, True)
        ew("vector", lambda e: e.tensor_tensor(out=t1[0:32, 0:n], in0=pz[0:32, 0:n], in1=rS[0:32, 0:n], op=ALU.mult),
           [pz, rS], [t1])
        ew("gpsimd", lambda e: e.tensor_tensor(out=t2[0:32, 0:n], in0=t2[0:32, 0:n], in1=rC[0:32, 0:n], op=ALU.mult),
           [t2, rC], [t2])
        ew("vector", lambda e: e.tensor_tensor(out=kr_o[0:32, 0:n], in0=t1[0:32, 0:n], in1=t2[0:32, 0:n], op=ALU.add),
           [t1, t2], [kr_o])
        P.dma("sync", lambda e: e.dma_start(out=o_kr_s[:, :], in_=kr_o[0:32, 0:n]), key=kr_o, reads=[kr_o], writes=[])
        ew("vector", lambda e: e.tensor_copy(out=kfull[64:96, 0:n], in_=kr_o[0:32, 0:n]), [kr_o], [kfull])
        for j in range(2):
            proj(T, pz, 128, n, win_sb, 416 + 128 * j)
            ew("scalar", lambda e: e.copy(out=t0[:, 0:n], in_=pz[:, 0:n]), [pz], [t0])
            proj(T, pz, 128, n, win_sb, 928 + 128 * j)
            ew("vector", lambda e, j=j: e.tensor_tensor(out=uext[:, j, 0:n], in0=t0[:, 0:n], in1=pz[:, 0:n], op=ALU.mult),
               [t0, pz], [uext])
            P.dma("sync", lambda e, j=j: e.dma_start(out=o_conv_s[j * 128:(j + 1) * 128, :], in_=uext[:, j, 0:n]),
                  key=uext, reads=[uext], writes=[])
            proj(T, pz, 128, n, win_sb, 672 + 128 * j)
            ew("vector", lambda e, j=j: e.tensor_scalar(out=t0[:, 0:n], in0=stt[:, j, 0, :], scalar1=gn[:, G_CW0 + j:G_CW0 + j + 1],
                                                        scalar2=None, op0=ALU.mult), [stt, gn], [t0])
            ew("vector", lambda e, j=j: e.scalar_tensor_tensor(out=t0[:, 0:n], in0=stt[:, j, 1, :],
                                                               scalar=gn[:, G_CW1 + j:G_CW1 + j + 1], in1=t0[:, 0:n],
                                                               op0=ALU.mult, op1=ALU.add), [stt, gn, t0], [t0])
            ew("vector", lambda e, j=j: e.scalar_tensor_tensor(out=t0[:, 0:n], in0=uext[:, j, 0:n],
                                                               scalar=gn[:, G_CW2 + j:G_CW2 + j + 1], in1=t0[:, 0:n],
                                                               op0=ALU.mult, op1=ALU.add), [uext, gn, t0], [t0])
            ew("vector", lambda e, j=j: e.tensor_tensor(out=gbt[:, j, 0:n], in0=t0[:, 0:n], in1=pz[:, 0:n], op=ALU.mult),
               [t0, pz], [gbt])
        rms_rstd(T, [gbt[:, 0, 0:n], gbt[:, 1, 0:n]], [gbt, gbt], cstb[:, C_ONES:C_ONES + 128], 128, n, 256.0)
        for j in range(2):
            norm_apply(T, YCs, YCs[:, j, 0:n], gbt, gbt[:, j, 0:n], G_OCONV + j, 128, n)
        for j in range(2):
            proj(T, pz, 128, n, win_sb, 1184 + 128 * j)
            ew("scalar", lambda e: e.copy(out=t0[:, 0:n], in_=pz[:, 0:n]), [pz], [t0])
            rms_rstd(T, [t0[:, 0:n]], [t0], cstb[:, C_BLK64:C_BLK64 + 128], 128, n, 64.0)
            norm_apply(T, mqs, mqs[:, j, 0:n], t0, t0[:, 0:n], G_MQ, 128, n)
        for j in range(2):
            proj(T, pz, 128, n, win_sb, 128 * j)
            ew("scalar", lambda e, j=j: e.copy(out=cq[:, j, 0:n], in_=pz[:, 0:n]), [pz], [cq])
        rms_rstd(T, [cq[:, 0, 0:n], cq[:, 1, 0:n]], [cq, cq], cstb[:, C_ONES:C_ONES + 128], 128, n, 256.0)
        for j in range(2):
            norm_apply(T, cqn, cqn[:, j, 0:n], cq, cq[:, j, 0:n], G_QL + j, 128, n)
        psn = pb[0]
        for h in range(8):
            for j in range(2):
                mm(pz, pz[0:96, 0:n], wuq_sb, wuq_sb[:, j, h * 96:(h + 1) * 96], cqn, cqn[:, j, 0:n], j == 0, j == 1)
            ew("scalar", lambda e: e.copy(out=t1[0:96, 0:n], in_=pz[0:96, 0:n]), [pz], [t1])
            rms_rstd(T, [t1[0:96, 0:n]], [t1], cstb[0:96, C_BLK96:C_BLK96 + 96], 96, n, 1.0)
            norm_apply(T, qn, qn[0:96, 0:n], t1, t1[0:96, 0:n], G_Q96, 96, n)
            mm(pz, pz[0:96, 0:n], cst, cst[0:96, C_ROT96:C_ROT96 + 96], qn, qn[0:96, 0:n], True, True)
            ew("vector", lambda e: e.tensor_tensor(out=t1[0:96, 0:n], in0=pz[0:96, 0:n], in1=rS96[0:96, 0:n], op=ALU.mult),
               [pz, rS96], [t1])
            ew("gpsimd", lambda e: e.tensor_tensor(out=qn[0:96, 0:n], in0=qn[0:96, 0:n], in1=rC96[0:96, 0:n], op=ALU.mult),
               [qn, rC96], [qn])
            ew("vector", lambda e: e.tensor_tensor(out=qfs[0:96, 0:n], in0=t1[0:96, 0:n], in1=qn[0:96, 0:n], op=ALU.add),
               [t1, qn], [qfs])
            ew("vector", lambda e, h=h: e.tensor_copy(out=qrs[0:32, :, h], in_=qfs[64:96, 0:n]), [qfs], [qrs])
            ew("vector", lambda e: e.tensor_scalar(out=qg[0:64, 0:n], in0=qfs[0:64, 0:n], scalar1=gn[0:64, G_KN:G_KN + 1],
                                                   scalar2=None, op0=ALU.mult), [qfs, gn], [qg])
            mm(pz, pz[:, 0:n], wukT_sb, wukT_sb[0:64, 0, h * 128:(h + 1) * 128], qg, qg[0:64, 0:n], True, True)
            ew("scalar", lambda e, h=h: e.copy(out=qabs[:, :, h], in_=pz[:, 0:n]), [pz], [qabs])
            mm(pz, pz[0:64, 0:n], wuk_sb, wuk_sb[:, 0, h * 64:(h + 1) * 64], ckvb, ckvb[:, 0:n], True, True)
            ew("scalar", lambda e: e.copy(out=t1[0:64, 0:n], in_=pz[0:64, 0:n]), [pz], [t1])
            rms_rstd(T, [t1[0:64, 0:n]], [t1], cstb[0:64, C_ONES64:C_ONES64 + 64], 64, n, 64.0)
            norm_apply(T, kfull, kfull[0:64, 0:n], t1, t1[0:64, 0:n], G_KN, 64, n)
            ew("vector", lambda e: e.tensor_tensor(out=t2[0:96, 0:n], in0=qfs[0:96, 0:n], in1=kfull[0:96, 0:n], op=ALU.mult),
               [qfs, kfull], [t2])
            mm(psn, psn[:, h * NS:(h + 1) * NS], cst, cst[0:96, C_ONES:C_ONES + 128], t2, t2[0:96, 0:n], True, True)
        ew("scalar", lambda e: e.activation(out=pnew[:, :, :], in_=psn[:, 0:8 * NS].rearrange("p (h b) -> p h b", h=8),
                                            func=AF.Exp, scale=MLA_SCALE), [psn], [pnew])

    sampleA()
    P.barrier()
    print("phase1 SBUF peak bytes/partition:", A.peak - A.BASE)
    A.reset(persist_mark)


    def sampleB():
        mB = A.mark()
        wuk2 = sb("wuk2", (128, 1, 512), BF16)
        wuv2 = sb("wuv2", (128, 1, 512), BF16)
        load_w(wuk2, w_uk, 1, 512)
        load_w(wuv2, w_uv, 1, 512)
        pt_sb = sb("pt_sb", (128, NS), I32)
        idx8 = [sb("idx8_%d" % i, (128, 8), I32) for i in range(2)]
        idx2 = [sb("idx2_%d" % i, (128, 2), I32) for i in range(2)]
        G = [sb("G%d" % i, (128, 128, 128), BF16) for i in range(2)]
        GR = [sb("GR%d" % i, (128, 128, 32), BF16) for i in range(2)]
        ckvT = [sb("ckvT%d" % i, (128, 512), BF16) for i in range(2)]
        krT = [sb("krT%d" % i, (32, 512), BF16) for i in range(2)]
        sq = [sb("sq%d" % i, (128, 512), BF16) for i in range(2)]
        SS = sb("SS", (128, 128, 8), F32)
        inv = sb("inv", (128, 128, 8), F32)
        s1 = sb("s1", (128, 128, 8), F32)
        pT = sb("pT", (128, 128, 8), BF16)
        LP = sb("LP", (128, NS, 8), F32)
        load("sync", pt_sb, pt_sb[:, :], ptT, ptT[:, :])
        PA = [pb[0], pb[1]]
        PR = [pb[2], pb[3]]
        PKN = [pb[4], pb[5]]
        ptr_ = pb[6]
        pacc = pb[7]
        ptr_bf = ptr_.t[:, :].bitcast(BF16)
        identb = cstb[:, C_ID:C_ID + 128]
        for b in range(NS):
            k = b % 2
            g_, gr_ = G[k], GR[k]
            for s_ in range(8):
                ew("vector", lambda e, k=k, s_=s_, b=b: e.tensor_scalar(out=idx8[k][:, s_:s_ + 1], in0=pt_sb[:, b:b + 1], scalar1=8,
                                                                        scalar2=s_, op0=ALU.mult, op1=ALU.add), [pt_sb], [idx8[k]])
            for s_ in range(2):
                ew("vector", lambda e, k=k, s_=s_, b=b: e.tensor_scalar(out=idx2[k][:, s_:s_ + 1], in0=pt_sb[:, b:b + 1], scalar1=2,
                                                                        scalar2=s_, op0=ALU.mult, op1=ALU.add), [pt_sb], [idx2[k]])
            for s_ in range(8):
                P.dma("gpsimd", lambda e, k=k, s_=s_, g_=g_: e.indirect_dma_start(
                    out=g_[:, s_ * 16:(s_ + 1) * 16, :].rearrange("p t l -> p (t l)"), out_offset=None, in_=cckv[:, :],
                    in_offset=bass.IndirectOffsetOnAxis(ap=idx8[k][:, s_:s_ + 1], axis=0)), key=g_, reads=[idx8[k]], writes=[g_])
            for s_ in range(2):
                P.dma("gpsimd", lambda e, k=k, s_=s_, gr_=gr_: e.indirect_dma_start(
                    out=gr_[:, s_ * 64:(s_ + 1) * 64, :].rearrange("p t l -> p (t l)"), out_offset=None, in_=ckr[:, :],
                    in_offset=bass.IndirectOffsetOnAxis(ap=idx2[k][:, s_:s_ + 1], axis=0)), key=gr_, reads=[idx2[k]], writes=[gr_])
            for t4 in range(32):
                ct, kt_ = ckvT[t4 % 2], krT[t4 % 2]
                for i in range(4):
                    t = t4 * 4 + i
                    P.op("tensor", lambda e, t=t, i=i, g_=g_: e.transpose(ptr_bf[:, i * 128:(i + 1) * 128], g_[:, t, :], identb),
                         reads=[g_, cstb], writes=[ptr_])
                ew("vector", lambda e, ct=ct: e.tensor_copy(out=ct[:, :], in_=ptr_bf[:, 0:512]), [ptr_], [ct])
                for i in range(4):
                    t = t4 * 4 + i
                    P.op("tensor", lambda e, t=t, i=i, gr_=gr_: e.transpose(ptr_bf[0:32, 512 + i * 128:512 + (i + 1) * 128],
                                                                            gr_[:, t, :], identb),
                         reads=[gr_, cstb], writes=[ptr_])
                ew("vector", lambda e, kt_=kt_: e.tensor_copy(out=kt_[:, :], in_=ptr_bf[0:32, 512:1024]), [ptr_], [kt_])
                for i in range(4):
                    t = t4 * 4 + i
                    pk = PKN[t % 2]
                    sq_ = sq[t % 2]
                    mm(pk, pk[:, :], ct, ct[:, i * 128:(i + 1) * 128], wuk2, wuk2[:, 0, :], True, True)
                    ew("scalar", lambda e, pk=pk, sq_=sq_: e.activation(out=sq_[:, :], in_=pk[:, :], func=AF.Square), [pk], [sq_])
                    ew("vector", lambda e, sq_=sq_, t=t: e.tensor_reduce(out=SS[:, t, :], in_=sq_[:, :].rearrange("p (h d) -> p h d", h=8),
                                                                         axis=AX.X, op=ALU.add), [sq_], [SS])
                    pa, pr = PA[t // 64], PR[t // 64]
                    c8 = (t % 64) * 8
                    mm(pa, pa[:, c8:c8 + 8], ct, ct[:, i * 128:(i + 1) * 128], qabs, qabs[:, b, :], True, True)
                    mm(pr, pr[:, c8:c8 + 8], kt_, kt_[0:32, i * 128:(i + 1) * 128], qrs, qrs[0:32, b, :], True, True)
            SSf = SS[:, :, :].rearrange("p t h -> p (t h)")
            invf = inv[:, :, :].rearrange("p t h -> p (t h)")
            s1f = s1[:, :, :].rearrange("p t h -> p (t h)")
            pTf = pT[:, :, :].rearrange("p t h -> p (t h)")
            ew("scalar", lambda e, SSf=SSf, invf=invf: e.activation(out=invf, in_=SSf, func=AF.Sqrt, bias=EPS, scale=1.0 / 64), [SS], [inv])
            ew("vector", lambda e, invf=invf: e.reciprocal(out=invf, in_=invf), [inv], [inv])
            for hf in range(2):
                ew("vector", lambda e, hf=hf, invf=invf, s1f=s1f: e.tensor_tensor(out=s1f[:, hf * 512:(hf + 1) * 512], in0=PA[hf][:, :],
                                                                                  in1=invf[:, hf * 512:(hf + 1) * 512], op=ALU.mult),
                   [PA[hf], inv], [s1])
                ew("vector", lambda e, hf=hf, s1f=s1f: e.tensor_tensor(out=s1f[:, hf * 512:(hf + 1) * 512], in0=PR[hf][:, :],
                                                                       in1=s1f[:, hf * 512:(hf + 1) * 512], op=ALU.add),
                   [PR[hf], s1], [s1])
            ew("scalar", lambda e, s1f=s1f, pTf=pTf: e.activation(out=pTf, in_=s1f, func=AF.Exp, scale=MLA_SCALE), [s1], [pT])
            ew("vector", lambda e, b=b: e.tensor_reduce(out=LP[:, b, :], in_=pT[:, :, :].rearrange("p t h -> p h t"),
                                                        axis=AX.X, op=ALU.add), [pT], [LP])
            for t in range(128):
                mm(pacc, pacc[:, b * 8:(b + 1) * 8], g_, g_[:, t, :], pT, pT[:, t, :], t == 0, t == 127)
        lsum = pb[0]
        mm(lsum, lsum[:, 0:NS * 8], cst, cst[:, C_ONES:C_ONES + 128], LP, LP[:, :, :].rearrange("p b h -> p (b h)"), True, True)
        num = sb("num", (128, NS, 8), F32)
        den = sb("den", (128, NS, 8), F32)
        olat = sb("olat", (128, NS, 8), BF16)
        pnv = pnew[:, :, :].rearrange("p h b -> p b h")
        ew("vector", lambda e: e.tensor_tensor(out=num[:, :, :], in0=pnv, in1=ckvs[:, 0:NS].unsqueeze(2).to_broadcast([128, NS, 8]),
                                               op=ALU.mult), [pnew, ckvs], [num])
        ew("vector", lambda e: e.tensor_tensor(out=num[:, :, :], in0=pacc[:, 0:NS * 8].rearrange("p (b h) -> p b h", h=8),
                                               in1=num[:, :, :], op=ALU.add), [pacc, num], [num])
        ew("vector", lambda e: e.tensor_tensor(out=den[:, :, :], in0=lsum[:, 0:NS * 8].rearrange("p (b h) -> p b h", h=8),
                                               in1=pnv, op=ALU.add), [lsum, pnew], [den])
        ew("vector", lambda e: e.reciprocal(out=den[:, :, :], in_=den[:, :, :]), [den], [den])
        ew("vector", lambda e: e.tensor_tensor(out=olat[:, :, :], in0=num[:, :, :], in1=den[:, :, :], op=ALU.mult), [num, den], [olat])
        for h in range(8):
            po = pb[1 + h % 2]
            mm(po, po[0:64, 0:NS], wuv2, wuv2[:, 0, h * 64:(h + 1) * 64], olat, olat[:, :, h], True, True)
            r0, jc = (h % 2) * 64, h // 2
            ew("scalar", lambda e, po=po, r0=r0, jc=jc: e.copy(out=OAs[r0:r0 + 64, jc, :], in_=po[0:64, 0:NS]), [po], [OAs])
        Kb = [sb("Kb%d" % i, (128, 2, 256), F32) for i in range(2)]
        Vb = [sb("Vb%d" % i, (128, 2, 256), F32) for i in range(2)]
        Vbb = [sb("Vbb%d" % i, (128, 2, 256), BF16) for i in range(2)]
        qbc = sb("qbc", (128, 2, 128), F32)
        prod = sb("prod", (128, 2, 256), F32)
        scm = sb("scm", (128, 2, 4), F32)
        pm = sb("pm", (128, 2, 4), BF16)
        rden = sb("rden", (128, NS, 4), F32)
        omf = sb("omf", (128, 2, NS), F32)
        pq, pom, pdn = pb[3], pb[4], pb[5]
        ident = cst[:, C_ID:C_ID + 128]
        for b in range(NS):
            k = b % 2
            P.dma("sync", lambda e, k=k, b=b: e.dma_start(out=Kb[k][:, :, :], in_=cmk[b].rearrange("(blk p) f -> p blk f", p=128)),
                  key=Kb[k], reads=[], writes=[Kb[k]])
            P.dma("sync", lambda e, k=k, b=b: e.dma_start(out=Vb[k][:, :, :], in_=cmv[b].rearrange("(blk p) f -> p blk f", p=128)),
                  key=Vb[k], reads=[], writes=[Vb[k]])
            ew("gpsimd", lambda e, k=k: e.tensor_copy(out=Vbb[k][:, :, :], in_=Vb[k][:, :, :]), [Vb[k]], [Vbb[k]])
            ew("vector", lambda e, b=b: e.tensor_copy(out=qbc[:, :, :], in_=mqs[:, :, b:b + 1].to_broadcast([128, 2, 128])), [mqs], [qbc])
            for c in range(2):
                mm(pq, pq[:, c * 128:(c + 1) * 128], qbc, qbc[:, c, :], cst, ident, True, True)
            ew("vector", lambda e, k=k: e.tensor_tensor(out=prod[:, :, :], in0=Kb[k][:, :, :],
                                                        in1=pq[:, 0:256].unsqueeze(1).to_broadcast([128, 2, 256]), op=ALU.mult),
               [Kb[k], pq], [prod])
            ew("vector", lambda e: e.tensor_reduce(out=scm[:, :, :], in_=prod[:, :, :].rearrange("p k (h d) -> p k h d", h=4),
                                                   axis=AX.X, op=ALU.add), [prod], [scm])
            ew("scalar", lambda e: e.activation(out=pm[:, :, :], in_=scm[:, :, :], func=AF.Exp, scale=MEM_SCALE), [scm], [pm])
            for c in range(2):
                for blk in range(2):
                    mm(pom, pom[:, b * 4 + 2 * c:b * 4 + 2 * c + 2], Vbb[k], Vbb[k][:, blk, c * 128:(c + 1) * 128],
                       pm, pm[:, blk, 2 * c:2 * c + 2], blk == 0, blk == 1)
            for blk in range(2):
                mm(pdn, pdn[:, b * 4:(b + 1) * 4], cstb, cstb[:, C_ONES:C_ONES + 128], pm, pm[:, blk, :], blk == 0, blk == 1)
        ew("vector", lambda e: e.reciprocal(out=rden[:, :, :], in_=pdn[:, 0:NS * 4].rearrange("p (b h) -> p b h", h=4)), [pdn], [rden])
        pomv = pom[:, 0:NS * 4].rearrange("p (b h) -> p b h", h=4)
        for c in range(2):
            for hl in range(2):
                r0 = hl * 64
                ew("vector", lambda e, c=c, hl=hl, r0=r0: e.tensor_tensor(out=omf[r0:r0 + 64, c, :], in0=pomv[r0:r0 + 64, :, 2 * c + hl],
                                                                          in1=rden[r0:r0 + 64, :, 2 * c + hl], op=ALU.mult),
                   [pom, rden], [omf])
        TB = {"sq": sb("abB", (128, 8, NT), BF16), "std": sb("stdB", (128, NT), F32), "rstd": sb("rstdB", (128, NT), F32), "pn": pb[6]}
        rms_rstd(TB, [omf[:, 0, :], omf[:, 1, :]], [omf, omf], cstb[:, C_ONES:C_ONES + 128], 128, NS, 256.0)
        for j in range(2):
            norm_apply(TB, OMs, OMs[:, j, :], omf, omf[:, j, :], G_OMEM + j, 128, NS)
        rms_rstd(TB, [OAs[:, c, :] for c in range(4)], [OAs] * 4, cstb[:, C_ONES:C_ONES + 128], 128, NS, 512.0)
        for c in range(4):
            norm_apply(TB, onorm_s, onorm_s[:, c, :], OAs, OAs[:, c, :], G_OMLA + c, 128, NS)
        P.barrier()
        print("sampleB SBUF peak bytes/partition:", A.peak - A.BASE)
        A.reset(mB)

    if do_sample:
        sampleB()

    OA = sb("OA", (128, 4, NOWN * NT), F32)
    if do_att:
        KT = [sb("KT%d" % i, (96, NSLOT * NT), BF16) for i in range(2)]
        VA = [sb("VA%d" % i, (128, NSLOT * NT // 128, 128), BF16) for i in range(2)]
        QT = [sb("QT%d" % i, (96, NOWN * NT), BF16) for i in range(2)]
        pts = [sb("pts%d" % i, (128, NT), BF16) for i in range(4)]
        rec2 = sb("rec2", (128, NT), F32)
        for i in range(2):
            ew("gpsimd", lambda e, i=i: e.memset(VA[i][:, :, 64:128], 1.0), [], [VA[i]])
        vsv = VS.t.rearrange("(blk p) (h d) -> p blk h d", p=128, h=8)
        nblk = 0
        for h in range(8):
            kt, va, qt = KT[h % 2], VA[h % 2], QT[h % 2]
            P.dma("sync", lambda e, kt=kt, h=h: e.dma_start(out=kt[0:64, :], in_=KN[h, :, :]), key=kt, reads=[], writes=[kt])
            P.dma("sync", lambda e, kt=kt: e.dma_start(out=kt[64:96, :], in_=KR[:, :]), key=kt, reads=[], writes=[kt])
            for q4 in range(8):
                b0, b1 = q4 * 8, (q4 + 1) * 8
                P.dma("sync", lambda e, va=va, h=h, b0=b0, b1=b1: e.dma_start(out=va[:, b0:b1, 0:64], in_=vsv[:, b0:b1, h, :]),
                      key=va, reads=[], writes=[va])
            P.dma("sync", lambda e, qt=qt, h=h: e.dma_start(out=qt[0:96, :], in_=QS[h, :, :]), key=qt, reads=[], writes=[qt])
            for j in range(NOWN):
                tiles = [(8 + j, "flag")] + [(s, "full") for s in range(j)] + [(8 + s, "full") for s in range(j)] + [(j, "diag")]
                po = pb[4 + (h * NOWN + j) % 2]
                blocks = []
                for (s, kind) in tiles:
                    for kb in range(4):
                        blocks.append((s * NT + kb * 128, kb * 128 if kind == "diag" else 0, kind))
                nb = len(blocks)
                LAG = 2
                issued = []
                for i in range(nb + LAG):
                    if i < nb:
                        k0, q0, kind = blocks[i]
                        ncol = NT - q0
                        ps_ = pb[nblk % 4]
                        pt_ = pts[nblk % 4]
                        nblk += 1
                        issued.append((pt_, q0, ncol, k0))
                        mm(ps_, ps_[:, 0:ncol], kt, kt[0:96, k0:k0 + 128], qt, qt[0:96, j * NT + q0:(j + 1) * NT], True, True)
                        if kind == "flag":
                            ew("scalar", lambda e, ps_=ps_, pt_=pt_, ncol=ncol: e.activation(
                                out=pt_[:, 0:ncol], in_=ps_[:, 0:ncol], func=AF.Exp, scale=MLA_SCALE,
                                bias=gn[:, G_EXPB:G_EXPB + 1]), [ps_, gn], [pt_])
                        else:
                            ew("scalar", lambda e, ps_=ps_, pt_=pt_, ncol=ncol: e.activation(
                                out=pt_[:, 0:ncol], in_=ps_[:, 0:ncol], func=AF.Exp, scale=MLA_SCALE), [ps_], [pt_])
                        if kind == "diag":
                            ew("gpsimd", lambda e, pt_=pt_: e.tensor_tensor(out=pt_[:, 0:128], in0=pt_[:, 0:128],
                                                                            in1=cstb[:, C_TRI:C_TRI + 128], op=ALU.mult),
                               [pt_, cstb], [pt_])
                    if i >= LAG:
                        ii = i - LAG
                        pt_, q0, ncol, k0 = issued[ii]
                        mm(po, po[:, q0:NT], va, va[:, k0 // 128, :], pt_, pt_[:, 0:ncol], ii == 0, ii == nb - 1)
                r0, jc = (h % 2) * 64, h // 2
                ew("vector", lambda e, po=po: e.reciprocal(out=rec2[64:128, :], in_=po[64:128, :]), [po], [rec2])
                ew("vector", lambda e, po=po, r0=r0, jc=jc, j=j: e.tensor_tensor(
                    out=OA[r0:r0 + 64, jc, j * NT:(j + 1) * NT], in0=po[0:64, :], in1=rec2[64:128, :], op=ALU.mult),
                   [po, rec2], [OA])
        P.barrier()
        print("phase2 SBUF peak bytes/partition:", A.peak - A.BASE)
    else:
        ew("gpsimd", lambda e: e.memset(OA[:, :, :], 1.0), [], [OA])
        P.barrier()
    pm2 = A.mark()
    A.reset(persist_mark)
    OA2 = sb("OA", (128, 4, NOWN * NT), F32)
    assert A.mark() <= pm2

    if do_post:
        T3 = alloc_common()
        wo_sb = sb("wo_sb", (128, 8, D), BF16)
        load_w(wo_sb, w_o, 8, D)
        xb3 = sb("xb3", (128, 8, NT), F32)
        onb = sb("onb", (128, 4, NT), BF16)
        yv = o_y.t.rearrange("(c p) t -> p c t", p=128)
        for j in range(NOWN):
            c0 = j * NT
            P.dma("sync", lambda e, c0=c0: e.dma_start(out=xb3[:, :, :], in_=x1v[:, :, c0:c0 + NT]), key=xb3, reads=[], writes=[xb3])
            rms_rstd(T3, [OA[:, c, c0:c0 + NT] for c in range(4)], [OA] * 4, cstb[:, C_ONES:C_ONES + 128], 128, NT, 512.0)
            for c in range(4):
                norm_apply(T3, onb, onb[:, c, 0:NT], OA, OA[:, c, c0:c0 + NT], G_OMLA + c, 128, NT)
            rhs = [(onb, onb[:, c, 0:NT]) for c in range(4)] + [(YC, YC[:, c, c0:c0 + NT]) for c in range(2)] + \
                  [(OM, OM[:, c, c0:c0 + NT]) for c in range(2)]
            for c in range(8):
                y = pb[4 + c % 2]
                for k, (rb, ra) in enumerate(rhs):
                    mm(y, y[:, 0:NT], wo_sb, wo_sb[:, k, c * 128:(c + 1) * 128], rb, ra, k == 0, k == 7)
                ew("vector", lambda e, y=y, c=c: e.tensor_tensor(out=xb3[:, c, 0:NT], in0=y[:, 0:NT], in1=xb3[:, c, 0:NT], op=ALU.add),
                   [y, xb3], [xb3])
            ffn_half(T3, xb3, NT, sc2, G_FFN2)
            P.dma("sync", lambda e, c0=c0: e.dma_start(out=yv[:, :, c0:c0 + NT], in_=xb3[:, :, :]), key=xb3, reads=[xb3], writes=[])
        rhs = [(onorm_s, onorm_s[:, c, :]) for c in range(4)] + [(YCs, YCs[:, c, :]) for c in range(2)] + \
              [(OMs, OMs[:, c, :]) for c in range(2)]
        for c in range(8):
            y = pb[4 + c % 2]
            for k, (rb, ra) in enumerate(rhs):
                mm(y, y[:, 0:NS], wo_sb, wo_sb[:, k, c * 128:(c + 1) * 128], rb, ra, k == 0, k == 7)
            ew("vector", lambda e, y=y, c=c: e.tensor_tensor(out=xs[:, c, :], in0=y[:, 0:NS], in1=xs[:, c, :], op=ALU.add),
               [y, xs], [xs])
        ffn_half(T3, xs, NS, sc2, G_FFN2)
        P.dma("sync", lambda e: e.dma_start(out=o_ys.t.rearrange("(c p) t -> p c t", p=128), in_=xs[:, :, :]), key=xs, reads=[xs], writes=[])
        print("phase3 SBUF peak bytes/partition:", A.peak - A.BASE)

    P.emit()
    return nc, stack


def _rope_tables(pos):
    inv_freq = (np.float32(10000.0) ** (-np.arange(0, 32, 2, dtype=np.float32) / np.float32(32))).astype(np.float32)
    ang = (pos.astype(np.float32)[:, None] * inv_freq[None, :]).astype(np.float32)
    c = np.cos(ang.astype(np.float64)).astype(np.float32).T
    s = np.sin(ang.astype(np.float64)).astype(np.float32).T
    C = np.concatenate([c, c], axis=0)
    S = np.concatenate([-s, s], axis=0)
    return np.ascontiguousarray(C), np.ascontiguousarray(S)


def _slot_tiles(par):
    own = [2 * j + par for j in range(8)]
    oth = [2 * j + 1 - par for j in range(8)]
    return own + oth


def _constants():
    f32 = np.float32
    c = np.zeros((128, NCONST), f32)
    c[:, C_ONES:C_ONES + 128] = 1.0
    c[0:64, C_BLK64:C_BLK64 + 64] = 1.0
    c[64:128, C_BLK64 + 64:C_BLK64 + 128] = 1.0
    for m in range(32):
        c[(m + 16) % 32, C_ROT32 + m] = 1.0
    c[0:32, C_ONES32:C_ONES32 + 32] = 1.0
    c[0:64, C_BLK96:C_BLK96 + 64] = 1.0 / 64
    c[64:96, C_BLK96 + 64:C_BLK96 + 96] = 1.0 / 32
    for m in range(64, 96):
        c[64 + ((m - 64 + 16) % 32), C_ROT96 + m] = 1.0
    k = np.arange(128)[:, None]
    q = np.arange(128)[None, :]
    c[:, C_TRI:C_TRI + 128] = (k <= q).astype(f32)
    c[0:64, C_ONES64:C_ONES64 + 64] = 1.0
    c[:, C_ID:C_ID + 128] = np.eye(128, dtype=f32)
    return c


def _gains(inp, par):
    f32 = np.float32
    g = np.zeros((128, NGAIN), f32)
    v = lambda k: np.asarray(inp[k], f32)[0]
    g[:, G_FFN1:G_FFN1 + 8] = v("g_ffn1").reshape(8, 128).T
    g[:, G_MIX:G_MIX + 8] = v("g_mix").reshape(8, 128).T
    g[:, G_MEM:G_MEM + 8] = v("g_mem").reshape(8, 128).T
    g[:, G_KV] = v("g_kv_lora")
    g[0:32, G_KR] = v("g_kr")
    g[:, G_MK] = np.tile(v("g_mk"), 2)
    g[:, G_QL:G_QL + 2] = v("g_q_lora").reshape(2, 128).T
    g[0:64, G_Q96] = v("g_qn")
    g[64:96, G_Q96] = v("g_qr")
    g[0:64, G_KN] = v("g_kn")
    g[:, G_KN2] = np.tile(v("g_kn"), 2)
    cw = np.asarray(inp["conv_w"], f32)[0]
    g[:, G_CW0:G_CW0 + 2] = cw[0].reshape(2, 128).T
    g[:, G_CW1:G_CW1 + 2] = cw[1].reshape(2, 128).T
    g[:, G_CW2:G_CW2 + 2] = cw[2].reshape(2, 128).T
    g[:, G_OCONV:G_OCONV + 2] = v("g_out_conv").reshape(2, 128).T
    g[:, G_MQ] = np.tile(v("g_mq"), 2)
    g[:, G_OMEM:G_OMEM + 2] = v("g_out_mem").reshape(2, 128).T
    g[:, G_OMLA:G_OMLA + 4] = v("g_out_mla").reshape(4, 128).T
    g[:, G_FFN2:G_FFN2 + 8] = v("g_ffn2").reshape(8, 128).T
    g[:, G_FLA] = 1.0 - par
    g[:, G_FLB] = float(par)
    g[:, G_EXPB] = (par - 1) * 30000.0
    return g


def kernel(**inp):
    f32 = np.float32
    x_prompt = np.asarray(inp["x_prompt"], f32)
    n_cores = 8
    consts = _constants()
    w = lambda k: np.asarray(inp[k], f32)[0]
    shared = {
        "w1_gate": w("w1_gate"), "w1_up": w("w1_up"), "w1_down": w("w1_down"),
        "w2_gate": w("w2_gate"), "w2_up": w("w2_up"), "w2_down": w("w2_down"),
        "w_in": w("w_in"), "w_uq": w("w_uq"), "w_uk": w("w_uk").reshape(128, 512), "w_uv": w("w_uv").reshape(128, 512),
        "w_o": w("w_o"), "w_mem_k": w("w_mem_k"), "w_mem_v": w("w_mem_v"), "consts": consts,
        "w_ukT": np.ascontiguousarray(w("w_uk").transpose(2, 1, 0).reshape(64, 8 * 128)),
        "cache_ckv": np.asarray(inp["cache_ckv"], f32)[0].reshape(-1, 2048),
        "cache_krope": np.asarray(inp["cache_krope"], f32)[0].reshape(-1, 2048),
    }
    assert shared["cache_ckv"].shape[0] == N_PHYS * 8
    Cs, Ss = _rope_tables(np.full((NS,), PAST, dtype=np.int64))
    x_sample = np.asarray(inp["x_sample"], f32)
    state_conv = np.asarray(inp["state_conv"], f32)[0]
    page_table = np.asarray(inp["page_table"]).astype(np.int32)
    cache_mem_k = np.asarray(inp["cache_mem_k"], f32)[0]
    cache_mem_v = np.asarray(inp["cache_mem_v"], f32)[0]
    in_maps = []
    for c in range(n_cores):
        b, par = c // 2, c % 2
        tiles = _slot_tiles(par)
        tok = np.concatenate([np.arange(t * NT, (t + 1) * NT) for t in tiles])
        C, S = _rope_tables(tok)
        m = dict(shared)
        m["xT"] = np.ascontiguousarray(x_prompt[b][tok].T)
        m["memT"] = np.ascontiguousarray(np.asarray(inp["mem_prompt"], f32)[b].T)
        m["ropeC"] = C
        m["ropeS"] = S
        m["gains"] = _gains(inp, par)
        sl = slice(c * NS, (c + 1) * NS)
        m["xsT"] = np.ascontiguousarray(x_sample[sl, 0, :].T)
        m["ropeCs"] = Cs
        m["ropeSs"] = Ss
        m["stateT"] = np.ascontiguousarray(state_conv[sl].transpose(2, 1, 0))
        m["ptT"] = np.ascontiguousarray(page_table[sl].T)
        m["cache_mem_k"] = np.ascontiguousarray(cache_mem_k[sl].reshape(NS, N_MEM, 256))
        m["cache_mem_v"] = np.ascontiguousarray(cache_mem_v[sl].reshape(NS, N_MEM, 256))
        in_maps.append(m)

    nc, stack = build_program()
    with stack:
        res = run_bass_kernel_spmd(nc, in_maps, core_ids=list(range(n_cores)))
    R = res.results

    B = 4
    y_prompt = np.zeros((B, SEQ, D), f32)
    y_sample = np.zeros((DEC, 1, D), f32)
    ckv_p = np.zeros((1, B, SEQ, 128), f32)
    kr_p = np.zeros((1, B, SEQ, 32), f32)
    conv_p = np.zeros((1, B, 2, 256), f32)
    mk_p = np.zeros((1, B, N_MEM, 4, 64), f32)
    mv_p = np.zeros((1, B, N_MEM, 4, 64), f32)
    for c in range(n_cores):
        b, par = c // 2, c % 2
        tiles = _slot_tiles(par)
        for s in range(8):
            t = tiles[s]
            y_prompt[b, t * NT:(t + 1) * NT, :] = R[c]["o_y"][:, s * NT:(s + 1) * NT].T
            ckv_p[0, b, t * NT:(t + 1) * NT, :] = R[c]["o_ckv"][:, s * NT:(s + 1) * NT].T
            kr_p[0, b, t * NT:(t + 1) * NT, :] = R[c]["o_kr"][:, s * NT:(s + 1) * NT].T
            if t == 15:
                conv_p[0, b] = R[c]["o_conv"][:, 2 * s:2 * s + 2].T
        if par == 0:
            mk_p[0, b] = R[c]["o_mk"].T.reshape(N_MEM, 4, 64)
            mv_p[0, b] = R[c]["o_mv"].T.reshape(N_MEM, 4, 64)
    ckv_s = np.zeros((1, DEC, 1, 128), f32)
    kr_s = np.zeros((1, DEC, 1, 32), f32)
    conv_s = np.zeros((1, DEC, 2, 256), f32)
    for c in range(n_cores):
        sl = slice(c * NS, (c + 1) * NS)
        y_sample[sl, 0, :] = R[c]["o_ys"].T
        ckv_s[0, sl, 0, :] = R[c]["o_ckv_s"].T
        kr_s[0, sl, 0, :] = R[c]["o_kr_s"].T
        conv_s[0, sl, 0, :] = R[c]["o_state_s"].T
        conv_s[0, sl, 1, :] = R[c]["o_conv_s"].T
    return (y_prompt, y_sample, ckv_p, kr_p, conv_p, mk_p, mv_p, ckv_s, kr_s, conv_s)
```

```python
import contextlib
import numpy as np
import concourse.bass as bass
import concourse.mybir as mybir
from concourse.bass_utils import run_bass_kernel_spmd

F32 = mybir.dt.float32
BF16 = mybir.dt.bfloat16
I32 = mybir.dt.int32
ALU = mybir.AluOpType
AF = mybir.ActivationFunctionType
AX = mybir.AxisListType

D = 1024
DFF = 2816
NM = DFF // 128
SEQ = 8192
NT = 512
NSLOT = 16
NOWN = 8
DEC = 128
EPS = 1e-6
INW = 1440
PAST = 16384
N_MEM = 256
MLA_SCALE = float(96 ** -0.5)
MEM_SCALE = float(64 ** -0.5)
NCONST = 1024
NS = 16
N_PHYS = 20480
NGAIN = 64

G_FFN1, G_MIX, G_MEM, G_KV, G_KR, G_MK, G_QL, G_Q96, G_KN = 0, 8, 16, 24, 25, 26, 27, 29, 30
G_CW0, G_CW1, G_CW2, G_OCONV, G_MQ, G_OMEM, G_OMLA, G_FFN2, G_FLA, G_FLB, G_EXPB, G_KN2 = 31, 33, 35, 37, 39, 40, 42, 46, 54, 55, 56, 57
C_ONES, C_BLK64, C_ROT32, C_ONES32, C_BLK96, C_ROT96, C_TRI, C_ONES64, C_ID = 0, 128, 256, 288, 320, 416, 512, 640, 704


class Buf:
    def __init__(self, name, t=None):
        self.name = name
        self.t = t
        self.last_w = None
        self.readers = []
        self.dma_sem = None
        self.dma_issued = 0

    def __getitem__(self, k):
        return self.t[k]


class Op:
    __slots__ = ("eng", "fn", "waits", "signaled", "sigval", "is_dma", "key", "dma_waits")

    def __init__(self, eng, fn, is_dma=False, key=None):
        self.eng = eng
        self.fn = fn
        self.waits = []
        self.dma_waits = []
        self.signaled = False
        self.sigval = None
        self.is_dma = is_dma
        self.key = key


class Prog:
    ENGS = ("tensor", "scalar", "vector", "gpsimd", "sync")

    def __init__(self, nc, stack):
        self.nc = nc
        self.stack = stack
        self.ops = {e: [] for e in self.ENGS}
        self.esem = {}
        self.dma_keys = []
        self.free_sems = []

    def _dep(self, op, prev):
        if prev is None or prev is op:
            return
        if prev.is_dma:
            op.dma_waits.append((prev.key, prev.key.dma_issued * 16))
            return
        if prev.eng == "tensor" and op.eng == "tensor":
            return
        op.waits.append(prev)
        prev.signaled = True

    def _track(self, op, reads, writes):
        for b in reads:
            self._dep(op, b.last_w)
        for b in writes:
            self._dep(op, b.last_w)
            for r in b.readers:
                self._dep(op, r)
        for b in reads:
            b.readers.append(op)
        for b in writes:
            b.last_w = op
            b.readers = []

    def op(self, eng, fn, reads=(), writes=()):
        o = Op(eng, fn)
        self._track(o, reads, writes)
        self.ops[eng].append(o)
        return o

    def dma(self, eng, fn, key, reads=(), writes=()):
        o = Op(eng, fn, is_dma=True, key=key)
        self._track(o, reads, writes)
        key.dma_issued += 1
        if key.dma_sem is None:
            key.dma_sem = self.stack.enter_context(self.nc.semaphore("d%d_%s" % (len(self.dma_keys), key.name)))
            self.dma_keys.append(key)
        self.ops[eng].append(o)
        return o

    def barrier(self):
        targets = []
        for e in self.ENGS:
            for o in reversed(self.ops[e]):
                if o.fn is not None and not o.is_dma:
                    targets.append(o)
                    o.signaled = True
                    break
        dma_targets = [(k, k.dma_issued * 16) for k in self.dma_keys]
        for e in self.ENGS:
            o = Op(e, None)
            o.waits = [t for t in targets if not (t.eng == e and e in ("tensor", "sync"))]
            o.dma_waits = list(dma_targets)
            self.ops[e].append(o)

    def emit(self):
        nc = self.nc
        for e in self.ENGS:
            self.esem[e] = self.stack.enter_context(nc.semaphore("e_" + e))
            n = 0
            for o in self.ops[e]:
                if o.signaled and not o.is_dma and o.fn is not None:
                    n += 1
                    o.sigval = n
        prog = self

        def run(e, eng):
            waited = {}
            for o in prog.ops[e]:
                need = {}
                for p in o.waits:
                    s = prog.esem[p.eng]
                    need[s] = max(need.get(s, 0), p.sigval)
                for k, v in o.dma_waits:
                    need[k.dma_sem] = max(need.get(k.dma_sem, 0), v)
                for s, v in need.items():
                    if waited.get(s, 0) < v:
                        eng.wait_ge(s, v)
                        waited[s] = v
                if o.fn is None:
                    continue
                ins = o.fn(eng)
                if o.is_dma:
                    ins.then_inc(o.key.dma_sem, 16)
                elif o.signaled:
                    ins.then_inc(prog.esem[e], 1)
            if e == "sync":
                for k in prog.dma_keys:
                    eng.wait_ge(k.dma_sem, k.dma_issued * 16)

        with nc.Block() as block:
            @block.tensor
            def _(eng):
                run("tensor", eng)

            @block.scalar
            def _(eng):
                run("scalar", eng)

            @block.vector
            def _(eng):
                run("vector", eng)

            @block.gpsimd
            def _(eng):
                run("gpsimd", eng)

            @block.sync
            def _(eng):
                run("sync", eng)


class Arena:
    BASE = 16512
    TOP = 229344

    def __init__(self, nc):
        self.nc = nc
        self.ptr = self.BASE
        self.n = 0
        self.peak = 0

    def alloc(self, name, shape, dt):
        esz = 2 if dt == BF16 else 4
        size = int(np.prod(shape[1:])) * esz
        size = (size + 63) // 64 * 64
        assert self.ptr + size <= self.TOP, "SBUF overflow at %s: %d + %d" % (name, self.ptr, size)
        self.n += 1
        h = self.nc.alloc_sbuf_tensor_at("%s_%d" % (name, self.n), list(shape), dt, offset=self.ptr)
        self.ptr += size
        self.peak = max(self.peak, self.ptr)
        return Buf(name, h)

    def mark(self):
        return self.ptr

    def reset(self, m):
        self.ptr = m


def build_program(do_att=True, do_post=True, do_sample=True):
    nc = bass.Bass("TRN2", target_bir_lowering=False)
    stack = contextlib.ExitStack()
    P = Prog(nc, stack)
    A = Arena(nc)

    def dram_in(name, shape, dt=F32):
        return Buf(name, nc.dram_tensor(name, list(shape), dt, kind="ExternalInput").ap())

    def dram_out(name, shape, dt=F32):
        return Buf(name, nc.dram_tensor(name, list(shape), dt, kind="ExternalOutput").ap())

    def dram_tmp(name, shape, dt):
        return Buf(name, nc.dram_tensor(name, list(shape), dt).ap())

    sb = A.alloc

    xT = dram_in("xT", (D, NSLOT * NT))
    memT = dram_in("memT", (D, N_MEM))
    ropeC = dram_in("ropeC", (32, NSLOT * NT))
    ropeS = dram_in("ropeS", (32, NSLOT * NT))
    w1g = dram_in("w1_gate", (D, DFF))
    w1u = dram_in("w1_up", (D, DFF))
    w1d = dram_in("w1_down", (DFF, D))
    w2g = dram_in("w2_gate", (D, DFF))
    w2u = dram_in("w2_up", (D, DFF))
    w2d = dram_in("w2_down", (DFF, D))
    w_in = dram_in("w_in", (D, INW))
    w_uq = dram_in("w_uq", (256, 768))
    w_uk = dram_in("w_uk", (128, 512))
    w_uv = dram_in("w_uv", (128, 512))
    w_o = dram_in("w_o", (D, D))
    wmk = dram_in("w_mem_k", (D, 256))
    wmv = dram_in("w_mem_v", (D, 256))
    gains = dram_in("gains", (128, NGAIN))
    xsT = dram_in("xsT", (D, NS))
    ropeCs = dram_in("ropeCs", (32, NS))
    ropeSs = dram_in("ropeSs", (32, NS))
    stateT = dram_in("stateT", (256, 2, NS))
    ptT = dram_in("ptT", (128, NS), I32)
    cckv = dram_in("cache_ckv", (N_PHYS * 8, 2048))
    ckr = dram_in("cache_krope", (N_PHYS * 2, 2048))
    cmk = dram_in("cache_mem_k", (NS, N_MEM, 256))
    cmv = dram_in("cache_mem_v", (NS, N_MEM, 256))
    w_ukT = dram_in("w_ukT", (64, 8 * 128))
    consts = dram_in("consts", (128, NCONST))

    o_y = dram_out("o_y", (D, NOWN * NT))
    o_ckv = dram_out("o_ckv", (128, NSLOT * NT))
    o_kr = dram_out("o_kr", (32, NSLOT * NT))
    o_conv = dram_out("o_conv", (256, NSLOT * 2))
    o_mk = dram_out("o_mk", (256, N_MEM))
    o_mv = dram_out("o_mv", (256, N_MEM))
    o_ys = dram_out("o_ys", (D, NS))
    o_ckv_s = dram_out("o_ckv_s", (128, NS))
    o_kr_s = dram_out("o_kr_s", (32, NS))
    o_conv_s = dram_out("o_conv_s", (256, NS))
    o_state_s = dram_out("o_state_s", (256, NS))

    sc1 = [dram_tmp("sc1g", (D, DFF), BF16), dram_tmp("sc1u", (D, DFF), BF16), dram_tmp("sc1d", (DFF, D), BF16)]
    sc2 = [dram_tmp("sc2g", (D, DFF), BF16), dram_tmp("sc2u", (D, DFF), BF16), dram_tmp("sc2d", (DFF, D), BF16)]
    X1 = dram_tmp("X1", (D, NOWN * NT), F32)
    KN = dram_tmp("KN", (8, 64, NSLOT * NT), BF16)
    KR = dram_tmp("KR", (32, NSLOT * NT), BF16)
    VS = dram_tmp("VS", (NSLOT * NT, 512), BF16)
    QS = dram_tmp("QS", (8, 96, NOWN * NT), BF16)

    pb = [Buf("pb%d" % i, stack.enter_context(nc.psum_tensor("pb%d" % i, [128, 512], F32))) for i in range(8)]

    gn = sb("gn", (128, NGAIN), F32)
    cst = sb("cst", (128, NCONST), F32)
    cstb = sb("cstb", (128, NCONST), BF16)
    YC = sb("YC", (128, 2, NOWN * NT), BF16)
    OM = sb("OM", (128, 2, NOWN * NT), BF16)
    memKT = sb("memKT", (128, 2, N_MEM), BF16)
    memVA = sb("memVA", (128, 2, 4, 128), BF16)
    UB = sb("UB", (128, 2, NSLOT, 2), F32)
    xs = sb("xs", (128, 8, NS), F32)
    qabs = sb("qabs", (128, NS, 8), BF16)
    qrs = sb("qrs", (32, NS, 8), BF16)
    pnew = sb("pnew", (128, 8, NS), F32)
    ckvs = sb("ckvs", (128, NS), F32)
    YCs = sb("YCs", (128, 2, NS), BF16)
    OMs = sb("OMs", (128, 2, NS), BF16)
    mqs = sb("mqs", (128, 2, NS), F32)
    OAs = sb("OAs", (128, 4, NS), F32)
    onorm_s = sb("onorm_s", (128, 4, NS), BF16)
    persist_mark = A.mark()

    def load(eng, dst, dst_ap, src, src_ap):
        P.dma(eng, lambda e: e.dma_start(out=dst_ap, in_=src_ap), key=dst, reads=[src], writes=[dst])

    def load_w(dst, src, nk, m):
        v = src.t.rearrange("(c p) m -> p c m", p=128)
        step = 1024
        for c in range(nk):
            for m0 in range(0, m, step):
                m1 = min(m, m0 + step)
                load("gpsimd", dst, dst[:, c, m0:m1], src, v[:, c, m0:m1])

    def mm(out_buf, out_ap, lhsT_buf, lhsT_ap, rhs_buf, rhs_ap, start, stop):
        P.op("tensor", lambda e: e.matmul(out_ap, lhsT_ap, rhs_ap, start=start, stop=stop),
             reads=[lhsT_buf, rhs_buf], writes=[out_buf])

    def ew(eng, fn, reads, writes):
        P.op(eng, fn, reads=reads, writes=writes)

    load("sync", gn, gn[:, :], gains, gains[:, :])
    load("sync", cst, cst[:, :], consts, consts[:, :])
    ew("vector", lambda e: e.tensor_copy(out=cstb[:, :], in_=cst[:, :]), [cst], [cstb])
    ew("gpsimd", lambda e: e.memset(memVA[:, :, :, 64:128], 1.0), [], [memVA])
    ew("gpsimd", lambda e: e.memset(UB[:, :, :, :], 0.0), [], [UB])

    def cast_w(dst, src, rows, cols):
        step = 1408 if cols == DFF else 1024
        for r0 in range(0, rows, 128):
            for m0 in range(0, cols, step):
                P.dma("gpsimd", lambda e, r0=r0, m0=m0: e.dma_start(out=dst[r0:r0 + 128, m0:m0 + step],
                                                                    in_=src[r0:r0 + 128, m0:m0 + step]),
                      key=dst, reads=[], writes=[])

    for dst, src, r, c in ((sc1[0], w1g, D, DFF), (sc1[1], w1u, D, DFF), (sc1[2], w1d, DFF, D),
                           (sc2[0], w2g, D, DFF), (sc2[1], w2u, D, DFF), (sc2[2], w2d, DFF, D)):
        cast_w(dst, src, r, c)

    def alloc_common():
        T = {}
        T["hb"] = sb("hb", (128, 8, NT), BF16)
        T["ab"] = sb("ab", (128, NM, NT), BF16)
        T["std"] = sb("std", (128, NT), F32)
        T["rstd"] = sb("rstd", (128, NT), F32)
        T["t0"] = sb("t0", (128, NT), F32)
        T["t1"] = sb("t1", (128, NT), F32)
        T["t2"] = sb("t2", (128, NT), F32)
        T["sgb"] = [sb("sgb%d" % i, (128, NT), BF16) for i in range(2)]
        T["wgc"] = [sb("wgc%d" % i, (128, 8, 128), BF16) for i in range(2)]
        T["wuc"] = [sb("wuc%d" % i, (128, 8, 128), BF16) for i in range(2)]
        T["wdc"] = [sb("wdc%d" % i, (128, NM, 128), BF16) for i in range(2)]
        T["sq"] = T["ab"]
        T["pn"] = pb[6]
        return T

    pn = pb[6]
    pz = pb[7]

    def rms_rstd(T, srcs, src_bufs, ones_ap, rows, n, count):
        sqb, pn_, std, rstd = T["sq"], T["pn"], T["std"], T["rstd"]
        for i, (a, b) in enumerate(zip(srcs, src_bufs)):
            if b in pb:
                ew("scalar", lambda e, a=a, i=i: e.activation(out=sqb[0:rows, i, 0:n], in_=a, func=AF.Square), [b], [sqb])
            else:
                ew("gpsimd", lambda e, a=a, i=i: e.tensor_tensor(out=sqb[0:rows, i, 0:n], in0=a, in1=a, op=ALU.mult),
                   [b], [sqb])
        for i in range(len(srcs)):
            mm(pn_, pn_[0:rows, 0:n], cstb, ones_ap, sqb, sqb[0:rows, i, 0:n], i == 0, i == len(srcs) - 1)
        ew("scalar", lambda e: e.activation(out=std[0:rows, 0:n], in_=pn_[0:rows, 0:n], func=AF.Sqrt,
                                            bias=EPS, scale=1.0 / count), [pn_], [std])
        ew("vector", lambda e: e.reciprocal(out=rstd[0:rows, 0:n], in_=std[0:rows, 0:n]), [std], [rstd])

    def norm_apply(T, out_buf, out_ap, src_buf, src_ap, gcol, rows, n):
        rstd = T["rstd"]
        ew("vector", lambda e: e.scalar_tensor_tensor(out=out_ap, in0=src_ap, scalar=gn[0:rows, gcol:gcol + 1],
                                                      in1=rstd[0:rows, 0:n], op0=ALU.mult, op1=ALU.mult),
           [src_buf, gn, rstd], [out_buf])

    def rmsnorm_d(T, x, n, gbase):
        hb = T["hb"]
        rms_rstd(T, [x[:, c, 0:n] for c in range(8)], [x] * 8, cstb[:, C_ONES:C_ONES + 128], 128, n, float(D))
        for c in range(8):
            norm_apply(T, hb, hb[:, c, 0:n], x, x[:, c, 0:n], gbase + c, 128, n)

    def ffn_half(T, x, n, sc, gbase):
        hb, ab = T["hb"], T["ab"]
        sgv = sc[0].t.rearrange("(c p) m -> p c m", p=128)
        suv = sc[1].t.rearrange("(c p) m -> p c m", p=128)
        sdv = sc[2].t.rearrange("(m p) c -> p m c", p=128)
        rmsnorm_d(T, x, n, gbase)
        for m in range(NM):
            g, u, sg = pb[m % 2], pb[2 + m % 2], T["sgb"][m % 2]
            wg_, wu_ = T["wgc"][m % 2], T["wuc"][m % 2]
            P.dma("sync", lambda e, wg_=wg_, m=m: e.dma_start(out=wg_[:, :, :], in_=sgv[:, :, m * 128:(m + 1) * 128]),
                  key=wg_, reads=[], writes=[wg_])
            P.dma("sync", lambda e, wu_=wu_, m=m: e.dma_start(out=wu_[:, :, :], in_=suv[:, :, m * 128:(m + 1) * 128]),
                  key=wu_, reads=[], writes=[wu_])
            for c in range(8):
                mm(g, g[:, 0:n], wg_, wg_[:, c, :], hb, hb[:, c, 0:n], c == 0, c == 7)
            for c in range(8):
                mm(u, u[:, 0:n], wu_, wu_[:, c, :], hb, hb[:, c, 0:n], c == 0, c == 7)
            ew("scalar", lambda e, g=g, sg=sg: e.activation(out=sg[:, 0:n], in_=g[:, 0:n], func=AF.Silu), [g], [sg])
            ew("vector", lambda e, u=u, sg=sg, m=m: e.tensor_tensor(out=ab[:, m, 0:n], in0=sg[:, 0:n], in1=u[:, 0:n],
                                                                    op=ALU.mult), [sg, u], [ab])
        for c in range(8):
            y = pb[4 + c % 2]
            wd_ = T["wdc"][c % 2]
            P.dma("sync", lambda e, wd_=wd_, c=c: e.dma_start(out=wd_[:, :, :], in_=sdv[:, :, c * 128:(c + 1) * 128]),
                  key=wd_, reads=[], writes=[wd_])
            for m in range(NM):
                mm(y, y[:, 0:n], wd_, wd_[:, m, :], ab, ab[:, m, 0:n], m == 0, m == NM - 1)
            ew("vector", lambda e, y=y, c=c: e.scalar_tensor_tensor(out=x[:, c, 0:n], in0=y[:, 0:n], scalar=0.5,
                                                                    in1=x[:, c, 0:n], op0=ALU.mult, op1=ALU.add),
               [y, x], [x])

    def proj(T, out_ps, rows, n, wbuf, col0):
        hb = T["hb"]
        for c in range(8):
            mm(out_ps, out_ps[0:rows, 0:n], wbuf, wbuf[:, c, col0:col0 + rows], hb, hb[:, c, 0:n], c == 0, c == 7)

    m0 = A.mark()
    T0 = alloc_common()
    wmk_sb = sb("wmk_sb", (128, 8, 256), BF16)
    wmv_sb = sb("wmv_sb", (128, 8, 256), BF16)
    xm = sb("xm", (128, 8, N_MEM), F32)
    mko = [sb("mko%d" % i, (128, N_MEM), F32) for i in range(2)]
    mvo = sb("mvo", (128, 2, N_MEM), F32)
    load_w(wmk_sb, wmk, 8, 256)
    load_w(wmv_sb, wmv, 8, 256)
    load("sync", xm, xm[:, :, :], memT, memT.t.rearrange("(c p) t -> p c t", p=128))
    rmsnorm_d(T0, xm, N_MEM, G_MEM)
    for j in range(2):
        proj(T0, pz, 128, N_MEM, wmk_sb, 128 * j)
        ew("scalar", lambda e: e.copy(out=T0["t0"][:, 0:N_MEM], in_=pz[:, 0:N_MEM]), [pz], [T0["t0"]])
        rms_rstd(T0, [T0["t0"][:, 0:N_MEM]], [T0["t0"]], cstb[:, C_BLK64:C_BLK64 + 128], 128, N_MEM, 64.0)
        norm_apply(T0, mko[j], mko[j][:, 0:N_MEM], T0["t0"], T0["t0"][:, 0:N_MEM], G_MK, 128, N_MEM)
        P.dma("sync", lambda e, j=j: e.dma_start(out=o_mk[j * 128:(j + 1) * 128, :], in_=mko[j][:, 0:N_MEM]),
              key=mko[j], reads=[mko[j]], writes=[])
        ew("gpsimd", lambda e, j=j: e.tensor_copy(out=memKT[:, j, :], in_=mko[j][:, 0:N_MEM]), [mko[j]], [memKT])
        proj(T0, pz, 128, N_MEM, wmv_sb, 128 * j)
        ew("scalar", lambda e, j=j: e.copy(out=mvo[:, j, 0:N_MEM], in_=pz[:, 0:N_MEM]), [pz], [mvo])
        P.dma("sync", lambda e, j=j: e.dma_start(out=o_mv[j * 128:(j + 1) * 128, :], in_=mvo[:, j, 0:N_MEM]),
              key=mvo, reads=[mvo], writes=[])
    for kb in range(2):
        for c in range(8):
            mm(pz, pz[:, 0:256], T0["hb"], T0["hb"][:, c, kb * 128:(kb + 1) * 128], wmv_sb, wmv_sb[:, c, :], c == 0, c == 7)
        ew("scalar", lambda e, kb=kb: e.copy(out=memVA[:, kb, :, 0:64],
                                             in_=pz[:, 0:256].rearrange("p (h d) -> p h d", h=4)), [pz], [memVA])
    P.barrier()
    A.reset(m0)

    T = alloc_common()
    win_sb = sb("win_sb", (128, 8, INW), BF16)
    wuq_sb = sb("wuq_sb", (128, 2, 768), BF16)
    wuk_sb = sb("wuk_sb", (128, 1, 512), BF16)
    wuv_sb = sb("wuv_sb", (128, 1, 512), BF16)
    load_w(win_sb, w_in, 8, INW)
    load_w(wuq_sb, w_uq, 2, 768)
    load_w(wuk_sb, w_uk, 1, 512)
    load_w(wuv_sb, w_uv, 1, 512)
    xb = sb("xb", (128, 8, NT), F32)
    ckv_o = sb("ckv_o", (128, NT), F32)
    ckvb = sb("ckvb", (128, NT), BF16)
    kr_o = sb("kr_o", (32, NT), F32)
    krb = sb("krb", (32, NT), BF16)
    uext = sb("uext", (128, 2, NT + 2), F32)
    rC = sb("rC", (32, NT), F32)
    rS = sb("rS", (32, NT), F32)
    rC96 = sb("rC96", (96, NT), F32)
    rS96 = sb("rS96", (96, NT), F32)
    knb = [sb("knb%d" % i, (128, NT), BF16) for i in range(2)]
    PZ = [pb[7], pb[5]]
    CS = [{"sq": sb("csq0", (128, 2, NT), BF16), "pn": pb[6], "std": T["std"], "rstd": T["rstd"]},
          {"sq": sb("csq1", (128, 2, NT), BF16), "pn": pb[4], "std": sb("cstd1", (128, NT), F32), "rstd": sb("crstd1", (128, NT), F32)}]
    vb = [sb("vb%d" % i, (128, 512), BF16) for i in range(2)]
    cq = sb("cq", (128, 2, NT), F32)
    cqn = sb("cqn", (128, 2, NT), BF16)
    qn = sb("qn", (96, NT), F32)
    qb16 = [sb("qb16%d" % i, (96, NT), BF16) for i in range(2)]
    gbt = sb("gbt", (128, 2, NT), F32)
    mqn = sb("mqn", (128, 2, NT), BF16)
    ptm = [sb("ptm%d" % i, (128, NT), BF16) for i in range(2)]
    rec = sb("rec", (128, NT), F32)
    t0, t1, t2 = T["t0"], T["t1"], T["t2"]
    ew("gpsimd", lambda e: e.memset(rC96[0:64, :], 1.0), [], [rC96])
    ew("gpsimd", lambda e: e.memset(rS96[0:64, :], 0.0), [], [rS96])

    xv = xT.t.rearrange("(c p) t -> p c t", p=128)
    x1v = X1.t.rearrange("(c p) t -> p c t", p=128)

    def phase1(s):
        own = s < NOWN
        n = NT
        x = xb
        c0 = s * NT
        load("sync", x, x[:, :, :], xT, xv[:, :, c0:c0 + NT])
        load("sync", rC, rC[:, :], ropeC, ropeC[:, c0:c0 + NT])
        load("sync", rS, rS[:, :], ropeS, ropeS[:, c0:c0 + NT])
        if own:
            load("sync", rC96, rC96[64:96, :], ropeC, ropeC[:, c0:c0 + NT])
            load("sync", rS96, rS96[64:96, :], ropeS, ropeS[:, c0:c0 + NT])
        ffn_half(T, x, n, sc1, G_FFN1)
        if own:
            P.dma("sync", lambda e: e.dma_start(out=x1v[:, :, c0:c0 + NT], in_=x[:, :, :]), key=x, reads=[x], writes=[])
        rmsnorm_d(T, x, n, G_MIX)
        pzA, pzB = PZ
        proj(T, pzA, 128, n, win_sb, 256)
        rms_rstd(CS[0], [pzA[:, 0:n]], [pzA], cstb[:, C_ONES:C_ONES + 128], 128, n, 128.0)
        norm_apply(CS[0], ckv_o, ckv_o[:, 0:n], pzA, pzA[:, 0:n], G_KV, 128, n)
        P.dma("sync", lambda e: e.dma_start(out=o_ckv[:, c0:c0 + NT], in_=ckv_o[:, 0:n]), key=ckv_o, reads=[ckv_o], writes=[])
        ew("gpsimd", lambda e: e.tensor_copy(out=ckvb[:, 0:n], in_=ckv_o[:, 0:n]), [ckv_o], [ckvb])
        proj(T, pzB, 32, n, win_sb, 384)
        rms_rstd(CS[1], [pzB[0:32, 0:n]], [pzB], cstb[0:32, C_ONES32:C_ONES32 + 32], 32, n, 32.0)
        norm_apply(CS[1], t2, t2[0:32, 0:n], pzB, pzB[0:32, 0:n], G_KR, 32, n)
        mm(pzB, pzB[0:32, 0:n], cst, cst[0:32, C_ROT32:C_ROT32 + 32], t2, t2[0:32, 0:n], True, True)
        ew("vector", lambda e: e.tensor_tensor(out=t1[0:32, 0:n], in0=pzB[0:32, 0:n], in1=rS[0:32, 0:n], op=ALU.mult),
           [pzB, rS], [t1])
        ew("gpsimd", lambda e: e.tensor_tensor(out=t2[0:32, 0:n], in0=t2[0:32, 0:n], in1=rC[0:32, 0:n], op=ALU.mult),
           [t2, rC], [t2])
        ew("vector", lambda e: e.tensor_tensor(out=kr_o[0:32, 0:n], in0=t1[0:32, 0:n], in1=t2[0:32, 0:n], op=ALU.add),
           [t1, t2], [kr_o])
        P.dma("sync", lambda e: e.dma_start(out=o_kr[:, c0:c0 + NT], in_=kr_o[0:32, 0:n]), key=kr_o, reads=[kr_o], writes=[])
        ew("gpsimd", lambda e: e.tensor_copy(out=krb[0:32, 0:n], in_=kr_o[0:32, 0:n]), [kr_o], [krb])
        P.dma("sync", lambda e: e.dma_start(out=KR[:, c0:c0 + NT], in_=krb[0:32, 0:n]), key=krb, reads=[krb], writes=[])
        for hp in range(4):
            k2 = hp % 2
            pz_, cs_, kb_ = PZ[k2], CS[k2], knb[k2]
            mm(pz_, pz_[:, 0:n], wuk_sb, wuk_sb[:, 0, hp * 128:(hp + 1) * 128], ckvb, ckvb[:, 0:n], True, True)
            rms_rstd(cs_, [pz_[:, 0:n]], [pz_], cstb[:, C_BLK64:C_BLK64 + 128], 128, n, 64.0)
            norm_apply(cs_, kb_, kb_[:, 0:n], pz_, pz_[:, 0:n], G_KN2, 128, n)
            for hl in range(2):
                P.dma("sync", lambda e, hp=hp, hl=hl, kb_=kb_: e.dma_start(out=KN[2 * hp + hl, :, c0:c0 + NT],
                                                                           in_=kb_[hl * 64:(hl + 1) * 64, 0:n]),
                      key=kb_, reads=[kb_], writes=[])
        for tb in range(n // 128):
            v_ = vb[tb % 2]
            pz_ = PZ[tb % 2]
            mm(pz_, pz_[:, 0:512], ckvb, ckvb[:, tb * 128:(tb + 1) * 128], wuv_sb, wuv_sb[:, 0, :], True, True)
            ew("scalar", lambda e, v_=v_, pz_=pz_: e.copy(out=v_[:, :], in_=pz_[:, 0:512]), [pz_], [v_])
            P.dma("sync", lambda e, v_=v_, tb=tb: e.dma_start(out=VS[c0 + tb * 128:c0 + (tb + 1) * 128, :], in_=v_[:, :]),
                  key=v_, reads=[v_], writes=[])
        for j in range(2):
            proj(T, pzA, 128, n, win_sb, 416 + 128 * j)
            ew("scalar", lambda e: e.copy(out=t0[:, 0:n], in_=pzA[:, 0:n]), [pzA], [t0])
            proj(T, pzB, 128, n, win_sb, 928 + 128 * j)
            ew("vector", lambda e, j=j: e.tensor_tensor(out=uext[:, j, 2:n + 2], in0=t0[:, 0:n], in1=pzB[:, 0:n], op=ALU.mult),
               [t0, pzB], [uext])
        ew("gpsimd", lambda e: e.tensor_copy(out=UB[:, :, s, :], in_=uext[:, :, n:n + 2]), [uext], [UB])
        for j in range(2):
            P.dma("sync", lambda e, j=j: e.dma_start(out=o_conv[j * 128:(j + 1) * 128, 2 * s:2 * s + 2],
                                                     in_=uext[:, j, n:n + 2]), key=uext, reads=[uext], writes=[])
        if not own:
            return
        j_own = s
        if j_own == 0:
            ew("vector", lambda e: e.tensor_scalar(out=uext[:, :, 0:2], in0=UB[:, :, 8, :], scalar1=gn[:, G_FLB:G_FLB + 1],
                                                   scalar2=None, op0=ALU.mult), [UB, gn], [uext])
        else:
            ew("vector", lambda e: e.tensor_scalar(out=uext[:, :, 0:2], in0=UB[:, :, 8 + j_own, :],
                                                   scalar1=gn[:, G_FLB:G_FLB + 1], scalar2=None, op0=ALU.mult),
               [UB, gn], [uext])
            ew("vector", lambda e: e.scalar_tensor_tensor(out=uext[:, :, 0:2], in0=UB[:, :, 8 + j_own - 1, :],
                                                          scalar=gn[:, G_FLA:G_FLA + 1], in1=uext[:, :, 0:2],
                                                          op0=ALU.mult, op1=ALU.add), [UB, gn, uext], [uext])
        for j in range(2):
            pz_ = PZ[j]
            tt = (t0, t1)[j]
            proj(T, pz_, 128, n, win_sb, 672 + 128 * j)
            ew("gpsimd", lambda e, j=j, tt=tt: e.tensor_scalar(out=tt[:, 0:n], in0=uext[:, j, 0:n], scalar1=gn[:, G_CW0 + j:G_CW0 + j + 1],
                                                               scalar2=None, op0=ALU.mult), [uext, gn], [tt])
            ew("vector", lambda e, j=j, tt=tt: e.scalar_tensor_tensor(out=tt[:, 0:n], in0=uext[:, j, 1:n + 1],
                                                                      scalar=gn[:, G_CW1 + j:G_CW1 + j + 1], in1=tt[:, 0:n],
                                                                      op0=ALU.mult, op1=ALU.add), [uext, gn, tt], [tt])
            ew("vector", lambda e, j=j, tt=tt: e.scalar_tensor_tensor(out=tt[:, 0:n], in0=uext[:, j, 2:n + 2],
                                                                      scalar=gn[:, G_CW2 + j:G_CW2 + j + 1], in1=tt[:, 0:n],
                                                                      op0=ALU.mult, op1=ALU.add), [uext, gn, tt], [tt])
            ew("vector", lambda e, j=j, tt=tt, pz_=pz_: e.tensor_tensor(out=gbt[:, j, 0:n], in0=tt[:, 0:n], in1=pz_[:, 0:n], op=ALU.mult),
               [tt, pz_], [gbt])
        rms_rstd(CS[0], [gbt[:, 0, 0:n], gbt[:, 1, 0:n]], [gbt, gbt], cstb[:, C_ONES:C_ONES + 128], 128, n, 256.0)
        for j in range(2):
            norm_apply(CS[0], YC, YC[:, j, c0:c0 + n], gbt, gbt[:, j, 0:n], G_OCONV + j, 128, n)
        proj(T, pzA, 128, n, win_sb, 0)
        proj(T, pzB, 128, n, win_sb, 128)
        rms_rstd(CS[1], [pzA[:, 0:n], pzB[:, 0:n]], [pzA, pzB], cstb[:, C_ONES:C_ONES + 128], 128, n, 256.0)
        norm_apply(CS[1], cqn, cqn[:, 0, 0:n], pzA, pzA[:, 0:n], G_QL, 128, n)
        norm_apply(CS[1], cqn, cqn[:, 1, 0:n], pzB, pzB[:, 0:n], G_QL + 1, 128, n)
        for h in range(8):
            k2 = h % 2
            pz_, cs_, qb_ = PZ[k2], CS[k2], qb16[k2]
            qn_, tq_ = (qn, t2)[k2], (t1, t0)[k2]
            for j in range(2):
                mm(pz_, pz_[0:96, 0:n], wuq_sb, wuq_sb[:, j, h * 96:(h + 1) * 96], cqn, cqn[:, j, 0:n], j == 0, j == 1)
            rms_rstd(cs_, [pz_[0:96, 0:n]], [pz_], cstb[0:96, C_BLK96:C_BLK96 + 96], 96, n, 1.0)
            norm_apply(cs_, qn_, qn_[0:96, 0:n], pz_, pz_[0:96, 0:n], G_Q96, 96, n)
            mm(pz_, pz_[0:96, 0:n], cst, cst[0:96, C_ROT96:C_ROT96 + 96], qn_, qn_[0:96, 0:n], True, True)
            ew("vector", lambda e, pz_=pz_, tq_=tq_: e.tensor_tensor(out=tq_[0:96, 0:n], in0=pz_[0:96, 0:n], in1=rS96[0:96, 0:n], op=ALU.mult),
               [pz_, rS96], [tq_])
            ew("gpsimd", lambda e, qn_=qn_: e.tensor_tensor(out=qn_[0:96, 0:n], in0=qn_[0:96, 0:n], in1=rC96[0:96, 0:n], op=ALU.mult),
               [qn_, rC96], [qn_])
            ew("vector", lambda e, qb_=qb_, tq_=tq_, qn_=qn_: e.tensor_tensor(out=qb_[0:96, 0:n], in0=tq_[0:96, 0:n], in1=qn_[0:96, 0:n], op=ALU.add),
               [tq_, qn_], [qb_])
            P.dma("sync", lambda e, h=h, qb_=qb_: e.dma_start(out=QS[h, :, c0:c0 + NT], in_=qb_[0:96, 0:n]),
                  key=qb_, reads=[qb_], writes=[])
        for j in range(2):
            pz_, cs_ = PZ[j], CS[j]
            proj(T, pz_, 128, n, win_sb, 1184 + 128 * j)
            rms_rstd(cs_, [pz_[:, 0:n]], [pz_], cstb[:, C_BLK64:C_BLK64 + 128], 128, n, 64.0)
            norm_apply(cs_, mqn, mqn[:, j, 0:n], pz_, pz_[:, 0:n], G_MQ, 128, n)
        for h in range(4):
            j, r0 = h // 2, (h % 2) * 64
            po = pb[4 + h % 2]
            for kb in range(2):
                ps_ = pb[2 * (h % 2) + kb]
                pt_ = ptm[kb]
                mm(ps_, ps_[:, 0:n], memKT, memKT[r0:r0 + 64, j, kb * 128:(kb + 1) * 128], mqn, mqn[r0:r0 + 64, j, 0:n], True, True)
                ew("scalar", lambda e, ps_=ps_, pt_=pt_: e.activation(out=pt_[:, 0:n], in_=ps_[:, 0:n], func=AF.Exp, scale=MEM_SCALE),
                   [ps_], [pt_])
                mm(po, po[:, 0:n], memVA, memVA[:, kb, h, :], pt_, pt_[:, 0:n], kb == 0, kb == 1)
            ew("vector", lambda e, po=po: e.reciprocal(out=rec[64:128, 0:n], in_=po[64:128, 0:n]), [po], [rec])
            ew("vector", lambda e, po=po, r0=r0, j=j: e.tensor_tensor(out=gbt[r0:r0 + 64, j, 0:n], in0=po[0:64, 0:n],
                                                                      in1=rec[64:128, 0:n], op=ALU.mult), [po, rec], [gbt])
        rms_rstd(CS[0], [gbt[:, 0, 0:n], gbt[:, 1, 0:n]], [gbt, gbt], cstb[:, C_ONES:C_ONES + 128], 128, n, 256.0)
        for j in range(2):
            norm_apply(CS[0], OM, OM[:, j, c0:c0 + n], gbt, gbt[:, j, 0:n], G_OMEM + j, 128, n)

    for s in list(range(NOWN, NSLOT)) + list(range(NOWN)):
        phase1(s)

    wukT_sb = sb("wukT_sb", (64, 1, 8 * 128), BF16)
    kfull = sb("kfull", (96, NS), F32)
    qfs = sb("qfs", (96, NS), F32)
    stt = sb("stt", (128, 2, 2, NS), F32)
    qg = sb("qg", (64, NS), BF16)

    def sampleA():
        n = NS
        x = xs
        P.dma("gpsimd", lambda e: e.dma_start(out=wukT_sb[:, 0, :], in_=w_ukT[:, :]), key=wukT_sb, reads=[], writes=[wukT_sb])
        load("sync", x, x[:, :, :], xsT, xsT.t.rearrange("(c p) t -> p c t", p=128))
        load("sync", rC, rC[:, 0:n], ropeCs, ropeCs[:, :])
        load("sync", rS, rS[:, 0:n], ropeSs, ropeSs[:, :])
        load("sync", rC96, rC96[64:96, 0:n], ropeCs, ropeCs[:, :])
        load("sync", rS96, rS96[64:96, 0:n], ropeSs, ropeSs[:, :])
        load("sync", stt, stt[:, :, :, :], stateT, stateT.t.rearrange("(c p) k t -> p c k t", p=128))
        for j in range(2):
            P.dma("sync", lambda e, j=j: e.dma_start(out=o_state_s[j * 128:(j + 1) * 128, :], in_=stt[:, j, 1, :]),
                  key=stt, reads=[stt], writes=[])
        ffn_half(T, x, n, sc1, G_FFN1)
        rmsnorm_d(T, x, n, G_MIX)
        proj(T, pz, 128, n, win_sb, 256)
        ew("scalar", lambda e: e.copy(out=t0[:, 0:n], in_=pz[:, 0:n]), [pz], [t0])
        rms_rstd(T, [t0[:, 0:n]], [t0], cstb[:, C_ONES:C_ONES + 128], 128, n, 128.0)
        norm_apply(T, ckvs, ckvs[:, 0:n], t0, t0[:, 0:n], G_KV, 128, n)
        P.dma("sync", lambda e: e.dma_start(out=o_ckv_s[:, :], in_=ckvs[:, 0:n]), key=ckvs, reads=[ckvs], writes=[])
        ew("gpsimd", lambda e: e.tensor_copy(out=ckvb[:, 0:n], in_=ckvs[:, 0:n]), [ckvs], [ckvb])
        proj(T, pz, 32, n, win_sb, 384)
        ew("scalar", lambda e: e.copy(out=t1[0:32, 0:n], in_=pz[0:32, 0:n]), [pz], [t1])
        rms_rstd(T, [t1[0:32, 0:n]], [t1], cstb[0:32, C_ONES32:C_ONES32 + 32], 32, n, 32.0)
        norm_apply(T, t2, t2[0:32, 0:n], t1, t1[0:32, 0:n], G_KR, 32, n)
        mm(pz, pz[0:32, 0:n], cst, cst[0:32, C_ROT32:C_ROT32 + 32], t2, t2[0:32, 0:n], True, True)
        ew("vector", lambda e: e.tensor_tensor(out=t1[0:32, 0:n], in0=pz[0:32, 0:n], in1=rS[0:32, 0:n], op=ALU.mult),
           [pz, rS], [t1])
        ew("gpsimd", lambda e: e.tensor_tensor(out=t2[0:32, 0:n], in0=t2[0:32, 0:n], in1=rC[0:32, 0:n], op=ALU.mult),
           [t2, rC], [t2])
        ew("vector", lambda e: e.tensor_tensor(out=kr_o[0:32, 0:n], in0=t1[0:32, 0:n], in1=t2[0:32, 0:n], op=ALU.add),
           [t1, t2], [kr_o])
        P.dma("sync", lambda e: e.dma_start(out=o_kr_s[:, :], in_=kr_o[0:32, 0:n]), key=kr_o, reads=[kr_o], writes=[])
        ew("vector", lambda e: e.tensor_copy(out=kfull[64:96, 0:n], in_=kr_o[0:32, 0:n]), [kr_o], [kfull])
        for j in range(2):
            proj(T, pz, 128, n, win_sb, 416 + 128 * j)
            ew("scalar", lambda e: e.copy(out=t0[:, 0:n], in_=pz[:, 0:n]), [pz], [t0])
            proj(T, pz, 128, n, win_sb, 928 + 128 * j)
            ew("vector", lambda e, j=j: e.tensor_tensor(out=uext[:, j, 0:n], in0=t0[:, 0:n], in1=pz[:, 0:n], op=ALU.mult),
               [t0, pz], [uext])
            P.dma("sync", lambda e, j=j: e.dma_start(out=o_conv_s[j * 128:(j + 1) * 128, :], in_=uext[:, j, 0:n]),
                  key=uext, reads=[uext], writes=[])
            proj(T, pz, 128, n, win_sb, 672 + 128 * j)
            ew("vector", lambda e, j=j: e.tensor_scalar(out=t0[:, 0:n], in0=stt[:, j, 0, :], scalar1=gn[:, G_CW0 + j:G_CW0 + j + 1],
                                                        scalar2=None, op0=ALU.mult), [stt, gn], [t0])
            ew("vector", lambda e, j=j: e.scalar_tensor_tensor(out=t0[:, 0:n], in0=stt[:, j, 1, :],
                                                               scalar=gn[:, G_CW1 + j:G_CW1 + j + 1], in1=t0[:, 0:n],
                                                               op0=ALU.mult, op1=ALU.add), [stt, gn, t0], [t0])
            ew("vector", lambda e, j=j: e.scalar_tensor_tensor(out=t0[:, 0:n], in0=uext[:, j, 0:n],
                                                               scalar=gn[:, G_CW2 + j:G_CW2 + j + 1], in1=t0[:, 0:n],
                                                               op0=ALU.mult, op1=ALU.add), [uext, gn, t0], [t0])
            ew("vector", lambda e, j=j: e.tensor_tensor(out=gbt[:, j, 0:n], in0=t0[:, 0:n], in1=pz[:, 0:n], op=ALU.mult),
               [t0, pz], [gbt])
        rms_rstd(T, [gbt[:, 0, 0:n], gbt[:, 1, 0:n]], [gbt, gbt], cstb[:, C_ONES:C_ONES + 128], 128, n, 256.0)
        for j in range(2):
            norm_apply(T, YCs, YCs[:, j, 0:n], gbt, gbt[:, j, 0:n], G_OCONV + j, 128, n)
        for j in range(2):
            proj(T, pz, 128, n, win_sb, 1184 + 128 * j)
            ew("scalar", lambda e: e.copy(out=t0[:, 0:n], in_=pz[:, 0:n]), [pz], [t0])
            rms_rstd(T, [t0[:, 0:n]], [t0], cstb[:, C_BLK64:C_BLK64 + 128], 128, n, 64.0)
            norm_apply(T, mqs, mqs[:, j, 0:n], t0, t0[:, 0:n], G_MQ, 128, n)
        for j in range(2):
            proj(T, pz, 128, n, win_sb, 128 * j)
            ew("scalar", lambda e, j=j: e.copy(out=cq[:, j, 0:n], in_=pz[:, 0:n]), [pz], [cq])
        rms_rstd(T, [cq[:, 0, 0:n], cq[:, 1, 0:n]], [cq, cq], cstb[:, C_ONES:C_ONES + 128], 128, n, 256.0)
        for j in range(2):
            norm_apply(T, cqn, cqn[:, j, 0:n], cq, cq[:, j, 0:n], G_QL + j, 128, n)
        psn = pb[0]
        for h in range(8):
            for j in range(2):
                mm(pz, pz[0:96, 0:n], wuq_sb, wuq_sb[:, j, h * 96:(h + 1) * 96], cqn, cqn[:, j, 0:n], j == 0, j == 1)
            ew("scalar", lambda e: e.copy(out=t1[0:96, 0:n], in_=pz[0:96, 0:n]), [pz], [t1])
            rms_rstd(T, [t1[0:96, 0:n]], [t1], cstb[0:96, C_BLK96:C_BLK96 + 96], 96, n, 1.0)
            norm_apply(T, qn, qn[0:96, 0:n], t1, t1[0:96, 0:n], G_Q96, 96, n)
            mm(pz, pz[0:96, 0:n], cst, cst[0:96, C_ROT96:C_ROT96 + 96], qn, qn[0:96, 0:n], True, True)
            ew("vector", lambda e: e.tensor_tensor(out=t1[0:96, 0:n], in0=pz[0:96, 0:n], in1=rS96[0:96, 0:n], op=ALU.mult),
               [pz, rS96], [t1])
            ew("gpsimd", lambda e: e.tensor_tensor(out=qn[0:96, 0:n], in0=qn[0:96, 0:n], in1=rC96[0:96, 0:n], op=ALU.mult),
               [qn, rC96], [qn])
            ew("vector", lambda e: e.tensor_tensor(out=qfs[0:96, 0:n], in0=t1[0:96, 0:n], in1=qn[0:96, 0:n], op=ALU.add),
               [t1, qn], [qfs])
            ew("vector", lambda e, h=h: e.tensor_copy(out=qrs[0:32, :, h], in_=qfs[64:96, 0:n]), [qfs], [qrs])
            ew("vector", lambda e: e.tensor_scalar(out=qg[0:64, 0:n], in0=qfs[0:64, 0:n], scalar1=gn[0:64, G_KN:G_KN + 1],
                                                   scalar2=None, op0=ALU.mult), [qfs, gn], [qg])
            mm(pz, pz[:, 0:n], wukT_sb, wukT_sb[0:64, 0, h * 128:(h + 1) * 128], qg, qg[0:64, 0:n], True, True)
            ew("scalar", lambda e, h=h: e.copy(out=qabs[:, :, h], in_=pz[:, 0:n]), [pz], [qabs])
            mm(pz, pz[0:64, 0:n], wuk_sb, wuk_sb[:, 0, h * 64:(h + 1) * 64], ckvb, ckvb[:, 0:n], True, True)
            ew("scalar", lambda e: e.copy(out=t1[0:64, 0:n], in_=pz[0:64, 0:n]), [pz], [t1])
            rms_rstd(T, [t1[0:64, 0:n]], [t1], cstb[0:64, C_ONES64:C_ONES64 + 64], 64, n, 64.0)
            norm_apply(T, kfull, kfull[0:64, 0:n], t1, t1[0:64, 0:n], G_KN, 64, n)
            ew("vector", lambda e: e.tensor_tensor(out=t2[0:96, 0:n], in0=qfs[0:96, 0:n], in1=kfull[0:96, 0:n], op=ALU.mult),
               [qfs, kfull], [t2])
            mm(psn, psn[:, h * NS:(h + 1) * NS], cst, cst[0:96, C_ONES:C_ONES + 128], t2, t2[0:96, 0:n], True, True)
        ew("scalar", lambda e: e.activation(out=pnew[:, :, :], in_=psn[:, 0:8 * NS].rearrange("p (h b) -> p h b", h=8),
                                            func=AF.Exp, scale=MLA_SCALE), [psn], [pnew])

    sampleA()
    P.barrier()
    print("phase1 SBUF peak bytes/partition:", A.peak - A.BASE)
    A.reset(persist_mark)


    def sampleB():
        mB = A.mark()
        wuk2 = sb("wuk2", (128, 1, 512), BF16)
        wuv2 = sb("wuv2", (128, 1, 512), BF16)
        load_w(wuk2, w_uk, 1, 512)
        load_w(wuv2, w_uv, 1, 512)
        pt_sb = sb("pt_sb", (128, NS), I32)
        idx8a = sb("idx8a", (128, NS, 8), I32)
        idx2a = sb("idx2a", (128, NS, 2), I32)
        G = [sb("G%d" % i, (128, 128, 128), BF16) for i in range(2)]
        GR = [sb("GR%d" % i, (128, 128, 32), BF16) for i in range(2)]
        ckvT = [sb("ckvT%d" % i, (128, 512), BF16) for i in range(2)]
        krT = [sb("krT%d" % i, (32, 512), BF16) for i in range(2)]
        sq = [sb("sq%d" % i, (128, 512), BF16) for i in range(2)]
        SS = sb("SS", (128, 128, 8), F32)
        inv = sb("inv", (128, 128, 8), F32)
        s1 = sb("s1", (128, 128, 8), F32)
        pT = sb("pT", (128, 128, 8), BF16)
        LP = sb("LP", (128, NS, 8), F32)
        load("sync", pt_sb, pt_sb[:, :], ptT, ptT[:, :])
        PA = [pb[0], pb[1]]
        PR = [pb[2], pb[3]]
        PKN = [pb[4], pb[5]]
        ptr_ = pb[6]
        pacc = pb[7]
        ptr_bf = ptr_.t[:, :].bitcast(BF16)
        identb = cstb[:, C_ID:C_ID + 128]
        for s_ in range(8):
            ew("vector", lambda e, s_=s_: e.tensor_scalar(out=idx8a[:, :, s_], in0=pt_sb[:, :], scalar1=8, scalar2=s_,
                                                          op0=ALU.mult, op1=ALU.add), [pt_sb], [idx8a])
        for s_ in range(2):
            ew("vector", lambda e, s_=s_: e.tensor_scalar(out=idx2a[:, :, s_], in0=pt_sb[:, :], scalar1=2, scalar2=s_,
                                                          op0=ALU.mult, op1=ALU.add), [pt_sb], [idx2a])
        for b in range(NS):
            k = b % 2
            g_, gr_ = G[k], GR[k]
            for s_ in range(8):
                P.dma("gpsimd", lambda e, b=b, s_=s_, g_=g_: e.indirect_dma_start(
                    out=g_[:, s_ * 16:(s_ + 1) * 16, :].rearrange("p t l -> p (t l)"), out_offset=None, in_=cckv[:, :],
                    in_offset=bass.IndirectOffsetOnAxis(ap=idx8a[:, b, s_:s_ + 1], axis=0)), key=g_, reads=[idx8a], writes=[g_])
            for s_ in range(2):
                P.dma("gpsimd", lambda e, b=b, s_=s_, gr_=gr_: e.indirect_dma_start(
                    out=gr_[:, s_ * 64:(s_ + 1) * 64, :].rearrange("p t l -> p (t l)"), out_offset=None, in_=ckr[:, :],
                    in_offset=bass.IndirectOffsetOnAxis(ap=idx2a[:, b, s_:s_ + 1], axis=0)), key=gr_, reads=[idx2a], writes=[gr_])
            for t4 in range(32):
                ct, kt_ = ckvT[t4 % 2], krT[t4 % 2]
                for i in range(4):
                    t = t4 * 4 + i
                    P.op("tensor", lambda e, t=t, i=i, g_=g_: e.transpose(ptr_bf[:, i * 128:(i + 1) * 128], g_[:, t, :], identb),
                         reads=[g_, cstb], writes=[ptr_])
                for i in range(4):
                    t = t4 * 4 + i
                    P.op("tensor", lambda e, t=t, i=i, gr_=gr_: e.transpose(ptr_bf[0:32, 512 + i * 128:512 + (i + 1) * 128],
                                                                            gr_[:, t, :], identb),
                         reads=[gr_, cstb], writes=[ptr_])
                ew("vector", lambda e, ct=ct: e.tensor_copy(out=ct[:, :], in_=ptr_bf[:, 0:512]), [ptr_], [ct])
                ew("vector", lambda e, kt_=kt_: e.tensor_copy(out=kt_[:, :], in_=ptr_bf[0:32, 512:1024]), [ptr_], [kt_])
                for i in range(4):
                    t = t4 * 4 + i
                    pk = PKN[t % 2]
                    sq_ = sq[t % 2]
                    mm(pk, pk[:, :], ct, ct[:, i * 128:(i + 1) * 128], wuk2, wuk2[:, 0, :], True, True)
                    ew("scalar", lambda e, pk=pk, sq_=sq_: e.activation(out=sq_[:, :], in_=pk[:, :], func=AF.Square), [pk], [sq_])
                    ew("vector", lambda e, sq_=sq_, t=t: e.tensor_reduce(out=SS[:, t, :], in_=sq_[:, :].rearrange("p (h d) -> p h d", h=8),
                                                                         axis=AX.X, op=ALU.add), [sq_], [SS])
                    pa, pr = PA[t // 64], PR[t // 64]
                    c8 = (t % 64) * 8
                    mm(pa, pa[:, c8:c8 + 8], ct, ct[:, i * 128:(i + 1) * 128], qabs, qabs[:, b, :], True, True)
                    mm(pr, pr[:, c8:c8 + 8], kt_, kt_[0:32, i * 128:(i + 1) * 128], qrs, qrs[0:32, b, :], True, True)
            SSf = SS[:, :, :].rearrange("p t h -> p (t h)")
            invf = inv[:, :, :].rearrange("p t h -> p (t h)")
            s1f = s1[:, :, :].rearrange("p t h -> p (t h)")
            pTf = pT[:, :, :].rearrange("p t h -> p (t h)")
            ew("scalar", lambda e, SSf=SSf, invf=invf: e.activation(out=invf, in_=SSf, func=AF.Sqrt, bias=EPS, scale=1.0 / 64), [SS], [inv])
            ew("vector", lambda e, invf=invf: e.reciprocal(out=invf, in_=invf), [inv], [inv])
            for hf in range(2):
                ew("vector", lambda e, hf=hf, invf=invf, s1f=s1f: e.tensor_tensor(out=s1f[:, hf * 512:(hf + 1) * 512], in0=PA[hf][:, :],
                                                                                  in1=invf[:, hf * 512:(hf + 1) * 512], op=ALU.mult),
                   [PA[hf], inv], [s1])
                ew("vector", lambda e, hf=hf, s1f=s1f: e.tensor_tensor(out=s1f[:, hf * 512:(hf + 1) * 512], in0=PR[hf][:, :],
                                                                       in1=s1f[:, hf * 512:(hf + 1) * 512], op=ALU.add),
                   [PR[hf], s1], [s1])
            ew("scalar", lambda e, s1f=s1f, pTf=pTf: e.activation(out=pTf, in_=s1f, func=AF.Exp, scale=MLA_SCALE), [s1], [pT])
            ew("vector", lambda e, b=b: e.tensor_reduce(out=LP[:, b, :], in_=pT[:, :, :].rearrange("p t h -> p h t"),
                                                        axis=AX.X, op=ALU.add), [pT], [LP])
            for t in range(128):
                mm(pacc, pacc[:, b * 8:(b + 1) * 8], g_, g_[:, t, :], pT, pT[:, t, :], t == 0, t == 127)
        lsum = pb[0]
        mm(lsum, lsum[:, 0:NS * 8], cst, cst[:, C_ONES:C_ONES + 128], LP, LP[:, :, :].rearrange("p b h -> p (b h)"), True, True)
        num = sb("num", (128, NS, 8), F32)
        den = sb("den", (128, NS, 8), F32)
        olat = sb("olat", (128, NS, 8), BF16)
        pnv = pnew[:, :, :].rearrange("p h b -> p b h")
        ew("vector", lambda e: e.tensor_tensor(out=num[:, :, :], in0=pnv, in1=ckvs[:, 0:NS].unsqueeze(2).to_broadcast([128, NS, 8]),
                                               op=ALU.mult), [pnew, ckvs], [num])
        ew("vector", lambda e: e.tensor_tensor(out=num[:, :, :], in0=pacc[:, 0:NS * 8].rearrange("p (b h) -> p b h", h=8),
                                               in1=num[:, :, :], op=ALU.add), [pacc, num], [num])
        ew("vector", lambda e: e.tensor_tensor(out=den[:, :, :], in0=lsum[:, 0:NS * 8].rearrange("p (b h) -> p b h", h=8),
                                               in1=pnv, op=ALU.add), [lsum, pnew], [den])
        ew("vector", lambda e: e.reciprocal(out=den[:, :, :], in_=den[:, :, :]), [den], [den])
        ew("vector", lambda e: e.tensor_tensor(out=olat[:, :, :], in0=num[:, :, :], in1=den[:, :, :], op=ALU.mult), [num, den], [olat])
        for h in range(8):
            po = pb[1 + h % 2]
            mm(po, po[0:64, 0:NS], wuv2, wuv2[:, 0, h * 64:(h + 1) * 64], olat, olat[:, :, h], True, True)
            r0, jc = (h % 2) * 64, h // 2
            ew("scalar", lambda e, po=po, r0=r0, jc=jc: e.copy(out=OAs[r0:r0 + 64, jc, :], in_=po[0:64, 0:NS]), [po], [OAs])
        Kb = [sb("Kb%d" % i, (128, 2, 256), F32) for i in range(2)]
        Vb = [sb("Vb%d" % i, (128, 2, 256), F32) for i in range(2)]
        Vbb = [sb("Vbb%d" % i, (128, 2, 256), BF16) for i in range(2)]
        qbc = sb("qbc", (128, 2, 128), F32)
        prod = sb("prod", (128, 2, 256), F32)
        scm = sb("scm", (128, 2, 4), F32)
        pm = sb("pm", (128, 2, 4), BF16)
        rden = sb("rden", (128, NS, 4), F32)
        omf = sb("omf", (128, 2, NS), F32)
        pq, pom, pdn = pb[3], pb[4], pb[5]
        ident = cst[:, C_ID:C_ID + 128]
        for b in range(NS):
            k = b % 2
            P.dma("sync", lambda e, k=k, b=b: e.dma_start(out=Kb[k][:, :, :], in_=cmk[b].rearrange("(blk p) f -> p blk f", p=128)),
                  key=Kb[k], reads=[], writes=[Kb[k]])
            P.dma("sync", lambda e, k=k, b=b: e.dma_start(out=Vb[k][:, :, :], in_=cmv[b].rearrange("(blk p) f -> p blk f", p=128)),
                  key=Vb[k], reads=[], writes=[Vb[k]])
            ew("gpsimd", lambda e, k=k: e.tensor_copy(out=Vbb[k][:, :, :], in_=Vb[k][:, :, :]), [Vb[k]], [Vbb[k]])
            ew("vector", lambda e, b=b: e.tensor_copy(out=qbc[:, :, :], in_=mqs[:, :, b:b + 1].to_broadcast([128, 2, 128])), [mqs], [qbc])
            for c in range(2):
                mm(pq, pq[:, c * 128:(c + 1) * 128], qbc, qbc[:, c, :], cst, ident, True, True)
            ew("vector", lambda e, k=k: e.tensor_tensor(out=prod[:, :, :], in0=Kb[k][:, :, :],
                                                        in1=pq[:, 0:256].unsqueeze(1).to_broadcast([128, 2, 256]), op=ALU.mult),
               [Kb[k], pq], [prod])
            ew("vector", lambda e: e.tensor_reduce(out=scm[:, :, :], in_=prod[:, :, :].rearrange("p k (h d) -> p k h d", h=4),
                                                   axis=AX.X, op=ALU.add), [prod], [scm])
            ew("scalar", lambda e: e.activation(out=pm[:, :, :], in_=scm[:, :, :], func=AF.Exp, scale=MEM_SCALE), [scm], [pm])
            for c in range(2):
                for blk in range(2):
                    mm(pom, pom[:, b * 4 + 2 * c:b * 4 + 2 * c + 2], Vbb[k], Vbb[k][:, blk, c * 128:(c + 1) * 128],
                       pm, pm[:, blk, 2 * c:2 * c + 2], blk == 0, blk == 1)
            for blk in range(2):
                mm(pdn, pdn[:, b * 4:(b + 1) * 4], cstb, cstb[:, C_ONES:C_ONES + 128], pm, pm[:, blk, :], blk == 0, blk == 1)
        ew("vector", lambda e: e.reciprocal(out=rden[:, :, :], in_=pdn[:, 0:NS * 4].rearrange("p (b h) -> p b h", h=4)), [pdn], [rden])
        pomv = pom[:, 0:NS * 4].rearrange("p (b h) -> p b h", h=4)
        for c in range(2):
            for hl in range(2):
                r0 = hl * 64
                ew("vector", lambda e, c=c, hl=hl, r0=r0: e.tensor_tensor(out=omf[r0:r0 + 64, c, :], in0=pomv[r0:r0 + 64, :, 2 * c + hl],
                                                                          in1=rden[r0:r0 + 64, :, 2 * c + hl], op=ALU.mult),
                   [pom, rden], [omf])
        TB = {"sq": sb("abB", (128, 8, NT), BF16), "std": sb("stdB", (128, NT), F32), "rstd": sb("rstdB", (128, NT), F32), "pn": pb[6]}
        rms_rstd(TB, [omf[:, 0, :], omf[:, 1, :]], [omf, omf], cstb[:, C_ONES:C_ONES + 128], 128, NS, 256.0)
        for j in range(2):
            norm_apply(TB, OMs, OMs[:, j, :], omf, omf[:, j, :], G_OMEM + j, 128, NS)
        rms_rstd(TB, [OAs[:, c, :] for c in range(4)], [OAs] * 4, cstb[:, C_ONES:C_ONES + 128], 128, NS, 512.0)
        for c in range(4):
            norm_apply(TB, onorm_s, onorm_s[:, c, :], OAs, OAs[:, c, :], G_OMLA + c, 128, NS)
        P.barrier()
        print("sampleB SBUF peak bytes/partition:", A.peak - A.BASE)
        A.reset(mB)

    if do_sample:
        sampleB()

    OA = sb("OA", (128, 4, NOWN * NT), F32)
    if do_att:
        KT = [sb("KT%d" % i, (96, NSLOT * NT), BF16) for i in range(2)]
        VA = [sb("VA%d" % i, (128, NSLOT * NT // 128, 128), BF16) for i in range(2)]
        QT = [sb("QT%d" % i, (96, NOWN * NT), BF16) for i in range(2)]
        pts = [sb("pts%d" % i, (128, NT), BF16) for i in range(4)]
        rec2 = sb("rec2", (128, NT), F32)
        for i in range(2):
            ew("gpsimd", lambda e, i=i: e.memset(VA[i][:, :, 64:128], 1.0), [], [VA[i]])
        vsv = VS.t.rearrange("(blk p) (h d) -> p blk h d", p=128, h=8)
        nblk = 0
        for h in range(8):
            kt, va, qt = KT[h % 2], VA[h % 2], QT[h % 2]
            P.dma("sync", lambda e, kt=kt, h=h: e.dma_start(out=kt[0:64, :], in_=KN[h, :, :]), key=kt, reads=[], writes=[kt])
            P.dma("sync", lambda e, kt=kt: e.dma_start(out=kt[64:96, :], in_=KR[:, :]), key=kt, reads=[], writes=[kt])
            for q4 in range(8):
                b0, b1 = q4 * 8, (q4 + 1) * 8
                P.dma("sync", lambda e, va=va, h=h, b0=b0, b1=b1: e.dma_start(out=va[:, b0:b1, 0:64], in_=vsv[:, b0:b1, h, :]),
                      key=va, reads=[], writes=[va])
            P.dma("sync", lambda e, qt=qt, h=h: e.dma_start(out=qt[0:96, :], in_=QS[h, :, :]), key=qt, reads=[], writes=[qt])
            for j in range(NOWN):
                tiles = [(8 + j, "flag")] + [(s, "full") for s in range(j)] + [(8 + s, "full") for s in range(j)] + [(j, "diag")]
                po = pb[4 + (h * NOWN + j) % 2]
                blocks = []
                for (s, kind) in tiles:
                    for kb in range(4):
                        blocks.append((s * NT + kb * 128, kb * 128 if kind == "diag" else 0, kind))
                nb = len(blocks)
                LAG = 2
                issued = []
                for i in range(nb + LAG):
                    if i < nb:
                        k0, q0, kind = blocks[i]
                        ncol = NT - q0
                        ps_ = pb[nblk % 4]
                        pt_ = pts[nblk % 4]
                        nblk += 1
                        issued.append((pt_, q0, ncol, k0))
                        mm(ps_, ps_[:, 0:ncol], kt, kt[0:96, k0:k0 + 128], qt, qt[0:96, j * NT + q0:(j + 1) * NT], True, True)
                        if kind == "flag":
                            ew("scalar", lambda e, ps_=ps_, pt_=pt_, ncol=ncol: e.activation(
                                out=pt_[:, 0:ncol], in_=ps_[:, 0:ncol], func=AF.Exp, scale=MLA_SCALE,
                                bias=gn[:, G_EXPB:G_EXPB + 1]), [ps_, gn], [pt_])
                        else:
                            ew("scalar", lambda e, ps_=ps_, pt_=pt_, ncol=ncol: e.activation(
                                out=pt_[:, 0:ncol], in_=ps_[:, 0:ncol], func=AF.Exp, scale=MLA_SCALE), [ps_], [pt_])
                        if kind == "diag":
                            ew("gpsimd", lambda e, pt_=pt_: e.tensor_tensor(out=pt_[:, 0:128], in0=pt_[:, 0:128],
                                                                            in1=cstb[:, C_TRI:C_TRI + 128], op=ALU.mult),
                               [pt_, cstb], [pt_])
                    if i >= LAG:
                        ii = i - LAG
                        pt_, q0, ncol, k0 = issued[ii]
                        mm(po, po[:, q0:NT], va, va[:, k0 // 128, :], pt_, pt_[:, 0:ncol], ii == 0, ii == nb - 1)
                r0, jc = (h % 2) * 64, h // 2
                ew("vector", lambda e, po=po: e.reciprocal(out=rec2[64:128, :], in_=po[64:128, :]), [po], [rec2])
                ew("vector", lambda e, po=po, r0=r0, jc=jc, j=j: e.tensor_tensor(
                    out=OA[r0:r0 + 64, jc, j * NT:(j + 1) * NT], in0=po[0:64, :], in1=rec2[64:128, :], op=ALU.mult),
                   [po, rec2], [OA])
        P.barrier()
        print("phase2 SBUF peak bytes/partition:", A.peak - A.BASE)
    else:
        ew("gpsimd", lambda e: e.memset(OA[:, :, :], 1.0), [], [OA])
        P.barrier()
    pm2 = A.mark()
    A.reset(persist_mark)
    OA2 = sb("OA", (128, 4, NOWN * NT), F32)
    assert A.mark() <= pm2

    if do_post:
        T3 = alloc_common()
        wo_sb = sb("wo_sb", (128, 8, D), BF16)
        load_w(wo_sb, w_o, 8, D)
        xb3 = sb("xb3", (128, 8, NT), F32)
        onb = sb("onb", (128, 4, NT), BF16)
        yv = o_y.t.rearrange("(c p) t -> p c t", p=128)
        for j in range(NOWN):
            c0 = j * NT
            P.dma("sync", lambda e, c0=c0: e.dma_start(out=xb3[:, :, :], in_=x1v[:, :, c0:c0 + NT]), key=xb3, reads=[], writes=[xb3])
            rms_rstd(T3, [OA[:, c, c0:c0 + NT] for c in range(4)], [OA] * 4, cstb[:, C_ONES:C_ONES + 128], 128, NT, 512.0)
            for c in range(4):
                norm_apply(T3, onb, onb[:, c, 0:NT], OA, OA[:, c, c0:c0 + NT], G_OMLA + c, 128, NT)
            rhs = [(onb, onb[:, c, 0:NT]) for c in range(4)] + [(YC, YC[:, c, c0:c0 + NT]) for c in range(2)] + \
                  [(OM, OM[:, c, c0:c0 + NT]) for c in range(2)]
            for c in range(8):
                y = pb[4 + c % 2]
                for k, (rb, ra) in enumerate(rhs):
                    mm(y, y[:, 0:NT], wo_sb, wo_sb[:, k, c * 128:(c + 1) * 128], rb, ra, k == 0, k == 7)
                ew("vector", lambda e, y=y, c=c: e.tensor_tensor(out=xb3[:, c, 0:NT], in0=y[:, 0:NT], in1=xb3[:, c, 0:NT], op=ALU.add),
                   [y, xb3], [xb3])
            ffn_half(T3, xb3, NT, sc2, G_FFN2)
            P.dma("sync", lambda e, c0=c0: e.dma_start(out=yv[:, :, c0:c0 + NT], in_=xb3[:, :, :]), key=xb3, reads=[xb3], writes=[])
        rhs = [(onorm_s, onorm_s[:, c, :]) for c in range(4)] + [(YCs, YCs[:, c, :]) for c in range(2)] + \
              [(OMs, OMs[:, c, :]) for c in range(2)]
        for c in range(8):
            y = pb[4 + c % 2]
            for k, (rb, ra) in enumerate(rhs):
                mm(y, y[:, 0:NS], wo_sb, wo_sb[:, k, c * 128:(c + 1) * 128], rb, ra, k == 0, k == 7)
            ew("vector", lambda e, y=y, c=c: e.tensor_tensor(out=xs[:, c, :], in0=y[:, 0:NS], in1=xs[:, c, :], op=ALU.add),
               [y, xs], [xs])
        ffn_half(T3, xs, NS, sc2, G_FFN2)
        P.dma("sync", lambda e: e.dma_start(out=o_ys.t.rearrange("(c p) t -> p c t", p=128), in_=xs[:, :, :]), key=xs, reads=[xs], writes=[])
        print("phase3 SBUF peak bytes/partition:", A.peak - A.BASE)

    P.emit()
    return nc, stack


def _rope_tables(pos):
    inv_freq = (np.float32(10000.0) ** (-np.arange(0, 32, 2, dtype=np.float32) / np.float32(32))).astype(np.float32)
    ang = (pos.astype(np.float32)[:, None] * inv_freq[None, :]).astype(np.float32)
    c = np.cos(ang.astype(np.float64)).astype(np.float32).T
    s = np.sin(ang.astype(np.float64)).astype(np.float32).T
    C = np.concatenate([c, c], axis=0)
    S = np.concatenate([-s, s], axis=0)
    return np.ascontiguousarray(C), np.ascontiguousarray(S)


def _slot_tiles(par):
    own = [2 * j + par for j in range(8)]
    oth = [2 * j + 1 - par for j in range(8)]
    return own + oth


def _constants():
    f32 = np.float32
    c = np.zeros((128, NCONST), f32)
    c[:, C_ONES:C_ONES + 128] = 1.0
    c[0:64, C_BLK64:C_BLK64 + 64] = 1.0
    c[64:128, C_BLK64 + 64:C_BLK64 + 128] = 1.0
    for m in range(32):
        c[(m + 16) % 32, C_ROT32 + m] = 1.0
    c[0:32, C_ONES32:C_ONES32 + 32] = 1.0
    c[0:64, C_BLK96:C_BLK96 + 64] = 1.0 / 64
    c[64:96, C_BLK96 + 64:C_BLK96 + 96] = 1.0 / 32
    for m in range(64, 96):
        c[64 + ((m - 64 + 16) % 32), C_ROT96 + m] = 1.0
    k = np.arange(128)[:, None]
    q = np.arange(128)[None, :]
    c[:, C_TRI:C_TRI + 128] = (k <= q).astype(f32)
    c[0:64, C_ONES64:C_ONES64 + 64] = 1.0
    c[:, C_ID:C_ID + 128] = np.eye(128, dtype=f32)
    return c


def _gains(inp, par):
    f32 = np.float32
    g = np.zeros((128, NGAIN), f32)
    v = lambda k: np.asarray(inp[k], f32)[0]
    g[:, G_FFN1:G_FFN1 + 8] = v("g_ffn1").reshape(8, 128).T
    g[:, G_MIX:G_MIX + 8] = v("g_mix").reshape(8, 128).T
    g[:, G_MEM:G_MEM + 8] = v("g_mem").reshape(8, 128).T
    g[:, G_KV] = v("g_kv_lora")
    g[0:32, G_KR] = v("g_kr")
    g[:, G_MK] = np.tile(v("g_mk"), 2)
    g[:, G_QL:G_QL + 2] = v("g_q_lora").reshape(2, 128).T
    g[0:64, G_Q96] = v("g_qn")
    g[64:96, G_Q96] = v("g_qr")
    g[0:64, G_KN] = v("g_kn")
    g[:, G_KN2] = np.tile(v("g_kn"), 2)
    cw = np.asarray(inp["conv_w"], f32)[0]
    g[:, G_CW0:G_CW0 + 2] = cw[0].reshape(2, 128).T
    g[:, G_CW1:G_CW1 + 2] = cw[1].reshape(2, 128).T
    g[:, G_CW2:G_CW2 + 2] = cw[2].reshape(2, 128).T
    g[:, G_OCONV:G_OCONV + 2] = v("g_out_conv").reshape(2, 128).T
    g[:, G_MQ] = np.tile(v("g_mq"), 2)
    g[:, G_OMEM:G_OMEM + 2] = v("g_out_mem").reshape(2, 128).T
    g[:, G_OMLA:G_OMLA + 4] = v("g_out_mla").reshape(4, 128).T
    g[:, G_FFN2:G_FFN2 + 8] = v("g_ffn2").reshape(8, 128).T
    g[:, G_FLA] = 1.0 - par
    g[:, G_FLB] = float(par)
    g[:, G_EXPB] = (par - 1) * 30000.0
    return g


def kernel(**inp):
    f32 = np.float32
    x_prompt = np.asarray(inp["x_prompt"], f32)
    n_cores = 8
    consts = _constants()
    w = lambda k: np.asarray(inp[k], f32)[0]
    shared = {
        "w1_gate": w("w1_gate"), "w1_up": w("w1_up"), "w1_down": w("w1_down"),
        "w2_gate": w("w2_gate"), "w2_up": w("w2_up"), "w2_down": w("w2_down"),
        "w_in": w("w_in"), "w_uq": w("w_uq"), "w_uk": w("w_uk").reshape(128, 512), "w_uv": w("w_uv").reshape(128, 512),
        "w_o": w("w_o"), "w_mem_k": w("w_mem_k"), "w_mem_v": w("w_mem_v"), "consts": consts,
        "w_ukT": np.ascontiguousarray(w("w_uk").transpose(2, 1, 0).reshape(64, 8 * 128)),
        "cache_ckv": np.asarray(inp["cache_ckv"], f32)[0].reshape(-1, 2048),
        "cache_krope": np.asarray(inp["cache_krope"], f32)[0].reshape(-1, 2048),
    }
    assert shared["cache_ckv"].shape[0] == N_PHYS * 8
    Cs, Ss = _rope_tables(np.full((NS,), PAST, dtype=np.int64))
    x_sample = np.asarray(inp["x_sample"], f32)
    state_conv = np.asarray(inp["state_conv"], f32)[0]
    page_table = np.asarray(inp["page_table"]).astype(np.int32)
    cache_mem_k = np.asarray(inp["cache_mem_k"], f32)[0]
    cache_mem_v = np.asarray(inp["cache_mem_v"], f32)[0]
    in_maps = []
    for c in range(n_cores):
        b, par = c // 2, c % 2
        tiles = _slot_tiles(par)
        tok = np.concatenate([np.arange(t * NT, (t + 1) * NT) for t in tiles])
        C, S = _rope_tables(tok)
        m = dict(shared)
        m["xT"] = np.ascontiguousarray(x_prompt[b][tok].T)
        m["memT"] = np.ascontiguousarray(np.asarray(inp["mem_prompt"], f32)[b].T)
        m["ropeC"] = C
        m["ropeS"] = S
        m["gains"] = _gains(inp, par)
        sl = slice(c * NS, (c + 1) * NS)
        m["xsT"] = np.ascontiguousarray(x_sample[sl, 0, :].T)
        m["ropeCs"] = Cs
        m["ropeSs"] = Ss
        m["stateT"] = np.ascontiguousarray(state_conv[sl].transpose(2, 1, 0))
        m["ptT"] = np.ascontiguousarray(page_table[sl].T)
        m["cache_mem_k"] = np.ascontiguousarray(cache_mem_k[sl].reshape(NS, N_MEM, 256))
        m["cache_mem_v"] = np.ascontiguousarray(cache_mem_v[sl].reshape(NS, N_MEM, 256))
        in_maps.append(m)

    nc, stack = build_program()
    with stack:
        res = run_bass_kernel_spmd(nc, in_maps, core_ids=list(range(n_cores)))
    R = res.results

    B = 4
    y_prompt = np.zeros((B, SEQ, D), f32)
    y_sample = np.zeros((DEC, 1, D), f32)
    ckv_p = np.zeros((1, B, SEQ, 128), f32)
    kr_p = np.zeros((1, B, SEQ, 32), f32)
    conv_p = np.zeros((1, B, 2, 256), f32)
    mk_p = np.zeros((1, B, N_MEM, 4, 64), f32)
    mv_p = np.zeros((1, B, N_MEM, 4, 64), f32)
    for c in range(n_cores):
        b, par = c // 2, c % 2
        tiles = _slot_tiles(par)
        for s in range(8):
            t = tiles[s]
            y_prompt[b, t * NT:(t + 1) * NT, :] = R[c]["o_y"][:, s * NT:(s + 1) * NT].T
            ckv_p[0, b, t * NT:(t + 1) * NT, :] = R[c]["o_ckv"][:, s * NT:(s + 1) * NT].T
            kr_p[0, b, t * NT:(t + 1) * NT, :] = R[c]["o_kr"][:, s * NT:(s + 1) * NT].T
            if t == 15:
                conv_p[0, b] = R[c]["o_conv"][:, 2 * s:2 * s + 2].T
        if par == 0:
            mk_p[0, b] = R[c]["o_mk"].T.reshape(N_MEM, 4, 64)
            mv_p[0, b] = R[c]["o_mv"].T.reshape(N_MEM, 4, 64)
    ckv_s = np.zeros((1, DEC, 1, 128), f32)
    kr_s = np.zeros((1, DEC, 1, 32), f32)
    conv_s = np.zeros((1, DEC, 2, 256), f32)
    for c in range(n_cores):
        sl = slice(c * NS, (c + 1) * NS)
        y_sample[sl, 0, :] = R[c]["o_ys"].T
        ckv_s[0, sl, 0, :] = R[c]["o_ckv_s"].T
        kr_s[0, sl, 0, :] = R[c]["o_kr_s"].T
        conv_s[0, sl, 0, :] = R[c]["o_state_s"].T
        conv_s[0, sl, 1, :] = R[c]["o_conv_s"].T
    return (y_prompt, y_sample, ckv_p, kr_p, conv_p, mk_p, mv_p, ckv_s, kr_s, conv_s)
```

```python
import contextlib
import numpy as np
import concourse.bass as bass
import concourse.mybir as mybir
from concourse.bass_utils import run_bass_kernel_spmd

F32 = mybir.dt.float32
BF16 = mybir.dt.bfloat16
I32 = mybir.dt.int32
ALU = mybir.AluOpType
AF = mybir.ActivationFunctionType
AX = mybir.AxisListType

D = 1024
DFF = 2816
NM = DFF // 128
SEQ = 8192
NT = 512
NSLOT = 16
NOWN = 8
DEC = 128
EPS = 1e-6
INW = 1440
PAST = 16384
N_MEM = 256
MLA_SCALE = float(96 ** -0.5)
MEM_SCALE = float(64 ** -0.5)
NCONST = 1024
NS = 16
N_PHYS = 20480
NGAIN = 64

G_FFN1, G_MIX, G_MEM, G_KV, G_KR, G_MK, G_QL, G_Q96, G_KN = 0, 8, 16, 24, 25, 26, 27, 29, 30
G_CW0, G_CW1, G_CW2, G_OCONV, G_MQ, G_OMEM, G_OMLA, G_FFN2, G_FLA, G_FLB, G_EXPB, G_KN2 = 31, 33, 35, 37, 39, 40, 42, 46, 54, 55, 56, 57
C_ONES, C_BLK64, C_ROT32, C_ONES32, C_BLK96, C_ROT96, C_TRI, C_ONES64, C_ID = 0, 128, 256, 288, 320, 416, 512, 640, 704


class Buf:
    def __init__(self, name, t=None):
        self.name = name
        self.t = t
        self.last_w = None
        self.readers = []
        self.dma_sem = None
        self.dma_issued = 0

    def __getitem__(self, k):
        return self.t[k]


class Op:
    __slots__ = ("eng", "fn", "waits", "signaled", "sigval", "is_dma", "key", "dma_waits")

    def __init__(self, eng, fn, is_dma=False, key=None):
        self.eng = eng
        self.fn = fn
        self.waits = []
        self.dma_waits = []
        self.signaled = False
        self.sigval = None
        self.is_dma = is_dma
        self.key = key


class Prog:
    ENGS = ("tensor", "scalar", "vector", "gpsimd", "sync")

    def __init__(self, nc, stack):
        self.nc = nc
        self.stack = stack
        self.ops = {e: [] for e in self.ENGS}
        self.esem = {}
        self.dma_keys = []
        self.free_sems = []

    def _dep(self, op, prev):
        if prev is None or prev is op:
            return
        if prev.is_dma:
            op.dma_waits.append((prev.key, prev.key.dma_issued * 16))
            return
        if prev.eng == "tensor" and op.eng == "tensor":
            return
        op.waits.append(prev)
        prev.signaled = True

    def _track(self, op, reads, writes):
        for b in reads:
            self._dep(op, b.last_w)
        for b in writes:
            self._dep(op, b.last_w)
            for r in b.readers:
                self._dep(op, r)
        for b in reads:
            b.readers.append(op)
        for b in writes:
            b.last_w = op
            b.readers = []

    def op(self, eng, fn, reads=(), writes=()):
        o = Op(eng, fn)
        self._track(o, reads, writes)
        self.ops[eng].append(o)
        return o

    def dma(self, eng, fn, key, reads=(), writes=()):
        o = Op(eng, fn, is_dma=True, key=key)
        self._track(o, reads, writes)
        key.dma_issued += 1
        if key.dma_sem is None:
            key.dma_sem = self.stack.enter_context(self.nc.semaphore("d%d_%s" % (len(self.dma_keys), key.name)))
            self.dma_keys.append(key)
        self.ops[eng].append(o)
        return o

    def barrier(self):
        targets = []
        for e in self.ENGS:
            for o in reversed(self.ops[e]):
                if o.fn is not None and not o.is_dma:
                    targets.append(o)
                    o.signaled = True
                    break
        dma_targets = [(k, k.dma_issued * 16) for k in self.dma_keys]
        for e in self.ENGS:
            o = Op(e, None)
            o.waits = [t for t in targets if not (t.eng == e and e in ("tensor", "sync"))]
            o.dma_waits = list(dma_targets)
            self.ops[e].append(o)

    def emit(self):
        nc = self.nc
        for e in self.ENGS:
            self.esem[e] = self.stack.enter_context(nc.semaphore("e_" + e))
            n = 0
            for o in self.ops[e]:
                if o.signaled and not o.is_dma and o.fn is not None:
                    n += 1
                    o.sigval = n
        prog = self

        def run(e, eng):
            waited = {}
            for o in prog.ops[e]:
                need = {}
                for p in o.waits:
                    s = prog.esem[p.eng]
                    need[s] = max(need.get(s, 0), p.sigval)
                for k, v in o.dma_waits:
                    need[k.dma_sem] = max(need.get(k.dma_sem, 0), v)
                for s, v in need.items():
                    if waited.get(s, 0) < v:
                        eng.wait_ge(s, v)
                        waited[s] = v
                if o.fn is None:
                    continue
                ins = o.fn(eng)
                if o.is_dma:
                    ins.then_inc(o.key.dma_sem, 16)
                elif o.signaled:
                    ins.then_inc(prog.esem[e], 1)
            if e == "sync":
                for k in prog.dma_keys:
                    eng.wait_ge(k.dma_sem, k.dma_issued * 16)

        with nc.Block() as block:
            @block.tensor
            def _(eng):
                run("tensor", eng)

            @block.scalar
            def _(eng):
                run("scalar", eng)

            @block.vector
            def _(eng):
                run("vector", eng)

            @block.gpsimd
            def _(eng):
                run("gpsimd", eng)

            @block.sync
            def _(eng):
                run("sync", eng)


class Arena:
    BASE = 16512
    TOP = 229344

    def __init__(self, nc):
        self.nc = nc
        self.ptr = self.BASE
        self.n = 0
        self.peak = 0

    def alloc(self, name, shape, dt):
        esz = 2 if dt == BF16 else 4
        size = int(np.prod(shape[1:])) * esz
        size = (size + 63) // 64 * 64
        assert self.ptr + size <= self.TOP, "SBUF overflow at %s: %d + %d" % (name, self.ptr, size)
        self.n += 1
        h = self.nc.alloc_sbuf_tensor_at("%s_%d" % (name, self.n), list(shape), dt, offset=self.ptr)
        self.ptr += size
        self.peak = max(self.peak, self.ptr)
        return Buf(name, h)

    def mark(self):
        return self.ptr

    def reset(self, m):
        self.ptr = m


def build_program(do_att=True, do_post=True, do_sample=True):
    nc = bass.Bass("TRN2", target_bir_lowering=False)
    stack = contextlib.ExitStack()
    P = Prog(nc, stack)
    A = Arena(nc)

    def dram_in(name, shape, dt=F32):
        return Buf(name, nc.dram_tensor(name, list(shape), dt, kind="ExternalInput").ap())

    def dram_out(name, shape, dt=F32):
        return Buf(name, nc.dram_tensor(name, list(shape), dt, kind="ExternalOutput").ap())

    def dram_tmp(name, shape, dt):
        return Buf(name, nc.dram_tensor(name, list(shape), dt).ap())

    sb = A.alloc

    xT = dram_in("xT", (D, NSLOT * NT))
    memT = dram_in("memT", (D, N_MEM))
    ropeC = dram_in("ropeC", (32, NSLOT * NT))
    ropeS = dram_in("ropeS", (32, NSLOT * NT))
    w1g = dram_in("w1_gate", (D, DFF))
    w1u = dram_in("w1_up", (D, DFF))
    w1d = dram_in("w1_down", (DFF, D))
    w2g = dram_in("w2_gate", (D, DFF))
    w2u = dram_in("w2_up", (D, DFF))
    w2d = dram_in("w2_down", (DFF, D))
    w_in = dram_in("w_in", (D, INW))
    w_uq = dram_in("w_uq", (256, 768))
    w_uk = dram_in("w_uk", (128, 512))
    w_uv = dram_in("w_uv", (128, 512))
    w_o = dram_in("w_o", (D, D))
    wmk = dram_in("w_mem_k", (D, 256))
    wmv = dram_in("w_mem_v", (D, 256))
    gains = dram_in("gains", (128, NGAIN))
    xsT = dram_in("xsT", (D, NS))
    ropeCs = dram_in("ropeCs", (32, NS))
    ropeSs = dram_in("ropeSs", (32, NS))
    stateT = dram_in("stateT", (256, 2, NS))
    ptT = dram_in("ptT", (128, NS), I32)
    cckv = dram_in("cache_ckv", (N_PHYS * 8, 2048))
    ckr = dram_in("cache_krope", (N_PHYS * 2, 2048))
    cmk = dram_in("cache_mem_k", (NS, N_MEM, 256))
    cmv = dram_in("cache_mem_v", (NS, N_MEM, 256))
    w_ukT = dram_in("w_ukT", (64, 8 * 128))
    consts = dram_in("consts", (128, NCONST))

    o_y = dram_out("o_y", (D, NOWN * NT))
    o_ckv = dram_out("o_ckv", (128, NSLOT * NT))
    o_kr = dram_out("o_kr", (32, NSLOT * NT))
    o_conv = dram_out("o_conv", (256, NSLOT * 2))
    o_mk = dram_out("o_mk", (256, N_MEM))
    o_mv = dram_out("o_mv", (256, N_MEM))
    o_ys = dram_out("o_ys", (D, NS))
    o_ckv_s = dram_out("o_ckv_s", (128, NS))
    o_kr_s = dram_out("o_kr_s", (32, NS))
    o_conv_s = dram_out("o_conv_s", (256, NS))
    o_state_s = dram_out("o_state_s", (256, NS))

    sc1 = [dram_tmp("sc1g", (D, DFF), BF16), dram_tmp("sc1u", (D, DFF), BF16), dram_tmp("sc1d", (DFF, D), BF16)]
    sc2 = [dram_tmp("sc2g", (D, DFF), BF16), dram_tmp("sc2u", (D, DFF), BF16), dram_tmp("sc2d", (DFF, D), BF16)]
    X1 = dram_tmp("X1", (D, NOWN * NT), F32)
    KN = dram_tmp("KN", (8, 64, NSLOT * NT), BF16)
    KR = dram_tmp("KR", (32, NSLOT * NT), BF16)
    VS = dram_tmp("VS", (NSLOT * NT, 512), BF16)
    QS = dram_tmp("QS", (8, 96, NOWN * NT), BF16)

    pb = [Buf("pb%d" % i, stack.enter_context(nc.psum_tensor("pb%d" % i, [128, 512], F32))) for i in range(8)]

    gn = sb("gn", (128, NGAIN), F32)
    cst = sb("cst", (128, NCONST), F32)
    cstb = sb("cstb", (128, NCONST), BF16)
    YC = sb("YC", (128, 2, NOWN * NT), BF16)
    OM = sb("OM", (128, 2, NOWN * NT), BF16)
    memKT = sb("memKT", (128, 2, N_MEM), BF16)
    memVA = sb("memVA", (128, 2, 4, 128), BF16)
    UB = sb("UB", (128, 2, NSLOT, 2), F32)
    xs = sb("xs", (128, 8, NS), F32)
    qabs = sb("qabs", (128, NS, 8), BF16)
    qrs = sb("qrs", (32, NS, 8), BF16)
    pnew = sb("pnew", (128, 8, NS), F32)
    ckvs = sb("ckvs", (128, NS), F32)
    YCs = sb("YCs", (128, 2, NS), BF16)
    OMs = sb("OMs", (128, 2, NS), BF16)
    mqs = sb("mqs", (128, 2, NS), F32)
    OAs = sb("OAs", (128, 4, NS), F32)
    onorm_s = sb("onorm_s", (128, 4, NS), BF16)
    persist_mark = A.mark()

    def load(eng, dst, dst_ap, src, src_ap):
        P.dma(eng, lambda e: e.dma_start(out=dst_ap, in_=src_ap), key=dst, reads=[src], writes=[dst])

    def load_w(dst, src, nk, m):
        v = src.t.rearrange("(c p) m -> p c m", p=128)
        step = 1024
        for c in range(nk):
            for m0 in range(0, m, step):
                m1 = min(m, m0 + step)
                load("gpsimd", dst, dst[:, c, m0:m1], src, v[:, c, m0:m1])

    def mm(out_buf, out_ap, lhsT_buf, lhsT_ap, rhs_buf, rhs_ap, start, stop):
        P.op("tensor", lambda e: e.matmul(out_ap, lhsT_ap, rhs_ap, start=start, stop=stop),
             reads=[lhsT_buf, rhs_buf], writes=[out_buf])

    def ew(eng, fn, reads, writes):
        P.op(eng, fn, reads=reads, writes=writes)

    load("sync", gn, gn[:, :], gains, gains[:, :])
    load("sync", cst, cst[:, :], consts, consts[:, :])
    ew("vector", lambda e: e.tensor_copy(out=cstb[:, :], in_=cst[:, :]), [cst], [cstb])
    ew("gpsimd", lambda e: e.memset(memVA[:, :, :, 64:128], 1.0), [], [memVA])
    ew("gpsimd", lambda e: e.memset(UB[:, :, :, :], 0.0), [], [UB])

    def cast_w(dst, src, rows, cols):
        step = 1408 if cols == DFF else 1024
        for r0 in range(0, rows, 128):
            for m0 in range(0, cols, step):
                P.dma("gpsimd", lambda e, r0=r0, m0=m0: e.dma_start(out=dst[r0:r0 + 128, m0:m0 + step],
                                                                    in_=src[r0:r0 + 128, m0:m0 + step]),
                      key=dst, reads=[], writes=[])

    for dst, src, r, c in ((sc1[0], w1g, D, DFF), (sc1[1], w1u, D, DFF), (sc1[2], w1d, DFF, D)):
        cast_w(dst, src, r, c)

    def alloc_common():
        T = {}
        T["hb"] = sb("hb", (128, 8, NT), BF16)
        T["ab"] = sb("ab", (128, NM, NT), BF16)
        T["std"] = sb("std", (128, NT), F32)
        T["rstd"] = sb("rstd", (128, NT), F32)
        T["t0"] = sb("t0", (128, NT), F32)
        T["t1"] = sb("t1", (128, NT), F32)
        T["t2"] = sb("t2", (128, NT), F32)
        T["sgb"] = [sb("sgb%d" % i, (128, NT), BF16) for i in range(2)]
        T["wgc"] = [sb("wgc%d" % i, (128, 8, 128), BF16) for i in range(2)]
        T["wuc"] = [sb("wuc%d" % i, (128, 8, 128), BF16) for i in range(2)]
        T["wdc"] = [sb("wdc%d" % i, (128, NM, 128), BF16) for i in range(2)]
        T["sq"] = T["ab"]
        T["pn"] = pb[6]
        return T

    pn = pb[6]
    pz = pb[7]

    def rms_rstd(T, srcs, src_bufs, ones_ap, rows, n, count):
        sqb, pn_, std, rstd = T["sq"], T["pn"], T["std"], T["rstd"]
        for i, (a, b) in enumerate(zip(srcs, src_bufs)):
            if b in pb:
                ew("scalar", lambda e, a=a, i=i: e.activation(out=sqb[0:rows, i, 0:n], in_=a, func=AF.Square), [b], [sqb])
            else:
                eng_ = ("gpsimd", "vector", "gpsimd", "vector", "scalar", "gpsimd", "vector", "scalar")[i % 8] if len(srcs) >= 4 else "gpsimd"
                if eng_ == "scalar":
                    ew("scalar", lambda e, a=a, i=i: e.activation(out=sqb[0:rows, i, 0:n], in_=a, func=AF.Square), [b], [sqb])
                else:
                    ew(eng_, lambda e, a=a, i=i: e.tensor_tensor(out=sqb[0:rows, i, 0:n], in0=a, in1=a, op=ALU.mult),
                       [b], [sqb])
        for i in range(len(srcs)):
            mm(pn_, pn_[0:rows, 0:n], cstb, ones_ap, sqb, sqb[0:rows, i, 0:n], i == 0, i == len(srcs) - 1)
        ew("scalar", lambda e: e.activation(out=std[0:rows, 0:n], in_=pn_[0:rows, 0:n], func=AF.Sqrt,
                                            bias=EPS, scale=1.0 / count), [pn_], [std])
        ew("vector", lambda e: e.reciprocal(out=rstd[0:rows, 0:n], in_=std[0:rows, 0:n]), [std], [rstd])

    def norm_apply(T, out_buf, out_ap, src_buf, src_ap, gcol, rows, n):
        rstd = T["rstd"]
        ew("vector", lambda e: e.scalar_tensor_tensor(out=out_ap, in0=src_ap, scalar=gn[0:rows, gcol:gcol + 1],
                                                      in1=rstd[0:rows, 0:n], op0=ALU.mult, op1=ALU.mult),
           [src_buf, gn, rstd], [out_buf])

    def rmsnorm_d(T, x, n, gbase):
        hb = T["hb"]
        rms_rstd(T, [x[:, c, 0:n] for c in range(8)], [x] * 8, cstb[:, C_ONES:C_ONES + 128], 128, n, float(D))
        for c in range(8):
            norm_apply(T, hb, hb[:, c, 0:n], x, x[:, c, 0:n], gbase + c, 128, n)

    def ffn_half(T, x, n, sc, gbase):
        hb, ab = T["hb"], T["ab"]
        sgv = sc[0].t.rearrange("(c p) m -> p c m", p=128)
        suv = sc[1].t.rearrange("(c p) m -> p c m", p=128)
        sdv = sc[2].t.rearrange("(m p) c -> p m c", p=128)
        rmsnorm_d(T, x, n, gbase)
        for m in range(NM):
            g, u, sg = pb[m % 2], pb[2 + m % 2], T["sgb"][m % 2]
            wg_, wu_ = T["wgc"][m % 2], T["wuc"][m % 2]
            P.dma("sync", lambda e, wg_=wg_, m=m: e.dma_start(out=wg_[:, :, :], in_=sgv[:, :, m * 128:(m + 1) * 128]),
                  key=wg_, reads=[], writes=[wg_])
            P.dma("sync", lambda e, wu_=wu_, m=m: e.dma_start(out=wu_[:, :, :], in_=suv[:, :, m * 128:(m + 1) * 128]),
                  key=wu_, reads=[], writes=[wu_])
            for c in range(8):
                mm(g, g[:, 0:n], wg_, wg_[:, c, :], hb, hb[:, c, 0:n], c == 0, c == 7)
            for c in range(8):
                mm(u, u[:, 0:n], wu_, wu_[:, c, :], hb, hb[:, c, 0:n], c == 0, c == 7)
            ew("scalar", lambda e, g=g, sg=sg: e.activation(out=sg[:, 0:n], in_=g[:, 0:n], func=AF.Silu), [g], [sg])
            ew("vector", lambda e, u=u, sg=sg, m=m: e.tensor_tensor(out=ab[:, m, 0:n], in0=sg[:, 0:n], in1=u[:, 0:n],
                                                                    op=ALU.mult), [sg, u], [ab])
        for c in range(8):
            y = pb[4 + c % 2]
            wd_ = T["wdc"][c % 2]
            P.dma("sync", lambda e, wd_=wd_, c=c: e.dma_start(out=wd_[:, :, :], in_=sdv[:, :, c * 128:(c + 1) * 128]),
                  key=wd_, reads=[], writes=[wd_])
            for m in range(NM):
                mm(y, y[:, 0:n], wd_, wd_[:, m, :], ab, ab[:, m, 0:n], m == 0, m == NM - 1)
            ew("vector", lambda e, y=y, c=c: e.scalar_tensor_tensor(out=x[:, c, 0:n], in0=y[:, 0:n], scalar=0.5,
                                                                    in1=x[:, c, 0:n], op0=ALU.mult, op1=ALU.add),
               [y, x], [x])

    def proj(T, out_ps, rows, n, wbuf, col0):
        hb = T["hb"]
        for c in range(8):
            mm(out_ps, out_ps[0:rows, 0:n], wbuf, wbuf[:, c, col0:col0 + rows], hb, hb[:, c, 0:n], c == 0, c == 7)

    m0 = A.mark()
    T0 = alloc_common()
    wmk_sb = sb("wmk_sb", (128, 8, 256), BF16)
    wmv_sb = sb("wmv_sb", (128, 8, 256), BF16)
    xm = sb("xm", (128, 8, N_MEM), F32)
    mko = [sb("mko%d" % i, (128, N_MEM), F32) for i in range(2)]
    mvo = sb("mvo", (128, 2, N_MEM), F32)
    load_w(wmk_sb, wmk, 8, 256)
    load_w(wmv_sb, wmv, 8, 256)
    load("sync", xm, xm[:, :, :], memT, memT.t.rearrange("(c p) t -> p c t", p=128))
    rmsnorm_d(T0, xm, N_MEM, G_MEM)
    for j in range(2):
        proj(T0, pz, 128, N_MEM, wmk_sb, 128 * j)
        ew("scalar", lambda e: e.copy(out=T0["t0"][:, 0:N_MEM], in_=pz[:, 0:N_MEM]), [pz], [T0["t0"]])
        rms_rstd(T0, [T0["t0"][:, 0:N_MEM]], [T0["t0"]], cstb[:, C_BLK64:C_BLK64 + 128], 128, N_MEM, 64.0)
        norm_apply(T0, mko[j], mko[j][:, 0:N_MEM], T0["t0"], T0["t0"][:, 0:N_MEM], G_MK, 128, N_MEM)
        P.dma("sync", lambda e, j=j: e.dma_start(out=o_mk[j * 128:(j + 1) * 128, :], in_=mko[j][:, 0:N_MEM]),
              key=mko[j], reads=[mko[j]], writes=[])
        ew("gpsimd", lambda e, j=j: e.tensor_copy(out=memKT[:, j, :], in_=mko[j][:, 0:N_MEM]), [mko[j]], [memKT])
        proj(T0, pz, 128, N_MEM, wmv_sb, 128 * j)
        ew("scalar", lambda e, j=j: e.copy(out=mvo[:, j, 0:N_MEM], in_=pz[:, 0:N_MEM]), [pz], [mvo])
        P.dma("sync", lambda e, j=j: e.dma_start(out=o_mv[j * 128:(j + 1) * 128, :], in_=mvo[:, j, 0:N_MEM]),
              key=mvo, reads=[mvo], writes=[])
    for kb in range(2):
        for c in range(8):
            mm(pz, pz[:, 0:256], T0["hb"], T0["hb"][:, c, kb * 128:(kb + 1) * 128], wmv_sb, wmv_sb[:, c, :], c == 0, c == 7)
        ew("scalar", lambda e, kb=kb: e.copy(out=memVA[:, kb, :, 0:64],
                                             in_=pz[:, 0:256].rearrange("p (h d) -> p h d", h=4)), [pz], [memVA])
    P.barrier()
    A.reset(m0)
    sc2_jobs = []
    for dst, src, r, c in ((sc2[0], w2g, D, DFF), (sc2[1], w2u, D, DFF), (sc2[2], w2d, DFF, D)):
        step = 1408 if c == DFF else 1024
        for r0 in range(0, r, 128):
            for m0_ in range(0, c, step):
                sc2_jobs.append((dst, src, r0, m0_, step))

    T = alloc_common()
    win_sb = sb("win_sb", (128, 8, INW), BF16)
    wuq_sb = sb("wuq_sb", (128, 2, 768), BF16)
    wuk_sb = sb("wuk_sb", (128, 1, 512), BF16)
    wuv_sb = sb("wuv_sb", (128, 1, 512), BF16)
    load_w(win_sb, w_in, 8, INW)
    load_w(wuq_sb, w_uq, 2, 768)
    load_w(wuk_sb, w_uk, 1, 512)
    load_w(wuv_sb, w_uv, 1, 512)
    xb = sb("xb", (128, 8, NT), F32)
    ckv_o = sb("ckv_o", (128, NT), F32)
    ckvb = sb("ckvb", (128, NT), BF16)
    kr_o = sb("kr_o", (32, NT), F32)
    krb = sb("krb", (32, NT), BF16)
    uext = sb("uext", (128, 2, NT + 2), F32)
    rC = sb("rC", (32, NT), F32)
    rS = sb("rS", (32, NT), F32)
    rC96 = sb("rC96", (96, NT), F32)
    rS96 = sb("rS96", (96, NT), F32)
    knb = [sb("knb%d" % i, (128, NT), BF16) for i in range(2)]
    PZ = [pb[7], pb[5]]
    CS = [{"sq": sb("csq0", (128, 2, NT), BF16), "pn": pb[6], "std": T["std"], "rstd": T["rstd"]},
          {"sq": sb("csq1", (128, 2, NT), BF16), "pn": pb[4], "std": sb("cstd1", (128, NT), F32), "rstd": sb("crstd1", (128, NT), F32)}]
    vb = [sb("vb%d" % i, (128, 512), BF16) for i in range(2)]
    cq = sb("cq", (128, 2, NT), F32)
    cqn = sb("cqn", (128, 2, NT), BF16)
    qn = sb("qn", (96, NT), F32)
    qb16 = [sb("qb16%d" % i, (96, NT), BF16) for i in range(2)]
    gbt = sb("gbt", (128, 2, NT), F32)
    mqn = sb("mqn", (128, 2, NT), BF16)
    ptm = [sb("ptm%d" % i, (128, NT), BF16) for i in range(2)]
    rec = sb("rec", (128, NT), F32)
    t0, t1, t2 = T["t0"], T["t1"], T["t2"]
    ew("gpsimd", lambda e: e.memset(rC96[0:64, :], 1.0), [], [rC96])
    ew("gpsimd", lambda e: e.memset(rS96[0:64, :], 0.0), [], [rS96])

    xv = xT.t.rearrange("(c p) t -> p c t", p=128)
    x1v = X1.t.rearrange("(c p) t -> p c t", p=128)

    def phase1(s):
        own = s < NOWN
        n = NT
        x = xb
        c0 = s * NT
        if s == ORDER[0]:
            load("sync", x, x[:, :, :], xT, xv[:, :, c0:c0 + NT])
        load("sync", rC, rC[:, :], ropeC, ropeC[:, c0:c0 + NT])
        load("sync", rS, rS[:, :], ropeS, ropeS[:, c0:c0 + NT])
        if own:
            load("sync", rC96, rC96[64:96, :], ropeC, ropeC[:, c0:c0 + NT])
            load("sync", rS96, rS96[64:96, :], ropeS, ropeS[:, c0:c0 + NT])
        ffn_half(T, x, n, sc1, G_FFN1)
        if own:
            P.dma("sync", lambda e: e.dma_start(out=x1v[:, :, c0:c0 + NT], in_=x[:, :, :]), key=x, reads=[x], writes=[])
        rmsnorm_d(T, x, n, G_MIX)
        nxt = ORDER.index(s) + 1
        if nxt < len(ORDER):
            cn = ORDER[nxt] * NT
            load("sync", x, x[:, :, :], xT, xv[:, :, cn:cn + NT])
        pzA, pzB = PZ
        proj(T, pzA, 128, n, win_sb, 256)
        rms_rstd(CS[0], [pzA[:, 0:n]], [pzA], cstb[:, C_ONES:C_ONES + 128], 128, n, 128.0)
        norm_apply(CS[0], ckv_o, ckv_o[:, 0:n], pzA, pzA[:, 0:n], G_KV, 128, n)
        P.dma("sync", lambda e: e.dma_start(out=o_ckv[:, c0:c0 + NT], in_=ckv_o[:, 0:n]), key=ckv_o, reads=[ckv_o], writes=[])
        ew("gpsimd", lambda e: e.tensor_copy(out=ckvb[:, 0:n], in_=ckv_o[:, 0:n]), [ckv_o], [ckvb])
        proj(T, pzB, 32, n, win_sb, 384)
        rms_rstd(CS[1], [pzB[0:32, 0:n]], [pzB], cstb[0:32, C_ONES32:C_ONES32 + 32], 32, n, 32.0)
        norm_apply(CS[1], t2, t2[0:32, 0:n], pzB, pzB[0:32, 0:n], G_KR, 32, n)
        mm(pzB, pzB[0:32, 0:n], cst, cst[0:32, C_ROT32:C_ROT32 + 32], t2, t2[0:32, 0:n], True, True)
        ew("vector", lambda e: e.tensor_tensor(out=t1[0:32, 0:n], in0=pzB[0:32, 0:n], in1=rS[0:32, 0:n], op=ALU.mult),
           [pzB, rS], [t1])
        ew("gpsimd", lambda e: e.tensor_tensor(out=t2[0:32, 0:n], in0=t2[0:32, 0:n], in1=rC[0:32, 0:n], op=ALU.mult),
           [t2, rC], [t2])
        ew("vector", lambda e: e.tensor_tensor(out=kr_o[0:32, 0:n], in0=t1[0:32, 0:n], in1=t2[0:32, 0:n], op=ALU.add),
           [t1, t2], [kr_o])
        P.dma("sync", lambda e: e.dma_start(out=o_kr[:, c0:c0 + NT], in_=kr_o[0:32, 0:n]), key=kr_o, reads=[kr_o], writes=[])
        ew("gpsimd", lambda e: e.tensor_copy(out=krb[0:32, 0:n], in_=kr_o[0:32, 0:n]), [kr_o], [krb])
        P.dma("sync", lambda e: e.dma_start(out=KR[:, c0:c0 + NT], in_=krb[0:32, 0:n]), key=krb, reads=[krb], writes=[])
        for hp in range(4):
            k2 = hp % 2
            pz_, cs_, kb_ = PZ[k2], CS[k2], knb[k2]
            mm(pz_, pz_[:, 0:n], wuk_sb, wuk_sb[:, 0, hp * 128:(hp + 1) * 128], ckvb, ckvb[:, 0:n], True, True)
            rms_rstd(cs_, [pz_[:, 0:n]], [pz_], cstb[:, C_BLK64:C_BLK64 + 128], 128, n, 64.0)
            norm_apply(cs_, kb_, kb_[:, 0:n], pz_, pz_[:, 0:n], G_KN2, 128, n)
            for hl in range(2):
                P.dma("sync", lambda e, hp=hp, hl=hl, kb_=kb_: e.dma_start(out=KN[2 * hp + hl, :, c0:c0 + NT],
                                                                           in_=kb_[hl * 64:(hl + 1) * 64, 0:n]),
                      key=kb_, reads=[kb_], writes=[])
        for tb in range(n // 128):
            v_ = vb[tb % 2]
            pz_ = PZ[tb % 2]
            mm(pz_, pz_[:, 0:512], ckvb, ckvb[:, tb * 128:(tb + 1) * 128], wuv_sb, wuv_sb[:, 0, :], True, True)
            ew("scalar", lambda e, v_=v_, pz_=pz_: e.copy(out=v_[:, :], in_=pz_[:, 0:512]), [pz_], [v_])
            P.dma("sync", lambda e, v_=v_, tb=tb: e.dma_start(out=VS[c0 + tb * 128:c0 + (tb + 1) * 128, :], in_=v_[:, :]),
                  key=v_, reads=[v_], writes=[])
        for j in range(2):
            proj(T, pzA, 128, n, win_sb, 416 + 128 * j)
            ew("scalar", lambda e: e.copy(out=t0[:, 0:n], in_=pzA[:, 0:n]), [pzA], [t0])
            proj(T, pzB, 128, n, win_sb, 928 + 128 * j)
            ew("vector", lambda e, j=j: e.tensor_tensor(out=uext[:, j, 2:n + 2], in0=t0[:, 0:n], in1=pzB[:, 0:n], op=ALU.mult),
               [t0, pzB], [uext])
        ew("gpsimd", lambda e: e.tensor_copy(out=UB[:, :, s, :], in_=uext[:, :, n:n + 2]), [uext], [UB])
        for j in range(2):
            P.dma("sync", lambda e, j=j: e.dma_start(out=o_conv[j * 128:(j + 1) * 128, 2 * s:2 * s + 2],
                                                     in_=uext[:, j, n:n + 2]), key=uext, reads=[uext], writes=[])
        if not own:
            return
        j_own = s
        if j_own == 0:
            ew("vector", lambda e: e.tensor_scalar(out=uext[:, :, 0:2], in0=UB[:, :, 8, :], scalar1=gn[:, G_FLB:G_FLB + 1],
                                                   scalar2=None, op0=ALU.mult), [UB, gn], [uext])
        else:
            ew("vector", lambda e: e.tensor_scalar(out=uext[:, :, 0:2], in0=UB[:, :, 8 + j_own, :],
                                                   scalar1=gn[:, G_FLB:G_FLB + 1], scalar2=None, op0=ALU.mult),
               [UB, gn], [uext])
            ew("vector", lambda e: e.scalar_tensor_tensor(out=uext[:, :, 0:2], in0=UB[:, :, 8 + j_own - 1, :],
                                                          scalar=gn[:, G_FLA:G_FLA + 1], in1=uext[:, :, 0:2],
                                                          op0=ALU.mult, op1=ALU.add), [UB, gn, uext], [uext])
        for j in range(2):
            pz_ = PZ[j]
            tt = (t0, t1)[j]
            proj(T, pz_, 128, n, win_sb, 672 + 128 * j)
            ew("gpsimd", lambda e, j=j, tt=tt: e.tensor_scalar(out=tt[:, 0:n], in0=uext[:, j, 0:n], scalar1=gn[:, G_CW0 + j:G_CW0 + j + 1],
                                                               scalar2=None, op0=ALU.mult), [uext, gn], [tt])
            ew("vector", lambda e, j=j, tt=tt: e.scalar_tensor_tensor(out=tt[:, 0:n], in0=uext[:, j, 1:n + 1],
                                                                      scalar=gn[:, G_CW1 + j:G_CW1 + j + 1], in1=tt[:, 0:n],
                                                                      op0=ALU.mult, op1=ALU.add), [uext, gn, tt], [tt])
            ew("vector", lambda e, j=j, tt=tt: e.scalar_tensor_tensor(out=tt[:, 0:n], in0=uext[:, j, 2:n + 2],
                                                                      scalar=gn[:, G_CW2 + j:G_CW2 + j + 1], in1=tt[:, 0:n],
                                                                      op0=ALU.mult, op1=ALU.add), [uext, gn, tt], [tt])
            ew("vector", lambda e, j=j, tt=tt, pz_=pz_: e.tensor_tensor(out=gbt[:, j, 0:n], in0=tt[:, 0:n], in1=pz_[:, 0:n], op=ALU.mult),
               [tt, pz_], [gbt])
        rms_rstd(CS[0], [gbt[:, 0, 0:n], gbt[:, 1, 0:n]], [gbt, gbt], cstb[:, C_ONES:C_ONES + 128], 128, n, 256.0)
        for j in range(2):
            norm_apply(CS[0], YC, YC[:, j, c0:c0 + n], gbt, gbt[:, j, 0:n], G_OCONV + j, 128, n)
        proj(T, pzA, 128, n, win_sb, 0)
        proj(T, pzB, 128, n, win_sb, 128)
        rms_rstd(CS[1], [pzA[:, 0:n], pzB[:, 0:n]], [pzA, pzB], cstb[:, C_ONES:C_ONES + 128], 128, n, 256.0)
        norm_apply(CS[1], cqn, cqn[:, 0, 0:n], pzA, pzA[:, 0:n], G_QL, 128, n)
        norm_apply(CS[1], cqn, cqn[:, 1, 0:n], pzB, pzB[:, 0:n], G_QL + 1, 128, n)
        for h in range(8):
            k2 = h % 2
            pz_, cs_, qb_ = PZ[k2], CS[k2], qb16[k2]
            qn_, tq_ = (qn, t2)[k2], (t1, t0)[k2]
            for j in range(2):
                mm(pz_, pz_[0:96, 0:n], wuq_sb, wuq_sb[:, j, h * 96:(h + 1) * 96], cqn, cqn[:, j, 0:n], j == 0, j == 1)
            rms_rstd(cs_, [pz_[0:96, 0:n]], [pz_], cstb[0:96, C_BLK96:C_BLK96 + 96], 96, n, 1.0)
            norm_apply(cs_, qn_, qn_[0:96, 0:n], pz_, pz_[0:96, 0:n], G_Q96, 96, n)
            mm(pz_, pz_[0:96, 0:n], cst, cst[0:96, C_ROT96:C_ROT96 + 96], qn_, qn_[0:96, 0:n], True, True)
            ew("vector", lambda e, pz_=pz_, tq_=tq_: e.tensor_tensor(out=tq_[0:96, 0:n], in0=pz_[0:96, 0:n], in1=rS96[0:96, 0:n], op=ALU.mult),
               [pz_, rS96], [tq_])
            ew("gpsimd", lambda e, qn_=qn_: e.tensor_tensor(out=qn_[0:96, 0:n], in0=qn_[0:96, 0:n], in1=rC96[0:96, 0:n], op=ALU.mult),
               [qn_, rC96], [qn_])
            ew("vector", lambda e, qb_=qb_, tq_=tq_, qn_=qn_: e.tensor_tensor(out=qb_[0:96, 0:n], in0=tq_[0:96, 0:n], in1=qn_[0:96, 0:n], op=ALU.add),
               [tq_, qn_], [qb_])
            P.dma("sync", lambda e, h=h, qb_=qb_: e.dma_start(out=QS[h, :, c0:c0 + NT], in_=qb_[0:96, 0:n]),
                  key=qb_, reads=[qb_], writes=[])
        for j in range(2):
            pz_, cs_ = PZ[j], CS[j]
            proj(T, pz_, 128, n, win_sb, 1184 + 128 * j)
            rms_rstd(cs_, [pz_[:, 0:n]], [pz_], cstb[:, C_BLK64:C_BLK64 + 128], 128, n, 64.0)
            norm_apply(cs_, mqn, mqn[:, j, 0:n], pz_, pz_[:, 0:n], G_MQ, 128, n)
        for h in range(4):
            j, r0 = h // 2, (h % 2) * 64
            po = pb[4 + h % 2]
            for kb in range(2):
                ps_ = pb[2 * (h % 2) + kb]
                pt_ = ptm[kb]
                mm(ps_, ps_[:, 0:n], memKT, memKT[r0:r0 + 64, j, kb * 128:(kb + 1) * 128], mqn, mqn[r0:r0 + 64, j, 0:n], True, True)
                ew("scalar", lambda e, ps_=ps_, pt_=pt_: e.activation(out=pt_[:, 0:n], in_=ps_[:, 0:n], func=AF.Exp, scale=MEM_SCALE),
                   [ps_], [pt_])
                mm(po, po[:, 0:n], memVA, memVA[:, kb, h, :], pt_, pt_[:, 0:n], kb == 0, kb == 1)
            ew("vector", lambda e, po=po: e.reciprocal(out=rec[64:128, 0:n], in_=po[64:128, 0:n]), [po], [rec])
            ew("vector", lambda e, po=po, r0=r0, j=j: e.tensor_tensor(out=gbt[r0:r0 + 64, j, 0:n], in0=po[0:64, 0:n],
                                                                      in1=rec[64:128, 0:n], op=ALU.mult), [po, rec], [gbt])
        rms_rstd(CS[0], [gbt[:, 0, 0:n], gbt[:, 1, 0:n]], [gbt, gbt], cstb[:, C_ONES:C_ONES + 128], 128, n, 256.0)
        for j in range(2):
            norm_apply(CS[0], OM, OM[:, j, c0:c0 + n], gbt, gbt[:, j, 0:n], G_OMEM + j, 128, n)

    ORDER = list(range(NOWN, NSLOT)) + list(range(NOWN))
    per = (len(sc2_jobs) + 7) // 8
    for si, s in enumerate(ORDER):
        phase1(s)
        for (dst, src, r0, m0_, step) in sc2_jobs[si * per:(si + 1) * per]:
            P.dma("gpsimd", lambda e, dst=dst, src=src, r0=r0, m0_=m0_, step=step: e.dma_start(
                out=dst[r0:r0 + 128, m0_:m0_ + step], in_=src[r0:r0 + 128, m0_:m0_ + step]), key=dst, reads=[], writes=[])

    wukT_sb = sb("wukT_sb", (64, 1, 8 * 128), BF16)
    kfull = sb("kfull", (96, NS), F32)
    qfs = sb("qfs", (96, NS), F32)
    stt = sb("stt", (128, 2, 2, NS), F32)
    qg = sb("qg", (64, NS), BF16)

    def sampleA():
        n = NS
        x = xs
        P.dma("gpsimd", lambda e: e.dma_start(out=wukT_sb[:, 0, :], in_=w_ukT[:, :]), key=wukT_sb, reads=[], writes=[wukT_sb])
        load("sync", x, x[:, :, :], xsT, xsT.t.rearrange("(c p) t -> p c t", p=128))
        load("sync", rC, rC[:, 0:n], ropeCs, ropeCs[:, :])
        load("sync", rS, rS[:, 0:n], ropeSs, ropeSs[:, :])
        load("sync", rC96, rC96[64:96, 0:n], ropeCs, ropeCs[:, :])
        load("sync", rS96, rS96[64:96, 0:n], ropeSs, ropeSs[:, :])
        load("sync", stt, stt[:, :, :, :], stateT, stateT.t.rearrange("(c p) k t -> p c k t", p=128))
        for j in range(2):
            P.dma("sync", lambda e, j=j: e.dma_start(out=o_state_s[j * 128:(j + 1) * 128, :], in_=stt[:, j, 1, :]),
                  key=stt, reads=[stt], writes=[])
        ffn_half(T, x, n, sc1, G_FFN1)
        rmsnorm_d(T, x, n, G_MIX)
        proj(T, pz, 128, n, win_sb, 256)
        ew("scalar", lambda e: e.copy(out=t0[:, 0:n], in_=pz[:, 0:n]), [pz], [t0])
        rms_rstd(T, [t0[:, 0:n]], [t0], cstb[:, C_ONES:C_ONES + 128], 128, n, 128.0)
        norm_apply(T, ckvs, ckvs[:, 0:n], t0, t0[:, 0:n], G_KV, 128, n)
        P.dma("sync", lambda e: e.dma_start(out=o_ckv_s[:, :], in_=ckvs[:, 0:n]), key=ckvs, reads=[ckvs], writes=[])
        ew("gpsimd", lambda e: e.tensor_copy(out=ckvb[:, 0:n], in_=ckvs[:, 0:n]), [ckvs], [ckvb])
        proj(T, pz, 32, n, win_sb, 384)
        ew("scalar", lambda e: e.copy(out=t1[0:32, 0:n], in_=pz[0:32, 0:n]), [pz], [t1])
        rms_rstd(T, [t1[0:32, 0:n]], [t1], cstb[0:32, C_ONES32:C_ONES32 + 32], 32, n, 32.0)
        norm_apply(T, t2, t2[0:32, 0:n], t1, t1[0:32, 0:n], G_KR, 32, n)
        mm(pz, pz[0:32, 0:n], cst, cst[0:32, C_ROT32:C_ROT32 + 32], t2, t2[0:32, 0:n], True, True)
        ew("vector", lambda e: e.tensor_tensor(out=t1[0:32, 0:n], in0=pz[0:32, 0:n], in1=rS[0:32, 0:n], op=ALU.mult),
           [pz, rS], [t1])
        ew("gpsimd", lambda e: e.tensor_tensor(out=t2[0:32, 0:n], in0=t2[0:32, 0:n], in1=rC[0:32, 0:n], op=ALU.mult),
           [t2, rC], [t2])
        ew("vector", lambda e: e.tensor_tensor(out=kr_o[0:32, 0:n], in0=t1[0:32, 0:n], in1=t2[0:32, 0:n], op=ALU.add),
           [t1, t2], [kr_o])
        P.dma("sync", lambda e: e.dma_start(out=o_kr_s[:, :], in_=kr_o[0:32, 0:n]), key=kr_o, reads=[kr_o], writes=[])
        ew("vector", lambda e: e.tensor_copy(out=kfull[64:96, 0:n], in_=kr_o[0:32, 0:n]), [kr_o], [kfull])
        for j in range(2):
            proj(T, pz, 128, n, win_sb, 416 + 128 * j)
            ew("scalar", lambda e: e.copy(out=t0[:, 0:n], in_=pz[:, 0:n]), [pz], [t0])
            proj(T, pz, 128, n, win_sb, 928 + 128 * j)
            ew("vector", lambda e, j=j: e.tensor_tensor(out=uext[:, j, 0:n], in0=t0[:, 0:n], in1=pz[:, 0:n], op=ALU.mult),
               [t0, pz], [uext])
            P.dma("sync", lambda e, j=j: e.dma_start(out=o_conv_s[j * 128:(j + 1) * 128, :], in_=uext[:, j, 0:n]),
                  key=uext, reads=[uext], writes=[])
            proj(T, pz, 128, n, win_sb, 672 + 128 * j)
            ew("vector", lambda e, j=j: e.tensor_scalar(out=t0[:, 0:n], in0=stt[:, j, 0, :], scalar1=gn[:, G_CW0 + j:G_CW0 + j + 1],
                                                        scalar2=None, op0=ALU.mult), [stt, gn], [t0])
            ew("vector", lambda e, j=j: e.scalar_tensor_tensor(out=t0[:, 0:n], in0=stt[:, j, 1, :],
                                                               scalar=gn[:, G_CW1 + j:G_CW1 + j + 1], in1=t0[:, 0:n],
                                                               op0=ALU.mult, op1=ALU.add), [stt, gn, t0], [t0])
            ew("vector", lambda e, j=j: e.scalar_tensor_tensor(out=t0[:, 0:n], in0=uext[:, j, 0:n],
                                                               scalar=gn[:, G_CW2 + j:G_CW2 + j + 1], in1=t0[:, 0:n],
                                                               op0=ALU.mult, op1=ALU.add), [uext, gn, t0], [t0])
            ew("vector", lambda e, j=j: e.tensor_tensor(out=gbt[:, j, 0:n], in0=t0[:, 0:n], in1=pz[:, 0:n], op=ALU.mult),
               [t0, pz], [gbt])
        rms_rstd(T, [gbt[:, 0, 0:n], gbt[:, 1, 0:n]], [gbt, gbt], cstb[:, C_ONES:C_ONES + 128], 128, n, 256.0)
        for j in range(2):
            norm_apply(T, YCs, YCs[:, j, 0:n], gbt, gbt[:, j, 0:n], G_OCONV + j, 128, n)
        for j in range(2):
            proj(T, pz, 128, n, win_sb, 1184 + 128 * j)
            ew("scalar", lambda e: e.copy(out=t0[:, 0:n], in_=pz[:, 0:n]), [pz], [t0])
            rms_rstd(T, [t0[:, 0:n]], [t0], cstb[:, C_BLK64:C_BLK64 + 128], 128, n, 64.0)
            norm_apply(T, mqs, mqs[:, j, 0:n], t0, t0[:, 0:n], G_MQ, 128, n)
        for j in range(2):
            proj(T, pz, 128, n, win_sb, 128 * j)
            ew("scalar", lambda e, j=j: e.copy(out=cq[:, j, 0:n], in_=pz[:, 0:n]), [pz], [cq])
        rms_rstd(T, [cq[:, 0, 0:n], cq[:, 1, 0:n]], [cq, cq], cstb[:, C_ONES:C_ONES + 128], 128, n, 256.0)
        for j in range(2):
            norm_apply(T, cqn, cqn[:, j, 0:n], cq, cq[:, j, 0:n], G_QL + j, 128, n)
        psn = pb[0]
        for h in range(8):
            for j in range(2):
                mm(pz, pz[0:96, 0:n], wuq_sb, wuq_sb[:, j, h * 96:(h + 1) * 96], cqn, cqn[:, j, 0:n], j == 0, j == 1)
            ew("scalar", lambda e: e.copy(out=t1[0:96, 0:n], in_=pz[0:96, 0:n]), [pz], [t1])
            rms_rstd(T, [t1[0:96, 0:n]], [t1], cstb[0:96, C_BLK96:C_BLK96 + 96], 96, n, 1.0)
            norm_apply(T, qn, qn[0:96, 0:n], t1, t1[0:96, 0:n], G_Q96, 96, n)
            mm(pz, pz[0:96, 0:n], cst, cst[0:96, C_ROT96:C_ROT96 + 96], qn, qn[0:96, 0:n], True, True)
            ew("vector", lambda e: e.tensor_tensor(out=t1[0:96, 0:n], in0=pz[0:96, 0:n], in1=rS96[0:96, 0:n], op=ALU.mult),
               [pz, rS96], [t1])
            ew("gpsimd", lambda e: e.tensor_tensor(out=qn[0:96, 0:n], in0=qn[0:96, 0:n], in1=rC96[0:96, 0:n], op=ALU.mult),
               [qn, rC96], [qn])
            ew("vector", lambda e: e.tensor_tensor(out=qfs[0:96, 0:n], in0=t1[0:96, 0:n], in1=qn[0:96, 0:n], op=ALU.add),
               [t1, qn], [qfs])
            ew("vector", lambda e, h=h: e.tensor_copy(out=qrs[0:32, :, h], in_=qfs[64:96, 0:n]), [qfs], [qrs])
            ew("vector", lambda e: e.tensor_scalar(out=qg[0:64, 0:n], in0=qfs[0:64, 0:n], scalar1=gn[0:64, G_KN:G_KN + 1],
                                                   scalar2=None, op0=ALU.mult), [qfs, gn], [qg])
            mm(pz, pz[:, 0:n], wukT_sb, wukT_sb[0:64, 0, h * 128:(h + 1) * 128], qg, qg[0:64, 0:n], True, True)
            ew("scalar", lambda e, h=h: e.copy(out=qabs[:, :, h], in_=pz[:, 0:n]), [pz], [qabs])
            mm(pz, pz[0:64, 0:n], wuk_sb, wuk_sb[:, 0, h * 64:(h + 1) * 64], ckvb, ckvb[:, 0:n], True, True)
            ew("scalar", lambda e: e.copy(out=t1[0:64, 0:n], in_=pz[0:64, 0:n]), [pz], [t1])
            rms_rstd(T, [t1[0:64, 0:n]], [t1], cstb[0:64, C_ONES64:C_ONES64 + 64], 64, n, 64.0)
            norm_apply(T, kfull, kfull[0:64, 0:n], t1, t1[0:64, 0:n], G_KN, 64, n)
            ew("vector", lambda e: e.tensor_tensor(out=t2[0:96, 0:n], in0=qfs[0:96, 0:n], in1=kfull[0:96, 0:n], op=ALU.mult),
               [qfs, kfull], [t2])
            mm(psn, psn[:, h * NS:(h + 1) * NS], cst, cst[0:96, C_ONES:C_ONES + 128], t2, t2[0:96, 0:n], True, True)
        ew("scalar", lambda e: e.activation(out=pnew[:, :, :], in_=psn[:, 0:8 * NS].rearrange("p (h b) -> p h b", h=8),
                                            func=AF.Exp, scale=MLA_SCALE), [psn], [pnew])

    sampleA()
    P.barrier()
    print("phase1 SBUF peak bytes/partition:", A.peak - A.BASE)
    A.reset(persist_mark)


    def sampleB():
        mB = A.mark()
        wuk2 = sb("wuk2", (128, 1, 512), BF16)
        wuv2 = sb("wuv2", (128, 1, 512), BF16)
        load_w(wuk2, w_uk, 1, 512)
        load_w(wuv2, w_uv, 1, 512)
        pt_sb = sb("pt_sb", (128, NS), I32)
        idx8a = sb("idx8a", (128, NS, 8), I32)
        idx2a = sb("idx2a", (128, NS, 2), I32)
        G = [sb("G%d" % i, (128, 128, 128), BF16) for i in range(2)]
        GR = [sb("GR%d" % i, (128, 128, 32), BF16) for i in range(2)]
        ckvT = [sb("ckvT%d" % i, (128, 512), BF16) for i in range(2)]
        krT = [sb("krT%d" % i, (32, 512), BF16) for i in range(2)]
        sq = [sb("sq%d" % i, (128, 512), BF16) for i in range(2)]
        SS = sb("SS", (128, 128, 8), F32)
        inv = sb("inv", (128, 128, 8), F32)
        s1 = sb("s1", (128, 128, 8), F32)
        pT = sb("pT", (128, 128, 8), BF16)
        LP = sb("LP", (128, NS, 8), F32)
        load("sync", pt_sb, pt_sb[:, :], ptT, ptT[:, :])
        PA = [pb[0], pb[1]]
        PR = [pb[2], pb[3]]
        PKN = [pb[4], pb[5]]
        ptr_ = pb[6]
        pacc = pb[7]
        ptr_bf = ptr_.t[:, :].bitcast(BF16)
        identb = cstb[:, C_ID:C_ID + 128]
        for s_ in range(8):
            ew("vector", lambda e, s_=s_: e.tensor_scalar(out=idx8a[:, :, s_], in0=pt_sb[:, :], scalar1=8, scalar2=s_,
                                                          op0=ALU.mult, op1=ALU.add), [pt_sb], [idx8a])
        for s_ in range(2):
            ew("vector", lambda e, s_=s_: e.tensor_scalar(out=idx2a[:, :, s_], in0=pt_sb[:, :], scalar1=2, scalar2=s_,
                                                          op0=ALU.mult, op1=ALU.add), [pt_sb], [idx2a])
        for b in range(NS):
            k = b % 2
            g_, gr_ = G[k], GR[k]
            for s_ in range(8):
                P.dma("gpsimd", lambda e, b=b, s_=s_, g_=g_: e.indirect_dma_start(
                    out=g_[:, s_ * 16:(s_ + 1) * 16, :].rearrange("p t l -> p (t l)"), out_offset=None, in_=cckv[:, :],
                    in_offset=bass.IndirectOffsetOnAxis(ap=idx8a[:, b, s_:s_ + 1], axis=0)), key=g_, reads=[idx8a], writes=[g_])
            for s_ in range(2):
                P.dma("gpsimd", lambda e, b=b, s_=s_, gr_=gr_: e.indirect_dma_start(
                    out=gr_[:, s_ * 64:(s_ + 1) * 64, :].rearrange("p t l -> p (t l)"), out_offset=None, in_=ckr[:, :],
                    in_offset=bass.IndirectOffsetOnAxis(ap=idx2a[:, b, s_:s_ + 1], axis=0)), key=gr_, reads=[idx2a], writes=[gr_])
            for t4 in range(32):
                ct, kt_ = ckvT[t4 % 2], krT[t4 % 2]
                for i in range(4):
                    t = t4 * 4 + i
                    P.op("tensor", lambda e, t=t, i=i, g_=g_: e.transpose(ptr_bf[:, i * 128:(i + 1) * 128], g_[:, t, :], identb),
                         reads=[g_, cstb], writes=[ptr_])
                for i in range(4):
                    t = t4 * 4 + i
                    P.op("tensor", lambda e, t=t, i=i, gr_=gr_: e.transpose(ptr_bf[0:32, 512 + i * 128:512 + (i + 1) * 128],
                                                                            gr_[:, t, :], identb),
                         reads=[gr_, cstb], writes=[ptr_])
                ew("vector", lambda e, ct=ct: e.tensor_copy(out=ct[:, :], in_=ptr_bf[:, 0:512]), [ptr_], [ct])
                ew("vector", lambda e, kt_=kt_: e.tensor_copy(out=kt_[:, :], in_=ptr_bf[0:32, 512:1024]), [ptr_], [kt_])
                for i in range(4):
                    t = t4 * 4 + i
                    pk = PKN[t % 2]
                    sq_ = sq[t % 2]
                    mm(pk, pk[:, :], ct, ct[:, i * 128:(i + 1) * 128], wuk2, wuk2[:, 0, :], True, True)
                    ew("scalar", lambda e, pk=pk, sq_=sq_: e.activation(out=sq_[:, :], in_=pk[:, :], func=AF.Square), [pk], [sq_])
                    ew("vector", lambda e, sq_=sq_, t=t: e.tensor_reduce(out=SS[:, t, :], in_=sq_[:, :].rearrange("p (h d) -> p h d", h=8),
                                                                         axis=AX.X, op=ALU.add), [sq_], [SS])
                    pa, pr = PA[t // 64], PR[t // 64]
                    c8 = (t % 64) * 8
                    mm(pa, pa[:, c8:c8 + 8], ct, ct[:, i * 128:(i + 1) * 128], qabs, qabs[:, b, :], True, True)
                    mm(pr, pr[:, c8:c8 + 8], kt_, kt_[0:32, i * 128:(i + 1) * 128], qrs, qrs[0:32, b, :], True, True)
            SSf = SS[:, :, :].rearrange("p t h -> p (t h)")
            invf = inv[:, :, :].rearrange("p t h -> p (t h)")
            s1f = s1[:, :, :].rearrange("p t h -> p (t h)")
            pTf = pT[:, :, :].rearrange("p t h -> p (t h)")
            ew("scalar", lambda e, SSf=SSf, invf=invf: e.activation(out=invf, in_=SSf, func=AF.Sqrt, bias=EPS, scale=1.0 / 64), [SS], [inv])
            ew("vector", lambda e, invf=invf: e.reciprocal(out=invf, in_=invf), [inv], [inv])
            for hf in range(2):
                ew("vector", lambda e, hf=hf, invf=invf, s1f=s1f: e.tensor_tensor(out=s1f[:, hf * 512:(hf + 1) * 512], in0=PA[hf][:, :],
                                                                                  in1=invf[:, hf * 512:(hf + 1) * 512], op=ALU.mult),
                   [PA[hf], inv], [s1])
                ew("vector", lambda e, hf=hf, s1f=s1f: e.tensor_tensor(out=s1f[:, hf * 512:(hf + 1) * 512], in0=PR[hf][:, :],
                                                                       in1=s1f[:, hf * 512:(hf + 1) * 512], op=ALU.add),
                   [PR[hf], s1], [s1])
            ew("scalar", lambda e, s1f=s1f, pTf=pTf: e.activation(out=pTf, in_=s1f, func=AF.Exp, scale=MLA_SCALE), [s1], [pT])
            ew("vector", lambda e, b=b: e.tensor_reduce(out=LP[:, b, :], in_=pT[:, :, :].rearrange("p t h -> p h t"),
                                                        axis=AX.X, op=ALU.add), [pT], [LP])
            for t in range(128):
                mm(pacc, pacc[:, b * 8:(b + 1) * 8], g_, g_[:, t, :], pT, pT[:, t, :], t == 0, t == 127)
        lsum = pb[0]
        mm(lsum, lsum[:, 0:NS * 8], cst, cst[:, C_ONES:C_ONES + 128], LP, LP[:, :, :].rearrange("p b h -> p (b h)"), True, True)
        num = sb("num", (128, NS, 8), F32)
        den = sb("den", (128, NS, 8), F32)
        olat = sb("olat", (128, NS, 8), BF16)
        pnv = pnew[:, :, :].rearrange("p h b -> p b h")
        ew("vector", lambda e: e.tensor_tensor(out=num[:, :, :], in0=pnv, in1=ckvs[:, 0:NS].unsqueeze(2).to_broadcast([128, NS, 8]),
                                               op=ALU.mult), [pnew, ckvs], [num])
        ew("vector", lambda e: e.tensor_tensor(out=num[:, :, :], in0=pacc[:, 0:NS * 8].rearrange("p (b h) -> p b h", h=8),
                                               in1=num[:, :, :], op=ALU.add), [pacc, num], [num])
        ew("vector", lambda e: e.tensor_tensor(out=den[:, :, :], in0=lsum[:, 0:NS * 8].rearrange("p (b h) -> p b h", h=8),
                                               in1=pnv, op=ALU.add), [lsum, pnew], [den])
        ew("vector", lambda e: e.reciprocal(out=den[:, :, :], in_=den[:, :, :]), [den], [den])
        ew("vector", lambda e: e.tensor_tensor(out=olat[:, :, :], in0=num[:, :, :], in1=den[:, :, :], op=ALU.mult), [num, den], [olat])
        for h in range(8):
            po = pb[1 + h % 2]
            mm(po, po[0:64, 0:NS], wuv2, wuv2[:, 0, h * 64:(h + 1) * 64], olat, olat[:, :, h], True, True)
            r0, jc = (h % 2) * 64, h // 2
            ew("scalar", lambda e, po=po, r0=r0, jc=jc: e.copy(out=OAs[r0:r0 + 64, jc, :], in_=po[0:64, 0:NS]), [po], [OAs])
        Kb = [sb("Kb%d" % i, (128, 2, 256), F32) for i in range(2)]
        Vb = [sb("Vb%d" % i, (128, 2, 256), F32) for i in range(2)]
        Vbb = [sb("Vbb%d" % i, (128, 2, 256), BF16) for i in range(2)]
        qbc = sb("qbc", (128, 2, 128), F32)
        prod = sb("prod", (128, 2, 256), F32)
        scm = sb("scm", (128, 2, 4), F32)
        pm = sb("pm", (128, 2, 4), BF16)
        rden = sb("rden", (128, NS, 4), F32)
        omf = sb("omf", (128, 2, NS), F32)
        pq, pom, pdn = pb[3], pb[4], pb[5]
        ident = cst[:, C_ID:C_ID + 128]
        for b in range(NS):
            k = b % 2
            P.dma("sync", lambda e, k=k, b=b: e.dma_start(out=Kb[k][:, :, :], in_=cmk[b].rearrange("(blk p) f -> p blk f", p=128)),
                  key=Kb[k], reads=[], writes=[Kb[k]])
            P.dma("sync", lambda e, k=k, b=b: e.dma_start(out=Vb[k][:, :, :], in_=cmv[b].rearrange("(blk p) f -> p blk f", p=128)),
                  key=Vb[k], reads=[], writes=[Vb[k]])
            ew("gpsimd", lambda e, k=k: e.tensor_copy(out=Vbb[k][:, :, :], in_=Vb[k][:, :, :]), [Vb[k]], [Vbb[k]])
            ew("vector", lambda e, b=b: e.tensor_copy(out=qbc[:, :, :], in_=mqs[:, :, b:b + 1].to_broadcast([128, 2, 128])), [mqs], [qbc])
            for c in range(2):
                mm(pq, pq[:, c * 128:(c + 1) * 128], qbc, qbc[:, c, :], cst, ident, True, True)
            ew("vector", lambda e, k=k: e.tensor_tensor(out=prod[:, :, :], in0=Kb[k][:, :, :],
                                                        in1=pq[:, 0:256].unsqueeze(1).to_broadcast([128, 2, 256]), op=ALU.mult),
               [Kb[k], pq], [prod])
            ew("vector", lambda e: e.tensor_reduce(out=scm[:, :, :], in_=prod[:, :, :].rearrange("p k (h d) -> p k h d", h=4),
                                                   axis=AX.X, op=ALU.add), [prod], [scm])
            ew("scalar", lambda e: e.activation(out=pm[:, :, :], in_=scm[:, :, :], func=AF.Exp, scale=MEM_SCALE), [scm], [pm])
            for c in range(2):
                for blk in range(2):
                    mm(pom, pom[:, b * 4 + 2 * c:b * 4 + 2 * c + 2], Vbb[k], Vbb[k][:, blk, c * 128:(c + 1) * 128],
                       pm, pm[:, blk, 2 * c:2 * c + 2], blk == 0, blk == 1)
            for blk in range(2):
                mm(pdn, pdn[:, b * 4:(b + 1) * 4], cstb, cstb[:, C_ONES:C_ONES + 128], pm, pm[:, blk, :], blk == 0, blk == 1)
        ew("vector", lambda e: e.reciprocal(out=rden[:, :, :], in_=pdn[:, 0:NS * 4].rearrange("p (b h) -> p b h", h=4)), [pdn], [rden])
        pomv = pom[:, 0:NS * 4].rearrange("p (b h) -> p b h", h=4)
        for c in range(2):
            for hl in range(2):
                r0 = hl * 64
                ew("vector", lambda e, c=c, hl=hl, r0=r0: e.tensor_tensor(out=omf[r0:r0 + 64, c, :], in0=pomv[r0:r0 + 64, :, 2 * c + hl],
                                                                          in1=rden[r0:r0 + 64, :, 2 * c + hl], op=ALU.mult),
                   [pom, rden], [omf])
        TB = {"sq": sb("abB", (128, 8, NT), BF16), "std": sb("stdB", (128, NT), F32), "rstd": sb("rstdB", (128, NT), F32), "pn": pb[6]}
        rms_rstd(TB, [omf[:, 0, :], omf[:, 1, :]], [omf, omf], cstb[:, C_ONES:C_ONES + 128], 128, NS, 256.0)
        for j in range(2):
            norm_apply(TB, OMs, OMs[:, j, :], omf, omf[:, j, :], G_OMEM + j, 128, NS)
        rms_rstd(TB, [OAs[:, c, :] for c in range(4)], [OAs] * 4, cstb[:, C_ONES:C_ONES + 128], 128, NS, 512.0)
        for c in range(4):
            norm_apply(TB, onorm_s, onorm_s[:, c, :], OAs, OAs[:, c, :], G_OMLA + c, 128, NS)
        P.barrier()
        print("sampleB SBUF peak bytes/partition:", A.peak - A.BASE)
        A.reset(mB)

    if do_sample:
        sampleB()

    OA = sb("OA", (128, 4, NOWN * NT), F32)
    if do_att:
        KT = [sb("KT%d" % i, (96, NSLOT * NT), BF16) for i in range(2)]
        VA = [sb("VA%d" % i, (128, NSLOT * NT // 128, 128), BF16) for i in range(2)]
        QT = [sb("QT%d" % i, (96, NOWN * NT), BF16) for i in range(2)]
        pts = [sb("pts%d" % i, (128, NT), BF16) for i in range(4)]
        rec2 = sb("rec2", (128, NT), F32)
        for i in range(2):
            ew("gpsimd", lambda e, i=i: e.memset(VA[i][:, :, 64:128], 1.0), [], [VA[i]])
        vsv = VS.t.rearrange("(blk p) (h d) -> p blk h d", p=128, h=8)
        nblk = 0
        for h in range(8):
            kt, va, qt = KT[h % 2], VA[h % 2], QT[h % 2]
            P.dma("sync", lambda e, kt=kt, h=h: e.dma_start(out=kt[0:64, :], in_=KN[h, :, :]), key=kt, reads=[], writes=[kt])
            P.dma("sync", lambda e, kt=kt: e.dma_start(out=kt[64:96, :], in_=KR[:, :]), key=kt, reads=[], writes=[kt])
            for q4 in range(8):
                b0, b1 = q4 * 8, (q4 + 1) * 8
                P.dma("sync", lambda e, va=va, h=h, b0=b0, b1=b1: e.dma_start(out=va[:, b0:b1, 0:64], in_=vsv[:, b0:b1, h, :]),
                      key=va, reads=[], writes=[va])
            P.dma("sync", lambda e, qt=qt, h=h: e.dma_start(out=qt[0:96, :], in_=QS[h, :, :]), key=qt, reads=[], writes=[qt])
            for j in range(NOWN):
                tiles = [(8 + j, "flag")] + [(s, "full") for s in range(j)] + [(8 + s, "full") for s in range(j)] + [(j, "diag")]
                po = pb[4 + (h * NOWN + j) % 2]
                blocks = []
                for (s, kind) in tiles:
                    for kb in range(4):
                        blocks.append((s * NT + kb * 128, kb * 128 if kind == "diag" else 0, kind))
                nb = len(blocks)
                LAG = 2
                issued = []
                for i in range(nb + LAG):
                    if i < nb:
                        k0, q0, kind = blocks[i]
                        ncol = NT - q0
                        ps_ = pb[nblk % 4]
                        pt_ = pts[nblk % 4]
                        nblk += 1
                        issued.append((pt_, q0, ncol, k0))
                        mm(ps_, ps_[:, 0:ncol], kt, kt[0:96, k0:k0 + 128], qt, qt[0:96, j * NT + q0:(j + 1) * NT], True, True)
                        if kind == "flag":
                            ew("scalar", lambda e, ps_=ps_, pt_=pt_, ncol=ncol: e.activation(
                                out=pt_[:, 0:ncol], in_=ps_[:, 0:ncol], func=AF.Exp, scale=MLA_SCALE,
                                bias=gn[:, G_EXPB:G_EXPB + 1]), [ps_, gn], [pt_])
                        else:
                            ew("scalar", lambda e, ps_=ps_, pt_=pt_, ncol=ncol: e.activation(
                                out=pt_[:, 0:ncol], in_=ps_[:, 0:ncol], func=AF.Exp, scale=MLA_SCALE), [ps_], [pt_])
                        if kind == "diag":
                            ew("gpsimd", lambda e, pt_=pt_: e.tensor_tensor(out=pt_[:, 0:128], in0=pt_[:, 0:128],
                                                                            in1=cstb[:, C_TRI:C_TRI + 128], op=ALU.mult),
                               [pt_, cstb], [pt_])
                    if i >= LAG:
                        ii = i - LAG
                        pt_, q0, ncol, k0 = issued[ii]
                        mm(po, po[:, q0:NT], va, va[:, k0 // 128, :], pt_, pt_[:, 0:ncol], ii == 0, ii == nb - 1)
                r0, jc = (h % 2) * 64, h // 2
                ew("vector", lambda e, po=po: e.reciprocal(out=rec2[64:128, :], in_=po[64:128, :]), [po], [rec2])
                ew("vector", lambda e, po=po, r0=r0, jc=jc, j=j: e.tensor_tensor(
                    out=OA[r0:r0 + 64, jc, j * NT:(j + 1) * NT], in0=po[0:64, :], in1=rec2[64:128, :], op=ALU.mult),
                   [po, rec2], [OA])
        P.barrier()
        print("phase2 SBUF peak bytes/partition:", A.peak - A.BASE)
    else:
        ew("gpsimd", lambda e: e.memset(OA[:, :, :], 1.0), [], [OA])
        P.barrier()
    pm2 = A.mark()
    A.reset(persist_mark)
    OA2 = sb("OA", (128, 4, NOWN * NT), F32)
    assert A.mark() <= pm2

    if do_post:
        T3 = alloc_common()
        wo_sb = sb("wo_sb", (128, 8, D), BF16)
        load_w(wo_sb, w_o, 8, D)
        xb3 = sb("xb3", (128, 8, NT), F32)
        onb = sb("onb", (128, 4, NT), BF16)
        yv = o_y.t.rearrange("(c p) t -> p c t", p=128)
        for j in range(NOWN):
            c0 = j * NT
            P.dma("sync", lambda e, c0=c0: e.dma_start(out=xb3[:, :, :], in_=x1v[:, :, c0:c0 + NT]), key=xb3, reads=[], writes=[xb3])
            rms_rstd(T3, [OA[:, c, c0:c0 + NT] for c in range(4)], [OA] * 4, cstb[:, C_ONES:C_ONES + 128], 128, NT, 512.0)
            for c in range(4):
                norm_apply(T3, onb, onb[:, c, 0:NT], OA, OA[:, c, c0:c0 + NT], G_OMLA + c, 128, NT)
            rhs = [(onb, onb[:, c, 0:NT]) for c in range(4)] + [(YC, YC[:, c, c0:c0 + NT]) for c in range(2)] + \
                  [(OM, OM[:, c, c0:c0 + NT]) for c in range(2)]
            for c in range(8):
                y = pb[4 + c % 2]
                for k, (rb, ra) in enumerate(rhs):
                    mm(y, y[:, 0:NT], wo_sb, wo_sb[:, k, c * 128:(c + 1) * 128], rb, ra, k == 0, k == 7)
                ew("vector", lambda e, y=y, c=c: e.tensor_tensor(out=xb3[:, c, 0:NT], in0=y[:, 0:NT], in1=xb3[:, c, 0:NT], op=ALU.add),
                   [y, xb3], [xb3])
            ffn_half(T3, xb3, NT, sc2, G_FFN2)
            P.dma("sync", lambda e, c0=c0: e.dma_start(out=yv[:, :, c0:c0 + NT], in_=xb3[:, :, :]), key=xb3, reads=[xb3], writes=[])
        rhs = [(onorm_s, onorm_s[:, c, :]) for c in range(4)] + [(YCs, YCs[:, c, :]) for c in range(2)] + \
              [(OMs, OMs[:, c, :]) for c in range(2)]
        for c in range(8):
            y = pb[4 + c % 2]
            for k, (rb, ra) in enumerate(rhs):
                mm(y, y[:, 0:NS], wo_sb, wo_sb[:, k, c * 128:(c + 1) * 128], rb, ra, k == 0, k == 7)
            ew("vector", lambda e, y=y, c=c: e.tensor_tensor(out=xs[:, c, :], in0=y[:, 0:NS], in1=xs[:, c, :], op=ALU.add),
               [y, xs], [xs])
        ffn_half(T3, xs, NS, sc2, G_FFN2)
        P.dma("sync", lambda e: e.dma_start(out=o_ys.t.rearrange("(c p) t -> p c t", p=128), in_=xs[:, :, :]), key=xs, reads=[xs], writes=[])
        print("phase3 SBUF peak bytes/partition:", A.peak - A.BASE)

    P.emit()
    return nc, stack


def _rope_tables(pos):
    inv_freq = (np.float32(10000.0) ** (-np.arange(0, 32, 2, dtype=np.float32) / np.float32(32))).astype(np.float32)
    ang = (pos.astype(np.float32)[:, None] * inv_freq[None, :]).astype(np.float32)
    c = np.cos(ang.astype(np.float64)).astype(np.float32).T
    s = np.sin(ang.astype(np.float64)).astype(np.float32).T
    C = np.concatenate([c, c], axis=0)
    S = np.concatenate([-s, s], axis=0)
    return np.ascontiguousarray(C), np.ascontiguousarray(S)


def _slot_tiles(par):
    own = [2 * j + par for j in range(8)]
    oth = [2 * j + 1 - par for j in range(8)]
    return own + oth


def _constants():
    f32 = np.float32
    c = np.zeros((128, NCONST), f32)
    c[:, C_ONES:C_ONES + 128] = 1.0
    c[0:64, C_BLK64:C_BLK64 + 64] = 1.0
    c[64:128, C_BLK64 + 64:C_BLK64 + 128] = 1.0
    for m in range(32):
        c[(m + 16) % 32, C_ROT32 + m] = 1.0
    c[0:32, C_ONES32:C_ONES32 + 32] = 1.0
    c[0:64, C_BLK96:C_BLK96 + 64] = 1.0 / 64
    c[64:96, C_BLK96 + 64:C_BLK96 + 96] = 1.0 / 32
    for m in range(64, 96):
        c[64 + ((m - 64 + 16) % 32), C_ROT96 + m] = 1.0
    k = np.arange(128)[:, None]
    q = np.arange(128)[None, :]
    c[:, C_TRI:C_TRI + 128] = (k <= q).astype(f32)
    c[0:64, C_ONES64:C_ONES64 + 64] = 1.0
    c[:, C_ID:C_ID + 128] = np.eye(128, dtype=f32)
    return c


def _gains(inp, par):
    f32 = np.float32
    g = np.zeros((128, NGAIN), f32)
    v = lambda k: np.asarray(inp[k], f32)[0]
    g[:, G_FFN1:G_FFN1 + 8] = v("g_ffn1").reshape(8, 128).T
    g[:, G_MIX:G_MIX + 8] = v("g_mix").reshape(8, 128).T
    g[:, G_MEM:G_MEM + 8] = v("g_mem").reshape(8, 128).T
    g[:, G_KV] = v("g_kv_lora")
    g[0:32, G_KR] = v("g_kr")
    g[:, G_MK] = np.tile(v("g_mk"), 2)
    g[:, G_QL:G_QL + 2] = v("g_q_lora").reshape(2, 128).T
    g[0:64, G_Q96] = v("g_qn")
    g[64:96, G_Q96] = v("g_qr")
    g[0:64, G_KN] = v("g_kn")
    g[:, G_KN2] = np.tile(v("g_kn"), 2)
    cw = np.asarray(inp["conv_w"], f32)[0]
    g[:, G_CW0:G_CW0 + 2] = cw[0].reshape(2, 128).T
    g[:, G_CW1:G_CW1 + 2] = cw[1].reshape(2, 128).T
    g[:, G_CW2:G_CW2 + 2] = cw[2].reshape(2, 128).T
    g[:, G_OCONV:G_OCONV + 2] = v("g_out_conv").reshape(2, 128).T
    g[:, G_MQ] = np.tile(v("g_mq"), 2)
    g[:, G_OMEM:G_OMEM + 2] = v("g_out_mem").reshape(2, 128).T
    g[:, G_OMLA:G_OMLA + 4] = v("g_out_mla").reshape(4, 128).T
    g[:, G_FFN2:G_FFN2 + 8] = v("g_ffn2").reshape(8, 128).T
    g[:, G_FLA] = 1.0 - par
    g[:, G_FLB] = float(par)
    g[:, G_EXPB] = (par - 1) * 30000.0
    return g


def kernel(**inp):
    f32 = np.float32
    x_prompt = np.asarray(inp["x_prompt"], f32)
    n_cores = 8
    consts = _constants()
    w = lambda k: np.asarray(inp[k], f32)[0]
    shared = {
        "w1_gate": w("w1_gate"), "w1_up": w("w1_up"), "w1_down": w("w1_down"),
        "w2_gate": w("w2_gate"), "w2_up": w("w2_up"), "w2_down": w("w2_down"),
        "w_in": w("w_in"), "w_uq": w("w_uq"), "w_uk": w("w_uk").reshape(128, 512), "w_uv": w("w_uv").reshape(128, 512),
        "w_o": w("w_o"), "w_mem_k": w("w_mem_k"), "w_mem_v": w("w_mem_v"), "consts": consts,
        "w_ukT": np.ascontiguousarray(w("w_uk").transpose(2, 1, 0).reshape(64, 8 * 128)),
        "cache_ckv": np.asarray(inp["cache_ckv"], f32)[0].reshape(-1, 2048),
        "cache_krope": np.asarray(inp["cache_krope"], f32)[0].reshape(-1, 2048),
    }
    assert shared["cache_ckv"].shape[0] == N_PHYS * 8
    Cs, Ss = _rope_tables(np.full((NS,), PAST, dtype=np.int64))
    x_sample = np.asarray(inp["x_sample"], f32)
    state_conv = np.asarray(inp["state_conv"], f32)[0]
    page_table = np.asarray(inp["page_table"]).astype(np.int32)
    cache_mem_k = np.asarray(inp["cache_mem_k"], f32)[0]
    cache_mem_v = np.asarray(inp["cache_mem_v"], f32)[0]
    in_maps = []
    for c in range(n_cores):
        b, par = c // 2, c % 2
        tiles = _slot_tiles(par)
        tok = np.concatenate([np.arange(t * NT, (t + 1) * NT) for t in tiles])
        C, S = _rope_tables(tok)
        m = dict(shared)
        m["xT"] = np.ascontiguousarray(x_prompt[b][tok].T)
        m["memT"] = np.ascontiguousarray(np.asarray(inp["mem_prompt"], f32)[b].T)
        m["ropeC"] = C
        m["ropeS"] = S
        m["gains"] = _gains(inp, par)
        sl = slice(c * NS, (c + 1) * NS)
        m["xsT"] = np.ascontiguousarray(x_sample[sl, 0, :].T)
        m["ropeCs"] = Cs
        m["ropeSs"] = Ss
        m["stateT"] = np.ascontiguousarray(state_conv[sl].transpose(2, 1, 0))
        m["ptT"] = np.ascontiguousarray(page_table[sl].T)
        m["cache_mem_k"] = np.ascontiguousarray(cache_mem_k[sl].reshape(NS, N_MEM, 256))
        m["cache_mem_v"] = np.ascontiguousarray(cache_mem_v[sl].reshape(NS, N_MEM, 256))
        in_maps.append(m)

    nc, stack = build_program()
    with stack:
        res = run_bass_kernel_spmd(nc, in_maps, core_ids=list(range(n_cores)))
    R = res.results

    B = 4
    y_prompt = np.zeros((B, SEQ, D), f32)
    y_sample = np.zeros((DEC, 1, D), f32)
    ckv_p = np.zeros((1, B, SEQ, 128), f32)
    kr_p = np.zeros((1, B, SEQ, 32), f32)
    conv_p = np.zeros((1, B, 2, 256), f32)
    mk_p = np.zeros((1, B, N_MEM, 4, 64), f32)
    mv_p = np.zeros((1, B, N_MEM, 4, 64), f32)
    for c in range(n_cores):
        b, par = c // 2, c % 2
        tiles = _slot_tiles(par)
        for s in range(8):
            t = tiles[s]
            y_prompt[b, t * NT:(t + 1) * NT, :] = R[c]["o_y"][:, s * NT:(s + 1) * NT].T
            ckv_p[0, b, t * NT:(t + 1) * NT, :] = R[c]["o_ckv"][:, s * NT:(s + 1) * NT].T
            kr_p[0, b, t * NT:(t + 1) * NT, :] = R[c]["o_kr"][:, s * NT:(s + 1) * NT].T
            if t == 15:
                conv_p[0, b] = R[c]["o_conv"][:, 2 * s:2 * s + 2].T
        if par == 0:
            mk_p[0, b] = R[c]["o_mk"].T.reshape(N_MEM, 4, 64)
            mv_p[0, b] = R[c]["o_mv"].T.reshape(N_MEM, 4, 64)
    ckv_s = np.zeros((1, DEC, 1, 128), f32)
    kr_s = np.zeros((1, DEC, 1, 32), f32)
    conv_s = np.zeros((1, DEC, 2, 256), f32)
    for c in range(n_cores):
        sl = slice(c * NS, (c + 1) * NS)
        y_sample[sl, 0, :] = R[c]["o_ys"].T
        ckv_s[0, sl, 0, :] = R[c]["o_ckv_s"].T
        kr_s[0, sl, 0, :] = R[c]["o_kr_s"].T
        conv_s[0, sl, 0, :] = R[c]["o_state_s"].T
        conv_s[0, sl, 1, :] = R[c]["o_conv_s"].T
    return (y_prompt, y_sample, ckv_p, kr_p, conv_p, mk_p, mv_p, ckv_s, kr_s, conv_s)
```

```python
import contextlib
import numpy as np
import concourse.bass as bass
import concourse.mybir as mybir
from concourse.bass_utils import run_bass_kernel_spmd

F32 = mybir.dt.float32
BF16 = mybir.dt.bfloat16
I32 = mybir.dt.int32
ALU = mybir.AluOpType
AF = mybir.ActivationFunctionType
AX = mybir.AxisListType

D = 1024
DFF = 2816
NM = DFF // 128
SEQ = 8192
NT = 512
NSLOT = 16
NOWN = 8
DEC = 128
EPS = 1e-6
INW = 1440
PAST = 16384
N_MEM = 256
MLA_SCALE = float(96 ** -0.5)
MEM_SCALE = float(64 ** -0.5)
NCONST = 1024
NS = 16
N_PHYS = 20480
NGAIN = 64

G_FFN1, G_MIX, G_MEM, G_KV, G_KR, G_MK, G_QL, G_Q96, G_KN = 0, 8, 16, 24, 25, 26, 27, 29, 30
G_CW0, G_CW1, G_CW2, G_OCONV, G_MQ, G_OMEM, G_OMLA, G_FFN2, G_FLA, G_FLB, G_EXPB, G_KN2 = 31, 33, 35, 37, 39, 40, 42, 46, 54, 55, 56, 57
C_ONES, C_BLK64, C_ROT32, C_ONES32, C_BLK96, C_ROT96, C_TRI, C_ONES64, C_ID = 0, 128, 256, 288, 320, 416, 512, 640, 704


class Buf:
    def __init__(self, name, t=None):
        self.name = name
        self.t = t
        self.last_w = None
        self.readers = []
        self.dma_sem = None
        self.dma_issued = 0

    def __getitem__(self, k):
        return self.t[k]


class Op:
    __slots__ = ("eng", "fn", "waits", "signaled", "sigval", "is_dma", "key", "dma_waits")

    def __init__(self, eng, fn, is_dma=False, key=None):
        self.eng = eng
        self.fn = fn
        self.waits = []
        self.dma_waits = []
        self.signaled = False
        self.sigval = None
        self.is_dma = is_dma
        self.key = key


class Prog:
    ENGS = ("tensor", "scalar", "vector", "gpsimd", "sync")

    def __init__(self, nc, stack):
        self.nc = nc
        self.stack = stack
        self.ops = {e: [] for e in self.ENGS}
        self.esem = {}
        self.dma_keys = []
        self.free_sems = []

    def _dep(self, op, prev):
        if prev is None or prev is op:
            return
        if prev.is_dma:
            op.dma_waits.append((prev.key, prev.key.dma_issued * 16))
            return
        if prev.eng == "tensor" and op.eng == "tensor":
            return
        op.waits.append(prev)
        prev.signaled = True

    def _track(self, op, reads, writes):
        for b in reads:
            self._dep(op, b.last_w)
        for b in writes:
            self._dep(op, b.last_w)
            for r in b.readers:
                self._dep(op, r)
        for b in reads:
            b.readers.append(op)
        for b in writes:
            b.last_w = op
            b.readers = []

    def op(self, eng, fn, reads=(), writes=()):
        o = Op(eng, fn)
        self._track(o, reads, writes)
        self.ops[eng].append(o)
        return o

    def dma(self, eng, fn, key, reads=(), writes=()):
        o = Op(eng, fn, is_dma=True, key=key)
        self._track(o, reads, writes)
        key.dma_issued += 1
        if key.dma_sem is None:
            key.dma_sem = self.stack.enter_context(self.nc.semaphore("d%d_%s" % (len(self.dma_keys), key.name)))
            self.dma_keys.append(key)
        self.ops[eng].append(o)
        return o

    def barrier(self):
        targets = []
        for e in self.ENGS:
            for o in reversed(self.ops[e]):
                if o.fn is not None and not o.is_dma:
                    targets.append(o)
                    o.signaled = True
                    break
        dma_targets = [(k, k.dma_issued * 16) for k in self.dma_keys]
        for e in self.ENGS:
            o = Op(e, None)
            o.waits = [t for t in targets if not (t.eng == e and e in ("tensor", "sync"))]
            o.dma_waits = list(dma_targets)
            self.ops[e].append(o)

    def emit(self):
        nc = self.nc
        for e in self.ENGS:
            self.esem[e] = self.stack.enter_context(nc.semaphore("e_" + e))
            n = 0
            for o in self.ops[e]:
                if o.signaled and not o.is_dma and o.fn is not None:
                    n += 1
                    o.sigval = n
        prog = self

        def run(e, eng):
            waited = {}
            for o in prog.ops[e]:
                need = {}
                for p in o.waits:
                    s = prog.esem[p.eng]
                    need[s] = max(need.get(s, 0), p.sigval)
                for k, v in o.dma_waits:
                    need[k.dma_sem] = max(need.get(k.dma_sem, 0), v)
                for s, v in need.items():
                    if waited.get(s, 0) < v:
                        eng.wait_ge(s, v)
                        waited[s] = v
                if o.fn is None:
                    continue
                ins = o.fn(eng)
                if o.is_dma:
                    ins.then_inc(o.key.dma_sem, 16)
                elif o.signaled:
                    ins.then_inc(prog.esem[e], 1)
            if e == "sync":
                for k in prog.dma_keys:
                    eng.wait_ge(k.dma_sem, k.dma_issued * 16)

        with nc.Block() as block:
            @block.tensor
            def _(eng):
                run("tensor", eng)

            @block.scalar
            def _(eng):
                run("scalar", eng)

            @block.vector
            def _(eng):
                run("vector", eng)

            @block.gpsimd
            def _(eng):
                run("gpsimd", eng)

            @block.sync
            def _(eng):
                run("sync", eng)


class Arena:
    BASE = 16512
    TOP = 229344

    def __init__(self, nc):
        self.nc = nc
        self.ptr = self.BASE
        self.n = 0
        self.peak = 0

    def alloc(self, name, shape, dt):
        esz = 2 if dt == BF16 else 4
        size = int(np.prod(shape[1:])) * esz
        size = (size + 63) // 64 * 64
        assert self.ptr + size <= self.TOP, "SBUF overflow at %s: %d + %d" % (name, self.ptr, size)
        self.n += 1
        h = self.nc.alloc_sbuf_tensor_at("%s_%d" % (name, self.n), list(shape), dt, offset=self.ptr)
        self.ptr += size
        self.peak = max(self.peak, self.ptr)
        return Buf(name, h)

    def mark(self):
        return self.ptr

    def reset(self, m):
        self.ptr = m


def build_program(do_att=True, do_post=True, do_sample=True):
    nc = bass.Bass("TRN2", target_bir_lowering=False)
    stack = contextlib.ExitStack()
    P = Prog(nc, stack)
    A = Arena(nc)

    def dram_in(name, shape, dt=F32):
        return Buf(name, nc.dram_tensor(name, list(shape), dt, kind="ExternalInput").ap())

    def dram_out(name, shape, dt=F32):
        return Buf(name, nc.dram_tensor(name, list(shape), dt, kind="ExternalOutput").ap())

    def dram_tmp(name, shape, dt):
        return Buf(name, nc.dram_tensor(name, list(shape), dt).ap())

    sb = A.alloc

    xT = dram_in("xT", (D, NSLOT * NT))
    memT = dram_in("memT", (D, N_MEM))
    ropeC = dram_in("ropeC", (32, NSLOT * NT))
    ropeS = dram_in("ropeS", (32, NSLOT * NT))
    w1g = dram_in("w1_gate", (D, DFF))
    w1u = dram_in("w1_up", (D, DFF))
    w1d = dram_in("w1_down", (DFF, D))
    w2g = dram_in("w2_gate", (D, DFF))
    w2u = dram_in("w2_up", (D, DFF))
    w2d = dram_in("w2_down", (DFF, D))
    w_in = dram_in("w_in", (D, INW))
    w_uq = dram_in("w_uq", (256, 768))
    w_uk = dram_in("w_uk", (128, 512))
    w_uv = dram_in("w_uv", (128, 512))
    w_o = dram_in("w_o", (D, D))
    wmk = dram_in("w_mem_k", (D, 256))
    wmv = dram_in("w_mem_v", (D, 256))
    gains = dram_in("gains", (128, NGAIN))
    xsT = dram_in("xsT", (D, NS))
    ropeCs = dram_in("ropeCs", (32, NS))
    ropeSs = dram_in("ropeSs", (32, NS))
    stateT = dram_in("stateT", (256, 2, NS))
    ptT = dram_in("ptT", (128, NS), I32)
    cckv = dram_in("cache_ckv", (N_PHYS * 8, 2048))
    ckr = dram_in("cache_krope", (N_PHYS * 2, 2048))
    cmk = dram_in("cache_mem_k", (NS, N_MEM, 256))
    cmv = dram_in("cache_mem_v", (NS, N_MEM, 256))
    w_ukT = dram_in("w_ukT", (64, 8 * 128))
    consts = dram_in("consts", (128, NCONST))

    o_y = dram_out("o_y", (D, NOWN * NT))
    o_ckv = dram_out("o_ckv", (128, NSLOT * NT))
    o_kr = dram_out("o_kr", (32, NSLOT * NT))
    o_conv = dram_out("o_conv", (256, NSLOT * 2))
    o_mk = dram_out("o_mk", (256, N_MEM))
    o_mv = dram_out("o_mv", (256, N_MEM))
    o_ys = dram_out("o_ys", (D, NS))
    o_ckv_s = dram_out("o_ckv_s", (128, NS))
    o_kr_s = dram_out("o_kr_s", (32, NS))
    o_conv_s = dram_out("o_conv_s", (256, NS))
    o_state_s = dram_out("o_state_s", (256, NS))

    sc1 = [dram_tmp("sc1g", (D, DFF), BF16), dram_tmp("sc1u", (D, DFF), BF16), dram_tmp("sc1d", (DFF, D), BF16)]
    sc2 = [dram_tmp("sc2g", (D, DFF), BF16), dram_tmp("sc2u", (D, DFF), BF16), dram_tmp("sc2d", (DFF, D), BF16)]
    X1 = dram_tmp("X1", (D, NOWN * NT), F32)
    KN = dram_tmp("KN", (8, 64, NSLOT * NT), BF16)
    KR = dram_tmp("KR", (32, NSLOT * NT), BF16)
    VS = dram_tmp("VS", (NSLOT * NT, 512), BF16)
    QS = dram_tmp("QS", (8, 96, NOWN * NT), BF16)

    pb = [Buf("pb%d" % i, stack.enter_context(nc.psum_tensor("pb%d" % i, [128, 512], F32))) for i in range(8)]

    gn = sb("gn", (128, NGAIN), F32)
    cst = sb("cst", (128, NCONST), F32)
    cstb = sb("cstb", (128, NCONST), BF16)
    YC = sb("YC", (128, 2, NOWN * NT), BF16)
    OM = sb("OM", (128, 2, NOWN * NT), BF16)
    memKT = sb("memKT", (128, 2, N_MEM), BF16)
    memVA = sb("memVA", (128, 2, 4, 128), BF16)
    UB = sb("UB", (128, 2, NSLOT, 2), F32)
    xs = sb("xs", (128, 8, NS), F32)
    qabs = sb("qabs", (128, NS, 8), BF16)
    qrs = sb("qrs", (32, NS, 8), BF16)
    pnew = sb("pnew", (128, 8, NS), F32)
    ckvs = sb("ckvs", (128, NS), F32)
    YCs = sb("YCs", (128, 2, NS), BF16)
    OMs = sb("OMs", (128, 2, NS), BF16)
    mqs = sb("mqs", (128, 2, NS), F32)
    OAs = sb("OAs", (128, 4, NS), F32)
    onorm_s = sb("onorm_s", (128, 4, NS), BF16)
    persist_mark = A.mark()

    def load(eng, dst, dst_ap, src, src_ap):
        P.dma(eng, lambda e: e.dma_start(out=dst_ap, in_=src_ap), key=dst, reads=[src], writes=[dst])

    def load_w(dst, src, nk, m):
        v = src.t.rearrange("(c p) m -> p c m", p=128)
        step = 1024
        for c in range(nk):
            for m0 in range(0, m, step):
                m1 = min(m, m0 + step)
                load("gpsimd", dst, dst[:, c, m0:m1], src, v[:, c, m0:m1])

    def mm(out_buf, out_ap, lhsT_buf, lhsT_ap, rhs_buf, rhs_ap, start, stop):
        P.op("tensor", lambda e: e.matmul(out_ap, lhsT_ap, rhs_ap, start=start, stop=stop),
             reads=[lhsT_buf, rhs_buf], writes=[out_buf])

    def ew(eng, fn, reads, writes):
        P.op(eng, fn, reads=reads, writes=writes)

    load("sync", gn, gn[:, :], gains, gains[:, :])
    load("sync", cst, cst[:, :], consts, consts[:, :])
    ew("vector", lambda e: e.tensor_copy(out=cstb[:, :], in_=cst[:, :]), [cst], [cstb])
    ew("gpsimd", lambda e: e.memset(memVA[:, :, :, 64:128], 1.0), [], [memVA])
    ew("gpsimd", lambda e: e.memset(UB[:, :, :, :], 0.0), [], [UB])

    def cast_w(dst, src, rows, cols):
        step = 1408 if cols == DFF else 1024
        for r0 in range(0, rows, 128):
            for m0 in range(0, cols, step):
                P.dma("gpsimd", lambda e, r0=r0, m0=m0: e.dma_start(out=dst[r0:r0 + 128, m0:m0 + step],
                                                                    in_=src[r0:r0 + 128, m0:m0 + step]),
                      key=dst, reads=[], writes=[])

    for dst, src, r, c in ((sc1[0], w1g, D, DFF), (sc1[1], w1u, D, DFF), (sc1[2], w1d, DFF, D)):
        cast_w(dst, src, r, c)

    def alloc_common():
        T = {}
        T["hb"] = sb("hb", (128, 8, NT), BF16)
        T["ab"] = sb("ab", (128, NM, NT), BF16)
        T["std"] = sb("std", (128, NT), F32)
        T["rstd"] = sb("rstd", (128, NT), F32)
        T["t0"] = sb("t0", (128, NT), F32)
        T["t1"] = sb("t1", (128, NT), F32)
        T["t2"] = sb("t2", (128, NT), F32)
        T["sgb"] = [sb("sgb%d" % i, (128, NT), BF16) for i in range(2)]
        T["wgc"] = [sb("wgc%d" % i, (128, 8, 128), BF16) for i in range(2)]
        T["wuc"] = [sb("wuc%d" % i, (128, 8, 128), BF16) for i in range(2)]
        T["wdc"] = [sb("wdc%d" % i, (128, NM, 128), BF16) for i in range(2)]
        T["sq"] = T["ab"]
        T["pn"] = pb[6]
        return T

    pn = pb[6]
    pz = pb[7]

    def rms_rstd(T, srcs, src_bufs, ones_ap, rows, n, count):
        sqb, pn_, std, rstd = T["sq"], T["pn"], T["std"], T["rstd"]
        for i, (a, b) in enumerate(zip(srcs, src_bufs)):
            if b in pb:
                ew("scalar", lambda e, a=a, i=i: e.activation(out=sqb[0:rows, i, 0:n], in_=a, func=AF.Square), [b], [sqb])
            else:
                eng_ = ("gpsimd", "vector", "gpsimd", "vector", "scalar", "gpsimd", "vector", "scalar")[i % 8] if len(srcs) >= 4 else "gpsimd"
                if eng_ == "scalar":
                    ew("scalar", lambda e, a=a, i=i: e.activation(out=sqb[0:rows, i, 0:n], in_=a, func=AF.Square), [b], [sqb])
                else:
                    ew(eng_, lambda e, a=a, i=i: e.tensor_tensor(out=sqb[0:rows, i, 0:n], in0=a, in1=a, op=ALU.mult),
                       [b], [sqb])
        for i in range(len(srcs)):
            mm(pn_, pn_[0:rows, 0:n], cstb, ones_ap, sqb, sqb[0:rows, i, 0:n], i == 0, i == len(srcs) - 1)
        ew("scalar", lambda e: e.activation(out=std[0:rows, 0:n], in_=pn_[0:rows, 0:n], func=AF.Sqrt,
                                            bias=EPS, scale=1.0 / count), [pn_], [std])
        ew("vector", lambda e: e.reciprocal(out=rstd[0:rows, 0:n], in_=std[0:rows, 0:n]), [std], [rstd])

    def norm_apply(T, out_buf, out_ap, src_buf, src_ap, gcol, rows, n):
        rstd = T["rstd"]
        ew("vector", lambda e: e.scalar_tensor_tensor(out=out_ap, in0=src_ap, scalar=gn[0:rows, gcol:gcol + 1],
                                                      in1=rstd[0:rows, 0:n], op0=ALU.mult, op1=ALU.mult),
           [src_buf, gn, rstd], [out_buf])

    def rmsnorm_d(T, x, n, gbase):
        hb = T["hb"]
        rms_rstd(T, [x[:, c, 0:n] for c in range(8)], [x] * 8, cstb[:, C_ONES:C_ONES + 128], 128, n, float(D))
        for c in range(8):
            norm_apply(T, hb, hb[:, c, 0:n], x, x[:, c, 0:n], gbase + c, 128, n)

    def ffn_half(T, x, n, sc, gbase):
        hb, ab = T["hb"], T["ab"]
        sgv = sc[0].t.rearrange("(c p) m -> p c m", p=128)
        suv = sc[1].t.rearrange("(c p) m -> p c m", p=128)
        sdv = sc[2].t.rearrange("(m p) c -> p m c", p=128)
        rmsnorm_d(T, x, n, gbase)
        for m in range(NM):
            g, u, sg = pb[m % 2], pb[2 + m % 2], T["sgb"][m % 2]
            wg_, wu_ = T["wgc"][m % 2], T["wuc"][m % 2]
            P.dma("sync", lambda e, wg_=wg_, m=m: e.dma_start(out=wg_[:, :, :], in_=sgv[:, :, m * 128:(m + 1) * 128]),
                  key=wg_, reads=[], writes=[wg_])
            P.dma("sync", lambda e, wu_=wu_, m=m: e.dma_start(out=wu_[:, :, :], in_=suv[:, :, m * 128:(m + 1) * 128]),
                  key=wu_, reads=[], writes=[wu_])
            for c in range(8):
                mm(g, g[:, 0:n], wg_, wg_[:, c, :], hb, hb[:, c, 0:n], c == 0, c == 7)
            for c in range(8):
                mm(u, u[:, 0:n], wu_, wu_[:, c, :], hb, hb[:, c, 0:n], c == 0, c == 7)
            ew("scalar", lambda e, g=g, sg=sg: e.activation(out=sg[:, 0:n], in_=g[:, 0:n], func=AF.Silu), [g], [sg])
            ew("vector", lambda e, u=u, sg=sg, m=m: e.tensor_tensor(out=ab[:, m, 0:n], in0=sg[:, 0:n], in1=u[:, 0:n],
                                                                    op=ALU.mult), [sg, u], [ab])
        for c in range(8):
            y = pb[4 + c % 2]
            wd_ = T["wdc"][c % 2]
            P.dma("sync", lambda e, wd_=wd_, c=c: e.dma_start(out=wd_[:, :, :], in_=sdv[:, :, c * 128:(c + 1) * 128]),
                  key=wd_, reads=[], writes=[wd_])
            for m in range(NM):
                mm(y, y[:, 0:n], wd_, wd_[:, m, :], ab, ab[:, m, 0:n], m == 0, m == NM - 1)
            ew("vector", lambda e, y=y, c=c: e.scalar_tensor_tensor(out=x[:, c, 0:n], in0=y[:, 0:n], scalar=0.5,
                                                                    in1=x[:, c, 0:n], op0=ALU.mult, op1=ALU.add),
               [y, x], [x])

    def proj(T, out_ps, rows, n, wbuf, col0):
        hb = T["hb"]
        for c in range(8):
            mm(out_ps, out_ps[0:rows, 0:n], wbuf, wbuf[:, c, col0:col0 + rows], hb, hb[:, c, 0:n], c == 0, c == 7)

    m0 = A.mark()
    T0 = alloc_common()
    wmk_sb = sb("wmk_sb", (128, 8, 256), BF16)
    wmv_sb = sb("wmv_sb", (128, 8, 256), BF16)
    xm = sb("xm", (128, 8, N_MEM), F32)
    mko = [sb("mko%d" % i, (128, N_MEM), F32) for i in range(2)]
    mvo = sb("mvo", (128, 2, N_MEM), F32)
    load_w(wmk_sb, wmk, 8, 256)
    load_w(wmv_sb, wmv, 8, 256)
    load("sync", xm, xm[:, :, :], memT, memT.t.rearrange("(c p) t -> p c t", p=128))
    rmsnorm_d(T0, xm, N_MEM, G_MEM)
    for j in range(2):
        proj(T0, pz, 128, N_MEM, wmk_sb, 128 * j)
        ew("scalar", lambda e: e.copy(out=T0["t0"][:, 0:N_MEM], in_=pz[:, 0:N_MEM]), [pz], [T0["t0"]])
        rms_rstd(T0, [T0["t0"][:, 0:N_MEM]], [T0["t0"]], cstb[:, C_BLK64:C_BLK64 + 128], 128, N_MEM, 64.0)
        norm_apply(T0, mko[j], mko[j][:, 0:N_MEM], T0["t0"], T0["t0"][:, 0:N_MEM], G_MK, 128, N_MEM)
        P.dma("sync", lambda e, j=j: e.dma_start(out=o_mk[j * 128:(j + 1) * 128, :], in_=mko[j][:, 0:N_MEM]),
              key=mko[j], reads=[mko[j]], writes=[])
        ew("gpsimd", lambda e, j=j: e.tensor_copy(out=memKT[:, j, :], in_=mko[j][:, 0:N_MEM]), [mko[j]], [memKT])
        proj(T0, pz, 128, N_MEM, wmv_sb, 128 * j)
        ew("scalar", lambda e, j=j: e.copy(out=mvo[:, j, 0:N_MEM], in_=pz[:, 0:N_MEM]), [pz], [mvo])
        P.dma("sync", lambda e, j=j: e.dma_start(out=o_mv[j * 128:(j + 1) * 128, :], in_=mvo[:, j, 0:N_MEM]),
              key=mvo, reads=[mvo], writes=[])
    for kb in range(2):
        for c in range(8):
            mm(pz, pz[:, 0:256], T0["hb"], T0["hb"][:, c, kb * 128:(kb + 1) * 128], wmv_sb, wmv_sb[:, c, :], c == 0, c == 7)
        ew("scalar", lambda e, kb=kb: e.copy(out=memVA[:, kb, :, 0:64],
                                             in_=pz[:, 0:256].rearrange("p (h d) -> p h d", h=4)), [pz], [memVA])
    P.barrier()
    A.reset(m0)
    sc2_jobs = []
    for dst, src, r, c in ((sc2[0], w2g, D, DFF), (sc2[1], w2u, D, DFF), (sc2[2], w2d, DFF, D)):
        step = 1408 if c == DFF else 1024
        for r0 in range(0, r, 128):
            for m0_ in range(0, c, step):
                sc2_jobs.append((dst, src, r0, m0_, step))

    T = alloc_common()
    win_sb = sb("win_sb", (128, 8, INW), BF16)
    wuq_sb = sb("wuq_sb", (128, 2, 768), BF16)
    wuk_sb = sb("wuk_sb", (128, 1, 512), BF16)
    wuv_sb = sb("wuv_sb", (128, 1, 512), BF16)
    load_w(win_sb, w_in, 8, INW)
    load_w(wuq_sb, w_uq, 2, 768)
    load_w(wuk_sb, w_uk, 1, 512)
    load_w(wuv_sb, w_uv, 1, 512)
    xb = sb("xb", (128, 8, NT), F32)
    ckv_o = sb("ckv_o", (128, NT), F32)
    ckvb = sb("ckvb", (128, NT), BF16)
    kr_o = sb("kr_o", (32, NT), F32)
    krb = sb("krb", (32, NT), BF16)
    uext = sb("uext", (128, 2, NT + 2), F32)
    rC = sb("rC", (32, NT), F32)
    rS = sb("rS", (32, NT), F32)
    rC96 = sb("rC96", (96, NT), F32)
    rS96 = sb("rS96", (96, NT), F32)
    knb = [sb("knb%d" % i, (128, NT), BF16) for i in range(2)]
    PZ = [pb[7], pb[5]]
    CS = [{"sq": sb("csq0", (128, 2, NT), BF16), "pn": pb[6], "std": T["std"], "rstd": T["rstd"]},
          {"sq": sb("csq1", (128, 2, NT), BF16), "pn": pb[4], "std": sb("cstd1", (128, NT), F32), "rstd": sb("crstd1", (128, NT), F32)}]
    vb = [sb("vb%d" % i, (128, 512), BF16) for i in range(2)]
    cq = sb("cq", (128, 2, NT), F32)
    cqn = sb("cqn", (128, 2, NT), BF16)
    qn = sb("qn", (96, NT), F32)
    qb16 = [sb("qb16%d" % i, (96, NT), BF16) for i in range(2)]
    gbt = sb("gbt", (128, 2, NT), F32)
    mqn = sb("mqn", (128, 2, NT), BF16)
    ptm = [sb("ptm%d" % i, (128, NT), BF16) for i in range(2)]
    rec = sb("rec", (128, NT), F32)
    t0, t1, t2 = T["t0"], T["t1"], T["t2"]
    ew("gpsimd", lambda e: e.memset(rC96[0:64, :], 1.0), [], [rC96])
    ew("gpsimd", lambda e: e.memset(rS96[0:64, :], 0.0), [], [rS96])

    xv = xT.t.rearrange("(c p) t -> p c t", p=128)
    x1v = X1.t.rearrange("(c p) t -> p c t", p=128)

    def phase1(s):
        own = s < NOWN
        n = NT
        x = xb
        c0 = s * NT
        if s == ORDER[0]:
            load("sync", x, x[:, :, :], xT, xv[:, :, c0:c0 + NT])
        load("sync", rC, rC[:, :], ropeC, ropeC[:, c0:c0 + NT])
        load("sync", rS, rS[:, :], ropeS, ropeS[:, c0:c0 + NT])
        if own:
            load("sync", rC96, rC96[64:96, :], ropeC, ropeC[:, c0:c0 + NT])
            load("sync", rS96, rS96[64:96, :], ropeS, ropeS[:, c0:c0 + NT])
        ffn_half(T, x, n, sc1, G_FFN1)
        if own:
            P.dma("sync", lambda e: e.dma_start(out=x1v[:, :, c0:c0 + NT], in_=x[:, :, :]), key=x, reads=[x], writes=[])
        rmsnorm_d(T, x, n, G_MIX)
        nxt = ORDER.index(s) + 1
        if nxt < len(ORDER):
            cn = ORDER[nxt] * NT
            load("sync", x, x[:, :, :], xT, xv[:, :, cn:cn + NT])
        pzA, pzB = PZ
        proj(T, pzA, 128, n, win_sb, 256)
        rms_rstd(CS[0], [pzA[:, 0:n]], [pzA], cstb[:, C_ONES:C_ONES + 128], 128, n, 128.0)
        norm_apply(CS[0], ckv_o, ckv_o[:, 0:n], pzA, pzA[:, 0:n], G_KV, 128, n)
        P.dma("sync", lambda e: e.dma_start(out=o_ckv[:, c0:c0 + NT], in_=ckv_o[:, 0:n]), key=ckv_o, reads=[ckv_o], writes=[])
        ew("gpsimd", lambda e: e.tensor_copy(out=ckvb[:, 0:n], in_=ckv_o[:, 0:n]), [ckv_o], [ckvb])
        proj(T, pzB, 32, n, win_sb, 384)
        rms_rstd(CS[1], [pzB[0:32, 0:n]], [pzB], cstb[0:32, C_ONES32:C_ONES32 + 32], 32, n, 32.0)
        norm_apply(CS[1], t2, t2[0:32, 0:n], pzB, pzB[0:32, 0:n], G_KR, 32, n)
        mm(pzB, pzB[0:32, 0:n], cst, cst[0:32, C_ROT32:C_ROT32 + 32], t2, t2[0:32, 0:n], True, True)
        ew("vector", lambda e: e.tensor_tensor(out=t1[0:32, 0:n], in0=pzB[0:32, 0:n], in1=rS[0:32, 0:n], op=ALU.mult),
           [pzB, rS], [t1])
        ew("gpsimd", lambda e: e.tensor_tensor(out=t2[0:32, 0:n], in0=t2[0:32, 0:n], in1=rC[0:32, 0:n], op=ALU.mult),
           [t2, rC], [t2])
        ew("vector", lambda e: e.tensor_tensor(out=kr_o[0:32, 0:n], in0=t1[0:32, 0:n], in1=t2[0:32, 0:n], op=ALU.add),
           [t1, t2], [kr_o])
        P.dma("sync", lambda e: e.dma_start(out=o_kr[:, c0:c0 + NT], in_=kr_o[0:32, 0:n]), key=kr_o, reads=[kr_o], writes=[])
        ew("gpsimd", lambda e: e.tensor_copy(out=krb[0:32, 0:n], in_=kr_o[0:32, 0:n]), [kr_o], [krb])
        P.dma("sync", lambda e: e.dma_start(out=KR[:, c0:c0 + NT], in_=krb[0:32, 0:n]), key=krb, reads=[krb], writes=[])
        for hp in range(4):
            k2 = hp % 2
            pz_, cs_, kb_ = PZ[k2], CS[k2], knb[k2]
            mm(pz_, pz_[:, 0:n], wuk_sb, wuk_sb[:, 0, hp * 128:(hp + 1) * 128], ckvb, ckvb[:, 0:n], True, True)
            rms_rstd(cs_, [pz_[:, 0:n]], [pz_], cstb[:, C_BLK64:C_BLK64 + 128], 128, n, 64.0)
            norm_apply(cs_, kb_, kb_[:, 0:n], pz_, pz_[:, 0:n], G_KN2, 128, n)
            for hl in range(2):
                P.dma("sync", lambda e, hp=hp, hl=hl, kb_=kb_: e.dma_start(out=KN[2 * hp + hl, :, c0:c0 + NT],
                                                                           in_=kb_[hl * 64:(hl + 1) * 64, 0:n]),
                      key=kb_, reads=[kb_], writes=[])
        for tb in range(n // 128):
            v_ = vb[tb % 2]
            pz_ = PZ[tb % 2]
            mm(pz_, pz_[:, 0:512], ckvb, ckvb[:, tb * 128:(tb + 1) * 128], wuv_sb, wuv_sb[:, 0, :], True, True)
            ew("scalar", lambda e, v_=v_, pz_=pz_: e.copy(out=v_[:, :], in_=pz_[:, 0:512]), [pz_], [v_])
            P.dma("sync", lambda e, v_=v_, tb=tb: e.dma_start(out=VS[c0 + tb * 128:c0 + (tb + 1) * 128, :], in_=v_[:, :]),
                  key=v_, reads=[v_], writes=[])
        for j in range(2):
            proj(T, pzA, 128, n, win_sb, 416 + 128 * j)
            ew("scalar", lambda e: e.copy(out=t0[:, 0:n], in_=pzA[:, 0:n]), [pzA], [t0])
            proj(T, pzB, 128, n, win_sb, 928 + 128 * j)
            ew("vector", lambda e, j=j: e.tensor_tensor(out=uext[:, j, 2:n + 2], in0=t0[:, 0:n], in1=pzB[:, 0:n], op=ALU.mult),
               [t0, pzB], [uext])
        ew("gpsimd", lambda e: e.tensor_copy(out=UB[:, :, s, :], in_=uext[:, :, n:n + 2]), [uext], [UB])
        for j in range(2):
            P.dma("sync", lambda e, j=j: e.dma_start(out=o_conv[j * 128:(j + 1) * 128, 2 * s:2 * s + 2],
                                                     in_=uext[:, j, n:n + 2]), key=uext, reads=[uext], writes=[])
        if not own:
            return
        j_own = s
        if j_own == 0:
            ew("vector", lambda e: e.tensor_scalar(out=uext[:, :, 0:2], in0=UB[:, :, 8, :], scalar1=gn[:, G_FLB:G_FLB + 1],
                                                   scalar2=None, op0=ALU.mult), [UB, gn], [uext])
        else:
            ew("vector", lambda e: e.tensor_scalar(out=uext[:, :, 0:2], in0=UB[:, :, 8 + j_own, :],
                                                   scalar1=gn[:, G_FLB:G_FLB + 1], scalar2=None, op0=ALU.mult),
               [UB, gn], [uext])
            ew("vector", lambda e: e.scalar_tensor_tensor(out=uext[:, :, 0:2], in0=UB[:, :, 8 + j_own - 1, :],
                                                          scalar=gn[:, G_FLA:G_FLA + 1], in1=uext[:, :, 0:2],
                                                          op0=ALU.mult, op1=ALU.add), [UB, gn, uext], [uext])
        for j in range(2):
            pz_ = PZ[j]
            tt = (t0, t1)[j]
            proj(T, pz_, 128, n, win_sb, 672 + 128 * j)
            ew("gpsimd", lambda e, j=j, tt=tt: e.tensor_scalar(out=tt[:, 0:n], in0=uext[:, j, 0:n], scalar1=gn[:, G_CW0 + j:G_CW0 + j + 1],
                                                               scalar2=None, op0=ALU.mult), [uext, gn], [tt])
            ew("vector", lambda e, j=j, tt=tt: e.scalar_tensor_tensor(out=tt[:, 0:n], in0=uext[:, j, 1:n + 1],
                                                                      scalar=gn[:, G_CW1 + j:G_CW1 + j + 1], in1=tt[:, 0:n],
                                                                      op0=ALU.mult, op1=ALU.add), [uext, gn, tt], [tt])
            ew("vector", lambda e, j=j, tt=tt: e.scalar_tensor_tensor(out=tt[:, 0:n], in0=uext[:, j, 2:n + 2],
                                                                      scalar=gn[:, G_CW2 + j:G_CW2 + j + 1], in1=tt[:, 0:n],
                                                                      op0=ALU.mult, op1=ALU.add), [uext, gn, tt], [tt])
            ew("vector", lambda e, j=j, tt=tt, pz_=pz_: e.tensor_tensor(out=gbt[:, j, 0:n], in0=tt[:, 0:n], in1=pz_[:, 0:n], op=ALU.mult),
               [tt, pz_], [gbt])
        rms_rstd(CS[0], [gbt[:, 0, 0:n], gbt[:, 1, 0:n]], [gbt, gbt], cstb[:, C_ONES:C_ONES + 128], 128, n, 256.0)
        for j in range(2):
            norm_apply(CS[0], YC, YC[:, j, c0:c0 + n], gbt, gbt[:, j, 0:n], G_OCONV + j, 128, n)
        proj(T, pzA, 128, n, win_sb, 0)
        proj(T, pzB, 128, n, win_sb, 128)
        rms_rstd(CS[1], [pzA[:, 0:n], pzB[:, 0:n]], [pzA, pzB], cstb[:, C_ONES:C_ONES + 128], 128, n, 256.0)
        norm_apply(CS[1], cqn, cqn[:, 0, 0:n], pzA, pzA[:, 0:n], G_QL, 128, n)
        norm_apply(CS[1], cqn, cqn[:, 1, 0:n], pzB, pzB[:, 0:n], G_QL + 1, 128, n)
        for h in range(8):
            k2 = h % 2
            pz_, cs_, qb_ = PZ[k2], CS[k2], qb16[k2]
            qn_, tq_ = (qn, t2)[k2], (t1, t0)[k2]
            for j in range(2):
                mm(pz_, pz_[0:96, 0:n], wuq_sb, wuq_sb[:, j, h * 96:(h + 1) * 96], cqn, cqn[:, j, 0:n], j == 0, j == 1)
            rms_rstd(cs_, [pz_[0:96, 0:n]], [pz_], cstb[0:96, C_BLK96:C_BLK96 + 96], 96, n, 1.0)
            norm_apply(cs_, qn_, qn_[0:96, 0:n], pz_, pz_[0:96, 0:n], G_Q96, 96, n)
            mm(pz_, pz_[0:96, 0:n], cst, cst[0:96, C_ROT96:C_ROT96 + 96], qn_, qn_[0:96, 0:n], True, True)
            ew("vector", lambda e, pz_=pz_, tq_=tq_: e.tensor_tensor(out=tq_[0:96, 0:n], in0=pz_[0:96, 0:n], in1=rS96[0:96, 0:n], op=ALU.mult),
               [pz_, rS96], [tq_])
            ew("gpsimd", lambda e, qn_=qn_: e.tensor_tensor(out=qn_[0:96, 0:n], in0=qn_[0:96, 0:n], in1=rC96[0:96, 0:n], op=ALU.mult),
               [qn_, rC96], [qn_])
            ew("vector", lambda e, qb_=qb_, tq_=tq_, qn_=qn_: e.tensor_tensor(out=qb_[0:96, 0:n], in0=tq_[0:96, 0:n], in1=qn_[0:96, 0:n], op=ALU.add),
               [tq_, qn_], [qb_])
            P.dma("sync", lambda e, h=h, qb_=qb_: e.dma_start(out=QS[h, :, c0:c0 + NT], in_=qb_[0:96, 0:n]),
                  key=qb_, reads=[qb_], writes=[])
        for j in range(2):
            pz_, cs_ = PZ[j], CS[j]
            proj(T, pz_, 128, n, win_sb, 1184 + 128 * j)
            rms_rstd(cs_, [pz_[:, 0:n]], [pz_], cstb[:, C_BLK64:C_BLK64 + 128], 128, n, 64.0)
            norm_apply(cs_, mqn, mqn[:, j, 0:n], pz_, pz_[:, 0:n], G_MQ, 128, n)
        for h in range(4):
            j, r0 = h // 2, (h % 2) * 64
            po = pb[4 + h % 2]
            for kb in range(2):
                ps_ = pb[2 * (h % 2) + kb]
                pt_ = ptm[kb]
                mm(ps_, ps_[:, 0:n], memKT, memKT[r0:r0 + 64, j, kb * 128:(kb + 1) * 128], mqn, mqn[r0:r0 + 64, j, 0:n], True, True)
                ew("scalar", lambda e, ps_=ps_, pt_=pt_: e.activation(out=pt_[:, 0:n], in_=ps_[:, 0:n], func=AF.Exp, scale=MEM_SCALE),
                   [ps_], [pt_])
                mm(po, po[:, 0:n], memVA, memVA[:, kb, h, :], pt_, pt_[:, 0:n], kb == 0, kb == 1)
            ew("vector", lambda e, po=po: e.reciprocal(out=rec[64:128, 0:n], in_=po[64:128, 0:n]), [po], [rec])
            ew("vector", lambda e, po=po, r0=r0, j=j: e.tensor_tensor(out=gbt[r0:r0 + 64, j, 0:n], in0=po[0:64, 0:n],
                                                                      in1=rec[64:128, 0:n], op=ALU.mult), [po, rec], [gbt])
        rms_rstd(CS[0], [gbt[:, 0, 0:n], gbt[:, 1, 0:n]], [gbt, gbt], cstb[:, C_ONES:C_ONES + 128], 128, n, 256.0)
        for j in range(2):
            norm_apply(CS[0], OM, OM[:, j, c0:c0 + n], gbt, gbt[:, j, 0:n], G_OMEM + j, 128, n)

    ORDER = list(range(NOWN, NSLOT)) + list(range(NOWN))
    per = (len(sc2_jobs) + 7) // 8
    for si, s in enumerate(ORDER):
        phase1(s)
        for (dst, src, r0, m0_, step) in sc2_jobs[si * per:(si + 1) * per]:
            P.dma("gpsimd", lambda e, dst=dst, src=src, r0=r0, m0_=m0_, step=step: e.dma_start(
                out=dst[r0:r0 + 128, m0_:m0_ + step], in_=src[r0:r0 + 128, m0_:m0_ + step]), key=dst, reads=[], writes=[])

    wukT_sb = sb("wukT_sb", (64, 1, 8 * 128), BF16)
    kfull = sb("kfull", (96, NS), F32)
    qfs = sb("qfs", (96, NS), F32)
    stt = sb("stt", (128, 2, 2, NS), F32)
    qg = sb("qg", (64, NS), BF16)

    def sampleA():
        n = NS
        x = xs
        P.dma("gpsimd", lambda e: e.dma_start(out=wukT_sb[:, 0, :], in_=w_ukT[:, :]), key=wukT_sb, reads=[], writes=[wukT_sb])
        load("sync", x, x[:, :, :], xsT, xsT.t.rearrange("(c p) t -> p c t", p=128))
        load("sync", rC, rC[:, 0:n], ropeCs, ropeCs[:, :])
        load("sync", rS, rS[:, 0:n], ropeSs, ropeSs[:, :])
        load("sync", rC96, rC96[64:96, 0:n], ropeCs, ropeCs[:, :])
        load("sync", rS96, rS96[64:96, 0:n], ropeSs, ropeSs[:, :])
        load("sync", stt, stt[:, :, :, :], stateT, stateT.t.rearrange("(c p) k t -> p c k t", p=128))
        for j in range(2):
            P.dma("sync", lambda e, j=j: e.dma_start(out=o_state_s[j * 128:(j + 1) * 128, :], in_=stt[:, j, 1, :]),
                  key=stt, reads=[stt], writes=[])
        ffn_half(T, x, n, sc1, G_FFN1)
        rmsnorm_d(T, x, n, G_MIX)
        proj(T, pz, 128, n, win_sb, 256)
        ew("scalar", lambda e: e.copy(out=t0[:, 0:n], in_=pz[:, 0:n]), [pz], [t0])
        rms_rstd(T, [t0[:, 0:n]], [t0], cstb[:, C_ONES:C_ONES + 128], 128, n, 128.0)
        norm_apply(T, ckvs, ckvs[:, 0:n], t0, t0[:, 0:n], G_KV, 128, n)
        P.dma("sync", lambda e: e.dma_start(out=o_ckv_s[:, :], in_=ckvs[:, 0:n]), key=ckvs, reads=[ckvs], writes=[])
        ew("gpsimd", lambda e: e.tensor_copy(out=ckvb[:, 0:n], in_=ckvs[:, 0:n]), [ckvs], [ckvb])
        proj(T, pz, 32, n, win_sb, 384)
        ew("scalar", lambda e: e.copy(out=t1[0:32, 0:n], in_=pz[0:32, 0:n]), [pz], [t1])
        rms_rstd(T, [t1[0:32, 0:n]], [t1], cstb[0:32, C_ONES32:C_ONES32 + 32], 32, n, 32.0)
        norm_apply(T, t2, t2[0:32, 0:n], t1, t1[0:32, 0:n], G_KR, 32, n)
        mm(pz, pz[0:32, 0:n], cst, cst[0:32, C_ROT32:C_ROT32 + 32], t2, t2[0:32, 0:n], True, True)
        ew("vector", lambda e: e.tensor_tensor(out=t1[0:32, 0:n], in0=pz[0:32, 0:n], in1=rS[0:32, 0:n], op=ALU.mult),
           [pz, rS], [t1])
        ew("gpsimd", lambda e: e.tensor_tensor(out=t2[0:32, 0:n], in0=t2[0:32, 0:n], in1=rC[0:32, 0:n], op=ALU.mult),
           [t2, rC], [t2])
        ew("vector", lambda e: e.tensor_tensor(out=kr_o[0:32, 0:n], in0=t1[0:32, 0:n], in1=t2[0:32, 0:n], op=ALU.add),
           [t1, t2], [kr_o])
        P.dma("sync", lambda e: e.dma_start(out=o_kr_s[:, :], in_=kr_o[0:32, 0:n]), key=kr_o, reads=[kr_o], writes=[])
        ew("vector", lambda e: e.tensor_copy(out=kfull[64:96, 0:n], in_=kr_o[0:32, 0:n]), [kr_o], [kfull])
        for j in range(2):
            proj(T, pz, 128, n, win_sb, 416 + 128 * j)
            ew("scalar", lambda e: e.copy(out=t0[:, 0:n], in_=pz[:, 0:n]), [pz], [t0])
            proj(T, pz, 128, n, win_sb, 928 + 128 * j)
            ew("vector", lambda e, j=j: e.tensor_tensor(out=uext[:, j, 0:n], in0=t0[:, 0:n], in1=pz[:, 0:n], op=ALU.mult),
               [t0, pz], [uext])
            P.dma("sync", lambda e, j=j: e.dma_start(out=o_conv_s[j * 128:(j + 1) * 128, :], in_=uext[:, j, 0:n]),
                  key=uext, reads=[uext], writes=[])
            proj(T, pz, 128, n, win_sb, 672 + 128 * j)
            ew("vector", lambda e, j=j: e.tensor_scalar(out=t0[:, 0:n], in0=stt[:, j, 0, :], scalar1=gn[:, G_CW0 + j:G_CW0 + j + 1],
                                                        scalar2=None, op0=ALU.mult), [stt, gn], [t0])
            ew("vector", lambda e, j=j: e.scalar_tensor_tensor(out=t0[:, 0:n], in0=stt[:, j, 1, :],
                                                               scalar=gn[:, G_CW1 + j:G_CW1 + j + 1], in1=t0[:, 0:n],
                                                               op0=ALU.mult, op1=ALU.add), [stt, gn, t0], [t0])
            ew("vector", lambda e, j=j: e.scalar_tensor_tensor(out=t0[:, 0:n], in0=uext[:, j, 0:n],
                                                               scalar=gn[:, G_CW2 + j:G_CW2 + j + 1], in1=t0[:, 0:n],
                                                               op0=ALU.mult, op1=ALU.add), [uext, gn, t0], [t0])
            ew("vector", lambda e, j=j: e.tensor_tensor(out=gbt[:, j, 0:n], in0=t0[:, 0:n], in1=pz[:, 0:n], op=ALU.mult),
               [t0, pz], [gbt])
        rms_rstd(T, [gbt[:, 0, 0:n], gbt[:, 1, 0:n]], [gbt, gbt], cstb[:, C_ONES:C_ONES + 128], 128, n, 256.0)
        for j in range(2):
            norm_apply(T, YCs, YCs[:, j, 0:n], gbt, gbt[:, j, 0:n], G_OCONV + j, 128, n)
        for j in range(2):
            proj(T, pz, 128, n, win_sb, 1184 + 128 * j)
            ew("scalar", lambda e: e.copy(out=t0[:, 0:n], in_=pz[:, 0:n]), [pz], [t0])
            rms_rstd(T, [t0[:, 0:n]], [t0], cstb[:, C_BLK64:C_BLK64 + 128], 128, n, 64.0)
            norm_apply(T, mqs, mqs[:, j, 0:n], t0, t0[:, 0:n], G_MQ, 128, n)
        for j in range(2):
            proj(T, pz, 128, n, win_sb, 128 * j)
            ew("scalar", lambda e, j=j: e.copy(out=cq[:, j, 0:n], in_=pz[:, 0:n]), [pz], [cq])
        rms_rstd(T, [cq[:, 0, 0:n], cq[:, 1, 0:n]], [cq, cq], cstb[:, C_ONES:C_ONES + 128], 128, n, 256.0)
        for j in range(2):
            norm_apply(T, cqn, cqn[:, j, 0:n], cq, cq[:, j, 0:n], G_QL + j, 128, n)
        psn = pb[0]
        for h in range(8):
            for j in range(2):
                mm(pz, pz[0:96, 0:n], wuq_sb, wuq_sb[:, j, h * 96:(h + 1) * 96], cqn, cqn[:, j, 0:n], j == 0, j == 1)
            ew("scalar", lambda e: e.copy(out=t1[0:96, 0:n], in_=pz[0:96, 0:n]), [pz], [t1])
            rms_rstd(T, [t1[0:96, 0:n]], [t1], cstb[0:96, C_BLK96:C_BLK96 + 96], 96, n, 1.0)
            norm_apply(T, qn, qn[0:96, 0:n], t1, t1[0:96, 0:n], G_Q96, 96, n)
            mm(pz, pz[0:96, 0:n], cst, cst[0:96, C_ROT96:C_ROT96 + 96], qn, qn[0:96, 0:n], True, True)
            ew("vector", lambda e: e.tensor_tensor(out=t1[0:96, 0:n], in0=pz[0:96, 0:n], in1=rS96[0:96, 0:n], op=ALU.mult),
               [pz, rS96], [t1])
            ew("gpsimd", lambda e: e.tensor_tensor(out=qn[0:96, 0:n], in0=qn[0:96, 0:n], in1=rC96[0:96, 0:n], op=ALU.mult),
               [qn, rC96], [qn])
            ew("vector", lambda e: e.tensor_tensor(out=qfs[0:96, 0:n], in0=t1[0:96, 0:n], in1=qn[0:96, 0:n], op=ALU.add),
               [t1, qn], [qfs])
            ew("vector", lambda e, h=h: e.tensor_copy(out=qrs[0:32, :, h], in_=qfs[64:96, 0:n]), [qfs], [qrs])
            ew("vector", lambda e: e.tensor_scalar(out=qg[0:64, 0:n], in0=qfs[0:64, 0:n], scalar1=gn[0:64, G_KN:G_KN + 1],
                                                   scalar2=None, op0=ALU.mult), [qfs, gn], [qg])
            mm(pz, pz[:, 0:n], wukT_sb, wukT_sb[0:64, 0, h * 128:(h + 1) * 128], qg, qg[0:64, 0:n], True, True)
            ew("scalar", lambda e, h=h: e.copy(out=qabs[:, :, h], in_=pz[:, 0:n]), [pz], [qabs])
            mm(pz, pz[0:64, 0:n], wuk_sb, wuk_sb[:, 0, h * 64:(h + 1) * 64], ckvb, ckvb[:, 0:n], True, True)
            ew("scalar", lambda e: e.copy(out=t1[0:64, 0:n], in_=pz[0:64, 0:n]), [pz], [t1])
            rms_rstd(T, [t1[0:64, 0:n]], [t1], cstb[0:64, C_ONES64:C_ONES64 + 64], 64, n, 64.0)
            norm_apply(T, kfull, kfull[0:64, 0:n], t1, t1[0:64, 0:n], G_KN, 64, n)
            ew("vector", lambda e: e.tensor_tensor(out=t2[0:96, 0:n], in0=qfs[0:96, 0:n], in1=kfull[0:96, 0:n], op=ALU.mult),
               [qfs, kfull], [t2])
            mm(psn, psn[:, h * NS:(h + 1) * NS], cst, cst[0:96, C_ONES:C_ONES + 128], t2, t2[0:96, 0:n], True, True)
        ew("scalar", lambda e: e.activation(out=pnew[:, :, :], in_=psn[:, 0:8 * NS].rearrange("p (h b) -> p h b", h=8),
                                            func=AF.Exp, scale=MLA_SCALE), [psn], [pnew])

    sampleA()
    P.barrier()
    print("phase1 SBUF peak bytes/partition:", A.peak - A.BASE)
    A.reset(persist_mark)


    def sampleB():
        mB = A.mark()
        wuk2 = sb("wuk2", (128, 1, 512), BF16)
        wuv2 = sb("wuv2", (128, 1, 512), BF16)
        load_w(wuk2, w_uk, 1, 512)
        load_w(wuv2, w_uv, 1, 512)
        pt_sb = sb("pt_sb", (128, NS), I32)
        idx8a = sb("idx8a", (128, NS, 8), I32)
        idx2a = sb("idx2a", (128, NS, 2), I32)
        G = [sb("G%d" % i, (128, 128, 128), BF16) for i in range(2)]
        GR = [sb("GR%d" % i, (128, 128, 32), BF16) for i in range(2)]
        ckvT = [sb("ckvT%d" % i, (128, 512), BF16) for i in range(2)]
        krT = [sb("krT%d" % i, (32, 512), BF16) for i in range(2)]
        sq = [sb("sq%d" % i, (128, 512), BF16) for i in range(2)]
        SS = sb("SS", (128, 128, 8), F32)
        inv = sb("inv", (128, 128, 8), F32)
        s1 = sb("s1", (128, 128, 8), F32)
        pT = sb("pT", (128, 128, 8), BF16)
        LP = sb("LP", (128, NS, 8), F32)
        load("sync", pt_sb, pt_sb[:, :], ptT, ptT[:, :])
        PA = [pb[0], pb[1]]
        PR = [pb[2], pb[3]]
        PKN = [pb[4], pb[5]]
        ptr_ = pb[6]
        pacc = pb[7]
        ptr_bf = ptr_.t[:, :].bitcast(BF16)
        identb = cstb[:, C_ID:C_ID + 128]
        for s_ in range(8):
            ew("vector", lambda e, s_=s_: e.tensor_scalar(out=idx8a[:, :, s_], in0=pt_sb[:, :], scalar1=8, scalar2=s_,
                                                          op0=ALU.mult, op1=ALU.add), [pt_sb], [idx8a])
        for s_ in range(2):
            ew("vector", lambda e, s_=s_: e.tensor_scalar(out=idx2a[:, :, s_], in0=pt_sb[:, :], scalar1=2, scalar2=s_,
                                                          op0=ALU.mult, op1=ALU.add), [pt_sb], [idx2a])
        for b in range(NS):
            k = b % 2
            g_, gr_ = G[k], GR[k]
            for s_ in range(8):
                P.dma("gpsimd", lambda e, b=b, s_=s_, g_=g_: e.indirect_dma_start(
                    out=g_[:, s_ * 16:(s_ + 1) * 16, :].rearrange("p t l -> p (t l)"), out_offset=None, in_=cckv[:, :],
                    in_offset=bass.IndirectOffsetOnAxis(ap=idx8a[:, b, s_:s_ + 1], axis=0)), key=g_, reads=[idx8a], writes=[g_])
            for s_ in range(2):
                P.dma("gpsimd", lambda e, b=b, s_=s_, gr_=gr_: e.indirect_dma_start(
                    out=gr_[:, s_ * 64:(s_ + 1) * 64, :].rearrange("p t l -> p (t l)"), out_offset=None, in_=ckr[:, :],
                    in_offset=bass.IndirectOffsetOnAxis(ap=idx2a[:, b, s_:s_ + 1], axis=0)), key=gr_, reads=[idx2a], writes=[gr_])
            def do_T(t4):
                ct, kt_ = ckvT[t4 % 2], krT[t4 % 2]
                for i in range(4):
                    t = t4 * 4 + i
                    P.op("tensor", lambda e, t=t, i=i, g_=g_: e.transpose(ptr_bf[:, i * 128:(i + 1) * 128], g_[:, t, :], identb),
                         reads=[g_, cstb], writes=[ptr_])
                for i in range(4):
                    t = t4 * 4 + i
                    P.op("tensor", lambda e, t=t, i=i, gr_=gr_: e.transpose(ptr_bf[0:32, 512 + i * 128:512 + (i + 1) * 128],
                                                                            gr_[:, t, :], identb),
                         reads=[gr_, cstb], writes=[ptr_])
                ew("vector", lambda e, ct=ct: e.tensor_copy(out=ct[:, :], in_=ptr_bf[:, 0:512]), [ptr_], [ct])
                ew("vector", lambda e, kt_=kt_: e.tensor_copy(out=kt_[:, :], in_=ptr_bf[0:32, 512:1024]), [ptr_], [kt_])

            def do_M(t4):
                ct, kt_ = ckvT[t4 % 2], krT[t4 % 2]
                for i in range(4):
                    t = t4 * 4 + i
                    pk = PKN[t % 2]
                    sq_ = sq[t % 2]
                    mm(pk, pk[:, :], ct, ct[:, i * 128:(i + 1) * 128], wuk2, wuk2[:, 0, :], True, True)
                    ew("scalar", lambda e, pk=pk, sq_=sq_: e.activation(out=sq_[:, :], in_=pk[:, :], func=AF.Square), [pk], [sq_])
                    ew("vector", lambda e, sq_=sq_, t=t: e.tensor_reduce(out=SS[:, t, :], in_=sq_[:, :].rearrange("p (h d) -> p h d", h=8),
                                                                         axis=AX.X, op=ALU.add), [sq_], [SS])
                    pa, pr = PA[t // 64], PR[t // 64]
                    c8 = (t % 64) * 8
                    mm(pa, pa[:, c8:c8 + 8], ct, ct[:, i * 128:(i + 1) * 128], qabs, qabs[:, b, :], True, True)
                    mm(pr, pr[:, c8:c8 + 8], kt_, kt_[0:32, i * 128:(i + 1) * 128], qrs, qrs[0:32, b, :], True, True)

            do_T(0)
            for t4 in range(32):
                if t4 + 1 < 32:
                    do_T(t4 + 1)
                do_M(t4)
            SSf = SS[:, :, :].rearrange("p t h -> p (t h)")
            invf = inv[:, :, :].rearrange("p t h -> p (t h)")
            s1f = s1[:, :, :].rearrange("p t h -> p (t h)")
            pTf = pT[:, :, :].rearrange("p t h -> p (t h)")
            ew("scalar", lambda e, SSf=SSf, invf=invf: e.activation(out=invf, in_=SSf, func=AF.Sqrt, bias=EPS, scale=1.0 / 64), [SS], [inv])
            ew("vector", lambda e, invf=invf: e.reciprocal(out=invf, in_=invf), [inv], [inv])
            for hf in range(2):
                ew("vector", lambda e, hf=hf, invf=invf, s1f=s1f: e.tensor_tensor(out=s1f[:, hf * 512:(hf + 1) * 512], in0=PA[hf][:, :],
                                                                                  in1=invf[:, hf * 512:(hf + 1) * 512], op=ALU.mult),
                   [PA[hf], inv], [s1])
                ew("vector", lambda e, hf=hf, s1f=s1f: e.tensor_tensor(out=s1f[:, hf * 512:(hf + 1) * 512], in0=PR[hf][:, :],
                                                                       in1=s1f[:, hf * 512:(hf + 1) * 512], op=ALU.add),
                   [PR[hf], s1], [s1])
            ew("scalar", lambda e, s1f=s1f, pTf=pTf: e.activation(out=pTf, in_=s1f, func=AF.Exp, scale=MLA_SCALE), [s1], [pT])
            ew("vector", lambda e, b=b: e.tensor_reduce(out=LP[:, b, :], in_=pT[:, :, :].rearrange("p t h -> p h t"),
                                                        axis=AX.X, op=ALU.add), [pT], [LP])
            for t in range(128):
                mm(pacc, pacc[:, b * 8:(b + 1) * 8], g_, g_[:, t, :], pT, pT[:, t, :], t == 0, t == 127)
        lsum = pb[0]
        mm(lsum, lsum[:, 0:NS * 8], cst, cst[:, C_ONES:C_ONES + 128], LP, LP[:, :, :].rearrange("p b h -> p (b h)"), True, True)
        num = sb("num", (128, NS, 8), F32)
        den = sb("den", (128, NS, 8), F32)
        olat = sb("olat", (128, NS, 8), BF16)
        pnv = pnew[:, :, :].rearrange("p h b -> p b h")
        ew("vector", lambda e: e.tensor_tensor(out=num[:, :, :], in0=pnv, in1=ckvs[:, 0:NS].unsqueeze(2).to_broadcast([128, NS, 8]),
                                               op=ALU.mult), [pnew, ckvs], [num])
        ew("vector", lambda e: e.tensor_tensor(out=num[:, :, :], in0=pacc[:, 0:NS * 8].rearrange("p (b h) -> p b h", h=8),
                                               in1=num[:, :, :], op=ALU.add), [pacc, num], [num])
        ew("vector", lambda e: e.tensor_tensor(out=den[:, :, :], in0=lsum[:, 0:NS * 8].rearrange("p (b h) -> p b h", h=8),
                                               in1=pnv, op=ALU.add), [lsum, pnew], [den])
        ew("vector", lambda e: e.reciprocal(out=den[:, :, :], in_=den[:, :, :]), [den], [den])
        ew("vector", lambda e: e.tensor_tensor(out=olat[:, :, :], in0=num[:, :, :], in1=den[:, :, :], op=ALU.mult), [num, den], [olat])
        for h in range(8):
            po = pb[1 + h % 2]
            mm(po, po[0:64, 0:NS], wuv2, wuv2[:, 0, h * 64:(h + 1) * 64], olat, olat[:, :, h], True, True)
            r0, jc = (h % 2) * 64, h // 2
            ew("scalar", lambda e, po=po, r0=r0, jc=jc: e.copy(out=OAs[r0:r0 + 64, jc, :], in_=po[0:64, 0:NS]), [po], [OAs])
        Kb = [sb("Kb%d" % i, (128, 2, 256), F32) for i in range(2)]
        Vb = [sb("Vb%d" % i, (128, 2, 256), F32) for i in range(2)]
        Vbb = [sb("Vbb%d" % i, (128, 2, 256), BF16) for i in range(2)]
        qbc = sb("qbc", (128, 2, 128), F32)
        prod = sb("prod", (128, 2, 256), F32)
        scm = sb("scm", (128, 2, 4), F32)
        pm = sb("pm", (128, 2, 4), BF16)
        rden = sb("rden", (128, NS, 4), F32)
        omf = sb("omf", (128, 2, NS), F32)
        pq, pom, pdn = pb[3], pb[4], pb[5]
        ident = cst[:, C_ID:C_ID + 128]
        for b in range(NS):
            k = b % 2
            P.dma("sync", lambda e, k=k, b=b: e.dma_start(out=Kb[k][:, :, :], in_=cmk[b].rearrange("(blk p) f -> p blk f", p=128)),
                  key=Kb[k], reads=[], writes=[Kb[k]])
            P.dma("sync", lambda e, k=k, b=b: e.dma_start(out=Vb[k][:, :, :], in_=cmv[b].rearrange("(blk p) f -> p blk f", p=128)),
                  key=Vb[k], reads=[], writes=[Vb[k]])
            ew("gpsimd", lambda e, k=k: e.tensor_copy(out=Vbb[k][:, :, :], in_=Vb[k][:, :, :]), [Vb[k]], [Vbb[k]])
            ew("vector", lambda e, b=b: e.tensor_copy(out=qbc[:, :, :], in_=mqs[:, :, b:b + 1].to_broadcast([128, 2, 128])), [mqs], [qbc])
            for c in range(2):
                mm(pq, pq[:, c * 128:(c + 1) * 128], qbc, qbc[:, c, :], cst, ident, True, True)
            ew("vector", lambda e, k=k: e.tensor_tensor(out=prod[:, :, :], in0=Kb[k][:, :, :],
                                                        in1=pq[:, 0:256].unsqueeze(1).to_broadcast([128, 2, 256]), op=ALU.mult),
               [Kb[k], pq], [prod])
            ew("vector", lambda e: e.tensor_reduce(out=scm[:, :, :], in_=prod[:, :, :].rearrange("p k (h d) -> p k h d", h=4),
                                                   axis=AX.X, op=ALU.add), [prod], [scm])
            ew("scalar", lambda e: e.activation(out=pm[:, :, :], in_=scm[:, :, :], func=AF.Exp, scale=MEM_SCALE), [scm], [pm])
            for c in range(2):
                for blk in range(2):
                    mm(pom, pom[:, b * 4 + 2 * c:b * 4 + 2 * c + 2], Vbb[k], Vbb[k][:, blk, c * 128:(c + 1) * 128],
                       pm, pm[:, blk, 2 * c:2 * c + 2], blk == 0, blk == 1)
            for blk in range(2):
                mm(pdn, pdn[:, b * 4:(b + 1) * 4], cstb, cstb[:, C_ONES:C_ONES + 128], pm, pm[:, blk, :], blk == 0, blk == 1)
        ew("vector", lambda e: e.reciprocal(out=rden[:, :, :], in_=pdn[:, 0:NS * 4].rearrange("p (b h) -> p b h", h=4)), [pdn], [rden])
        pomv = pom[:, 0:NS * 4].rearrange("p (b h) -> p b h", h=4)
        for c in range(2):
            for hl in range(2):
                r0 = hl * 64
                ew("vector", lambda e, c=c, hl=hl, r0=r0: e.tensor_tensor(out=omf[r0:r0 + 64, c, :], in0=pomv[r0:r0 + 64, :, 2 * c + hl],
                                                                          in1=rden[r0:r0 + 64, :, 2 * c + hl], op=ALU.mult),
                   [pom, rden], [omf])
        TB = {"sq": sb("abB", (128, 8, NT), BF16), "std": sb("stdB", (128, NT), F32), "rstd": sb("rstdB", (128, NT), F32), "pn": pb[6]}
        rms_rstd(TB, [omf[:, 0, :], omf[:, 1, :]], [omf, omf], cstb[:, C_ONES:C_ONES + 128], 128, NS, 256.0)
        for j in range(2):
            norm_apply(TB, OMs, OMs[:, j, :], omf, omf[:, j, :], G_OMEM + j, 128, NS)
        rms_rstd(TB, [OAs[:, c, :] for c in range(4)], [OAs] * 4, cstb[:, C_ONES:C_ONES + 128], 128, NS, 512.0)
        for c in range(4):
            norm_apply(TB, onorm_s, onorm_s[:, c, :], OAs, OAs[:, c, :], G_OMLA + c, 128, NS)
        P.barrier()
        print("sampleB SBUF peak bytes/partition:", A.peak - A.BASE)
        A.reset(mB)

    if do_sample:
        sampleB()

    OA = sb("OA", (128, 4, NOWN * NT), F32)
    if do_att:
        KT = [sb("KT%d" % i, (96, NSLOT * NT), BF16) for i in range(2)]
        VA = [sb("VA%d" % i, (128, NSLOT * NT // 128, 128), BF16) for i in range(2)]
        QT = [sb("QT%d" % i, (96, NOWN * NT), BF16) for i in range(2)]
        pts = [sb("pts%d" % i, (128, NT), BF16) for i in range(4)]
        rec2 = sb("rec2", (128, NT), F32)
        for i in range(2):
            ew("gpsimd", lambda e, i=i: e.memset(VA[i][:, :, 64:128], 1.0), [], [VA[i]])
        vsv = VS.t.rearrange("(blk p) (h d) -> p blk h d", p=128, h=8)
        nblk = 0
        for h in range(8):
            kt, va, qt = KT[h % 2], VA[h % 2], QT[h % 2]
            P.dma("sync", lambda e, kt=kt, h=h: e.dma_start(out=kt[0:64, :], in_=KN[h, :, :]), key=kt, reads=[], writes=[kt])
            P.dma("sync", lambda e, kt=kt: e.dma_start(out=kt[64:96, :], in_=KR[:, :]), key=kt, reads=[], writes=[kt])
            for q4 in range(8):
                b0, b1 = q4 * 8, (q4 + 1) * 8
                P.dma("sync", lambda e, va=va, h=h, b0=b0, b1=b1: e.dma_start(out=va[:, b0:b1, 0:64], in_=vsv[:, b0:b1, h, :]),
                      key=va, reads=[], writes=[va])
            P.dma("sync", lambda e, qt=qt, h=h: e.dma_start(out=qt[0:96, :], in_=QS[h, :, :]), key=qt, reads=[], writes=[qt])
            for j in range(NOWN):
                tiles = [(8 + j, "flag")] + [(s, "full") for s in range(j)] + [(8 + s, "full") for s in range(j)] + [(j, "diag")]
                po = pb[4 + (h * NOWN + j) % 2]
                blocks = []
                for (s, kind) in tiles:
                    for kb in range(4):
                        blocks.append((s * NT + kb * 128, kb * 128 if kind == "diag" else 0, kind))
                nb = len(blocks)
                LAG = 2
                issued = []
                for i in range(nb + LAG):
                    if i < nb:
                        k0, q0, kind = blocks[i]
                        ncol = NT - q0
                        ps_ = pb[nblk % 4]
                        pt_ = pts[nblk % 4]
                        nblk += 1
                        issued.append((pt_, q0, ncol, k0))
                        mm(ps_, ps_[:, 0:ncol], kt, kt[0:96, k0:k0 + 128], qt, qt[0:96, j * NT + q0:(j + 1) * NT], True, True)
                        if kind == "flag":
                            ew("scalar", lambda e, ps_=ps_, pt_=pt_, ncol=ncol: e.activation(
                                out=pt_[:, 0:ncol], in_=ps_[:, 0:ncol], func=AF.Exp, scale=MLA_SCALE,
                                bias=gn[:, G_EXPB:G_EXPB + 1]), [ps_, gn], [pt_])
                        else:
                            ew("scalar", lambda e, ps_=ps_, pt_=pt_, ncol=ncol: e.activation(
                                out=pt_[:, 0:ncol], in_=ps_[:, 0:ncol], func=AF.Exp, scale=MLA_SCALE), [ps_], [pt_])
                        if kind == "diag":
                            ew("gpsimd", lambda e, pt_=pt_: e.tensor_tensor(out=pt_[:, 0:128], in0=pt_[:, 0:128],
                                                                            in1=cstb[:, C_TRI:C_TRI + 128], op=ALU.mult),
                               [pt_, cstb], [pt_])
                    if i >= LAG:
                        ii = i - LAG
                        pt_, q0, ncol, k0 = issued[ii]
                        mm(po, po[:, q0:NT], va, va[:, k0 // 128, :], pt_, pt_[:, 0:ncol], ii == 0, ii == nb - 1)
                r0, jc = (h % 2) * 64, h // 2
                ew("vector", lambda e, po=po: e.reciprocal(out=rec2[64:128, :], in_=po[64:128, :]), [po], [rec2])
                ew("vector", lambda e, po=po, r0=r0, jc=jc, j=j: e.tensor_tensor(
                    out=OA[r0:r0 + 64, jc, j * NT:(j + 1) * NT], in0=po[0:64, :], in1=rec2[64:128, :], op=ALU.mult),
                   [po, rec2], [OA])
        P.barrier()
        print("phase2 SBUF peak bytes/partition:", A.peak - A.BASE)
    else:
        ew("gpsimd", lambda e: e.memset(OA[:, :, :], 1.0), [], [OA])
        P.barrier()
    pm2 = A.mark()
    A.reset(persist_mark)
    OA2 = sb("OA", (128, 4, NOWN * NT), F32)
    assert A.mark() <= pm2

    if do_post:
        T3 = alloc_common()
        wo_sb = sb("wo_sb", (128, 8, D), BF16)
        load_w(wo_sb, w_o, 8, D)
        xb3 = sb("xb3", (128, 8, NT), F32)
        onb = sb("onb", (128, 4, NT), BF16)
        yv = o_y.t.rearrange("(c p) t -> p c t", p=128)
        for j in range(NOWN):
            c0 = j * NT
            P.dma("sync", lambda e, c0=c0: e.dma_start(out=xb3[:, :, :], in_=x1v[:, :, c0:c0 + NT]), key=xb3, reads=[], writes=[xb3])
            rms_rstd(T3, [OA[:, c, c0:c0 + NT] for c in range(4)], [OA] * 4, cstb[:, C_ONES:C_ONES + 128], 128, NT, 512.0)
            for c in range(4):
                norm_apply(T3, onb, onb[:, c, 0:NT], OA, OA[:, c, c0:c0 + NT], G_OMLA + c, 128, NT)
            rhs = [(onb, onb[:, c, 0:NT]) for c in range(4)] + [(YC, YC[:, c, c0:c0 + NT]) for c in range(2)] + \
                  [(OM, OM[:, c, c0:c0 + NT]) for c in range(2)]
            for c in range(8):
                y = pb[4 + c % 2]
                for k, (rb, ra) in enumerate(rhs):
                    mm(y, y[:, 0:NT], wo_sb, wo_sb[:, k, c * 128:(c + 1) * 128], rb, ra, k == 0, k == 7)
                ew("vector", lambda e, y=y, c=c: e.tensor_tensor(out=xb3[:, c, 0:NT], in0=y[:, 0:NT], in1=xb3[:, c, 0:NT], op=ALU.add),
                   [y, xb3], [xb3])
            ffn_half(T3, xb3, NT, sc2, G_FFN2)
            P.dma("sync", lambda e, c0=c0: e.dma_start(out=yv[:, :, c0:c0 + NT], in_=xb3[:, :, :]), key=xb3, reads=[xb3], writes=[])
        rhs = [(onorm_s, onorm_s[:, c, :]) for c in range(4)] + [(YCs, YCs[:, c, :]) for c in range(2)] + \
              [(OMs, OMs[:, c, :]) for c in range(2)]
        for c in range(8):
            y = pb[4 + c % 2]
            for k, (rb, ra) in enumerate(rhs):
                mm(y, y[:, 0:NS], wo_sb, wo_sb[:, k, c * 128:(c + 1) * 128], rb, ra, k == 0, k == 7)
            ew("vector", lambda e, y=y, c=c: e.tensor_tensor(out=xs[:, c, :], in0=y[:, 0:NS], in1=xs[:, c, :], op=ALU.add),
               [y, xs], [xs])
        ffn_half(T3, xs, NS, sc2, G_FFN2)
        P.dma("sync", lambda e: e.dma_start(out=o_ys.t.rearrange("(c p) t -> p c t", p=128), in_=xs[:, :, :]), key=xs, reads=[xs], writes=[])
        print("phase3 SBUF peak bytes/partition:", A.peak - A.BASE)

    P.emit()
    return nc, stack


def _rope_tables(pos):
    inv_freq = (np.float32(10000.0) ** (-np.arange(0, 32, 2, dtype=np.float32) / np.float32(32))).astype(np.float32)
    ang = (pos.astype(np.float32)[:, None] * inv_freq[None, :]).astype(np.float32)
    c = np.cos(ang.astype(np.float64)).astype(np.float32).T
    s = np.sin(ang.astype(np.float64)).astype(np.float32).T
    C = np.concatenate([c, c], axis=0)
    S = np.concatenate([-s, s], axis=0)
    return np.ascontiguousarray(C), np.ascontiguousarray(S)


def _slot_tiles(par):
    own = [2 * j + par for j in range(8)]
    oth = [2 * j + 1 - par for j in range(8)]
    return own + oth


def _constants():
    f32 = np.float32
    c = np.zeros((128, NCONST), f32)
    c[:, C_ONES:C_ONES + 128] = 1.0
    c[0:64, C_BLK64:C_BLK64 + 64] = 1.0
    c[64:128, C_BLK64 + 64:C_BLK64 + 128] = 1.0
    for m in range(32):
        c[(m + 16) % 32, C_ROT32 + m] = 1.0
    c[0:32, C_ONES32:C_ONES32 + 32] = 1.0
    c[0:64, C_BLK96:C_BLK96 + 64] = 1.0 / 64
    c[64:96, C_BLK96 + 64:C_BLK96 + 96] = 1.0 / 32
    for m in range(64, 96):
        c[64 + ((m - 64 + 16) % 32), C_ROT96 + m] = 1.0
    k = np.arange(128)[:, None]
    q = np.arange(128)[None, :]
    c[:, C_TRI:C_TRI + 128] = (k <= q).astype(f32)
    c[0:64, C_ONES64:C_ONES64 + 64] = 1.0
    c[:, C_ID:C_ID + 128] = np.eye(128, dtype=f32)
    return c


def _gains(inp, par):
    f32 = np.float32
    g = np.zeros((128, NGAIN), f32)
    v = lambda k: np.asarray(inp[k], f32)[0]
    g[:, G_FFN1:G_FFN1 + 8] = v("g_ffn1").reshape(8, 128).T
    g[:, G_MIX:G_MIX + 8] = v("g_mix").reshape(8, 128).T
    g[:, G_MEM:G_MEM + 8] = v("g_mem").reshape(8, 128).T
    g[:, G_KV] = v("g_kv_lora")
    g[0:32, G_KR] = v("g_kr")
    g[:, G_MK] = np.tile(v("g_mk"), 2)
    g[:, G_QL:G_QL + 2] = v("g_q_lora").reshape(2, 128).T
    g[0:64, G_Q96] = v("g_qn")
    g[64:96, G_Q96] = v("g_qr")
    g[0:64, G_KN] = v("g_kn")
    g[:, G_KN2] = np.tile(v("g_kn"), 2)
    cw = np.asarray(inp["conv_w"], f32)[0]
    g[:, G_CW0:G_CW0 + 2] = cw[0].reshape(2, 128).T
    g[:, G_CW1:G_CW1 + 2] = cw[1].reshape(2, 128).T
    g[:, G_CW2:G_CW2 + 2] = cw[2].reshape(2, 128).T
    g[:, G_OCONV:G_OCONV + 2] = v("g_out_conv").reshape(2, 128).T
    g[:, G_MQ] = np.tile(v("g_mq"), 2)
    g[:, G_OMEM:G_OMEM + 2] = v("g_out_mem").reshape(2, 128).T
    g[:, G_OMLA:G_OMLA + 4] = v("g_out_mla").reshape(4, 128).T
    g[:, G_FFN2:G_FFN2 + 8] = v("g_ffn2").reshape(8, 128).T
    g[:, G_FLA] = 1.0 - par
    g[:, G_FLB] = float(par)
    g[:, G_EXPB] = (par - 1) * 30000.0
    return g


def kernel(**inp):
    f32 = np.float32
    x_prompt = np.asarray(inp["x_prompt"], f32)
    n_cores = 8
    consts = _constants()
    w = lambda k: np.asarray(inp[k], f32)[0]
    shared = {
        "w1_gate": w("w1_gate"), "w1_up": w("w1_up"), "w1_down": w("w1_down"),
        "w2_gate": w("w2_gate"), "w2_up": w("w2_up"), "w2_down": w("w2_down"),
        "w_in": w("w_in"), "w_uq": w("w_uq"), "w_uk": w("w_uk").reshape(128, 512), "w_uv": w("w_uv").reshape(128, 512),
        "w_o": w("w_o"), "w_mem_k": w("w_mem_k"), "w_mem_v": w("w_mem_v"), "consts": consts,
        "w_ukT": np.ascontiguousarray(w("w_uk").transpose(2, 1, 0).reshape(64, 8 * 128)),
        "cache_ckv": np.asarray(inp["cache_ckv"], f32)[0].reshape(-1, 2048),
        "cache_krope": np.asarray(inp["cache_krope"], f32)[0].reshape(-1, 2048),
    }
    assert shared["cache_ckv"].shape[0] == N_PHYS * 8
    Cs, Ss = _rope_tables(np.full((NS,), PAST, dtype=np.int64))
    x_sample = np.asarray(inp["x_sample"], f32)
    state_conv = np.asarray(inp["state_conv"], f32)[0]
    page_table = np.asarray(inp["page_table"]).astype(np.int32)
    cache_mem_k = np.asarray(inp["cache_mem_k"], f32)[0]
    cache_mem_v = np.asarray(inp["cache_mem_v"], f32)[0]
    in_maps = []
    for c in range(n_cores):
        b, par = c // 2, c % 2
        tiles = _slot_tiles(par)
        tok = np.concatenate([np.arange(t * NT, (t + 1) * NT) for t in tiles])
        C, S = _rope_tables(tok)
        m = dict(shared)
        m["xT"] = np.ascontiguousarray(x_prompt[b][tok].T)
        m["memT"] = np.ascontiguousarray(np.asarray(inp["mem_prompt"], f32)[b].T)
        m["ropeC"] = C
        m["ropeS"] = S
        m["gains"] = _gains(inp, par)
        sl = slice(c * NS, (c + 1) * NS)
        m["xsT"] = np.ascontiguousarray(x_sample[sl, 0, :].T)
        m["ropeCs"] = Cs
        m["ropeSs"] = Ss
        m["stateT"] = np.ascontiguousarray(state_conv[sl].transpose(2, 1, 0))
        m["ptT"] = np.ascontiguousarray(page_table[sl].T)
        m["cache_mem_k"] = np.ascontiguousarray(cache_mem_k[sl].reshape(NS, N_MEM, 256))
        m["cache_mem_v"] = np.ascontiguousarray(cache_mem_v[sl].reshape(NS, N_MEM, 256))
        in_maps.append(m)

    nc, stack = build_program()
    with stack:
        res = run_bass_kernel_spmd(nc, in_maps, core_ids=list(range(n_cores)))
    R = res.results

    B = 4
    y_prompt = np.zeros((B, SEQ, D), f32)
    y_sample = np.zeros((DEC, 1, D), f32)
    ckv_p = np.zeros((1, B, SEQ, 128), f32)
    kr_p = np.zeros((1, B, SEQ, 32), f32)
    conv_p = np.zeros((1, B, 2, 256), f32)
    mk_p = np.zeros((1, B, N_MEM, 4, 64), f32)
    mv_p = np.zeros((1, B, N_MEM, 4, 64), f32)
    for c in range(n_cores):
        b, par = c // 2, c % 2
        tiles = _slot_tiles(par)
        for s in range(8):
            t = tiles[s]
            y_prompt[b, t * NT:(t + 1) * NT, :] = R[c]["o_y"][:, s * NT:(s + 1) * NT].T
            ckv_p[0, b, t * NT:(t + 1) * NT, :] = R[c]["o_ckv"][:, s * NT:(s + 1) * NT].T
            kr_p[0, b, t * NT:(t + 1) * NT, :] = R[c]["o_kr"][:, s * NT:(s + 1) * NT].T
            if t == 15:
                conv_p[0, b] = R[c]["o_conv"][:, 2 * s:2 * s + 2].T
        if par == 0:
            mk_p[0, b] = R[c]["o_mk"].T.reshape(N_MEM, 4, 64)
            mv_p[0, b] = R[c]["o_mv"].T.reshape(N_MEM, 4, 64)
    ckv_s = np.zeros((1, DEC, 1, 128), f32)
    kr_s = np.zeros((1, DEC, 1, 32), f32)
    conv_s = np.zeros((1, DEC, 2, 256), f32)
    for c in range(n_cores):
        sl = slice(c * NS, (c + 1) * NS)
        y_sample[sl, 0, :] = R[c]["o_ys"].T
        ckv_s[0, sl, 0, :] = R[c]["o_ckv_s"].T
        kr_s[0, sl, 0, :] = R[c]["o_kr_s"].T
        conv_s[0, sl, 0, :] = R[c]["o_state_s"].T
        conv_s[0, sl, 1, :] = R[c]["o_conv_s"].T
    return (y_prompt, y_sample, ckv_p, kr_p, conv_p, mk_p, mv_p, ckv_s, kr_s, conv_s)
```

```python
import contextlib
import numpy as np
import concourse.bass as bass
import concourse.mybir as mybir
from concourse.bass_utils import run_bass_kernel_spmd

F32 = mybir.dt.float32
BF16 = mybir.dt.bfloat16
I32 = mybir.dt.int32
ALU = mybir.AluOpType
AF = mybir.ActivationFunctionType
AX = mybir.AxisListType

D = 1024
DFF = 2816
NM = DFF // 128
SEQ = 8192
NT = 512
NSLOT = 16
NOWN = 8
DEC = 128
EPS = 1e-6
INW = 1440
PAST = 16384
N_MEM = 256
MLA_SCALE = float(96 ** -0.5)
MEM_SCALE = float(64 ** -0.5)
NCONST = 1024
NS = 16
N_PHYS = 20480
NGAIN = 64

G_FFN1, G_MIX, G_MEM, G_KV, G_KR, G_MK, G_QL, G_Q96, G_KN = 0, 8, 16, 24, 25, 26, 27, 29, 30
G_CW0, G_CW1, G_CW2, G_OCONV, G_MQ, G_OMEM, G_OMLA, G_FFN2, G_FLA, G_FLB, G_EXPB, G_KN2 = 31, 33, 35, 37, 39, 40, 42, 46, 54, 55, 56, 57
C_ONES, C_BLK64, C_ROT32, C_ONES32, C_BLK96, C_ROT96, C_TRI, C_ONES64, C_ID = 0, 128, 256, 288, 320, 416, 512, 640, 704


class Buf:
    def __init__(self, name, t=None):
        self.name = name
        self.t = t
        self.last_w = None
        self.readers = []
        self.dma_sem = None
        self.dma_issued = 0

    def __getitem__(self, k):
        return self.t[k]


class Op:
    __slots__ = ("eng", "fn", "waits", "signaled", "sigval", "is_dma", "key", "dma_waits")

    def __init__(self, eng, fn, is_dma=False, key=None):
        self.eng = eng
        self.fn = fn
        self.waits = []
        self.dma_waits = []
        self.signaled = False
        self.sigval = None
        self.is_dma = is_dma
        self.key = key


class Prog:
    ENGS = ("tensor", "scalar", "vector", "gpsimd", "sync")

    def __init__(self, nc, stack):
        self.nc = nc
        self.stack = stack
        self.ops = {e: [] for e in self.ENGS}
        self.esem = {}
        self.dma_keys = []
        self.free_sems = []

    def _dep(self, op, prev):
        if prev is None or prev is op:
            return
        if prev.is_dma:
            op.dma_waits.append((prev.key, prev.key.dma_issued * 16))
            return
        if prev.eng == "tensor" and op.eng == "tensor":
            return
        op.waits.append(prev)
        prev.signaled = True

    def _track(self, op, reads, writes):
        for b in reads:
            self._dep(op, b.last_w)
        for b in writes:
            self._dep(op, b.last_w)
            last_by_eng = {}
            for r in b.readers:
                if r.is_dma:
                    self._dep(op, r)
                else:
                    last_by_eng[r.eng] = r
            for r in last_by_eng.values():
                self._dep(op, r)
        for b in reads:
            b.readers.append(op)
        for b in writes:
            b.last_w = op
            b.readers = []

    def op(self, eng, fn, reads=(), writes=()):
        o = Op(eng, fn)
        self._track(o, reads, writes)
        self.ops[eng].append(o)
        return o

    def dma(self, eng, fn, key, reads=(), writes=()):
        o = Op(eng, fn, is_dma=True, key=key)
        self._track(o, reads, writes)
        key.dma_issued += 1
        if key.dma_sem is None:
            key.dma_sem = self.stack.enter_context(self.nc.semaphore("d%d_%s" % (len(self.dma_keys), key.name)))
            self.dma_keys.append(key)
        self.ops[eng].append(o)
        return o

    def barrier(self):
        targets = []
        for e in self.ENGS:
            for o in reversed(self.ops[e]):
                if o.fn is not None and not o.is_dma:
                    targets.append(o)
                    o.signaled = True
                    break
        dma_targets = [(k, k.dma_issued * 16) for k in self.dma_keys]
        for e in self.ENGS:
            o = Op(e, None)
            o.waits = [t for t in targets if not (t.eng == e and e in ("tensor", "sync"))]
            o.dma_waits = list(dma_targets)
            self.ops[e].append(o)

    def emit(self):
        nc = self.nc
        for e in self.ENGS:
            self.esem[e] = self.stack.enter_context(nc.semaphore("e_" + e))
            n = 0
            for o in self.ops[e]:
                if o.signaled and not o.is_dma and o.fn is not None:
                    n += 1
                    o.sigval = n
        prog = self

        def run(e, eng):
            waited = {}
            for o in prog.ops[e]:
                need = {}
                for p in o.waits:
                    s = prog.esem[p.eng]
                    need[s] = max(need.get(s, 0), p.sigval)
                for k, v in o.dma_waits:
                    need[k.dma_sem] = max(need.get(k.dma_sem, 0), v)
                for s, v in need.items():
                    if waited.get(s, 0) < v:
                        eng.wait_ge(s, v)
                        waited[s] = v
                if o.fn is None:
                    continue
                ins = o.fn(eng)
                if o.is_dma:
                    ins.then_inc(o.key.dma_sem, 16)
                elif o.signaled:
                    ins.then_inc(prog.esem[e], 1)
            if e == "sync":
                for k in prog.dma_keys:
                    eng.wait_ge(k.dma_sem, k.dma_issued * 16)

        with nc.Block() as block:
            @block.tensor
            def _(eng):
                run("tensor", eng)

            @block.scalar
            def _(eng):
                run("scalar", eng)

            @block.vector
            def _(eng):
                run("vector", eng)

            @block.gpsimd
            def _(eng):
                run("gpsimd", eng)

            @block.sync
            def _(eng):
                run("sync", eng)


class Arena:
    BASE = 16512
    TOP = 229344

    def __init__(self, nc):
        self.nc = nc
        self.ptr = self.BASE
        self.n = 0
        self.peak = 0

    def alloc(self, name, shape, dt):
        esz = 2 if dt == BF16 else 4
        size = int(np.prod(shape[1:])) * esz
        size = (size + 63) // 64 * 64
        assert self.ptr + size <= self.TOP, "SBUF overflow at %s: %d + %d" % (name, self.ptr, size)
        self.n += 1
        h = self.nc.alloc_sbuf_tensor_at("%s_%d" % (name, self.n), list(shape), dt, offset=self.ptr)
        self.ptr += size
        self.peak = max(self.peak, self.ptr)
        return Buf(name, h)

    def mark(self):
        return self.ptr

    def reset(self, m):
        self.ptr = m


def build_program(do_att=True, do_post=True, do_sample=True):
    nc = bass.Bass("TRN2", target_bir_lowering=False)
    stack = contextlib.ExitStack()
    P = Prog(nc, stack)
    A = Arena(nc)

    def dram_in(name, shape, dt=F32):
        return Buf(name, nc.dram_tensor(name, list(shape), dt, kind="ExternalInput").ap())

    def dram_out(name, shape, dt=F32):
        return Buf(name, nc.dram_tensor(name, list(shape), dt, kind="ExternalOutput").ap())

    def dram_tmp(name, shape, dt):
        return Buf(name, nc.dram_tensor(name, list(shape), dt).ap())

    sb = A.alloc

    xT = dram_in("xT", (D, NSLOT * NT))
    memT = dram_in("memT", (D, N_MEM))
    ropeC = dram_in("ropeC", (32, NSLOT * NT))
    ropeS = dram_in("ropeS", (32, NSLOT * NT))
    w1g = dram_in("w1_gate", (D, DFF))
    w1u = dram_in("w1_up", (D, DFF))
    w1d = dram_in("w1_down", (DFF, D))
    w2g = dram_in("w2_gate", (D, DFF))
    w2u = dram_in("w2_up", (D, DFF))
    w2d = dram_in("w2_down", (DFF, D))
    w_in = dram_in("w_in", (D, INW))
    w_uq = dram_in("w_uq", (256, 768))
    w_uk = dram_in("w_uk", (128, 512))
    w_uv = dram_in("w_uv", (128, 512))
    w_o = dram_in("w_o", (D, D))
    wmk = dram_in("w_mem_k", (D, 256))
    wmv = dram_in("w_mem_v", (D, 256))
    gains = dram_in("gains", (128, NGAIN))
    xsT = dram_in("xsT", (D, NS))
    ropeCs = dram_in("ropeCs", (32, NS))
    ropeSs = dram_in("ropeSs", (32, NS))
    stateT = dram_in("stateT", (256, 2, NS))
    ptT = dram_in("ptT", (128, NS), I32)
    cckv = dram_in("cache_ckv", (N_PHYS * 8, 2048))
    ckr = dram_in("cache_krope", (N_PHYS * 2, 2048))
    cmk = dram_in("cache_mem_k", (NS, N_MEM, 256))
    cmv = dram_in("cache_mem_v", (NS, N_MEM, 256))
    w_ukT = dram_in("w_ukT", (64, 8 * 128))
    consts = dram_in("consts", (128, NCONST))

    o_y = dram_out("o_y", (D, NOWN * NT))
    o_ckv = dram_out("o_ckv", (128, NSLOT * NT))
    o_kr = dram_out("o_kr", (32, NSLOT * NT))
    o_conv = dram_out("o_conv", (256, NSLOT * 2))
    o_mk = dram_out("o_mk", (256, N_MEM))
    o_mv = dram_out("o_mv", (256, N_MEM))
    o_ys = dram_out("o_ys", (D, NS))
    o_ckv_s = dram_out("o_ckv_s", (128, NS))
    o_kr_s = dram_out("o_kr_s", (32, NS))
    o_conv_s = dram_out("o_conv_s", (256, NS))
    o_state_s = dram_out("o_state_s", (256, NS))

    sc1 = [dram_tmp("sc1g", (D, DFF), BF16), dram_tmp("sc1u", (D, DFF), BF16), dram_tmp("sc1d", (DFF, D), BF16)]
    sc2 = [dram_tmp("sc2g", (D, DFF), BF16), dram_tmp("sc2u", (D, DFF), BF16), dram_tmp("sc2d", (DFF, D), BF16)]
    X1 = dram_tmp("X1", (D, NOWN * NT), F32)
    KN = dram_tmp("KN", (8, 64, NSLOT * NT), BF16)
    KR = dram_tmp("KR", (32, NSLOT * NT), BF16)
    VS = dram_tmp("VS", (NSLOT * NT, 512), BF16)
    QS = dram_tmp("QS", (8, 96, NOWN * NT), BF16)

    pb = [Buf("pb%d" % i, stack.enter_context(nc.psum_tensor("pb%d" % i, [128, 512], F32))) for i in range(8)]

    gn = sb("gn", (128, NGAIN), F32)
    cst = sb("cst", (128, NCONST), F32)
    cstb = sb("cstb", (128, NCONST), BF16)
    YC = sb("YC", (128, 2, NOWN * NT), BF16)
    OM = sb("OM", (128, 2, NOWN * NT), BF16)
    memKT = sb("memKT", (128, 2, N_MEM), BF16)
    memVA = sb("memVA", (128, 2, 4, 128), BF16)
    UB = sb("UB", (128, 2, NSLOT, 2), F32)
    xs = sb("xs", (128, 8, NS), F32)
    qabs = sb("qabs", (128, NS, 8), BF16)
    qrs = sb("qrs", (32, NS, 8), BF16)
    pnew = sb("pnew", (128, 8, NS), F32)
    ckvs = sb("ckvs", (128, NS), F32)
    YCs = sb("YCs", (128, 2, NS), BF16)
    OMs = sb("OMs", (128, 2, NS), BF16)
    mqs = sb("mqs", (128, 2, NS), F32)
    OAs = sb("OAs", (128, 4, NS), F32)
    onorm_s = sb("onorm_s", (128, 4, NS), BF16)
    persist_mark = A.mark()

    def load(eng, dst, dst_ap, src, src_ap):
        P.dma(eng, lambda e: e.dma_start(out=dst_ap, in_=src_ap), key=dst, reads=[src], writes=[dst])

    def load_w(dst, src, nk, m):
        v = src.t.rearrange("(c p) m -> p c m", p=128)
        step = 1024
        for c in range(nk):
            for m0 in range(0, m, step):
                m1 = min(m, m0 + step)
                load("gpsimd", dst, dst[:, c, m0:m1], src, v[:, c, m0:m1])

    def mm(out_buf, out_ap, lhsT_buf, lhsT_ap, rhs_buf, rhs_ap, start, stop):
        P.op("tensor", lambda e: e.matmul(out_ap, lhsT_ap, rhs_ap, start=start, stop=stop),
             reads=[lhsT_buf, rhs_buf], writes=[out_buf])

    def ew(eng, fn, reads, writes):
        P.op(eng, fn, reads=reads, writes=writes)

    load("sync", gn, gn[:, :], gains, gains[:, :])
    load("sync", cst, cst[:, :], consts, consts[:, :])
    ew("vector", lambda e: e.tensor_copy(out=cstb[:, :], in_=cst[:, :]), [cst], [cstb])
    ew("gpsimd", lambda e: e.memset(memVA[:, :, :, 64:128], 1.0), [], [memVA])
    ew("gpsimd", lambda e: e.memset(UB[:, :, :, :], 0.0), [], [UB])

    def cast_w(dst, src, rows, cols):
        step = 1408 if cols == DFF else 1024
        for r0 in range(0, rows, 128):
            for m0 in range(0, cols, step):
                P.dma("gpsimd", lambda e, r0=r0, m0=m0: e.dma_start(out=dst[r0:r0 + 128, m0:m0 + step],
                                                                    in_=src[r0:r0 + 128, m0:m0 + step]),
                      key=dst, reads=[], writes=[])

    for dst, src, r, c in ((sc1[0], w1g, D, DFF), (sc1[1], w1u, D, DFF), (sc1[2], w1d, DFF, D)):
        cast_w(dst, src, r, c)

    def alloc_common():
        T = {}
        T["hb"] = sb("hb", (128, 8, NT), BF16)
        T["ab"] = sb("ab", (128, NM, NT), BF16)
        T["std"] = sb("std", (128, NT), F32)
        T["rstd"] = sb("rstd", (128, NT), F32)
        T["t0"] = sb("t0", (128, NT), F32)
        T["t1"] = sb("t1", (128, NT), F32)
        T["t2"] = sb("t2", (128, NT), F32)
        T["sgb"] = [sb("sgb%d" % i, (128, NT), BF16) for i in range(2)]
        T["wgc"] = [sb("wgc%d" % i, (128, 8, 128), BF16) for i in range(2)]
        T["wuc"] = [sb("wuc%d" % i, (128, 8, 128), BF16) for i in range(2)]
        T["wdc"] = [sb("wdc%d" % i, (128, NM, 128), BF16) for i in range(2)]
        T["sq"] = T["ab"]
        T["pn"] = pb[6]
        return T

    pn = pb[6]
    pz = pb[7]

    def rms_rstd(T, srcs, src_bufs, ones_ap, rows, n, count):
        sqb, pn_, std, rstd = T["sq"], T["pn"], T["std"], T["rstd"]
        for i, (a, b) in enumerate(zip(srcs, src_bufs)):
            if b in pb:
                ew("scalar", lambda e, a=a, i=i: e.activation(out=sqb[0:rows, i, 0:n], in_=a, func=AF.Square), [b], [sqb])
            else:
                eng_ = ("gpsimd", "vector", "gpsimd", "vector", "scalar", "gpsimd", "vector", "scalar")[i % 8] if len(srcs) >= 4 else "gpsimd"
                if eng_ == "scalar":
                    ew("scalar", lambda e, a=a, i=i: e.activation(out=sqb[0:rows, i, 0:n], in_=a, func=AF.Square), [b], [sqb])
                else:
                    ew(eng_, lambda e, a=a, i=i: e.tensor_tensor(out=sqb[0:rows, i, 0:n], in0=a, in1=a, op=ALU.mult),
                       [b], [sqb])
        for i in range(len(srcs)):
            mm(pn_, pn_[0:rows, 0:n], cstb, ones_ap, sqb, sqb[0:rows, i, 0:n], i == 0, i == len(srcs) - 1)
        ew("scalar", lambda e: e.activation(out=std[0:rows, 0:n], in_=pn_[0:rows, 0:n], func=AF.Sqrt,
                                            bias=EPS, scale=1.0 / count), [pn_], [std])
        ew("vector", lambda e: e.reciprocal(out=rstd[0:rows, 0:n], in_=std[0:rows, 0:n]), [std], [rstd])

    def norm_apply(T, out_buf, out_ap, src_buf, src_ap, gcol, rows, n):
        rstd = T["rstd"]
        ew("vector", lambda e: e.scalar_tensor_tensor(out=out_ap, in0=src_ap, scalar=gn[0:rows, gcol:gcol + 1],
                                                      in1=rstd[0:rows, 0:n], op0=ALU.mult, op1=ALU.mult),
           [src_buf, gn, rstd], [out_buf])

    def rmsnorm_d(T, x, n, gbase):
        hb = T["hb"]
        rms_rstd(T, [x[:, c, 0:n] for c in range(8)], [x] * 8, cstb[:, C_ONES:C_ONES + 128], 128, n, float(D))
        for c in range(8):
            norm_apply(T, hb, hb[:, c, 0:n], x, x[:, c, 0:n], gbase + c, 128, n)

    def ffn_half(T, x, n, sc, gbase):
        hb, ab = T["hb"], T["ab"]
        sgv = sc[0].t.rearrange("(c p) m -> p c m", p=128)
        suv = sc[1].t.rearrange("(c p) m -> p c m", p=128)
        sdv = sc[2].t.rearrange("(m p) c -> p m c", p=128)
        rmsnorm_d(T, x, n, gbase)
        for m in range(NM):
            g, u, sg = pb[m % 2], pb[2 + m % 2], T["sgb"][m % 2]
            wg_, wu_ = T["wgc"][m % 2], T["wuc"][m % 2]
            P.dma("sync", lambda e, wg_=wg_, m=m: e.dma_start(out=wg_[:, :, :], in_=sgv[:, :, m * 128:(m + 1) * 128]),
                  key=wg_, reads=[], writes=[wg_])
            P.dma("sync", lambda e, wu_=wu_, m=m: e.dma_start(out=wu_[:, :, :], in_=suv[:, :, m * 128:(m + 1) * 128]),
                  key=wu_, reads=[], writes=[wu_])
            for c in range(8):
                mm(g, g[:, 0:n], wg_, wg_[:, c, :], hb, hb[:, c, 0:n], c == 0, c == 7)
            for c in range(8):
                mm(u, u[:, 0:n], wu_, wu_[:, c, :], hb, hb[:, c, 0:n], c == 0, c == 7)
            ew("scalar", lambda e, g=g, sg=sg: e.activation(out=sg[:, 0:n], in_=g[:, 0:n], func=AF.Silu), [g], [sg])
            ew("vector", lambda e, u=u, sg=sg, m=m: e.tensor_tensor(out=ab[:, m, 0:n], in0=sg[:, 0:n], in1=u[:, 0:n],
                                                                    op=ALU.mult), [sg, u], [ab])
        for c in range(8):
            y = pb[4 + c % 2]
            wd_ = T["wdc"][c % 2]
            P.dma("sync", lambda e, wd_=wd_, c=c: e.dma_start(out=wd_[:, :, :], in_=sdv[:, :, c * 128:(c + 1) * 128]),
                  key=wd_, reads=[], writes=[wd_])
            for m in range(NM):
                mm(y, y[:, 0:n], wd_, wd_[:, m, :], ab, ab[:, m, 0:n], m == 0, m == NM - 1)
            ew("vector", lambda e, y=y, c=c: e.scalar_tensor_tensor(out=x[:, c, 0:n], in0=y[:, 0:n], scalar=0.5,
                                                                    in1=x[:, c, 0:n], op0=ALU.mult, op1=ALU.add),
               [y, x], [x])

    def proj(T, out_ps, rows, n, wbuf, col0):
        hb = T["hb"]
        for c in range(8):
            mm(out_ps, out_ps[0:rows, 0:n], wbuf, wbuf[:, c, col0:col0 + rows], hb, hb[:, c, 0:n], c == 0, c == 7)

    m0 = A.mark()
    T0 = alloc_common()
    wmk_sb = sb("wmk_sb", (128, 8, 256), BF16)
    wmv_sb = sb("wmv_sb", (128, 8, 256), BF16)
    xm = sb("xm", (128, 8, N_MEM), F32)
    mko = [sb("mko%d" % i, (128, N_MEM), F32) for i in range(2)]
    mvo = sb("mvo", (128, 2, N_MEM), F32)
    load_w(wmk_sb, wmk, 8, 256)
    load_w(wmv_sb, wmv, 8, 256)
    load("sync", xm, xm[:, :, :], memT, memT.t.rearrange("(c p) t -> p c t", p=128))
    rmsnorm_d(T0, xm, N_MEM, G_MEM)
    for j in range(2):
        proj(T0, pz, 128, N_MEM, wmk_sb, 128 * j)
        ew("scalar", lambda e: e.copy(out=T0["t0"][:, 0:N_MEM], in_=pz[:, 0:N_MEM]), [pz], [T0["t0"]])
        rms_rstd(T0, [T0["t0"][:, 0:N_MEM]], [T0["t0"]], cstb[:, C_BLK64:C_BLK64 + 128], 128, N_MEM, 64.0)
        norm_apply(T0, mko[j], mko[j][:, 0:N_MEM], T0["t0"], T0["t0"][:, 0:N_MEM], G_MK, 128, N_MEM)
        P.dma("sync", lambda e, j=j: e.dma_start(out=o_mk[j * 128:(j + 1) * 128, :], in_=mko[j][:, 0:N_MEM]),
              key=mko[j], reads=[mko[j]], writes=[])
        ew("gpsimd", lambda e, j=j: e.tensor_copy(out=memKT[:, j, :], in_=mko[j][:, 0:N_MEM]), [mko[j]], [memKT])
        proj(T0, pz, 128, N_MEM, wmv_sb, 128 * j)
        ew("scalar", lambda e, j=j: e.copy(out=mvo[:, j, 0:N_MEM], in_=pz[:, 0:N_MEM]), [pz], [mvo])
        P.dma("sync", lambda e, j=j: e.dma_start(out=o_mv[j * 128:(j + 1) * 128, :], in_=mvo[:, j, 0:N_MEM]),
              key=mvo, reads=[mvo], writes=[])
    for kb in range(2):
        for c in range(8):
            mm(pz, pz[:, 0:256], T0["hb"], T0["hb"][:, c, kb * 128:(kb + 1) * 128], wmv_sb, wmv_sb[:, c, :], c == 0, c == 7)
        ew("scalar", lambda e, kb=kb: e.copy(out=memVA[:, kb, :, 0:64],
                                             in_=pz[:, 0:256].rearrange("p (h d) -> p h d", h=4)), [pz], [memVA])
    P.barrier()
    A.reset(m0)
    sc2_jobs = []
    for dst, src, r, c in ((sc2[0], w2g, D, DFF), (sc2[1], w2u, D, DFF), (sc2[2], w2d, DFF, D)):
        step = 1408 if c == DFF else 1024
        for r0 in range(0, r, 128):
            for m0_ in range(0, c, step):
                sc2_jobs.append((dst, src, r0, m0_, step))

    T = alloc_common()
    win_sb = sb("win_sb", (128, 8, INW), BF16)
    wuq_sb = sb("wuq_sb", (128, 2, 768), BF16)
    wuk_sb = sb("wuk_sb", (128, 1, 512), BF16)
    wuv_sb = sb("wuv_sb", (128, 1, 512), BF16)
    load_w(win_sb, w_in, 8, INW)
    load_w(wuq_sb, w_uq, 2, 768)
    load_w(wuk_sb, w_uk, 1, 512)
    load_w(wuv_sb, w_uv, 1, 512)
    xb = sb("xb", (128, 8, NT), F32)
    ckv_o = sb("ckv_o", (128, NT), F32)
    ckvb = sb("ckvb", (128, NT), BF16)
    kr_o = sb("kr_o", (32, NT), F32)
    krb = sb("krb", (32, NT), BF16)
    uext = sb("uext", (128, 2, NT + 2), F32)
    rC = sb("rC", (32, NT), F32)
    rS = sb("rS", (32, NT), F32)
    rC96 = sb("rC96", (96, NT), F32)
    rS96 = sb("rS96", (96, NT), F32)
    knb = [sb("knb%d" % i, (128, NT), BF16) for i in range(2)]
    PZ = [pb[7], pb[5]]
    CS = [{"sq": sb("csq0", (128, 2, NT), BF16), "pn": pb[6], "std": T["std"], "rstd": T["rstd"]},
          {"sq": sb("csq1", (128, 2, NT), BF16), "pn": pb[4], "std": sb("cstd1", (128, NT), F32), "rstd": sb("crstd1", (128, NT), F32)}]
    vb = [sb("vb%d" % i, (128, 512), BF16) for i in range(2)]
    cq = sb("cq", (128, 2, NT), F32)
    cqn = sb("cqn", (128, 2, NT), BF16)
    qn = sb("qn", (96, NT), F32)
    qb16 = [sb("qb16%d" % i, (96, NT), BF16) for i in range(2)]
    gbt = sb("gbt", (128, 2, NT), F32)
    mqn = sb("mqn", (128, 2, NT), BF16)
    ptm = [sb("ptm%d" % i, (128, NT), BF16) for i in range(2)]
    rec = sb("rec", (128, NT), F32)
    t0, t1, t2 = T["t0"], T["t1"], T["t2"]
    ew("gpsimd", lambda e: e.memset(rC96[0:64, :], 1.0), [], [rC96])
    ew("gpsimd", lambda e: e.memset(rS96[0:64, :], 0.0), [], [rS96])

    xv = xT.t.rearrange("(c p) t -> p c t", p=128)
    x1v = X1.t.rearrange("(c p) t -> p c t", p=128)

    def phase1(s):
        own = s < NOWN
        n = NT
        x = xb
        c0 = s * NT
        if s == ORDER[0]:
            load("sync", x, x[:, :, :], xT, xv[:, :, c0:c0 + NT])
        load("sync", rC, rC[:, :], ropeC, ropeC[:, c0:c0 + NT])
        load("sync", rS, rS[:, :], ropeS, ropeS[:, c0:c0 + NT])
        if own:
            load("sync", rC96, rC96[64:96, :], ropeC, ropeC[:, c0:c0 + NT])
            load("sync", rS96, rS96[64:96, :], ropeS, ropeS[:, c0:c0 + NT])
        ffn_half(T, x, n, sc1, G_FFN1)
        if own:
            P.dma("sync", lambda e: e.dma_start(out=x1v[:, :, c0:c0 + NT], in_=x[:, :, :]), key=x, reads=[x], writes=[])
        rmsnorm_d(T, x, n, G_MIX)
        nxt = ORDER.index(s) + 1
        if nxt < len(ORDER):
            cn = ORDER[nxt] * NT
            load("sync", x, x[:, :, :], xT, xv[:, :, cn:cn + NT])
        pzA, pzB = PZ
        proj(T, pzA, 128, n, win_sb, 256)
        rms_rstd(CS[0], [pzA[:, 0:n]], [pzA], cstb[:, C_ONES:C_ONES + 128], 128, n, 128.0)
        norm_apply(CS[0], ckv_o, ckv_o[:, 0:n], pzA, pzA[:, 0:n], G_KV, 128, n)
        P.dma("sync", lambda e: e.dma_start(out=o_ckv[:, c0:c0 + NT], in_=ckv_o[:, 0:n]), key=ckv_o, reads=[ckv_o], writes=[])
        ew("gpsimd", lambda e: e.tensor_copy(out=ckvb[:, 0:n], in_=ckv_o[:, 0:n]), [ckv_o], [ckvb])
        proj(T, pzB, 32, n, win_sb, 384)
        rms_rstd(CS[1], [pzB[0:32, 0:n]], [pzB], cstb[0:32, C_ONES32:C_ONES32 + 32], 32, n, 32.0)
        norm_apply(CS[1], t2, t2[0:32, 0:n], pzB, pzB[0:32, 0:n], G_KR, 32, n)
        mm(pzB, pzB[0:32, 0:n], cst, cst[0:32, C_ROT32:C_ROT32 + 32], t2, t2[0:32, 0:n], True, True)
        ew("vector", lambda e: e.tensor_tensor(out=t1[0:32, 0:n], in0=pzB[0:32, 0:n], in1=rS[0:32, 0:n], op=ALU.mult),
           [pzB, rS], [t1])
        ew("gpsimd", lambda e: e.tensor_tensor(out=t2[0:32, 0:n], in0=t2[0:32, 0:n], in1=rC[0:32, 0:n], op=ALU.mult),
           [t2, rC], [t2])
        ew("vector", lambda e: e.tensor_tensor(out=kr_o[0:32, 0:n], in0=t1[0:32, 0:n], in1=t2[0:32, 0:n], op=ALU.add),
           [t1, t2], [kr_o])
        P.dma("sync", lambda e: e.dma_start(out=o_kr[:, c0:c0 + NT], in_=kr_o[0:32, 0:n]), key=kr_o, reads=[kr_o], writes=[])
        ew("gpsimd", lambda e: e.tensor_copy(out=krb[0:32, 0:n], in_=kr_o[0:32, 0:n]), [kr_o], [krb])
        P.dma("sync", lambda e: e.dma_start(out=KR[:, c0:c0 + NT], in_=krb[0:32, 0:n]), key=krb, reads=[krb], writes=[])
        for hp in range(4):
            k2 = hp % 2
            pz_, cs_, kb_ = PZ[k2], CS[k2], knb[k2]
            mm(pz_, pz_[:, 0:n], wuk_sb, wuk_sb[:, 0, hp * 128:(hp + 1) * 128], ckvb, ckvb[:, 0:n], True, True)
            rms_rstd(cs_, [pz_[:, 0:n]], [pz_], cstb[:, C_BLK64:C_BLK64 + 128], 128, n, 64.0)
            norm_apply(cs_, kb_, kb_[:, 0:n], pz_, pz_[:, 0:n], G_KN2, 128, n)
            for hl in range(2):
                P.dma("sync", lambda e, hp=hp, hl=hl, kb_=kb_: e.dma_start(out=KN[2 * hp + hl, :, c0:c0 + NT],
                                                                           in_=kb_[hl * 64:(hl + 1) * 64, 0:n]),
                      key=kb_, reads=[kb_], writes=[])
        for tb in range(n // 128):
            v_ = vb[tb % 2]
            pz_ = PZ[tb % 2]
            mm(pz_, pz_[:, 0:512], ckvb, ckvb[:, tb * 128:(tb + 1) * 128], wuv_sb, wuv_sb[:, 0, :], True, True)
            ew("scalar", lambda e, v_=v_, pz_=pz_: e.copy(out=v_[:, :], in_=pz_[:, 0:512]), [pz_], [v_])
            P.dma("sync", lambda e, v_=v_, tb=tb: e.dma_start(out=VS[c0 + tb * 128:c0 + (tb + 1) * 128, :], in_=v_[:, :]),
                  key=v_, reads=[v_], writes=[])
        for j in range(2):
            proj(T, pzA, 128, n, win_sb, 416 + 128 * j)
            ew("scalar", lambda e: e.copy(out=t0[:, 0:n], in_=pzA[:, 0:n]), [pzA], [t0])
            proj(T, pzB, 128, n, win_sb, 928 + 128 * j)
            ew("vector", lambda e, j=j: e.tensor_tensor(out=uext[:, j, 2:n + 2], in0=t0[:, 0:n], in1=pzB[:, 0:n], op=ALU.mult),
               [t0, pzB], [uext])
        ew("gpsimd", lambda e: e.tensor_copy(out=UB[:, :, s, :], in_=uext[:, :, n:n + 2]), [uext], [UB])
        for j in range(2):
            P.dma("sync", lambda e, j=j: e.dma_start(out=o_conv[j * 128:(j + 1) * 128, 2 * s:2 * s + 2],
                                                     in_=uext[:, j, n:n + 2]), key=uext, reads=[uext], writes=[])
        if not own:
            return
        j_own = s
        if j_own == 0:
            ew("vector", lambda e: e.tensor_scalar(out=uext[:, :, 0:2], in0=UB[:, :, 8, :], scalar1=gn[:, G_FLB:G_FLB + 1],
                                                   scalar2=None, op0=ALU.mult), [UB, gn], [uext])
        else:
            ew("vector", lambda e: e.tensor_scalar(out=uext[:, :, 0:2], in0=UB[:, :, 8 + j_own, :],
                                                   scalar1=gn[:, G_FLB:G_FLB + 1], scalar2=None, op0=ALU.mult),
               [UB, gn], [uext])
            ew("vector", lambda e: e.scalar_tensor_tensor(out=uext[:, :, 0:2], in0=UB[:, :, 8 + j_own - 1, :],
                                                          scalar=gn[:, G_FLA:G_FLA + 1], in1=uext[:, :, 0:2],
                                                          op0=ALU.mult, op1=ALU.add), [UB, gn, uext], [uext])
        for j in range(2):
            pz_ = PZ[j]
            tt = (t0, t1)[j]
            proj(T, pz_, 128, n, win_sb, 672 + 128 * j)
            ew("gpsimd", lambda e, j=j, tt=tt: e.tensor_scalar(out=tt[:, 0:n], in0=uext[:, j, 0:n], scalar1=gn[:, G_CW0 + j:G_CW0 + j + 1],
                                                               scalar2=None, op0=ALU.mult), [uext, gn], [tt])
            ew("vector", lambda e, j=j, tt=tt: e.scalar_tensor_tensor(out=tt[:, 0:n], in0=uext[:, j, 1:n + 1],
                                                                      scalar=gn[:, G_CW1 + j:G_CW1 + j + 1], in1=tt[:, 0:n],
                                                                      op0=ALU.mult, op1=ALU.add), [uext, gn, tt], [tt])
            ew("vector", lambda e, j=j, tt=tt: e.scalar_tensor_tensor(out=tt[:, 0:n], in0=uext[:, j, 2:n + 2],
                                                                      scalar=gn[:, G_CW2 + j:G_CW2 + j + 1], in1=tt[:, 0:n],
                                                                      op0=ALU.mult, op1=ALU.add), [uext, gn, tt], [tt])
            ew("vector", lambda e, j=j, tt=tt, pz_=pz_: e.tensor_tensor(out=gbt[:, j, 0:n], in0=tt[:, 0:n], in1=pz_[:, 0:n], op=ALU.mult),
               [tt, pz_], [gbt])
        rms_rstd(CS[0], [gbt[:, 0, 0:n], gbt[:, 1, 0:n]], [gbt, gbt], cstb[:, C_ONES:C_ONES + 128], 128, n, 256.0)
        for j in range(2):
            norm_apply(CS[0], YC, YC[:, j, c0:c0 + n], gbt, gbt[:, j, 0:n], G_OCONV + j, 128, n)
        proj(T, pzA, 128, n, win_sb, 0)
        proj(T, pzB, 128, n, win_sb, 128)
        rms_rstd(CS[1], [pzA[:, 0:n], pzB[:, 0:n]], [pzA, pzB], cstb[:, C_ONES:C_ONES + 128], 128, n, 256.0)
        norm_apply(CS[1], cqn, cqn[:, 0, 0:n], pzA, pzA[:, 0:n], G_QL, 128, n)
        norm_apply(CS[1], cqn, cqn[:, 1, 0:n], pzB, pzB[:, 0:n], G_QL + 1, 128, n)
        for h in range(8):
            k2 = h % 2
            pz_, cs_, qb_ = PZ[k2], CS[k2], qb16[k2]
            qn_, tq_ = (qn, t2)[k2], (t1, t0)[k2]
            for j in range(2):
                mm(pz_, pz_[0:96, 0:n], wuq_sb, wuq_sb[:, j, h * 96:(h + 1) * 96], cqn, cqn[:, j, 0:n], j == 0, j == 1)
            rms_rstd(cs_, [pz_[0:96, 0:n]], [pz_], cstb[0:96, C_BLK96:C_BLK96 + 96], 96, n, 1.0)
            norm_apply(cs_, qn_, qn_[0:96, 0:n], pz_, pz_[0:96, 0:n], G_Q96, 96, n)
            mm(pz_, pz_[0:96, 0:n], cst, cst[0:96, C_ROT96:C_ROT96 + 96], qn_, qn_[0:96, 0:n], True, True)
            ew("vector", lambda e, pz_=pz_, tq_=tq_: e.tensor_tensor(out=tq_[0:96, 0:n], in0=pz_[0:96, 0:n], in1=rS96[0:96, 0:n], op=ALU.mult),
               [pz_, rS96], [tq_])
            ew("gpsimd", lambda e, qn_=qn_: e.tensor_tensor(out=qn_[0:96, 0:n], in0=qn_[0:96, 0:n], in1=rC96[0:96, 0:n], op=ALU.mult),
               [qn_, rC96], [qn_])
            ew("vector", lambda e, qb_=qb_, tq_=tq_, qn_=qn_: e.tensor_tensor(out=qb_[0:96, 0:n], in0=tq_[0:96, 0:n], in1=qn_[0:96, 0:n], op=ALU.add),
               [tq_, qn_], [qb_])
            P.dma("sync", lambda e, h=h, qb_=qb_: e.dma_start(out=QS[h, :, c0:c0 + NT], in_=qb_[0:96, 0:n]),
                  key=qb_, reads=[qb_], writes=[])
        for j in range(2):
            pz_, cs_ = PZ[j], CS[j]
            proj(T, pz_, 128, n, win_sb, 1184 + 128 * j)
            rms_rstd(cs_, [pz_[:, 0:n]], [pz_], cstb[:, C_BLK64:C_BLK64 + 128], 128, n, 64.0)
            norm_apply(cs_, mqn, mqn[:, j, 0:n], pz_, pz_[:, 0:n], G_MQ, 128, n)
        for h in range(4):
            j, r0 = h // 2, (h % 2) * 64
            po = pb[4 + h % 2]
            for kb in range(2):
                ps_ = pb[2 * (h % 2) + kb]
                pt_ = ptm[kb]
                mm(ps_, ps_[:, 0:n], memKT, memKT[r0:r0 + 64, j, kb * 128:(kb + 1) * 128], mqn, mqn[r0:r0 + 64, j, 0:n], True, True)
                ew("scalar", lambda e, ps_=ps_, pt_=pt_: e.activation(out=pt_[:, 0:n], in_=ps_[:, 0:n], func=AF.Exp, scale=MEM_SCALE),
                   [ps_], [pt_])
                mm(po, po[:, 0:n], memVA, memVA[:, kb, h, :], pt_, pt_[:, 0:n], kb == 0, kb == 1)
            ew("vector", lambda e, po=po: e.reciprocal(out=rec[64:128, 0:n], in_=po[64:128, 0:n]), [po], [rec])
            ew("vector", lambda e, po=po, r0=r0, j=j: e.tensor_tensor(out=gbt[r0:r0 + 64, j, 0:n], in0=po[0:64, 0:n],
                                                                      in1=rec[64:128, 0:n], op=ALU.mult), [po, rec], [gbt])
        rms_rstd(CS[0], [gbt[:, 0, 0:n], gbt[:, 1, 0:n]], [gbt, gbt], cstb[:, C_ONES:C_ONES + 128], 128, n, 256.0)
        for j in range(2):
            norm_apply(CS[0], OM, OM[:, j, c0:c0 + n], gbt, gbt[:, j, 0:n], G_OMEM + j, 128, n)

    ORDER = list(range(NOWN, NSLOT)) + list(range(NOWN))
    per = (len(sc2_jobs) + 7) // 8
    for si, s in enumerate(ORDER):
        phase1(s)
        for (dst, src, r0, m0_, step) in sc2_jobs[si * per:(si + 1) * per]:
            P.dma("gpsimd", lambda e, dst=dst, src=src, r0=r0, m0_=m0_, step=step: e.dma_start(
                out=dst[r0:r0 + 128, m0_:m0_ + step], in_=src[r0:r0 + 128, m0_:m0_ + step]), key=dst, reads=[], writes=[])

    wukT_sb = sb("wukT_sb", (64, 1, 8 * 128), BF16)
    kfull = sb("kfull", (96, NS), F32)
    qfs = sb("qfs", (96, NS), F32)
    stt = sb("stt", (128, 2, 2, NS), F32)
    qg = sb("qg", (64, NS), BF16)

    def sampleA():
        n = NS
        x = xs
        P.dma("gpsimd", lambda e: e.dma_start(out=wukT_sb[:, 0, :], in_=w_ukT[:, :]), key=wukT_sb, reads=[], writes=[wukT_sb])
        load("sync", x, x[:, :, :], xsT, xsT.t.rearrange("(c p) t -> p c t", p=128))
        load("sync", rC, rC[:, 0:n], ropeCs, ropeCs[:, :])
        load("sync", rS, rS[:, 0:n], ropeSs, ropeSs[:, :])
        load("sync", rC96, rC96[64:96, 0:n], ropeCs, ropeCs[:, :])
        load("sync", rS96, rS96[64:96, 0:n], ropeSs, ropeSs[:, :])
        load("sync", stt, stt[:, :, :, :], stateT, stateT.t.rearrange("(c p) k t -> p c k t", p=128))
        for j in range(2):
            P.dma("sync", lambda e, j=j: e.dma_start(out=o_state_s[j * 128:(j + 1) * 128, :], in_=stt[:, j, 1, :]),
                  key=stt, reads=[stt], writes=[])
        ffn_half(T, x, n, sc1, G_FFN1)
        rmsnorm_d(T, x, n, G_MIX)
        proj(T, pz, 128, n, win_sb, 256)
        ew("scalar", lambda e: e.copy(out=t0[:, 0:n], in_=pz[:, 0:n]), [pz], [t0])
        rms_rstd(T, [t0[:, 0:n]], [t0], cstb[:, C_ONES:C_ONES + 128], 128, n, 128.0)
        norm_apply(T, ckvs, ckvs[:, 0:n], t0, t0[:, 0:n], G_KV, 128, n)
        P.dma("sync", lambda e: e.dma_start(out=o_ckv_s[:, :], in_=ckvs[:, 0:n]), key=ckvs, reads=[ckvs], writes=[])
        ew("gpsimd", lambda e: e.tensor_copy(out=ckvb[:, 0:n], in_=ckvs[:, 0:n]), [ckvs], [ckvb])
        proj(T, pz, 32, n, win_sb, 384)
        ew("scalar", lambda e: e.copy(out=t1[0:32, 0:n], in_=pz[0:32, 0:n]), [pz], [t1])
        rms_rstd(T, [t1[0:32, 0:n]], [t1], cstb[0:32, C_ONES32:C_ONES32 + 32], 32, n, 32.0)
        norm_apply(T, t2, t2[0:32, 0:n], t1, t1[0:32, 0:n], G_KR, 32, n)
        mm(pz, pz[0:32, 0:n], cst, cst[0:32, C_ROT32:C_ROT32 + 32], t2, t2[0:32, 0:n], True, True)
        ew("vector", lambda e: e.tensor_tensor(out=t1[0:32, 0:n], in0=pz[0:32, 0:n], in1=rS[0:32, 0:n], op=ALU.mult),
           [pz, rS], [t1])
        ew("gpsimd", lambda e: e.tensor_tensor(out=t2[0:32, 0:n], in0=t2[0:32, 0:n], in1=rC[0:32, 0:n], op=ALU.mult),
           [t2, rC], [t2])
        ew("vector", lambda e: e.tensor_tensor(out=kr_o[0:32, 0:n], in0=t1[0:32, 0:n], in1=t2[0:32, 0:n], op=ALU.add),
           [t1, t2], [kr_o])
        P.dma("sync", lambda e: e.dma_start(out=o_kr_s[:, :], in_=kr_o[0:32, 0:n]), key=kr_o, reads=[kr_o], writes=[])
        ew("vector", lambda e: e.tensor_copy(out=kfull[64:96, 0:n], in_=kr_o[0:32, 0:n]), [kr_o], [kfull])
        for j in range(2):
            proj(T, pz, 128, n, win_sb, 416 + 128 * j)
            ew("scalar", lambda e: e.copy(out=t0[:, 0:n], in_=pz[:, 0:n]), [pz], [t0])
            proj(T, pz, 128, n, win_sb, 928 + 128 * j)
            ew("vector", lambda e, j=j: e.tensor_tensor(out=uext[:, j, 0:n], in0=t0[:, 0:n], in1=pz[:, 0:n], op=ALU.mult),
               [t0, pz], [uext])
            P.dma("sync", lambda e, j=j: e.dma_start(out=o_conv_s[j * 128:(j + 1) * 128, :], in_=uext[:, j, 0:n]),
                  key=uext, reads=[uext], writes=[])
            proj(T, pz, 128, n, win_sb, 672 + 128 * j)
            ew("vector", lambda e, j=j: e.tensor_scalar(out=t0[:, 0:n], in0=stt[:, j, 0, :], scalar1=gn[:, G_CW0 + j:G_CW0 + j + 1],
                                                        scalar2=None, op0=ALU.mult), [stt, gn], [t0])
            ew("vector", lambda e, j=j: e.scalar_tensor_tensor(out=t0[:, 0:n], in0=stt[:, j, 1, :],
                                                               scalar=gn[:, G_CW1 + j:G_CW1 + j + 1], in1=t0[:, 0:n],
                                                               op0=ALU.mult, op1=ALU.add), [stt, gn, t0], [t0])
            ew("vector", lambda e, j=j: e.scalar_tensor_tensor(out=t0[:, 0:n], in0=uext[:, j, 0:n],
                                                               scalar=gn[:, G_CW2 + j:G_CW2 + j + 1], in1=t0[:, 0:n],
                                                               op0=ALU.mult, op1=ALU.add), [uext, gn, t0], [t0])
            ew("vector", lambda e, j=j: e.tensor_tensor(out=gbt[:, j, 0:n], in0=t0[:, 0:n], in1=pz[:, 0:n], op=ALU.mult),
               [t0, pz], [gbt])
        rms_rstd(T, [gbt[:, 0, 0:n], gbt[:, 1, 0:n]], [gbt, gbt], cstb[:, C_ONES:C_ONES + 128], 128, n, 256.0)
        for j in range(2):
            norm_apply(T, YCs, YCs[:, j, 0:n], gbt, gbt[:, j, 0:n], G_OCONV + j, 128, n)
        for j in range(2):
            proj(T, pz, 128, n, win_sb, 1184 + 128 * j)
            ew("scalar", lambda e: e.copy(out=t0[:, 0:n], in_=pz[:, 0:n]), [pz], [t0])
            rms_rstd(T, [t0[:, 0:n]], [t0], cstb[:, C_BLK64:C_BLK64 + 128], 128, n, 64.0)
            norm_apply(T, mqs, mqs[:, j, 0:n], t0, t0[:, 0:n], G_MQ, 128, n)
        for j in range(2):
            proj(T, pz, 128, n, win_sb, 128 * j)
            ew("scalar", lambda e, j=j: e.copy(out=cq[:, j, 0:n], in_=pz[:, 0:n]), [pz], [cq])
        rms_rstd(T, [cq[:, 0, 0:n], cq[:, 1, 0:n]], [cq, cq], cstb[:, C_ONES:C_ONES + 128], 128, n, 256.0)
        for j in range(2):
            norm_apply(T, cqn, cqn[:, j, 0:n], cq, cq[:, j, 0:n], G_QL + j, 128, n)
        psn = pb[0]
        for h in range(8):
            for j in range(2):
                mm(pz, pz[0:96, 0:n], wuq_sb, wuq_sb[:, j, h * 96:(h + 1) * 96], cqn, cqn[:, j, 0:n], j == 0, j == 1)
            ew("scalar", lambda e: e.copy(out=t1[0:96, 0:n], in_=pz[0:96, 0:n]), [pz], [t1])
            rms_rstd(T, [t1[0:96, 0:n]], [t1], cstb[0:96, C_BLK96:C_BLK96 + 96], 96, n, 1.0)
            norm_apply(T, qn, qn[0:96, 0:n], t1, t1[0:96, 0:n], G_Q96, 96, n)
            mm(pz, pz[0:96, 0:n], cst, cst[0:96, C_ROT96:C_ROT96 + 96], qn, qn[0:96, 0:n], True, True)
            ew("vector", lambda e: e.tensor_tensor(out=t1[0:96, 0:n], in0=pz[0:96, 0:n], in1=rS96[0:96, 0:n], op=ALU.mult),
               [pz, rS96], [t1])
            ew("gpsimd", lambda e: e.tensor_tensor(out=qn[0:96, 0:n], in0=qn[0:96, 0:n], in1=rC96[0:96, 0:n], op=ALU.mult),
               [qn, rC96], [qn])
            ew("vector", lambda e: e.tensor_tensor(out=qfs[0:96, 0:n], in0=t1[0:96, 0:n], in1=qn[0:96, 0:n], op=ALU.add),
               [t1, qn], [qfs])
            ew("vector", lambda e, h=h: e.tensor_copy(out=qrs[0:32, :, h], in_=qfs[64:96, 0:n]), [qfs], [qrs])
            ew("vector", lambda e: e.tensor_scalar(out=qg[0:64, 0:n], in0=qfs[0:64, 0:n], scalar1=gn[0:64, G_KN:G_KN + 1],
                                                   scalar2=None, op0=ALU.mult), [qfs, gn], [qg])
            mm(pz, pz[:, 0:n], wukT_sb, wukT_sb[0:64, 0, h * 128:(h + 1) * 128], qg, qg[0:64, 0:n], True, True)
            ew("scalar", lambda e, h=h: e.copy(out=qabs[:, :, h], in_=pz[:, 0:n]), [pz], [qabs])
            mm(pz, pz[0:64, 0:n], wuk_sb, wuk_sb[:, 0, h * 64:(h + 1) * 64], ckvb, ckvb[:, 0:n], True, True)
            ew("scalar", lambda e: e.copy(out=t1[0:64, 0:n], in_=pz[0:64, 0:n]), [pz], [t1])
            rms_rstd(T, [t1[0:64, 0:n]], [t1], cstb[0:64, C_ONES64:C_ONES64 + 64], 64, n, 64.0)
            norm_apply(T, kfull, kfull[0:64, 0:n], t1, t1[0:64, 0:n], G_KN, 64, n)
            ew("vector", lambda e: e.tensor_tensor(out=t2[0:96, 0:n], in0=qfs[0:96, 0:n], in1=kfull[0:96, 0:n], op=ALU.mult),
               [qfs, kfull], [t2])
            mm(psn, psn[:, h * NS:(h + 1) * NS], cst, cst[0:96, C_ONES:C_ONES + 128], t2, t2[0:96, 0:n], True, True)
        ew("scalar", lambda e: e.activation(out=pnew[:, :, :], in_=psn[:, 0:8 * NS].rearrange("p (h b) -> p h b", h=8),
                                            func=AF.Exp, scale=MLA_SCALE), [psn], [pnew])

    sampleA()
    P.barrier()
    print("phase1 SBUF peak bytes/partition:", A.peak - A.BASE)
    A.reset(persist_mark)


    def sampleB():
        mB = A.mark()
        wuk2 = sb("wuk2", (128, 1, 512), BF16)
        wuv2 = sb("wuv2", (128, 1, 512), BF16)
        load_w(wuk2, w_uk, 1, 512)
        load_w(wuv2, w_uv, 1, 512)
        pt_sb = sb("pt_sb", (128, NS), I32)
        idx8a = sb("idx8a", (128, NS, 8), I32)
        idx2a = sb("idx2a", (128, NS, 2), I32)
        G = [sb("G%d" % i, (128, 128, 128), BF16) for i in range(2)]
        GR = [sb("GR%d" % i, (128, 128, 32), BF16) for i in range(2)]
        ckvT = [sb("ckvT%d" % i, (128, 512), BF16) for i in range(2)]
        krT = [sb("krT%d" % i, (32, 512), BF16) for i in range(2)]
        sq = [sb("sq%d" % i, (128, 512), BF16) for i in range(2)]
        SS = sb("SS", (128, 128, 8), F32)
        inv = sb("inv", (128, 128, 8), F32)
        s1 = sb("s1", (128, 128, 8), F32)
        pT = sb("pT", (128, 128, 8), BF16)
        LP = sb("LP", (128, NS, 8), F32)
        load("sync", pt_sb, pt_sb[:, :], ptT, ptT[:, :])
        PA = [pb[0], pb[1]]
        PR = [pb[2], pb[3]]
        PKN = [pb[4], pb[5]]
        ptr_ = pb[6]
        pacc = pb[7]
        ptr_bf = ptr_.t[:, :].bitcast(BF16)
        identb = cstb[:, C_ID:C_ID + 128]
        for s_ in range(8):
            ew("vector", lambda e, s_=s_: e.tensor_scalar(out=idx8a[:, :, s_], in0=pt_sb[:, :], scalar1=8, scalar2=s_,
                                                          op0=ALU.mult, op1=ALU.add), [pt_sb], [idx8a])
        for s_ in range(2):
            ew("vector", lambda e, s_=s_: e.tensor_scalar(out=idx2a[:, :, s_], in0=pt_sb[:, :], scalar1=2, scalar2=s_,
                                                          op0=ALU.mult, op1=ALU.add), [pt_sb], [idx2a])
        for b in range(NS):
            k = b % 2
            g_, gr_ = G[k], GR[k]
            for s_ in range(8):
                P.dma("gpsimd", lambda e, b=b, s_=s_, g_=g_: e.indirect_dma_start(
                    out=g_[:, s_ * 16:(s_ + 1) * 16, :].rearrange("p t l -> p (t l)"), out_offset=None, in_=cckv[:, :],
                    in_offset=bass.IndirectOffsetOnAxis(ap=idx8a[:, b, s_:s_ + 1], axis=0)), key=g_, reads=[idx8a], writes=[g_])
            for s_ in range(2):
                P.dma("gpsimd", lambda e, b=b, s_=s_, gr_=gr_: e.indirect_dma_start(
                    out=gr_[:, s_ * 64:(s_ + 1) * 64, :].rearrange("p t l -> p (t l)"), out_offset=None, in_=ckr[:, :],
                    in_offset=bass.IndirectOffsetOnAxis(ap=idx2a[:, b, s_:s_ + 1], axis=0)), key=gr_, reads=[idx2a], writes=[gr_])
            def do_T(t4):
                ct, kt_ = ckvT[t4 % 2], krT[t4 % 2]
                for i in range(4):
                    t = t4 * 4 + i
                    P.op("tensor", lambda e, t=t, i=i, g_=g_: e.transpose(ptr_bf[:, i * 128:(i + 1) * 128], g_[:, t, :], identb),
                         reads=[g_, cstb], writes=[ptr_])
                for i in range(4):
                    t = t4 * 4 + i
                    P.op("tensor", lambda e, t=t, i=i, gr_=gr_: e.transpose(ptr_bf[0:32, 512 + i * 128:512 + (i + 1) * 128],
                                                                            gr_[:, t, :], identb),
                         reads=[gr_, cstb], writes=[ptr_])
                ew("vector", lambda e, ct=ct: e.tensor_copy(out=ct[:, :], in_=ptr_bf[:, 0:512]), [ptr_], [ct])
                ew("vector", lambda e, kt_=kt_: e.tensor_copy(out=kt_[:, :], in_=ptr_bf[0:32, 512:1024]), [ptr_], [kt_])

            def do_M(t4):
                ct, kt_ = ckvT[t4 % 2], krT[t4 % 2]
                for i in range(4):
                    t = t4 * 4 + i
                    pk = PKN[t % 2]
                    sq_ = sq[t % 2]
                    mm(pk, pk[:, :], ct, ct[:, i * 128:(i + 1) * 128], wuk2, wuk2[:, 0, :], True, True)
                    ew("scalar", lambda e, pk=pk, sq_=sq_: e.activation(out=sq_[:, :], in_=pk[:, :], func=AF.Square), [pk], [sq_])
                    ew("vector", lambda e, sq_=sq_, t=t: e.tensor_reduce(out=SS[:, t, :], in_=sq_[:, :].rearrange("p (h d) -> p h d", h=8),
                                                                         axis=AX.X, op=ALU.add), [sq_], [SS])
                    pa, pr = PA[t // 64], PR[t // 64]
                    c8 = (t % 64) * 8
                    mm(pa, pa[:, c8:c8 + 8], ct, ct[:, i * 128:(i + 1) * 128], qabs, qabs[:, b, :], True, True)
                    mm(pr, pr[:, c8:c8 + 8], kt_, kt_[0:32, i * 128:(i + 1) * 128], qrs, qrs[0:32, b, :], True, True)

            do_T(0)
            for t4 in range(32):
                if t4 + 1 < 32:
                    do_T(t4 + 1)
                do_M(t4)
            SSf = SS[:, :, :].rearrange("p t h -> p (t h)")
            invf = inv[:, :, :].rearrange("p t h -> p (t h)")
            s1f = s1[:, :, :].rearrange("p t h -> p (t h)")
            pTf = pT[:, :, :].rearrange("p t h -> p (t h)")
            ew("scalar", lambda e, SSf=SSf, invf=invf: e.activation(out=invf, in_=SSf, func=AF.Sqrt, bias=EPS, scale=1.0 / 64), [SS], [inv])
            ew("vector", lambda e, invf=invf: e.reciprocal(out=invf, in_=invf), [inv], [inv])
            for hf in range(2):
                ew("vector", lambda e, hf=hf, invf=invf, s1f=s1f: e.tensor_tensor(out=s1f[:, hf * 512:(hf + 1) * 512], in0=PA[hf][:, :],
                                                                                  in1=invf[:, hf * 512:(hf + 1) * 512], op=ALU.mult),
                   [PA[hf], inv], [s1])
                ew("vector", lambda e, hf=hf, s1f=s1f: e.tensor_tensor(out=s1f[:, hf * 512:(hf + 1) * 512], in0=PR[hf][:, :],
                                                                       in1=s1f[:, hf * 512:(hf + 1) * 512], op=ALU.add),
                   [PR[hf], s1], [s1])
            ew("scalar", lambda e, s1f=s1f, pTf=pTf: e.activation(out=pTf, in_=s1f, func=AF.Exp, scale=MLA_SCALE), [s1], [pT])
            ew("vector", lambda e, b=b: e.tensor_reduce(out=LP[:, b, :], in_=pT[:, :, :].rearrange("p t h -> p h t"),
                                                        axis=AX.X, op=ALU.add), [pT], [LP])
            for t in range(128):
                mm(pacc, pacc[:, b * 8:(b + 1) * 8], g_, g_[:, t, :], pT, pT[:, t, :], t == 0, t == 127)
        lsum = pb[0]
        mm(lsum, lsum[:, 0:NS * 8], cst, cst[:, C_ONES:C_ONES + 128], LP, LP[:, :, :].rearrange("p b h -> p (b h)"), True, True)
        num = sb("num", (128, NS, 8), F32)
        den = sb("den", (128, NS, 8), F32)
        olat = sb("olat", (128, NS, 8), BF16)
        pnv = pnew[:, :, :].rearrange("p h b -> p b h")
        ew("vector", lambda e: e.tensor_tensor(out=num[:, :, :], in0=pnv, in1=ckvs[:, 0:NS].unsqueeze(2).to_broadcast([128, NS, 8]),
                                               op=ALU.mult), [pnew, ckvs], [num])
        ew("vector", lambda e: e.tensor_tensor(out=num[:, :, :], in0=pacc[:, 0:NS * 8].rearrange("p (b h) -> p b h", h=8),
                                               in1=num[:, :, :], op=ALU.add), [pacc, num], [num])
        ew("vector", lambda e: e.tensor_tensor(out=den[:, :, :], in0=lsum[:, 0:NS * 8].rearrange("p (b h) -> p b h", h=8),
                                               in1=pnv, op=ALU.add), [lsum, pnew], [den])
        ew("vector", lambda e: e.reciprocal(out=den[:, :, :], in_=den[:, :, :]), [den], [den])
        ew("vector", lambda e: e.tensor_tensor(out=olat[:, :, :], in0=num[:, :, :], in1=den[:, :, :], op=ALU.mult), [num, den], [olat])
        for h in range(8):
            po = pb[1 + h % 2]
            mm(po, po[0:64, 0:NS], wuv2, wuv2[:, 0, h * 64:(h + 1) * 64], olat, olat[:, :, h], True, True)
            r0, jc = (h % 2) * 64, h // 2
            ew("scalar", lambda e, po=po, r0=r0, jc=jc: e.copy(out=OAs[r0:r0 + 64, jc, :], in_=po[0:64, 0:NS]), [po], [OAs])
        Kb = [sb("Kb%d" % i, (128, 2, 256), F32) for i in range(2)]
        Vb = [sb("Vb%d" % i, (128, 2, 256), F32) for i in range(2)]
        Vbb = [sb("Vbb%d" % i, (128, 2, 256), BF16) for i in range(2)]
        qbc = sb("qbc", (128, 2, 128), F32)
        prod = sb("prod", (128, 2, 256), F32)
        scm = sb("scm", (128, 2, 4), F32)
        pm = sb("pm", (128, 2, 4), BF16)
        rden = sb("rden", (128, NS, 4), F32)
        omf = sb("omf", (128, 2, NS), F32)
        pq, pom, pdn = pb[3], pb[4], pb[5]
        ident = cst[:, C_ID:C_ID + 128]
        for b in range(NS):
            k = b % 2
            P.dma("sync", lambda e, k=k, b=b: e.dma_start(out=Kb[k][:, :, :], in_=cmk[b].rearrange("(blk p) f -> p blk f", p=128)),
                  key=Kb[k], reads=[], writes=[Kb[k]])
            P.dma("sync", lambda e, k=k, b=b: e.dma_start(out=Vb[k][:, :, :], in_=cmv[b].rearrange("(blk p) f -> p blk f", p=128)),
                  key=Vb[k], reads=[], writes=[Vb[k]])
            ew("gpsimd", lambda e, k=k: e.tensor_copy(out=Vbb[k][:, :, :], in_=Vb[k][:, :, :]), [Vb[k]], [Vbb[k]])
            ew("vector", lambda e, b=b: e.tensor_copy(out=qbc[:, :, :], in_=mqs[:, :, b:b + 1].to_broadcast([128, 2, 128])), [mqs], [qbc])
            for c in range(2):
                mm(pq, pq[:, c * 128:(c + 1) * 128], qbc, qbc[:, c, :], cst, ident, True, True)
            ew("vector", lambda e, k=k: e.tensor_tensor(out=prod[:, :, :], in0=Kb[k][:, :, :],
                                                        in1=pq[:, 0:256].unsqueeze(1).to_broadcast([128, 2, 256]), op=ALU.mult),
               [Kb[k], pq], [prod])
            ew("vector", lambda e: e.tensor_reduce(out=scm[:, :, :], in_=prod[:, :, :].rearrange("p k (h d) -> p k h d", h=4),
                                                   axis=AX.X, op=ALU.add), [prod], [scm])
            ew("scalar", lambda e: e.activation(out=pm[:, :, :], in_=scm[:, :, :], func=AF.Exp, scale=MEM_SCALE), [scm], [pm])
            for c in range(2):
                for blk in range(2):
                    mm(pom, pom[:, b * 4 + 2 * c:b * 4 + 2 * c + 2], Vbb[k], Vbb[k][:, blk, c * 128:(c + 1) * 128],
                       pm, pm[:, blk, 2 * c:2 * c + 2], blk == 0, blk == 1)
            for blk in range(2):
                mm(pdn, pdn[:, b * 4:(b + 1) * 4], cstb, cstb[:, C_ONES:C_ONES + 128], pm, pm[:, blk, :], blk == 0, blk == 1)
        ew("vector", lambda e: e.reciprocal(out=rden[:, :, :], in_=pdn[:, 0:NS * 4].rearrange("p (b h) -> p b h", h=4)), [pdn], [rden])
        pomv = pom[:, 0:NS * 4].rearrange("p (b h) -> p b h", h=4)
        for c in range(2):
            for hl in range(2):
                r0 = hl * 64
                ew("vector", lambda e, c=c, hl=hl, r0=r0: e.tensor_tensor(out=omf[r0:r0 + 64, c, :], in0=pomv[r0:r0 + 64, :, 2 * c + hl],
                                                                          in1=rden[r0:r0 + 64, :, 2 * c + hl], op=ALU.mult),
                   [pom, rden], [omf])
        TB = {"sq": sb("abB", (128, 8, NT), BF16), "std": sb("stdB", (128, NT), F32), "rstd": sb("rstdB", (128, NT), F32), "pn": pb[6]}
        rms_rstd(TB, [omf[:, 0, :], omf[:, 1, :]], [omf, omf], cstb[:, C_ONES:C_ONES + 128], 128, NS, 256.0)
        for j in range(2):
            norm_apply(TB, OMs, OMs[:, j, :], omf, omf[:, j, :], G_OMEM + j, 128, NS)
        rms_rstd(TB, [OAs[:, c, :] for c in range(4)], [OAs] * 4, cstb[:, C_ONES:C_ONES + 128], 128, NS, 512.0)
        for c in range(4):
            norm_apply(TB, onorm_s, onorm_s[:, c, :], OAs, OAs[:, c, :], G_OMLA + c, 128, NS)
        P.barrier()
        print("sampleB SBUF peak bytes/partition:", A.peak - A.BASE)
        A.reset(mB)

    if do_sample:
        sampleB()

    OA = sb("OA", (128, 4, NOWN * NT), F32)
    if do_att:
        KT = [sb("KT%d" % i, (96, NSLOT * NT), BF16) for i in range(2)]
        VA = [sb("VA%d" % i, (128, NSLOT * NT // 128, 128), BF16) for i in range(2)]
        QT = [sb("QT%d" % i, (96, NOWN * NT), BF16) for i in range(2)]
        pts = [sb("pts%d" % i, (128, NT), BF16) for i in range(4)]
        rec2 = sb("rec2", (128, NT), F32)
        for i in range(2):
            ew("gpsimd", lambda e, i=i: e.memset(VA[i][:, :, 64:128], 1.0), [], [VA[i]])
        vsv = VS.t.rearrange("(blk p) (h d) -> p blk h d", p=128, h=8)
        nblk = 0
        for h in range(8):
            kt, va, qt = KT[h % 2], VA[h % 2], QT[h % 2]
            P.dma("sync", lambda e, kt=kt, h=h: e.dma_start(out=kt[0:64, :], in_=KN[h, :, :]), key=kt, reads=[], writes=[kt])
            P.dma("sync", lambda e, kt=kt: e.dma_start(out=kt[64:96, :], in_=KR[:, :]), key=kt, reads=[], writes=[kt])
            for q4 in range(8):
                b0, b1 = q4 * 8, (q4 + 1) * 8
                P.dma("sync", lambda e, va=va, h=h, b0=b0, b1=b1: e.dma_start(out=va[:, b0:b1, 0:64], in_=vsv[:, b0:b1, h, :]),
                      key=va, reads=[], writes=[va])
            P.dma("sync", lambda e, qt=qt, h=h: e.dma_start(out=qt[0:96, :], in_=QS[h, :, :]), key=qt, reads=[], writes=[qt])
            for j in range(NOWN):
                tiles = [(8 + j, "flag")] + [(s, "full") for s in range(j)] + [(8 + s, "full") for s in range(j)] + [(j, "diag")]
                po = pb[4 + (h * NOWN + j) % 2]
                blocks = []
                for (s, kind) in tiles:
                    for kb in range(4):
                        blocks.append((s * NT + kb * 128, kb * 128 if kind == "diag" else 0, kind))
                nb = len(blocks)
                LAG = 2
                issued = []
                for i in range(nb + LAG):
                    if i < nb:
                        k0, q0, kind = blocks[i]
                        ncol = NT - q0
                        ps_ = pb[nblk % 4]
                        pt_ = pts[nblk % 4]
                        nblk += 1
                        issued.append((pt_, q0, ncol, k0))
                        mm(ps_, ps_[:, 0:ncol], kt, kt[0:96, k0:k0 + 128], qt, qt[0:96, j * NT + q0:(j + 1) * NT], True, True)
                        if kind == "flag":
                            ew("scalar", lambda e, ps_=ps_, pt_=pt_, ncol=ncol: e.activation(
                                out=pt_[:, 0:ncol], in_=ps_[:, 0:ncol], func=AF.Exp, scale=MLA_SCALE,
                                bias=gn[:, G_EXPB:G_EXPB + 1]), [ps_, gn], [pt_])
                        else:
                            ew("scalar", lambda e, ps_=ps_, pt_=pt_, ncol=ncol: e.activation(
                                out=pt_[:, 0:ncol], in_=ps_[:, 0:ncol], func=AF.Exp, scale=MLA_SCALE), [ps_], [pt_])
                        if kind == "diag":
                            ew("gpsimd", lambda e, pt_=pt_: e.tensor_tensor(out=pt_[:, 0:128], in0=pt_[:, 0:128],
                                                                            in1=cstb[:, C_TRI:C_TRI + 128], op=ALU.mult),
                               [pt_, cstb], [pt_])
                    if i >= LAG:
                        ii = i - LAG
                        pt_, q0, ncol, k0 = issued[ii]
                        mm(po, po[:, q0:NT], va, va[:, k0 // 128, :], pt_, pt_[:, 0:ncol], ii == 0, ii == nb - 1)
                r0, jc = (h % 2) * 64, h // 2
                ew("vector", lambda e, po=po: e.reciprocal(out=rec2[64:128, :], in_=po[64:128, :]), [po], [rec2])
                ew("vector", lambda e, po=po, r0=r0, jc=jc, j=j: e.tensor_tensor(
                    out=OA[r0:r0 + 64, jc, j * NT:(j + 1) * NT], in0=po[0:64, :], in1=rec2[64:128, :], op=ALU.mult),
                   [po, rec2], [OA])
        P.barrier()
        print("phase2 SBUF peak bytes/partition:", A.peak - A.BASE)
    else:
        ew("gpsimd", lambda e: e.memset(OA[:, :, :], 1.0), [], [OA])
        P.barrier()
    pm2 = A.mark()
    A.reset(persist_mark)
    OA2 = sb("OA", (128, 4, NOWN * NT), F32)
    assert A.mark() <= pm2

    if do_post:
        T3 = alloc_common()
        wo_sb = sb("wo_sb", (128, 8, D), BF16)
        load_w(wo_sb, w_o, 8, D)
        xb3 = sb("xb3", (128, 8, NT), F32)
        onb = sb("onb", (128, 4, NT), BF16)
        yv = o_y.t.rearrange("(c p) t -> p c t", p=128)
        for j in range(NOWN):
            c0 = j * NT
            P.dma("sync", lambda e, c0=c0: e.dma_start(out=xb3[:, :, :], in_=x1v[:, :, c0:c0 + NT]), key=xb3, reads=[], writes=[xb3])
            rms_rstd(T3, [OA[:, c, c0:c0 + NT] for c in range(4)], [OA] * 4, cstb[:, C_ONES:C_ONES + 128], 128, NT, 512.0)
            for c in range(4):
                norm_apply(T3, onb, onb[:, c, 0:NT], OA, OA[:, c, c0:c0 + NT], G_OMLA + c, 128, NT)
            rhs = [(onb, onb[:, c, 0:NT]) for c in range(4)] + [(YC, YC[:, c, c0:c0 + NT]) for c in range(2)] + \
                  [(OM, OM[:, c, c0:c0 + NT]) for c in range(2)]
            for c in range(8):
                y = pb[4 + c % 2]
                for k, (rb, ra) in enumerate(rhs):
                    mm(y, y[:, 0:NT], wo_sb, wo_sb[:, k, c * 128:(c + 1) * 128], rb, ra, k == 0, k == 7)
                ew("vector", lambda e, y=y, c=c: e.tensor_tensor(out=xb3[:, c, 0:NT], in0=y[:, 0:NT], in1=xb3[:, c, 0:NT], op=ALU.add),
                   [y, xb3], [xb3])
            ffn_half(T3, xb3, NT, sc2, G_FFN2)
            P.dma("sync", lambda e, c0=c0: e.dma_start(out=yv[:, :, c0:c0 + NT], in_=xb3[:, :, :]), key=xb3, reads=[xb3], writes=[])
        rhs = [(onorm_s, onorm_s[:, c, :]) for c in range(4)] + [(YCs, YCs[:, c, :]) for c in range(2)] + \
              [(OMs, OMs[:, c, :]) for c in range(2)]
        for c in range(8):
            y = pb[4 + c % 2]
            for k, (rb, ra) in enumerate(rhs):
                mm(y, y[:, 0:NS], wo_sb, wo_sb[:, k, c * 128:(c + 1) * 128], rb, ra, k == 0, k == 7)
            ew("vector", lambda e, y=y, c=c: e.tensor_tensor(out=xs[:, c, :], in0=y[:, 0:NS], in1=xs[:, c, :], op=ALU.add),
               [y, xs], [xs])
        ffn_half(T3, xs, NS, sc2, G_FFN2)
        P.dma("sync", lambda e: e.dma_start(out=o_ys.t.rearrange("(c p) t -> p c t", p=128), in_=xs[:, :, :]), key=xs, reads=[xs], writes=[])
        print("phase3 SBUF peak bytes/partition:", A.peak - A.BASE)

    P.emit()
    return nc, stack


def _rope_tables(pos):
    inv_freq = (np.float32(10000.0) ** (-np.arange(0, 32, 2, dtype=np.float32) / np.float32(32))).astype(np.float32)
    ang = (pos.astype(np.float32)[:, None] * inv_freq[None, :]).astype(np.float32)
    c = np.cos(ang.astype(np.float64)).astype(np.float32).T
    s = np.sin(ang.astype(np.float64)).astype(np.float32).T
    C = np.concatenate([c, c], axis=0)
    S = np.concatenate([-s, s], axis=0)
    return np.ascontiguousarray(C), np.ascontiguousarray(S)


def _slot_tiles(par):
    own = [2 * j + par for j in range(8)]
    oth = [2 * j + 1 - par for j in range(8)]
    return own + oth


def _constants():
    f32 = np.float32
    c = np.zeros((128, NCONST), f32)
    c[:, C_ONES:C_ONES + 128] = 1.0
    c[0:64, C_BLK64:C_BLK64 + 64] = 1.0
    c[64:128, C_BLK64 + 64:C_BLK64 + 128] = 1.0
    for m in range(32):
        c[(m + 16) % 32, C_ROT32 + m] = 1.0
    c[0:32, C_ONES32:C_ONES32 + 32] = 1.0
    c[0:64, C_BLK96:C_BLK96 + 64] = 1.0 / 64
    c[64:96, C_BLK96 + 64:C_BLK96 + 96] = 1.0 / 32
    for m in range(64, 96):
        c[64 + ((m - 64 + 16) % 32), C_ROT96 + m] = 1.0
    k = np.arange(128)[:, None]
    q = np.arange(128)[None, :]
    c[:, C_TRI:C_TRI + 128] = (k <= q).astype(f32)
    c[0:64, C_ONES64:C_ONES64 + 64] = 1.0
    c[:, C_ID:C_ID + 128] = np.eye(128, dtype=f32)
    return c


def _gains(inp, par):
    f32 = np.float32
    g = np.zeros((128, NGAIN), f32)
    v = lambda k: np.asarray(inp[k], f32)[0]
    g[:, G_FFN1:G_FFN1 + 8] = v("g_ffn1").reshape(8, 128).T
    g[:, G_MIX:G_MIX + 8] = v("g_mix").reshape(8, 128).T
    g[:, G_MEM:G_MEM + 8] = v("g_mem").reshape(8, 128).T
    g[:, G_KV] = v("g_kv_lora")
    g[0:32, G_KR] = v("g_kr")
    g[:, G_MK] = np.tile(v("g_mk"), 2)
    g[:, G_QL:G_QL + 2] = v("g_q_lora").reshape(2, 128).T
    g[0:64, G_Q96] = v("g_qn")
    g[64:96, G_Q96] = v("g_qr")
    g[0:64, G_KN] = v("g_kn")
    g[:, G_KN2] = np.tile(v("g_kn"), 2)
    cw = np.asarray(inp["conv_w"], f32)[0]
    g[:, G_CW0:G_CW0 + 2] = cw[0].reshape(2, 128).T
    g[:, G_CW1:G_CW1 + 2] = cw[1].reshape(2, 128).T
    g[:, G_CW2:G_CW2 + 2] = cw[2].reshape(2, 128).T
    g[:, G_OCONV:G_OCONV + 2] = v("g_out_conv").reshape(2, 128).T
    g[:, G_MQ] = np.tile(v("g_mq"), 2)
    g[:, G_OMEM:G_OMEM + 2] = v("g_out_mem").reshape(2, 128).T
    g[:, G_OMLA:G_OMLA + 4] = v("g_out_mla").reshape(4, 128).T
    g[:, G_FFN2:G_FFN2 + 8] = v("g_ffn2").reshape(8, 128).T
    g[:, G_FLA] = 1.0 - par
    g[:, G_FLB] = float(par)
    g[:, G_EXPB] = (par - 1) * 30000.0
    return g


def kernel(**inp):
    f32 = np.float32
    x_prompt = np.asarray(inp["x_prompt"], f32)
    n_cores = 8
    consts = _constants()
    w = lambda k: np.asarray(inp[k], f32)[0]
    shared = {
        "w1_gate": w("w1_gate"), "w1_up": w("w1_up"), "w1_down": w("w1_down"),
        "w2_gate": w("w2_gate"), "w2_up": w("w2_up"), "w2_down": w("w2_down"),
        "w_in": w("w_in"), "w_uq": w("w_uq"), "w_uk": w("w_uk").reshape(128, 512), "w_uv": w("w_uv").reshape(128, 512),
        "w_o": w("w_o"), "w_mem_k": w("w_mem_k"), "w_mem_v": w("w_mem_v"), "consts": consts,
        "w_ukT": np.ascontiguousarray(w("w_uk").transpose(2, 1, 0).reshape(64, 8 * 128)),
        "cache_ckv": np.asarray(inp["cache_ckv"], f32)[0].reshape(-1, 2048),
        "cache_krope": np.asarray(inp["cache_krope"], f32)[0].reshape(-1, 2048),
    }
    assert shared["cache_ckv"].shape[0] == N_PHYS * 8
    Cs, Ss = _rope_tables(np.full((NS,), PAST, dtype=np.int64))
    x_sample = np.asarray(inp["x_sample"], f32)
    state_conv = np.asarray(inp["state_conv"], f32)[0]
    page_table = np.asarray(inp["page_table"]).astype(np.int32)
    cache_mem_k = np.asarray(inp["cache_mem_k"], f32)[0]
    cache_mem_v = np.asarray(inp["cache_mem_v"], f32)[0]
    in_maps = []
    for c in range(n_cores):
        b, par = c // 2, c % 2
        tiles = _slot_tiles(par)
        tok = np.concatenate([np.arange(t * NT, (t + 1) * NT) for t in tiles])
        C, S = _rope_tables(tok)
        m = dict(shared)
        m["xT"] = np.ascontiguousarray(x_prompt[b][tok].T)
        m["memT"] = np.ascontiguousarray(np.asarray(inp["mem_prompt"], f32)[b].T)
        m["ropeC"] = C
        m["ropeS"] = S
        m["gains"] = _gains(inp, par)
        sl = slice(c * NS, (c + 1) * NS)
        m["xsT"] = np.ascontiguousarray(x_sample[sl, 0, :].T)
        m["ropeCs"] = Cs
        m["ropeSs"] = Ss
        m["stateT"] = np.ascontiguousarray(state_conv[sl].transpose(2, 1, 0))
        m["ptT"] = np.ascontiguousarray(page_table[sl].T)
        m["cache_mem_k"] = np.ascontiguousarray(cache_mem_k[sl].reshape(NS, N_MEM, 256))
        m["cache_mem_v"] = np.ascontiguousarray(cache_mem_v[sl].reshape(NS, N_MEM, 256))
        in_maps.append(m)

    nc, stack = build_program()
    with stack:
        res = run_bass_kernel_spmd(nc, in_maps, core_ids=list(range(n_cores)))
    R = res.results

    B = 4
    y_prompt = np.zeros((B, SEQ, D), f32)
    y_sample = np.zeros((DEC, 1, D), f32)
    ckv_p = np.zeros((1, B, SEQ, 128), f32)
    kr_p = np.zeros((1, B, SEQ, 32), f32)
    conv_p = np.zeros((1, B, 2, 256), f32)
    mk_p = np.zeros((1, B, N_MEM, 4, 64), f32)
    mv_p = np.zeros((1, B, N_MEM, 4, 64), f32)
    for c in range(n_cores):
        b, par = c // 2, c % 2
        tiles = _slot_tiles(par)
        for s in range(8):
            t = tiles[s]
            y_prompt[b, t * NT:(t + 1) * NT, :] = R[c]["o_y"][:, s * NT:(s + 1) * NT].T
            ckv_p[0, b, t * NT:(t + 1) * NT, :] = R[c]["o_ckv"][:, s * NT:(s + 1) * NT].T
            kr_p[0, b, t * NT:(t + 1) * NT, :] = R[c]["o_kr"][:, s * NT:(s + 1) * NT].T
            if t == 15:
                conv_p[0, b] = R[c]["o_conv"][:, 2 * s:2 * s + 2].T
        if par == 0:
            mk_p[0, b] = R[c]["o_mk"].T.reshape(N_MEM, 4, 64)
            mv_p[0, b] = R[c]["o_mv"].T.reshape(N_MEM, 4, 64)
    ckv_s = np.zeros((1, DEC, 1, 128), f32)
    kr_s = np.zeros((1, DEC, 1, 32), f32)
    conv_s = np.zeros((1, DEC, 2, 256), f32)
    for c in range(n_cores):
        sl = slice(c * NS, (c + 1) * NS)
        y_sample[sl, 0, :] = R[c]["o_ys"].T
        ckv_s[0, sl, 0, :] = R[c]["o_ckv_s"].T
        kr_s[0, sl, 0, :] = R[c]["o_kr_s"].T
        conv_s[0, sl, 0, :] = R[c]["o_state_s"].T
        conv_s[0, sl, 1, :] = R[c]["o_conv_s"].T
    return (y_prompt, y_sample, ckv_p, kr_p, conv_p, mk_p, mv_p, ckv_s, kr_s, conv_s)
```
